# Optimizing a Trainium2 kernel written in Bass

```python
import math
import jax, jax.numpy as jnp
from jax import lax
import numpy as np

D_MODEL = 1024
BATCH = 4
SEQ = 8192
DEPTH = 4
DEC_BATCH = 8
DEC_SEQ = 4096
PAST_LEN = 128

N_MIXERS = 3
ROPE_THETA = 500000.0
NORM_EPS = 1e-6

A_HEADS = 8
A_HEAD_DIM = D_MODEL // (2 * A_HEADS)
A_V_DIM = 2 * A_HEAD_DIM
A_QK_WIDTH = 2 * A_HEADS * A_HEAD_DIM
A_V_WIDTH = A_HEADS * A_V_DIM
A_Q_BLOCK = 128
A_LAYERS = (DEPTH + 2) // N_MIXERS

B_HEADS = 16
B_HEAD_DIM = D_MODEL // B_HEADS
B_PATTERNS = ((128, 1), (512, 4), (2048, 16))
B_GROUPS = len(B_PATTERNS)
B_LAYERS = (DEPTH + 1) // N_MIXERS

C_EXPAND = 128
C_HEADS = D_MODEL // C_EXPAND
C_KEY_DIM = C_EXPAND
C_VAL_DIM = D_MODEL // C_HEADS
C_WIDTH = C_HEADS * C_KEY_DIM
C_VWIDTH = C_HEADS * C_VAL_DIM
C_CHUNK = 64
C_LAYERS = DEPTH // N_MIXERS

D_FF = 2816
CONV_WIDTH = 3

kernel_name = 'hybrid_diffattn_dilated_hgrn2_encoder'


def rms_norm(x, g, eps=NORM_EPS):
    xf = x.astype(jnp.float32)
    y = xf * lax.rsqrt(jnp.mean(xf * xf, axis=-1, keepdims=True) + eps)
    return (y * g.astype(jnp.float32)).astype(x.dtype)


def rope_partial(x, pos):
    hd = x.shape[-1]
    rot = hd // 4
    half = rot // 2
    inv_freq = ROPE_THETA ** (-jnp.arange(half, dtype=jnp.float32) / half)
    ang = pos[:, None] * inv_freq[None, :]
    cos = jnp.cos(ang)[:, None, :]
    sin = jnp.sin(ang)[:, None, :]
    xf = x.astype(jnp.float32)
    x1, x2, rest = xf[..., :half], xf[..., half:rot], xf[..., rot:]
    out = jnp.concatenate([x1 * cos - x2 * sin, x2 * cos + x1 * sin, rest], axis=-1)
    return out.astype(x.dtype)


def diff_lambda_init(layer):
    return 0.8 - 0.6 * math.exp(-0.3 * layer)


def diff_attention(h, w_qkv, lam_vecs, subln_g, w_o, pos, lam_init):
    B, S, _ = h.shape
    q, k, v = jnp.split(h @ w_qkv, [A_QK_WIDTH, 2 * A_QK_WIDTH], axis=-1)
    q = rope_partial(q.reshape(B, S, 2 * A_HEADS, A_HEAD_DIM), pos)
    k = rope_partial(k.reshape(B, S, 2 * A_HEADS, A_HEAD_DIM), pos)
    q = q.reshape(B, S, A_HEADS, 2, A_HEAD_DIM).transpose(0, 2, 3, 1, 4) * (A_HEAD_DIM ** -0.5)
    k = k.reshape(B, S, A_HEADS, 2, A_HEAD_DIM).transpose(0, 2, 3, 1, 4)
    v = v.reshape(B, S, A_HEADS, A_V_DIM).transpose(0, 2, 1, 3)
    lf = lam_vecs.astype(jnp.float32)
    lam = jnp.exp(jnp.sum(lf[0] * lf[1])) - jnp.exp(jnp.sum(lf[2] * lf[3])) + lam_init
    n_blk = S // A_Q_BLOCK
    q_blocks = jnp.moveaxis(q.reshape(B, A_HEADS, 2, n_blk, A_Q_BLOCK, A_HEAD_DIM), 3, 0)

    def attend(qb):
        s = jnp.einsum('bhiqd,bhikd->bhiqk', qb, k).astype(jnp.float32)
        p = jax.nn.softmax(s, axis=-1)
        a = p[:, :, 0] - lam * p[:, :, 1]
        return jnp.einsum('bhqk,bhkv->bhqv', a.astype(v.dtype), v)

    o = lax.map(attend, q_blocks)
    o = jnp.moveaxis(o, 0, 2).reshape(B, A_HEADS, S, A_V_DIM)
    o = rms_norm(o, subln_g, 1e-5) * (1.0 - lam_init)
    o = o.transpose(0, 2, 1, 3).reshape(B, S, A_V_WIDTH)
    return o @ w_o


def dilated_band_attention(q, k, v, dilation, half_w):
    B, S, H, hd = q.shape
    L = S // dilation
    nb = -(-L // half_w)
    Lp = nb * half_w

    def to_residue(t):
        t = t.reshape(B, L, dilation, H, t.shape[-1]).transpose(0, 2, 3, 1, 4)
        return jnp.pad(t, ((0, 0), (0, 0), (0, 0), (0, Lp - L), (0, 0)))

    qb = to_residue(q).reshape(B, dilation, H, nb, half_w, hd)

    def band(t):
        tp = jnp.pad(t, ((0, 0), (0, 0), (0, 0), (half_w, half_w), (0, 0)))
        tp = tp.reshape(B, dilation, H, nb + 2, half_w, hd)
        return jnp.concatenate([tp[:, :, :, :-2], tp[:, :, :, 1:-1], tp[:, :, :, 2:]], axis=4)

    kb = band(to_residue(k))
    vb = band(to_residue(v))
    s = jnp.einsum('brhnqe,brhnke->brhnqk', qb, kb).astype(jnp.float32) * (hd ** -0.5)
    qi = jnp.arange(nb)[:, None] * half_w + jnp.arange(half_w)[None, :]
    ki = jnp.arange(nb)[:, None] * half_w - half_w + jnp.arange(3 * half_w)[None, :]
    valid = ((jnp.abs(qi[:, :, None] - ki[:, None, :]) <= half_w)
             & (ki[:, None, :] >= 0) & (ki[:, None, :] < L))
    s = jnp.where(valid, s, -jnp.inf)
    lse = jax.nn.logsumexp(s, axis=-1)
    p = jnp.exp(s - lse[..., None])
    o = jnp.einsum('brhnqk,brhnke->brhnqe', p.astype(v.dtype), vb)

    def from_residue(t):
        c = t.shape[-1]
        t = t.reshape(B, dilation, H, Lp, c)[:, :, :, :L]
        return t.transpose(0, 3, 1, 2, 4).reshape(B, S, H, c)

    return from_residue(o), from_residue(lse[..., None])[..., 0]


def dilated_mixture_attention(h, w_qkv, w_o, pos):
    B, S, _ = h.shape
    qkv = (h @ w_qkv).reshape(B, S, B_GROUPS, 3, B_HEADS, B_HEAD_DIM)
    outs, lses = [], []
    for g, (window, dilation) in enumerate(B_PATTERNS):
        q = rope_partial(qkv[:, :, g, 0], pos)
        k = rope_partial(qkv[:, :, g, 1], pos)
        o, lse = dilated_band_attention(q, k, qkv[:, :, g, 2], dilation, window // (2 * dilation))
        outs.append(o)
        lses.append(lse)
    wts = jax.nn.softmax(jnp.stack(lses, axis=0), axis=0)
    o = jnp.einsum('gbsh,gbshe->bshe', wts, jnp.stack(outs, axis=0).astype(jnp.float32))
    return o.reshape(B, S, B_HEADS * B_HEAD_DIM).astype(h.dtype) @ w_o


def hgrn_lower_bounds(lb_logits):
    p = jax.nn.softmax(lb_logits.astype(jnp.float32), axis=0)
    return jnp.cumsum(p, axis=0) - p[0]


def gla_chunk_scan(q, k, v, log_f):
    B, T, H, K = q.shape
    V = v.shape[-1]
    n = T // C_CHUNK

    def split(t):
        return t.reshape(B, n, C_CHUNK, H, t.shape[-1]).transpose(1, 0, 3, 2, 4)

    causal = jnp.tril(jnp.ones((C_CHUNK, C_CHUNK), dtype=bool))[:, :, None]

    def step(state, blk):
        qb, kb, vb, gb = blk
        G = jnp.cumsum(gb, axis=2)
        o_inter = jnp.einsum('bhtk,bhkv->bhtv', qb * jnp.exp(G), state)
        decay = jnp.exp(jnp.where(causal, G[:, :, :, None, :] - G[:, :, None, :, :], -jnp.inf))
        scores = jnp.einsum('bhtk,bhsk,bhtsk->bhts', qb, kb, decay)
        o_intra = jnp.einsum('bhts,bhsv->bhtv', scores, vb)
        G_end = G[:, :, -1]
        state = (jnp.exp(G_end)[..., None] * state
                 + jnp.einsum('bhsk,bhsv->bhkv', kb * jnp.exp(G_end[:, :, None] - G), vb))
        return state, o_inter + o_intra

    state0 = jnp.zeros((B, H, K, V), jnp.float32)
    _, o = lax.scan(step, state0, (split(q), split(k), split(v), split(log_f)))
    return o.transpose(1, 0, 3, 2, 4).reshape(B, T, H, V)


def hgrn2_bidirectional(h, w_in, lb, gnorm_g, w_o):
    B, S, _ = h.shape
    q, f_fw, f_bw, i, g = jnp.split(
        (h @ w_in).astype(jnp.float32),
        [C_WIDTH, 2 * C_WIDTH, 3 * C_WIDTH, 3 * C_WIDTH + C_VWIDTH], axis=-1)

    def heads(t, d):
        return t.reshape(B, S, C_HEADS, d)

    q = heads(jax.nn.silu(q), C_KEY_DIM) * (C_KEY_DIM ** -0.5)
    v = heads(i, C_VAL_DIM)
    log_lb = jnp.log(lb)
    log_1m_lb = jnp.log1p(-lb)

    def scan_dir(fz, reverse):
        log_f = heads(jnp.logaddexp(log_lb, log_1m_lb + jax.nn.log_sigmoid(fz)), C_KEY_DIM)
        k = -jnp.expm1(log_f)
        flip = (lambda t: t[:, ::-1]) if reverse else (lambda t: t)
        return flip(gla_chunk_scan(flip(q), flip(k), flip(v), flip(log_f)))

    o = scan_dir(f_fw, False) + scan_dir(f_bw, True)
    o = rms_norm(o, gnorm_g) * jax.nn.silu(heads(g, C_VAL_DIM))
    return o.reshape(B, S, C_VWIDTH).astype(h.dtype) @ w_o


def conv_glu_ffn(x, w_in, conv_w, conv_b, w_out):
    S = x.shape[1]
    u = x @ w_in
    pad = CONV_WIDTH // 2
    up = jnp.pad(u, ((0, 0), (pad, pad), (0, 0)))
    acc = conv_b
    for t in range(CONV_WIDTH):
        acc = acc + up[:, t:t + S] * conv_w[t]
    a, b = jnp.split(acc, 2, axis=-1)
    return (jax.nn.gelu(a, approximate=True) * b) @ w_out


def trunk(x, norm_g, a_w_qkv, a_lambda, a_subln_g, a_w_o, b_w_qkv, b_w_o,
          c_w_in, c_lb_logits, c_gnorm_g, c_w_o, f_w_in, f_conv_w, f_conv_b, f_w_out):
    S = x.shape[1]
    pos = jnp.arange(S, dtype=jnp.float32)
    lbs = hgrn_lower_bounds(c_lb_logits)
    for layer in range(DEPTH):
        kind = layer % N_MIXERS
        j = layer // N_MIXERS
        h = rms_norm(x, norm_g[layer, 0])
        if kind == 0:
            h = diff_attention(h, a_w_qkv[j], a_lambda[j], a_subln_g[j], a_w_o[j], pos,
                               diff_lambda_init(layer))
        elif kind == 1:
            h = dilated_mixture_attention(h, b_w_qkv[j], b_w_o[j], pos)
        else:
            h = hgrn2_bidirectional(h, c_w_in[j], lbs[layer], c_gnorm_g[j], c_w_o[j])
        x = x + rms_norm(h, norm_g[layer, 1])
        h = rms_norm(x, norm_g[layer, 2])
        h = conv_glu_ffn(h, f_w_in[layer], f_conv_w[layer], f_conv_b[layer], f_w_out[layer])
        x = x + rms_norm(h, norm_g[layer, 3])
    return x


def setup_inputs(seed: int = 0) -> dict:
    key = jax.random.key(seed)
    ks = jax.random.split(key, 17)

    def nrm(k, shape, scale):
        return jax.random.normal(k, shape, jnp.float32) * scale

    return {
        'x_prompt': nrm(ks[0], (BATCH, SEQ, D_MODEL), 1.0),
        'x_sample': nrm(ks[1], (DEC_BATCH, DEC_SEQ, D_MODEL), 1.0),
        'norm_g': 1.0 + nrm(ks[2], (DEPTH, 4, D_MODEL), 0.02),
        'a_w_qkv': nrm(ks[3], (A_LAYERS, D_MODEL, 2 * A_QK_WIDTH + A_V_WIDTH), D_MODEL ** -0.5),
        'a_lambda': nrm(ks[4], (A_LAYERS, 4, A_HEAD_DIM), 0.1),
        'a_subln_g': 1.0 + nrm(ks[5], (A_LAYERS, A_V_DIM), 0.02),
        'a_w_o': nrm(ks[6], (A_LAYERS, A_V_WIDTH, D_MODEL), A_V_WIDTH ** -0.5),
        'b_w_qkv': nrm(ks[7], (B_LAYERS, D_MODEL, B_GROUPS * 3 * B_HEADS * B_HEAD_DIM), D_MODEL ** -0.5),
        'b_w_o': nrm(ks[8], (B_LAYERS, B_HEADS * B_HEAD_DIM, D_MODEL), (B_HEADS * B_HEAD_DIM) ** -0.5),
        'c_w_in': nrm(ks[9], (C_LAYERS, D_MODEL, 3 * C_WIDTH + 2 * C_VWIDTH), D_MODEL ** -0.5),
        'c_lb_logits': nrm(ks[10], (DEPTH, C_WIDTH), 0.1),
        'c_gnorm_g': 1.0 + nrm(ks[11], (C_LAYERS, C_VAL_DIM), 0.02),
        'c_w_o': nrm(ks[12], (C_LAYERS, C_VWIDTH, D_MODEL), C_VWIDTH ** -0.5),
        'f_w_in': nrm(ks[13], (DEPTH, D_MODEL, 2 * D_FF), D_MODEL ** -0.5),
        'f_conv_w': nrm(ks[14], (DEPTH, CONV_WIDTH, 2 * D_FF), CONV_WIDTH ** -0.5),
        'f_conv_b': nrm(ks[15], (DEPTH, 2 * D_FF), 0.01),
        'f_w_out': nrm(ks[16], (DEPTH, D_FF, D_MODEL), D_FF ** -0.5),
    }


def reference(x_prompt, x_sample, norm_g, a_w_qkv, a_lambda, a_subln_g, a_w_o, b_w_qkv, b_w_o,
              c_w_in, c_lb_logits, c_gnorm_g, c_w_o, f_w_in, f_conv_w, f_conv_b, f_w_out):
    y_prompt = trunk(x_prompt, norm_g, a_w_qkv, a_lambda, a_subln_g, a_w_o, b_w_qkv, b_w_o,
                     c_w_in, c_lb_logits, c_gnorm_g, c_w_o, f_w_in, f_conv_w, f_conv_b, f_w_out)
    y_sample = trunk(x_sample, norm_g, a_w_qkv, a_lambda, a_subln_g, a_w_o, b_w_qkv, b_w_o,
                     c_w_in, c_lb_logits, c_gnorm_g, c_w_o, f_w_in, f_conv_w, f_conv_b, f_w_out)
    return (y_prompt, y_sample)
```

```python
import math
from contextlib import ExitStack
import numpy as np
import concourse.bass as bass
import concourse.mybir as mybir
from concourse.bass_utils import run_bass_kernel_spmd

F32 = mybir.dt.float32
BF16 = mybir.dt.bfloat16
AF = mybir.ActivationFunctionType
ALU = mybir.AluOpType
AX = mybir.AxisListType

D = 1024
DFF = 2816
NEG = -30000.0
B_DIL = (1, 4, 16)


def _bmask_list():
    out = []
    for g, d in enumerate(B_DIL):
        for delta in range(-4096, 4097, 128):
            q = np.arange(512)[None, :]
            k = np.arange(128)[:, None]
            diff = delta + q - k
            if np.any((diff % d == 0) & (np.abs(diff) <= 64 * d)):
                out.append((g, delta))
    return out


BMASKS = _bmask_list()


def bmask_table():
    t = np.zeros((128, len(BMASKS), 512), np.float32)
    q = np.arange(512)[None, :]
    k = np.arange(128)[:, None]
    for i, (g, delta) in enumerate(BMASKS):
        d = B_DIL[g]
        diff = delta + q - k
        t[:, i, :] = np.where((diff % d == 0) & (np.abs(diff) <= 64 * d), 0.0, NEG)
    return t


class Stream:
    def __init__(self, name, eng, sem):
        self.name, self.eng, self.sem = name, eng, sem
        self.count = 0
        self.known = {}


class Buf:
    def __init__(self, t, name):
        self.t = t
        self.name = name
        self.w = {}
        self.r = {}
        self.isdram = False
        self.excl = False
        self.lsem = None
        self.lcount = 0
        self.ssem = None
        self.scount = 0

    def __getitem__(self, idx):
        return self.t[idx]


class KB:
    def __init__(self, nc):
        self.nc = nc
        self.top = ExitStack()
        self.sems = []
        self.free_dsems = []
        self.all_dsems = {}
        mk = lambda n, e: Stream(n, e, self.top.enter_context(nc.semaphore("s_" + n)))
        self.pe = mk("pe", nc.tensor)
        self.act = mk("act", nc.scalar)
        self.dve = mk("dve", nc.vector)
        self.pool = mk("pool", nc.gpsimd)
        import os
        self.ew2 = self.dve if os.environ.get("KPOOL", "dve") == "dve" else self.pool
        self.sp = mk("sp", nc.sync)
        self.streams = [self.pe, self.act, self.dve, self.pool, self.sp]
        self.phase = None
        self.phase_bufs = []
        self.uid = 0

    def begin_phase(self):
        self.phase = ExitStack()
        self.phase_bufs = []

    def end_phase(self):
        self.barrier()
        for b in self.phase_bufs:
            for s, c in ((b.lsem, b.lcount), (b.ssem, b.scount)):
                if s is not None:
                    self.free_dsems.append((s, c))
        self.phase.close()
        self.phase = None

    def sb(self, name, shape, dt, glob=False):
        self.uid += 1
        es = self.top if glob else self.phase
        t = es.enter_context(self.nc.sbuf_tensor(f"{name}_{self.uid}", list(shape), dt))
        b = Buf(t, name)
        if not glob:
            self.phase_bufs.append(b)
        return b

    def ps(self, name, shape, dt=F32):
        self.uid += 1
        t = self.phase.enter_context(self.nc.psum_tensor(f"{name}_{self.uid}", list(shape), dt))
        b = Buf(t, name)
        b.excl = True
        return b

    def dram(self, name, shape, dt):
        t = self.nc.dram_tensor(name, list(shape), dt, kind="Internal")
        b = Buf(t.ap(), name)
        b.isdram = True
        return b

    def _dsem(self):
        if self.free_dsems:
            return self.free_dsems.pop()
        s = self.top.enter_context(self.nc.semaphore(f"d{len(self.all_dsems)}"))
        self.all_dsems[id(s)] = s
        return (s, 0)

    def _deps(self, st, reads, writes, skip=None):
        need = {}

        def add(ev):
            s, v = ev
            if id(s) not in need or need[id(s)][1] < v:
                need[id(s)] = ev
        for b in reads:
            for ev in b.w.values():
                add(ev)
            if b.excl:
                for ev in b.r.values():
                    if ev[0] is not st.sem:
                        add(ev)
        for b in writes:
            if not b.isdram:
                for ev in b.w.values():
                    add(ev)
            for ev in b.r.values():
                add(ev)
        for s, v in need.values():
            if skip is not None and s is skip:
                continue
            if s is st.sem and st is self.pe:
                continue
            if st.known.get(id(s), 0) < v:
                st.eng.wait_ge(s, v)
                st.known[id(s)] = v

    def _mark(self, ev, reads, writes):
        for b in writes:
            if b.isdram:
                b.w[id(ev[0])] = ev
            else:
                b.w = {id(ev[0]): ev}
                b.r = {}
        for b in reads:
            if b not in writes:
                b.r[id(ev[0])] = ev

    def op(self, st, fn, reads=(), writes=()):
        reads, writes = list(reads), list(writes)
        self._deps(st, reads, writes)
        ins = fn(st.eng)
        st.count += 1
        ins.then_inc(st.sem, 1)
        self._mark((st.sem, st.count), reads, writes)
        return ins

    def load(self, sbuf, out_ap, dbuf, in_ap, st=None):
        st = st or self.sp
        if sbuf.lsem is None:
            sbuf.lsem, sbuf.lcount = self._dsem()
        self._deps(st, [dbuf], [sbuf], skip=sbuf.lsem)
        sbuf.lcount += 16
        st.eng.dma_start(out=out_ap, in_=in_ap).then_inc(sbuf.lsem, 16)
        self._mark((sbuf.lsem, sbuf.lcount), [dbuf], [sbuf])

    def store(self, dbuf, out_ap, sbuf, in_ap, st=None):
        st = st or self.pool
        if sbuf.ssem is None:
            sbuf.ssem, sbuf.scount = self._dsem()
        self._deps(st, [sbuf], [dbuf], skip=sbuf.ssem)
        sbuf.scount += 16
        st.eng.dma_start(out=out_ap, in_=in_ap).then_inc(sbuf.ssem, 16)
        self._mark((sbuf.ssem, sbuf.scount), [sbuf], [dbuf])

    def barrier(self):
        cur = [(st.sem, st.count) for st in self.streams]
        for b in self.phase_bufs:
            if b.lsem is not None:
                cur.append((b.lsem, b.lcount))
            if b.ssem is not None:
                cur.append((b.ssem, b.scount))
        for st in self.streams:
            for s, v in cur:
                if s is st.sem or v == 0:
                    continue
                if st.known.get(id(s), 0) < v:
                    st.eng.wait_ge(s, v)
                    st.known[id(s)] = v


def build(T=8192, layers=(0, 1, 2, 3)):
    nc = bass.Bass("TRN2", target_bir_lowering=False)
    kb = KB(nc)
    NT = T // 512
    NKC = T // 128
    HALF = T // 2

    def din(name, shape, dt=F32):
        b = Buf(nc.dram_tensor(name, list(shape), dt, kind="ExternalInput").ap(), name)
        b.isdram = True
        return b

    xT_in = din("xT", [D, T])
    yT_out = Buf(nc.dram_tensor("yT", [D, T], F32, kind="ExternalOutput").ap(), "yT")
    yT_out.isdram = True
    normg_in = din("normg", [128, 4 * 4 * 8])
    rope_in = din("rope", [128, 3, T])
    cmat_in = din("cmat", [128, 6, 128])
    flag_in = din("flag", [128, 2])
    abias_in = din("abias", [128, NKC * NT])
    a_wqkv = din("a_w_qkv", [2, D, 3072])
    a_lam = din("a_lam", [128, 2 * 256])
    a_sub = din("a_sub", [128, 2])
    a_wo = din("a_w_o", [2, D, D])
    b_wqkv = din("b_w_qkv", [1, D, 9216])
    b_wo = din("b_w_o", [1, D, D])
    bmask_in = din("bmask", [128, len(BMASKS), 512])
    c_win = din("c_w_in", [1, D, 5120])
    c_wo = din("c_w_o", [1, D, D])
    c_lbl = din("c_lbl", [128, 4, 8])
    c_gn = din("c_gn", [128, 1])
    m01_in = din("m01", [128, 512])
    f_win = din("f_w_in", [4, D, 2 * DFF])
    f_cw = din("f_cw", [128, 4 * 44 * 4])
    f_wout = din("f_w_out", [4, DFF, D])

    kb.begin_phase()
    ones_bf = kb.sb("ones", [128, 128], BF16, glob=True)
    ident_bf = kb.sb("ident", [128, 128], BF16, glob=True)
    rperm_bf = kb.sb("rperm", [128, 128], BF16, glob=True)
    normg = kb.sb("normg", [128, 128], F32, glob=True)
    flag = kb.sb("flag", [128, 2], F32, glob=True)
    cv = kb.sb("cv", [128, 4], F32, glob=True)
    cmask = kb.sb("cmask", [128, 2, 128], F32, glob=True)
    m01 = kb.sb("m01", [128, 512], F32, glob=True)
    cst = kb.sb("cst", [128, 6, 128], F32)
    kb.load(cst, cst[:], cmat_in, cmat_in[:])
    kb.load(normg, normg[:], normg_in, normg_in[:])
    kb.load(flag, flag[:], flag_in, flag_in[:])
    kb.op(kb.dve, lambda e: e.memset(cv[:, 0:1], 0.125), [], [cv])
    kb.op(kb.dve, lambda e: e.memset(cv[:, 1:2], 1.0), [cv], [cv])
    kb.op(kb.dve, lambda e: e.memset(cv[:, 2:3], float(128 ** -0.5)), [cv], [cv])
    kb.load(m01, m01[:], m01_in, m01_in[:])
    kb.op(kb.dve, lambda e: e.tensor_copy(out=cmask[:], in_=cst[:, 4:6, :]), [cst], [cmask])
    kb.op(kb.dve, lambda e: e.tensor_copy(out=ones_bf[:], in_=cst[:, 0, :]), [cst], [ones_bf])
    kb.op(kb.dve, lambda e: e.tensor_copy(out=ident_bf[:], in_=cst[:, 1, :]), [cst], [ident_bf])
    kb.op(kb.dve, lambda e: e.tensor_copy(out=rperm_bf[:], in_=cst[:, 2, :]), [cst], [rperm_bf])
    kb.end_phase()

    def gcol(layer, n, c):
        i = (layer * 4 + n) * 8 + c
        return normg[:, i:i + 1]

    cast_rr = [0]

    def load_cast_w(wdst, dst_ap_fn, src_buf, src_ap_fn, ncols, stg, kc=8, blk=512):
        for c0 in range(0, ncols, blk):
            c1 = min(ncols, c0 + blk)
            s = stg[cast_rr[0] % len(stg)]
            kb.load(s, s[:, :kc, :c1 - c0], src_buf, src_ap_fn(c0, c1))
            st = kb.dve if cast_rr[0] % 2 == 0 else kb.act
            if st is kb.dve:
                kb.op(st, lambda e: e.tensor_copy(out=dst_ap_fn(c0, c1), in_=s[:, :kc, :c1 - c0]), [s], [wdst])
            else:
                kb.op(st, lambda e: e.activation(out=dst_ap_fn(c0, c1), in_=s[:, :kc, :c1 - c0], func=AF.Copy), [s], [wdst])
            cast_rr[0] += 1

    def rstd_from(ps_stat, n, nfeat, eps, rstd, tmp):
        kb.op(kb.act, lambda e: e.activation(out=tmp[:, :n], in_=ps_stat[:, :n], func=AF.Sqrt,
                                             bias=float(eps), scale=1.0 / nfeat), [ps_stat], [tmp])
        kb.op(kb.dve, lambda e: e.reciprocal(out=rstd[:, :n], in_=tmp[:, :n]), [tmp], [rstd])

    def phase_proj_A(layer, wbuf, wap, x_d, QT_d, KT_d, V_d):
        kb.begin_phase()
        W = kb.sb("Wqkv", [128, 8, 3072], BF16)
        stg = [kb.sb(f"stg{i}", [128, 8, 256], F32) for i in range(2)]
        wsrc = wap.rearrange("(c p) n -> p c n", p=128)
        load_cast_w(W, lambda a, b: W[:, :, a:b], wbuf, lambda a, b: wsrc[:, :, a:b], 3072, stg, blk=256)
        xs = [kb.sb(f"xs{i}", [128, 8, 512], F32) for i in range(2)]
        rp = [kb.sb(f"rp{i}", [128, 3, 512], F32) for i in range(2)]
        hT = [kb.sb(f"hT{i}", [128, 8, 512], BF16) for i in range(2)]
        sq = [kb.sb(f"sq{i}", [128, 512], BF16) for i in range(2)]
        tmp = kb.sb("tmp", [128, 512], F32)
        rstd = kb.sb("rstd", [128, 512], F32)
        qsb = [kb.sb(f"qsb{i}", [128, 512], BF16) for i in range(2)]
        t1 = [kb.sb(f"t1{i}", [128, 512], F32) for i in range(2)]
        t2 = [kb.sb(f"t2{i}", [128, 512], F32) for i in range(2)]
        qo = [kb.sb(f"qo{i}", [128, 8, 512], BF16) for i in range(2)]
        ko = [kb.sb(f"ko{i}", [128, 8, 512], BF16) for i in range(2)]
        vo = [kb.sb(f"vo{i}", [128, 4, 1024], BF16) for i in range(2)]
        pst = kb.ps("pst", [128, 512])
        pq = [kb.ps(f"pq{i}", [128, 512]) for i in range(3)]
        pr = [kb.ps(f"pr{i}", [128, 512]) for i in range(2)]
        xv = x_d[:].rearrange("(c p) t -> p c t", p=128)
        Vv = V_d[:].rearrange("(n p) f -> p n f", p=128)
        QTv = QT_d[:].rearrange("(c p) t -> p c t", p=128)
        KTv = KT_d[:].rearrange("(c p) t -> p c t", p=128)

        def ld(tt):
            s = tt % 2
            kb.load(xs[s], xs[s][:], x_d, xv[:, :, tt * 512:(tt + 1) * 512])
            kb.load(rp[s], rp[s][:], rope_in, rope_in[:, :, tt * 512:(tt + 1) * 512])

        import os
        KSUB = int(os.environ.get("KSUB", "99"))
        if KSUB < 1:
            kb.end_phase()
            return
        ld(0)
        cnt = 0
        for tt in range(NT):
            s = tt % 2
            if tt + 1 < NT:
                ld(tt + 1)
            x, h = xs[s], hT[s]
            if KSUB < 2:
                continue
            for c in range(8):
                q = sq[c % 2]
                kb.op(kb.act, lambda e: e.activation(out=q[:], in_=x[:, c, :], func=AF.Square), [x], [q])
                kb.op(kb.pe, lambda e: e.matmul(pst[:], lhsT=ones_bf[:], rhs=q[:], start=(c == 0), stop=(c == 7)),
                      [ones_bf, q], [pst])
            if KSUB < 3:
                continue
            rstd_from(pst, 512, D, 1e-6, rstd, tmp)
            if KSUB < 4:
                continue
            for c in range(8):
                kb.op(kb.dve, lambda e: e.scalar_tensor_tensor(out=h[:, c, :], in0=x[:, c, :], scalar=gcol(layer, 0, c),
                                                                 in1=rstd[:], op0=ALU.mult, op1=ALU.mult),
                      [x, rstd, normg], [h])
            if KSUB < 5:
                continue
            for m in range(16):
                p = pq[cnt % 3]
                r_ = pr[cnt % 2]
                qs, a1, a2 = qsb[cnt % 2], t1[cnt % 2], t2[cnt % 2]
                dst = (qo if m < 8 else ko)[s]
                scale = 0.125 if m < 8 else 1.0
                cnt += 1

                def mm(e):
                    for k in range(8):
                        i = e.matmul(p[:], lhsT=W[:, k, m * 128:(m + 1) * 128], rhs=h[:, k, :], start=(k == 0), stop=(k == 7))
                    return i
                KQ = int(os.environ.get("KQ", "99"))
                kb.op(kb.pe, mm, [W, h], [p])
                if KQ < 2:
                    continue
                kb.op(kb.act, lambda e: e.activation(out=qs[:], in_=p[:], func=AF.Copy, scale=scale), [p], [qs])
                if KQ < 3:
                    continue
                kb.op(kb.pe, lambda e: e.matmul(r_[:], lhsT=rperm_bf[:], rhs=qs[:], start=True, stop=True), [rperm_bf, qs], [r_])
                if KQ < 4:
                    continue
                kb.op(kb.dve, lambda e: e.tensor_tensor(out=a1[:], in0=p[:], in1=rp[s][:, (2 if m < 8 else 0), :], op=ALU.mult), [p, rp[s]], [a1])
                if KQ < 5:
                    continue
                kb.op(kb.dve, lambda e: e.tensor_tensor(out=a2[:], in0=r_[:], in1=rp[s][:, 1, :], op=ALU.mult), [r_, rp[s]], [a2])
                if KQ < 6:
                    continue
                kb.op(kb.ew2, lambda e: e.tensor_tensor(out=dst[:, m % 8, :], in0=a1[:], in1=a2[:], op=ALU.add), [a1, a2], [dst])
            if KSUB < 6:
                continue
            kb.store(QT_d, QTv[:, :, tt * 512:(tt + 1) * 512], qo[s], qo[s][:])
            kb.store(KT_d, KTv[:, :, tt * 512:(tt + 1) * 512], ko[s], ko[s][:])
            if KSUB < 7:
                continue
            for tb in range(4):
                for nb in range(2):
                    p = pq[cnt % 3]
                    cnt += 1

                    def mm(e):
                        for k in range(8):
                            i = e.matmul(p[:], lhsT=h[:, k, tb * 128:(tb + 1) * 128],
                                         rhs=W[:, k, 2048 + nb * 512:2048 + (nb + 1) * 512], start=(k == 0), stop=(k == 7))
                        return i
                    kb.op(kb.pe, mm, [W, h], [p])
                    kb.op(kb.act, lambda e: e.activation(out=vo[s][:, tb, nb * 512:(nb + 1) * 512], in_=p[:], func=AF.Copy), [p], [vo[s]])
            kb.store(V_d, Vv[:, tt * 4:(tt + 1) * 4, :], vo[s], vo[s][:])
        kb.end_phase()

    def phase_attn_A(layer, j, QT_d, KT_d, V_d, oT_d):
        kb.begin_phase()
        lam_init = 0.8 - 0.6 * math.exp(-0.3 * layer)
        NQT = NT
        QTh = [kb.sb(f"QTh{i}", [128, T], BF16) for i in range(2)]
        KTh = [kb.sb(f"KTh{i}", [128, T], BF16) for i in range(2)]
        Vh = [kb.sb(f"Vh{i}", [128, NKC, 128], BF16) for i in range(2)]
        P = [kb.sb(f"P{i}", [128, 1024], BF16) for i in range(3)]
        abias = kb.sb("abias", [128, NKC * NT], F32)
        lamt = kb.sb("lamt", [128, 256], F32)
        lamp = kb.sb("lamp", [128, 128], F32)
        lams = kb.sb("lams", [128, 8], F32)
        gsub = kb.sb("gsub", [128, 2], F32)
        r0 = kb.sb("r0", [128, 512], F32)
        r1 = kb.sb("r1", [128, 512], F32)
        n0 = kb.sb("n0", [128, 512], F32)
        n1 = kb.sb("n1", [128, 512], F32)
        sqb = kb.sb("sqb", [128, 512], BF16)
        tmp = kb.sb("tmp", [128, 512], F32)
        rstd = kb.sb("rstd", [128, 512], F32)
        ob = [kb.sb(f"ob{i}", [128, 512], BF16) for i in range(2)]
        S = [kb.ps(f"S{i}", [128, 1024]) for i in range(2)]
        O0, O1 = kb.ps("O0", [128, 512]), kb.ps("O1", [128, 512])
        L0, L1 = kb.ps("L0", [128, 512]), kb.ps("L1", [128, 512])
        kb.load(abias, abias[:], abias_in, abias_in[:])
        kb.load(lamt, lamt[:], a_lam, a_lam[:, j * 256:(j + 1) * 256])
        kb.load(gsub, gsub[:], a_sub, a_sub[:])
        kb.op(kb.dve, lambda e: e.tensor_tensor(out=lamp[:, 0:64], in0=lamt[:, 0:64], in1=lamt[:, 64:128], op=ALU.mult), [lamt], [lamp])
        kb.op(kb.dve, lambda e: e.tensor_tensor(out=lamp[:, 64:128], in0=lamt[:, 128:192], in1=lamt[:, 192:256], op=ALU.mult), [lamt, lamp], [lamp])
        kb.op(kb.dve, lambda e: e.reduce_sum(out=lams[:, 0:1], in_=lamp[:, 0:64], axis=AX.X), [lamp], [lams])
        kb.op(kb.dve, lambda e: e.reduce_sum(out=lams[:, 1:2], in_=lamp[:, 64:128], axis=AX.X), [lamp, lams], [lams])
        kb.op(kb.act, lambda e: e.activation(out=lams[:, 2:4], in_=lams[:, 0:2], func=AF.Exp), [lams], [lams])
        kb.op(kb.dve, lambda e: e.tensor_tensor(out=lams[:, 4:5], in0=lams[:, 3:4], in1=lams[:, 2:3], op=ALU.subtract), [lams], [lams])
        kb.op(kb.dve, lambda e: e.tensor_scalar_add(out=lams[:, 5:6], in0=lams[:, 4:5], scalar1=-lam_init), [lams], [lams])
        kb.op(kb.dve, lambda e: e.tensor_scalar_mul(out=lams[:, 6:7], in0=gsub[:, j:j + 1], scalar1=1.0 - lam_init), [gsub, lams], [lams])
        neglam = lams[:, 5:6]
        gs = lams[:, 6:7]
        QTv, KTv = QT_d[:], KT_d[:]
        Vv = V_d[:].rearrange("(n p) f -> p n f", p=128)

        def ldh(h):
            s = h % 2
            kb.load(QTh[s], QTh[s][:], QT_d, QTv[h * 128:(h + 1) * 128, :])
            kb.load(KTh[s], KTh[s][:], KT_d, KTv[h * 128:(h + 1) * 128, :])
            step = max(1, NKC // 4)
            for c0 in range(0, NKC, step):
                kb.load(Vh[s], Vh[s][:, c0:c0 + step, :], V_d, Vv[:, c0:c0 + step, h * 128:(h + 1) * 128])

        ldh(0)
        it = 0
        oi = 0
        for h in range(8):
            s = h % 2
            if h + 1 < 8:
                ldh(h + 1)
            Q, K, V = QTh[s], KTh[s], Vh[s]
            for qt in range(NQT):
                def qk(kc, slot):
                    Sx = S[slot]

                    def f(e):
                        e.matmul(Sx[:, 0:512], lhsT=K[0:64, kc * 128:(kc + 1) * 128], rhs=Q[0:64, qt * 512:(qt + 1) * 512],
                                 start=True, stop=True, tile_position=(0, 0))
                        return e.matmul(Sx[:, 512:1024], lhsT=K[64:128, kc * 128:(kc + 1) * 128], rhs=Q[64:128, qt * 512:(qt + 1) * 512],
                                        start=True, stop=True, tile_position=(64, 0))
                    kb.op(kb.pe, f, [K, Q], [Sx])

                qk(0, it % 2)
                for kc in range(NKC):
                    slot = it % 2
                    Px = P[it % 3]
                    Sx = S[slot]
                    it += 1
                    if kc + 1 < NKC:
                        qk(kc + 1, it % 2)
                    bcol = abias[:, kc * NT + qt:kc * NT + qt + 1]
                    kb.op(kb.act, lambda e: e.activation(out=Px[:], in_=Sx[:], func=AF.Exp, bias=bcol), [Sx, abias], [Px])

                    def pv(e):
                        e.matmul(O0[:], lhsT=V[:, kc, :], rhs=Px[:, 0:512], start=(kc == 0), stop=(kc == NKC - 1))
                        e.matmul(O1[:], lhsT=V[:, kc, :], rhs=Px[:, 512:1024], start=(kc == 0), stop=(kc == NKC - 1))
                        e.matmul(L0[:], lhsT=ones_bf[:], rhs=Px[:, 0:512], start=(kc == 0), stop=(kc == NKC - 1))
                        return e.matmul(L1[:], lhsT=ones_bf[:], rhs=Px[:, 512:1024], start=(kc == 0), stop=(kc == NKC - 1))
                    kb.op(kb.pe, pv, [V, Px, ones_bf], [O0, O1, L0, L1])
                kb.op(kb.dve, lambda e: e.reciprocal(out=r0[:], in_=L0[:]), [L0], [r0])
                kb.op(kb.dve, lambda e: e.reciprocal(out=r1[:], in_=L1[:]), [L1], [r1])
                kb.op(kb.dve, lambda e: e.tensor_tensor(out=n0[:], in0=O0[:], in1=r0[:], op=ALU.mult), [O0, r0], [n0])
                kb.op(kb.dve, lambda e: e.tensor_tensor(out=n1[:], in0=O1[:], in1=r1[:], op=ALU.mult), [O1, r1], [n1])
                kb.op(kb.dve, lambda e: e.scalar_tensor_tensor(out=n0[:], in0=n1[:], scalar=neglam, in1=n0[:], op0=ALU.mult, op1=ALU.add),
                      [n1, n0, lams], [n0])
                kb.op(kb.act, lambda e: e.activation(out=sqb[:], in_=n0[:], func=AF.Square), [n0], [sqb])
                kb.op(kb.pe, lambda e: e.matmul(L0[:], lhsT=ones_bf[:], rhs=sqb[:], start=True, stop=True), [ones_bf, sqb], [L0])
                rstd_from(L0, 512, 128, 1e-5, rstd, tmp)
                o = ob[oi % 2]
                oi += 1
                kb.op(kb.dve, lambda e: e.scalar_tensor_tensor(out=o[:], in0=n0[:], scalar=gs, in1=rstd[:], op0=ALU.mult, op1=ALU.mult),
                      [n0, rstd, lams], [o])
                kb.store(oT_d, oT_d[h * 128:(h + 1) * 128, qt * 512:(qt + 1) * 512], o, o[:])
        kb.end_phase()

    def phase_attn_B(layer, QTb, KTb, Vb, oT_d):
        kb.begin_phase()
        NM = len(BMASKS)
        midx = {gd: i for i, gd in enumerate(BMASKS)}
        Qg = [kb.sb(f"Qg{g}", [128, T], BF16) for g in range(3)]
        Kg = [kb.sb(f"Kg{g}", [128, T], BF16) for g in range(3)]
        Vg = [kb.sb(f"Vg{g}", [128, NKC, 128], BF16) for g in range(3)]
        masks = kb.sb("masks", [128, NM, 512], BF16)
        stg = [kb.sb(f"mstg{i}", [128, 2, 512], F32) for i in range(2)]
        abias = kb.sb("abias", [128, NKC * NT], F32)
        P = [kb.sb(f"P{i}", [128, 512], BF16) for i in range(3)]
        r0 = kb.sb("r0", [128, 512], F32)
        ob = [kb.sb(f"ob{i}", [128, 512], BF16) for i in range(2)]
        S = [kb.ps(f"S{i}", [128, 512]) for i in range(2)]
        O0, L0 = kb.ps("O0", [128, 512]), kb.ps("L0", [128, 512])
        kb.load(abias, abias[:], abias_in, abias_in[:])
        for i0 in range(0, NM, 2):
            i1 = min(NM, i0 + 2)
            sg = stg[(i0 // 2) % 2]
            kb.load(sg, sg[:, :i1 - i0, :], bmask_in, bmask_in[:, i0:i1, :])
            kb.op(kb.dve, lambda e: e.tensor_copy(out=masks[:, i0:i1, :], in_=sg[:, :i1 - i0, :]), [sg], [masks])
        it = 0
        oi = 0
        for hp in range(8):
            for g in range(3):
                kb.load(Qg[g], Qg[g][:], QTb[g], QTb[g][hp * 128:(hp + 1) * 128, :])
                kb.load(Kg[g], Kg[g][:], KTb[g], KTb[g][hp * 128:(hp + 1) * 128, :])
                Vv = Vb[g][:].rearrange("(n p) f -> p n f", p=128)
                step = max(1, NKC // 4)
                for c0 in range(0, NKC, step):
                    kb.load(Vg[g], Vg[g][:, c0:c0 + step, :], Vb[g], Vv[:, c0:c0 + step, hp * 128:(hp + 1) * 128])
            for hh in range(2):
                r_0 = hh * 64
                for qt in range(NT):
                    tiles = [(g, kc) for g in range(3) for kc in range(NKC) if (g, qt * 512 - kc * 128) in midx]
                    for ti, (g, kc) in enumerate(tiles):
                        Sx = S[it % 2]
                        Px = P[it % 3]
                        it += 1
                        mi = midx[(g, qt * 512 - kc * 128)]

                        def f(e):
                            e.matmul(Sx[:], lhsT=Kg[g][r_0:r_0 + 64, kc * 128:(kc + 1) * 128], rhs=Qg[g][r_0:r_0 + 64, qt * 512:(qt + 1) * 512],
                                     start=True, stop=False, tile_position=(r_0, 0))
                            return e.matmul(Sx[:], lhsT=ident_bf[:], rhs=masks[:, mi, :], start=False, stop=True)
                        kb.op(kb.pe, f, [Kg[g], Qg[g], ident_bf, masks], [Sx])
                        bcol = abias[:, kc * NT + qt:kc * NT + qt + 1]
                        kb.op(kb.act, lambda e: e.activation(out=Px[:], in_=Sx[:], func=AF.Exp, bias=bcol), [Sx, abias], [Px])

                        def pv(e):
                            e.matmul(O0[0:64, :], lhsT=Vg[g][:, kc, r_0:r_0 + 64], rhs=Px[:], start=(ti == 0), stop=(ti == len(tiles) - 1))
                            return e.matmul(L0[0:64, :], lhsT=ones_bf[:, 0:64], rhs=Px[:], start=(ti == 0), stop=(ti == len(tiles) - 1))
                        kb.op(kb.pe, pv, [Vg[g], Px, ones_bf], [O0, L0])
                    kb.op(kb.dve, lambda e: e.reciprocal(out=r0[0:64, :], in_=L0[0:64, :]), [L0], [r0])
                    o = ob[oi % 2]
                    oi += 1
                    kb.op(kb.dve, lambda e: e.tensor_tensor(out=o[0:64, :], in0=O0[0:64, :], in1=r0[0:64, :], op=ALU.mult), [O0, r0], [o])
                    kb.store(oT_d, oT_d[hp * 128 + r_0:hp * 128 + r_0 + 64, qt * 512:(qt + 1) * 512], o, o[0:64, :])
        kb.end_phase()

    def phase_proj_C(layer, x_d, qt_d, kt_d, kh_d, et_d, V_d, gT_d):
        kb.begin_phase()
        W = kb.sb("Wc", [128, 8, 5120], BF16)
        stg = [kb.sb(f"stg{i}", [128, 8, 256], F32) for i in range(2)]
        wsrc = c_win[0].rearrange("(c p) n -> p c n", p=128)
        load_cast_w(W, lambda a, b: W[:, :, a:b], c_win, lambda a, b: wsrc[:, :, a:b], 5120, stg, blk=256)
        lbl = kb.sb("lbl", [128, 4, 8], F32)
        lbe = kb.sb("lbe", [128, 4, 8], F32)
        lbs = kb.sb("lbs", [128, 4, 8], F32)
        kb.load(lbl, lbl[:], c_lbl, c_lbl[:])
        kb.op(kb.act, lambda e: e.activation(out=lbe[:], in_=lbl[:], func=AF.Exp), [lbl], [lbe])
        kb.op(kb.dve, lambda e: e.tensor_tensor(out=lbs[:, 0, :], in0=lbe[:, 0, :], in1=lbe[:, 1, :], op=ALU.add), [lbe], [lbs])
        kb.op(kb.dve, lambda e: e.tensor_tensor(out=lbs[:, 0, :], in0=lbs[:, 0, :], in1=lbe[:, 2, :], op=ALU.add), [lbe, lbs], [lbs])
        kb.op(kb.dve, lambda e: e.tensor_tensor(out=lbs[:, 0, :], in0=lbs[:, 0, :], in1=lbe[:, 3, :], op=ALU.add), [lbe, lbs], [lbs])
        kb.op(kb.dve, lambda e: e.tensor_copy(out=lbs[:, 1, :], in_=lbe[:, 1, :]), [lbe, lbs], [lbs])
        for i in range(2, layer + 1):
            kb.op(kb.dve, lambda e: e.tensor_tensor(out=lbs[:, 1, :], in0=lbs[:, 1, :], in1=lbe[:, i, :], op=ALU.add), [lbe, lbs], [lbs])
        kb.op(kb.dve, lambda e: e.reciprocal(out=lbs[:, 2, :], in_=lbs[:, 0, :]), [lbs], [lbs])
        kb.op(kb.dve, lambda e: e.tensor_tensor(out=lbs[:, 3, :], in0=lbs[:, 0, :], in1=lbs[:, 1, :], op=ALU.subtract), [lbs], [lbs])
        kb.op(kb.dve, lambda e: e.tensor_tensor(out=lbs[:, 3, :], in0=lbs[:, 3, :], in1=lbs[:, 2, :], op=ALU.mult), [lbs], [lbs])
        xs = [kb.sb(f"xs{i}", [128, 8, 512], F32) for i in range(2)]
        hT = kb.sb("hT", [128, 8, 512], BF16)
        sq = [kb.sb(f"sq{i}", [128, 512], BF16) for i in range(2)]
        tmp = kb.sb("tmp", [128, 512], F32)
        rstd = kb.sb("rstd", [128, 512], F32)
        F = lambda n: kb.sb(n, [128, 512], F32)
        qs, sg, kk, lf, G, Gd, Ek, eG, enG, eK = [F(n) for n in ("qs", "sg", "kk", "lf", "G", "Gd", "Ek", "eG", "enG", "eK")]
        gate = [kb.sb(f"gate{i}", [128, 512], BF16) for i in range(2)]
        qo = [kb.sb(f"qo{i}", [128, 512], BF16) for i in range(2)]
        ko = [kb.sb(f"ko{i}", [128, 512], BF16) for i in range(2)]
        kh = [kb.sb(f"kh{i}", [128, 512], BF16) for i in range(2)]
        kht = [kb.sb(f"kht{i}", [128, 4, 128], BF16) for i in range(2)]
        eto = [kb.sb(f"eto{i}", [128, 8], F32) for i in range(2)]
        vo = kb.sb("vo", [128, 4, 1024], BF16)
        pst = kb.ps("pst", [128, 512])
        pq = [kb.ps(f"pq{i}", [128, 512]) for i in range(3)]
        ptr = [kb.ps(f"ptr{i}", [128, 128], BF16) for i in range(2)]
        xv = x_d[:].rearrange("(c p) t -> p c t", p=128)
        Vv = V_d[:].rearrange("(n p) f -> p n f", p=128)
        khv = [kh_d[d][:].rearrange("(n p) f -> p n f", p=128) for d in range(2)]

        def ld(tt):
            kb.load(xs[tt % 2], xs[tt % 2][:], x_d, xv[:, :, tt * 512:(tt + 1) * 512])
        ld(0)
        cnt = 0
        oc = 0
        for tt in range(NT):
            if tt + 1 < NT:
                ld(tt + 1)
            x, h = xs[tt % 2], hT
            for c in range(8):
                q = sq[c % 2]
                kb.op(kb.act, lambda e: e.activation(out=q[:], in_=x[:, c, :], func=AF.Square), [x], [q])
                kb.op(kb.pe, lambda e: e.matmul(pst[:], lhsT=ones_bf[:], rhs=q[:], start=(c == 0), stop=(c == 7)), [ones_bf, q], [pst])
            rstd_from(pst, 512, D, 1e-6, rstd, tmp)
            for c in range(8):
                kb.op(kb.dve, lambda e: e.scalar_tensor_tensor(out=h[:, c, :], in0=x[:, c, :], scalar=gcol(layer, 0, c),
                                                                 in1=rstd[:], op0=ALU.mult, op1=ALU.mult), [x, rstd, normg], [h])

            def proj(col0):
                nonlocal cnt
                p = pq[cnt % 3]
                cnt += 1

                def mm(e):
                    for k in range(8):
                        i = e.matmul(p[:], lhsT=W[:, k, col0:col0 + 128], rhs=h[:, k, :], start=(k == 0), stop=(k == 7))
                    return i
                kb.op(kb.pe, mm, [W, h], [p])
                return p
            csl = slice(tt * 512, (tt + 1) * 512)
            for hd in range(8):
                rows = slice(hd * 128, (hd + 1) * 128)
                p = proj(hd * 128)
                kb.op(kb.act, lambda e: e.activation(out=qs[:], in_=p[:], func=AF.Silu), [p], [qs])
                p = proj(4096 + hd * 128)
                gt = gate[oc % 2]
                kb.op(kb.act, lambda e: e.activation(out=gt[:], in_=p[:], func=AF.Silu), [p], [gt])
                kb.store(gT_d, gT_d[rows, csl], gt, gt[:])
                for d in range(2):
                    qo_, ko_, kh_, kht_, eto_ = qo[oc % 2], ko[oc % 2], kh[oc % 2], kht[oc % 2], eto[oc % 2]
                    oc += 1
                    p = proj(1024 + d * 1024 + hd * 128)
                    kb.op(kb.act, lambda e: e.activation(out=sg[:], in_=p[:], func=AF.Sigmoid, scale=-1.0), [p], [sg])
                    kb.op(kb.dve, lambda e: e.tensor_scalar_mul(out=kk[:], in0=sg[:], scalar1=lbs[:, 3, hd:hd + 1]), [sg, lbs], [kk])
                    kb.op(kb.act, lambda e: e.activation(out=lf[:], in_=kk[:], func=AF.Ln, scale=-1.0, bias=1.0), [kk], [lf])
                    kb.op(kb.dve, lambda e: e.tensor_tensor_scan(out=G[:], data0=m01[:], data1=lf[:], initial=0.0, op0=ALU.mult, op1=ALU.add),
                          [m01, lf], [G])
                    G3 = G[:].rearrange("p (c t) -> p c t", t=64)
                    TOTb = G3[:, :, 63:64].to_broadcast([128, 8, 64])
                    r3 = lambda b: b[:].rearrange("p (c t) -> p c t", t=64)
                    if d == 0:
                        Gdb = G
                        kb.op(kb.dve, lambda e: e.tensor_tensor(out=r3(Ek), in0=TOTb, in1=G3, op=ALU.subtract), [G], [Ek])
                    else:
                        Gdb = Gd
                        kb.op(kb.dve, lambda e: e.tensor_tensor(out=Ek[:], in0=G[:], in1=lf[:], op=ALU.subtract), [G, lf], [Ek])
                        kb.op(kb.dve, lambda e: e.tensor_tensor(out=r3(Gd), in0=TOTb, in1=r3(Ek), op=ALU.subtract), [G, Ek], [Gd])
                    kb.op(kb.act, lambda e: e.activation(out=eG[:], in_=Gdb[:], func=AF.Exp), [Gdb], [eG])
                    kb.op(kb.act, lambda e: e.activation(out=enG[:], in_=Gdb[:], func=AF.Exp, scale=-1.0), [Gdb], [enG])
                    kb.op(kb.act, lambda e: e.activation(out=eK[:], in_=Ek[:], func=AF.Exp), [Ek], [eK])
                    kb.op(kb.act, lambda e: e.activation(out=eto_[:], in_=G3[:, :, 63], func=AF.Exp), [G], [eto_])
                    kb.op(kb.dve, lambda e: e.scalar_tensor_tensor(out=qo_[:], in0=qs[:], scalar=cv[:, 2:3], in1=eG[:], op0=ALU.mult, op1=ALU.mult),
                          [qs, cv, eG], [qo_])
                    kb.op(kb.dve, lambda e: e.tensor_tensor(out=ko_[:], in0=kk[:], in1=enG[:], op=ALU.mult), [kk, enG], [ko_])
                    kb.op(kb.dve, lambda e: e.tensor_tensor(out=kh_[:], in0=kk[:], in1=eK[:], op=ALU.mult), [kk, eK], [kh_])
                    for tb in range(4):
                        pt = ptr[tb % 2]
                        kb.op(kb.pe, lambda e: e.transpose(pt[:], kh_[:, tb * 128:(tb + 1) * 128], ident_bf[:]), [kh_, ident_bf], [pt])
                        kb.op(kb.act, lambda e: e.activation(out=kht_[:, tb, :], in_=pt[:], func=AF.Copy), [pt], [kht_])
                    kb.store(qt_d[d], qt_d[d][rows, csl], qo_, qo_[:])
                    kb.store(kt_d[d], kt_d[d][rows, csl], ko_, ko_[:])
                    kb.store(kh_d[d], khv[d][:, tt * 4:(tt + 1) * 4, rows], kht_, kht_[:])
                    kb.store(et_d[d], et_d[d][rows, tt * 8:(tt + 1) * 8], eto_, eto_[:])
            for tb in range(4):
                for nb in range(2):
                    p = pq[cnt % 3]
                    cnt += 1

                    def mm(e):
                        for k in range(8):
                            i = e.matmul(p[:], lhsT=h[:, k, tb * 128:(tb + 1) * 128],
                                         rhs=W[:, k, 3072 + nb * 512:3072 + (nb + 1) * 512], start=(k == 0), stop=(k == 7))
                        return i
                    kb.op(kb.pe, mm, [W, h], [p])
                    kb.op(kb.act, lambda e: e.activation(out=vo[:, tb, nb * 512:(nb + 1) * 512], in_=p[:], func=AF.Copy), [p], [vo])
            kb.store(V_d, Vv[:, tt * 4:(tt + 1) * 4, :], vo, vo[:])
        kb.end_phase()

    def phase_scan_C(layer, qt_d, kt_d, kh_d, et_d, V_d, gT_d, oT_d):
        kb.begin_phase()
        NC = T // 64
        Qt = [kb.sb(f"Qt{i}", [128, T], BF16) for i in range(2)]
        Kt = [kb.sb(f"Kt{i}", [128, T], BF16) for i in range(2)]
        Kh = [kb.sb(f"Kh{i}", [128, NKC, 128], BF16) for i in range(2)]
        Et = [kb.sb(f"Et{i}", [128, NC], F32) for i in range(2)]
        Vh = kb.sb("Vh", [128, NKC, 128], BF16)
        Gt = kb.sb("Gt", [128, T], BF16)
        ofw = kb.sb("ofw", [128, T], F32)
        Sf = kb.sb("Sf", [128, 128], F32)
        Sb = kb.sb("Sb", [128, 128], BF16)
        sc = [kb.sb(f"sc{i}", [128, 128], BF16) for i in range(2)]
        gn = kb.sb("gn", [128, 1], F32)
        sqb = kb.sb("sqb", [128, 512], BF16)
        tmp = kb.sb("tmp", [128, 512], F32)
        rstd = kb.sb("rstd", [128, 512], F32)
        on = kb.sb("on", [128, 512], F32)
        ob = [kb.sb(f"ob{i}", [128, 512], BF16) for i in range(2)]
        scp = [kb.ps(f"scp{i}", [128, 128]) for i in range(2)]
        op_ = [kb.ps(f"op{i}", [128, 128]) for i in range(2)]
        dsp = [kb.ps(f"dsp{i}", [128, 128]) for i in range(2)]
        pst = kb.ps("pst", [128, 512])
        kb.load(gn, gn[:], c_gn, c_gn[:])
        Vv = V_d[:].rearrange("(n p) f -> p n f", p=128)
        khv = [kh_d[d][:].rearrange("(n p) f -> p n f", p=128) for d in range(2)]
        step = max(1, NKC // 4)
        it = 0
        oi = 0
        ci = 0
        for hd in range(8):
            rows = slice(hd * 128, (hd + 1) * 128)
            for c0 in range(0, NKC, step):
                kb.load(Vh, Vh[:, c0:c0 + step, :], V_d, Vv[:, c0:c0 + step, rows])
            kb.load(Gt, Gt[:], gT_d, gT_d[rows, :])
            for d in range(2):
                kb.load(Qt[d], Qt[d][:], qt_d[d], qt_d[d][rows, :])
                kb.load(Kt[d], Kt[d][:], kt_d[d], kt_d[d][rows, :])
                kb.load(Et[d], Et[d][:], et_d[d], et_d[d][rows, :])
                for c0 in range(0, NKC, step):
                    kb.load(Kh[d], Kh[d][:, c0:c0 + step, :], kh_d[d], khv[d][:, c0:c0 + step, rows])
            for d in range(2):
                Q, K, KH, ET = Qt[d], Kt[d], Kh[d], Et[d]
                kb.op(kb.dve, lambda e: e.memset(Sf[:], 0.0), [], [Sf])
                kb.op(kb.dve, lambda e: e.memset(Sb[:], 0.0), [], [Sb])
                blocks = range(NKC) if d == 0 else range(NKC - 1, -1, -1)
                for b in blocks:
                    bs = slice(b * 128, (b + 1) * 128)
                    scp_, sc_, o_ = scp[it % 2], sc[it % 2], op_[it % 2]
                    it += 1
                    kb.op(kb.pe, lambda e: e.matmul(scp_[:], lhsT=K[:, bs], rhs=Q[:, bs], start=True, stop=True), [K, Q], [scp_])
                    kb.op(kb.dve, lambda e: e.tensor_tensor(out=sc_[:], in0=scp_[:], in1=cmask[:, d, :], op=ALU.mult), [scp_, cmask], [sc_])
                    kb.op(kb.pe, lambda e: e.matmul(o_[:], lhsT=Vh[:, b, :], rhs=sc_[:], start=True, stop=False), [Vh, sc_], [o_])
                    chunks = (2 * b, 2 * b + 1) if d == 0 else (2 * b + 1, 2 * b)
                    for n_, c in enumerate(chunks):
                        p0 = (c % 2) * 64
                        first_of_other = (c == NC // 2) if d == 0 else (c == NC // 2 - 1)
                        if first_of_other:
                            kb.op(kb.dve, lambda e: e.tensor_scalar_mul(out=Sf[:], in0=Sf[:], scalar1=flag[:, 0:1]), [Sf, flag], [Sf])
                            kb.op(kb.act, lambda e: e.activation(out=Sb[:], in_=Sf[:], func=AF.Copy), [Sf], [Sb])
                        kb.op(kb.pe, lambda e: e.matmul(o_[:, p0:p0 + 64], lhsT=Sb[:], rhs=Q[:, c * 64:(c + 1) * 64], start=False, stop=(n_ == 1)),
                              [Sb, Q], [o_])
                        ds_ = dsp[ci % 2]
                        ci += 1
                        kb.op(kb.pe, lambda e: e.matmul(ds_[:], lhsT=KH[p0:p0 + 64, b, :], rhs=Vh[p0:p0 + 64, b, :], start=True, stop=True,
                                                        tile_position=(p0, 0)), [KH, Vh], [ds_])
                        kb.op(kb.dve, lambda e: e.scalar_tensor_tensor(out=Sf[:], in0=Sf[:], scalar=ET[:, c:c + 1], in1=ds_[:],
                                                                         op0=ALU.mult, op1=ALU.add), [Sf, ET, ds_], [Sf])
                        kb.op(kb.act, lambda e: e.activation(out=Sb[:], in_=Sf[:], func=AF.Copy), [Sf], [Sb])
                    if d == 0:
                        kb.op(kb.act, lambda e: e.activation(out=ofw[:, bs], in_=o_[:], func=AF.Copy), [o_], [ofw])
                    else:
                        kb.op(kb.dve, lambda e: e.tensor_tensor(out=ofw[:, bs], in0=o_[:], in1=ofw[:, bs], op=ALU.add), [o_, ofw], [ofw])
            for tt in range(NT):
                csl = slice(tt * 512, (tt + 1) * 512)
                kb.op(kb.act, lambda e: e.activation(out=sqb[:], in_=ofw[:, csl], func=AF.Square), [ofw], [sqb])
                kb.op(kb.pe, lambda e: e.matmul(pst[:], lhsT=ones_bf[:], rhs=sqb[:], start=True, stop=True), [ones_bf, sqb], [pst])
                rstd_from(pst, 512, 128, 1e-6, rstd, tmp)
                kb.op(kb.dve, lambda e: e.scalar_tensor_tensor(out=on[:], in0=ofw[:, csl], scalar=gn[:, 0:1], in1=rstd[:], op0=ALU.mult, op1=ALU.mult),
                      [ofw, gn, rstd], [on])
                o = ob[oi % 2]
                oi += 1
                kb.op(kb.dve, lambda e: e.tensor_tensor(out=o[:], in0=on[:], in1=Gt[:, csl], op=ALU.mult), [on, Gt], [o])
                kb.store(oT_d, oT_d[rows, csl], o, o[:])
        kb.end_phase()

    def phase_wo(layer, wo_buf, wo_ap, oT_d, x_d, x1_d):
        kb.begin_phase()
        Wo = kb.sb("Wo", [128, 8, 1024], BF16)
        stg = [kb.sb(f"stg{i}", [128, 8, 512], F32) for i in range(2)]
        wsrc = wo_ap.rearrange("(c p) n -> p c n", p=128)
        load_cast_w(Wo, lambda a, b: Wo[:, :, a:b], wo_buf, lambda a, b: wsrc[:, :, a:b], 1024, stg)
        xs = [kb.sb(f"xs{i}", [128, 8, 512], F32) for i in range(2)]
        os_ = [kb.sb(f"os{i}", [128, 8, 512], BF16) for i in range(2)]
        y = kb.sb("y", [128, 8, 512], F32)
        sq = [kb.sb(f"sq{i}", [128, 512], BF16) for i in range(2)]
        tmp = kb.sb("tmp", [128, 512], F32)
        rstd = kb.sb("rstd", [128, 512], F32)
        xo = [kb.sb(f"xo{i}", [128, 8, 512], F32) for i in range(2)]
        pst = kb.ps("pst", [128, 512])
        pq = [kb.ps(f"pq{i}", [128, 512]) for i in range(3)]
        xv = x_d[:].rearrange("(c p) t -> p c t", p=128)
        x1v = x1_d[:].rearrange("(c p) t -> p c t", p=128)
        ov = oT_d[:].rearrange("(c p) t -> p c t", p=128)

        def ld(tt):
            s = tt % 2
            kb.load(xs[s], xs[s][:], x_d, xv[:, :, tt * 512:(tt + 1) * 512])
            kb.load(os_[s], os_[s][:], oT_d, ov[:, :, tt * 512:(tt + 1) * 512])
        ld(0)
        cnt = 0
        for tt in range(NT):
            s = tt % 2
            if tt + 1 < NT:
                ld(tt + 1)
            x, o = xs[s], os_[s]
            for m in range(8):
                p = pq[cnt % 3]
                q = sq[cnt % 2]
                cnt += 1

                def mm(e):
                    for k in range(8):
                        i = e.matmul(p[:], lhsT=Wo[:, k, m * 128:(m + 1) * 128], rhs=o[:, k, :], start=(k == 0), stop=(k == 7))
                    return i
                kb.op(kb.pe, mm, [Wo, o], [p])
                kb.op(kb.act, lambda e: e.activation(out=q[:], in_=p[:], func=AF.Square), [p], [q])
                kb.op(kb.dve, lambda e: e.tensor_copy(out=y[:, m, :], in_=p[:]), [p], [y])
                kb.op(kb.pe, lambda e: e.matmul(pst[:], lhsT=ones_bf[:], rhs=q[:], start=(m == 0), stop=(m == 7)), [ones_bf, q], [pst])
            rstd_from(pst, 512, D, 1e-6, rstd, tmp)
            for m in range(8):
                kb.op(kb.dve, lambda e: e.scalar_tensor_tensor(out=y[:, m, :], in0=y[:, m, :], scalar=gcol(layer, 1, m), in1=rstd[:],
                                                                 op0=ALU.mult, op1=ALU.mult), [y, rstd, normg], [y])
                kb.op(kb.ew2, lambda e: e.tensor_tensor(out=xo[s][:, m, :], in0=y[:, m, :], in1=x[:, m, :], op=ALU.add), [y, x], [xo[s]])
            kb.store(x1_d, x1v[:, :, tt * 512:(tt + 1) * 512], xo[s], xo[s][:])
        kb.end_phase()

    def phase_ffn(layer, x1_d, x2_d):
        kb.begin_phase()
        NV = 256
        NW = NV + 2
        Win = kb.sb("Win", [128, 8, 2 * DFF], BF16)
        Wout = kb.sb("Wout", [128, 22, 1024], BF16)
        stg = [kb.sb(f"stg{i}", [128, 8, 128], F32) for i in range(1)]
        wsrc = f_win[layer].rearrange("(c p) n -> p c n", p=128)
        load_cast_w(Win, lambda a, b: Win[:, :, a:b], f_win, lambda a, b: wsrc[:, :, a:b], 2 * DFF, stg, blk=128)
        wsrc2 = f_wout[layer].rearrange("(c p) n -> p c n", p=128)
        for c0 in range(0, 22, 8):
            c1 = min(22, c0 + 8)
            for n0_ in range(0, 1024, 128):
                sg = stg[cast_rr[0] % len(stg)]
                cast_rr[0] += 1
                kb.load(sg, sg[:, :c1 - c0, :], f_wout, wsrc2[:, c0:c1, n0_:n0_ + 128])
                kb.op(kb.dve, lambda e: e.tensor_copy(out=Wout[:, c0:c1, n0_:n0_ + 128], in_=sg[:, :c1 - c0, :]), [sg], [Wout])
        cw = kb.sb("cw", [128, 44, 4], F32)
        kb.load(cw, cw[:], f_cw, f_cw[:, layer * 176:(layer + 1) * 176].rearrange("p (c f) -> p c f", f=4))
        xw = [kb.sb(f"xw{i}", [128, 8, NW], F32) for i in range(2)]
        h = kb.sb("h", [128, 8, NW], BF16)
        sq = [kb.sb(f"sq{i}", [128, NW], BF16) for i in range(2)]
        tmp = kb.sb("tmp", [128, NW], F32)
        rstd = kb.sb("rstd", [128, NW], F32)
        ta = [kb.sb(f"ta{i}", [128, NV], F32) for i in range(2)]
        tb_ = [kb.sb(f"tb{i}", [128, NV], F32) for i in range(2)]
        ga = [kb.sb(f"ga{i}", [128, NV], F32) for i in range(2)]
        gg = kb.sb("gg", [128, 22, NV], BF16)
        y = kb.sb("y", [128, 8, NV], F32)
        xo = [kb.sb(f"xo{i}", [128, 8, NV], F32) for i in range(2)]
        pst = kb.ps("pst", [128, 512])
        pa = [kb.ps(f"pa{i}", [128, 512]) for i in range(2)]
        pb = [kb.ps(f"pb{i}", [128, 512]) for i in range(2)]
        po = [kb.ps(f"po{i}", [128, 512]) for i in range(2)]
        xv = x1_d[:].rearrange("(c p) t -> p c t", p=128)
        x2v = x2_d[:].rearrange("(c p) t -> p c t", p=128)
        wins = list(range(0, T, NV))

        def ld(wi):
            s0 = wins[wi]
            b = xw[wi % 2]
            lo, hi = s0 - 1, s0 + NV + 1
            clo, chi = max(lo, 0), min(hi, T)
            if clo > lo:
                kb.op(kb.pool, lambda e: e.memset(b[:, :, 0:1], 0.0), [], [b])
            if chi < hi:
                kb.op(kb.pool, lambda e: e.memset(b[:, :, NW - 1:NW], 0.0), [], [b])
            kb.load(b, b[:, :, clo - lo:NW - (hi - chi)], x1_d, xv[:, :, clo:chi])
        ld(0)
        cnt = 0
        for wi, s0 in enumerate(wins):
            if wi + 1 < len(wins):
                ld(wi + 1)
            x = xw[wi % 2]
            for c in range(8):
                q = sq[c % 2]
                kb.op(kb.act, lambda e: e.activation(out=q[:], in_=x[:, c, :], func=AF.Square), [x], [q])
                kb.op(kb.pe, lambda e: e.matmul(pst[:, :NW], lhsT=ones_bf[:], rhs=q[:], start=(c == 0), stop=(c == 7)), [ones_bf, q], [pst])
            rstd_from(pst, NW, D, 1e-6, rstd, tmp)
            for c in range(8):
                kb.op(kb.dve, lambda e: e.scalar_tensor_tensor(out=h[:, c, :], in0=x[:, c, :], scalar=gcol(layer, 2, c), in1=rstd[:],
                                                                 op0=ALU.mult, op1=ALU.mult), [x, rstd, normg], [h])
            if s0 == HALF:
                kb.op(kb.dve, lambda e: e.tensor_scalar_mul(out=h[:, :, 0:1], in0=h[:, :, 0:1], scalar1=flag[:, 0:1]), [h, flag], [h])
            if s0 + NV == HALF:
                kb.op(kb.dve, lambda e: e.tensor_scalar_mul(out=h[:, :, NW - 1:NW], in0=h[:, :, NW - 1:NW], scalar1=flag[:, 0:1]), [h, flag], [h])
            for jj in range(22):
                A, B = pa[jj % 2], pb[jj % 2]
                a_, b_, g_ = ta[jj % 2], tb_[jj % 2], ga[jj % 2]

                def mma(e):
                    for k in range(8):
                        i = e.matmul(A[:, :NW], lhsT=Win[:, k, jj * 128:(jj + 1) * 128], rhs=h[:, k, :], start=(k == 0), stop=(k == 7))
                    return i

                def mmb(e):
                    for k in range(8):
                        i = e.matmul(B[:, :NW], lhsT=Win[:, k, DFF + jj * 128:DFF + (jj + 1) * 128], rhs=h[:, k, :], start=(k == 0), stop=(k == 7))
                    return i
                kb.op(kb.pe, mma, [Win, h], [A])
                kb.op(kb.pe, mmb, [Win, h], [B])
                for (Pp, tt_, ci, eng2) in ((A, a_, jj, kb.dve), (B, b_, 22 + jj, kb.dve)):
                    kb.op(kb.act, lambda e: e.activation(out=tt_[:], in_=Pp[:, 1:NV + 1], func=AF.Identity,
                                                         scale=cw[:, ci, 1:2], bias=cw[:, ci, 3:4]), [Pp, cw], [tt_])
                    kb.op(eng2, lambda e: e.scalar_tensor_tensor(out=tt_[:], in0=Pp[:, 0:NV], scalar=cw[:, ci, 0:1], in1=tt_[:],
                                                                  op0=ALU.mult, op1=ALU.add), [Pp, cw, tt_], [tt_])
                    kb.op(eng2, lambda e: e.scalar_tensor_tensor(out=tt_[:], in0=Pp[:, 2:NV + 2], scalar=cw[:, ci, 2:3], in1=tt_[:],
                                                                  op0=ALU.mult, op1=ALU.add), [Pp, cw, tt_], [tt_])
                kb.op(kb.act, lambda e: e.activation(out=g_[:], in_=a_[:], func=AF.Gelu_apprx_tanh), [a_], [g_])
                kb.op(kb.ew2, lambda e: e.tensor_tensor(out=gg[:, jj, :], in0=g_[:], in1=b_[:], op=ALU.mult), [g_, b_], [gg])
            for m in range(8):
                p = po[m % 2]
                q = sq[m % 2]

                def mm(e):
                    for k in range(22):
                        i = e.matmul(p[:, :NV], lhsT=Wout[:, k, m * 128:(m + 1) * 128], rhs=gg[:, k, :], start=(k == 0), stop=(k == 21))
                    return i
                kb.op(kb.pe, mm, [Wout, gg], [p])
                kb.op(kb.act, lambda e: e.activation(out=q[:, :NV], in_=p[:, :NV], func=AF.Square), [p], [q])
                kb.op(kb.dve, lambda e: e.tensor_copy(out=y[:, m, :], in_=p[:, :NV]), [p], [y])
                kb.op(kb.pe, lambda e: e.matmul(pst[:, :NV], lhsT=ones_bf[:], rhs=q[:, :NV], start=(m == 0), stop=(m == 7)), [ones_bf, q], [pst])
            rstd_from(pst, NV, D, 1e-6, rstd, tmp)
            xo_ = xo[wi % 2]
            for m in range(8):
                kb.op(kb.dve, lambda e: e.scalar_tensor_tensor(out=y[:, m, :], in0=y[:, m, :], scalar=gcol(layer, 3, m), in1=rstd[:, :NV],
                                                                 op0=ALU.mult, op1=ALU.mult), [y, rstd, normg], [y])
                kb.op(kb.ew2, lambda e: e.tensor_tensor(out=xo_[:, m, :], in0=y[:, m, :], in1=x[:, m, 1:NV + 1], op=ALU.add), [y, x], [xo_])
            kb.store(x2_d, x2v[:, :, s0:s0 + NV], xo_, xo_[:])
        kb.end_phase()

    QTb = [kb.dram(f"QTb{g}", [D, T], BF16) for g in range(3)]
    KTb = [kb.dram(f"KTb{g}", [D, T], BF16) for g in range(3)]
    Vb = [kb.dram(f"Vb{g}", [T, D], BF16) for g in range(3)]
    cq_d = [kb.dram(f"cq{d}", [D, T], BF16) for d in range(2)]
    ck_d = [kb.dram(f"ck{d}", [D, T], BF16) for d in range(2)]
    ckh_d = [kb.dram(f"ckh{d}", [T, D], BF16) for d in range(2)]
    cet_d = [kb.dram(f"cet{d}", [D, T // 64], F32) for d in range(2)]
    QT_d = kb.dram("QT_d", [D, T], BF16)
    KT_d = kb.dram("KT_d", [D, T], BF16)
    V_d = kb.dram("V_d", [T, D], BF16)
    oT_d = kb.dram("oT_d", [D, T], BF16)
    x1_d = kb.dram("x1_d", [D, T], F32)
    xa_d = kb.dram("xa_d", [D, T], F32)
    xb_d = kb.dram("xb_d", [D, T], F32)
    cur = xT_in
    for li, layer in enumerate(layers):
        last = li == len(layers) - 1
        nxt = yT_out if last else (xa_d if li % 2 == 0 else xb_d)
        kind = layer % 3
        j = layer // 3
        import os
        stop = int(os.environ.get("KSTOP", "99"))
        if kind == 0:
            if stop >= 1:
                phase_proj_A(layer, a_wqkv, a_wqkv[j], cur, QT_d, KT_d, V_d)
            if stop >= 2:
                phase_attn_A(layer, j, QT_d, KT_d, V_d, oT_d)
            if stop >= 3:
                phase_wo(layer, a_wo, a_wo[j], oT_d, cur, x1_d)
        elif kind == 1:
            for g in range(3):
                phase_proj_A(layer, b_wqkv, b_wqkv[0][:, g * 3072:(g + 1) * 3072], cur, QTb[g], KTb[g], Vb[g])
            phase_attn_B(layer, QTb, KTb, Vb, oT_d)
            phase_wo(layer, b_wo, b_wo[0], oT_d, cur, x1_d)
        else:
            phase_proj_C(layer, cur, cq_d, ck_d, ckh_d, cet_d, V_d, KT_d)
            phase_scan_C(layer, cq_d, ck_d, ckh_d, cet_d, V_d, KT_d, oT_d)
            phase_wo(layer, c_wo, c_wo[0], oT_d, cur, x1_d)
        if stop >= 4:
            phase_ffn(layer, x1_d, nxt)
        cur = nxt
    kb.begin_phase()
    kb.end_phase()
    return nc


def rope_tables(pos):
    half = 8
    inv = (500000.0 ** (-np.arange(half, dtype=np.float32) / half)).astype(np.float32)
    ang = pos.astype(np.float32)[:, None] * inv[None, :]
    cos, sin = np.cos(ang).astype(np.float32), np.sin(ang).astype(np.float32)
    T = pos.shape[0]
    tab = np.zeros((128, 3, T), np.float32)
    tab[:, 0, :] = 1.0
    for hd in range(2):
        b = hd * 64
        tab[b:b + 8, 0, :] = cos.T
        tab[b + 8:b + 16, 0, :] = cos.T
        tab[b:b + 8, 1, :] = -sin.T
        tab[b + 8:b + 16, 1, :] = sin.T
    tab[:, 2, :] = tab[:, 0, :] * np.float32(0.125)
    return tab


def const_mats():
    c = np.zeros((128, 6, 128), np.float32)
    c[:, 0, :] = 1.0
    c[:, 1, :] = np.eye(128, dtype=np.float32)
    for hd in range(2):
        for i in range(8):
            a, b = hd * 64 + i, hd * 64 + i + 8
            c[a, 2, b] = 1.0
            c[b, 2, a] = 1.0
    s = np.arange(128)
    c[:, 3, :] = (s[:, None] <= s[None, :]).astype(np.float32)
    same = (s[:, None] // 64) == (s[None, :] // 64)
    c[:, 4, :] = ((s[:, None] <= s[None, :]) & same).astype(np.float32)
    c[:, 5, :] = ((s[:, None] >= s[None, :]) & same).astype(np.float32)
    return c


def make_in_maps(inp, T, seqs_per_core):
    maps = []
    NT, NKC = T // 512, T // 128
    normg = np.ascontiguousarray(inp["norm_g"].reshape(4, 4, 8, 128).transpose(3, 0, 1, 2).reshape(128, 128))
    a_lam = np.ascontiguousarray(np.broadcast_to(inp["a_lambda"].reshape(1, -1), (128, 512)))
    a_sub = np.ascontiguousarray(inp["a_subln_g"].T)
    cwt = np.concatenate([inp["f_conv_w"], inp["f_conv_b"][:, None, :]], axis=1)
    f_cw = np.ascontiguousarray(cwt.reshape(4, 4, 44, 128).transpose(3, 0, 2, 1).reshape(128, 4 * 44 * 4))
    cm = const_mats()
    bm = bmask_table()
    c_lbl = np.ascontiguousarray(inp["c_lb_logits"].reshape(4, 8, 128).transpose(2, 0, 1))
    c_gn = np.ascontiguousarray(inp["c_gnorm_g"].reshape(1, 128).T)
    m01h = np.ones((128, 512), np.float32)
    m01h[:, ::64] = 0.0
    for seqs in seqs_per_core:
        xT = np.ascontiguousarray(np.concatenate(seqs, axis=0).T)
        pos = np.concatenate([np.arange(s.shape[0]) for s in seqs])
        sid = np.concatenate([np.full(s.shape[0], i) for i, s in enumerate(seqs)])
        ksid = sid[::128][:, None]
        qsid = sid[::512][None, :]
        ab = np.where(ksid == qsid, 0.0, NEG).astype(np.float32).reshape(1, NKC * NT)
        flag = np.zeros((128, 2), np.float32)
        flag[:, 0] = 1.0 if len(seqs) == 1 else 0.0
        m = {
            "xT": xT, "normg": normg, "rope": rope_tables(pos), "cmat": cm, "flag": flag,
            "abias": np.ascontiguousarray(np.broadcast_to(ab, (128, NKC * NT))),
            "a_w_qkv": inp["a_w_qkv"], "a_lam": a_lam, "a_sub": a_sub, "a_w_o": inp["a_w_o"],
            "c_w_in": inp["c_w_in"], "c_w_o": inp["c_w_o"], "c_lbl": c_lbl, "c_gn": c_gn, "m01": m01h,
            "b_w_qkv": inp["b_w_qkv"], "b_w_o": inp["b_w_o"], "bmask": bm,
            "f_w_in": inp["f_w_in"], "f_cw": f_cw, "f_w_out": inp["f_w_out"],
        }
        maps.append(m)
    return maps


_NC_CACHE = {}


def kernel(**inputs):
    inp = {k: np.asarray(v) for k, v in inputs.items()}
    T = 8192
    xp, xs = inp["x_prompt"], inp["x_sample"]
    seqs = [[xp[b]] for b in range(4)] + [[xs[2 * c], xs[2 * c + 1]] for c in range(4)]
    maps = make_in_maps(inp, T, seqs)
    if "nc" not in _NC_CACHE:
        _NC_CACHE["nc"] = build(T)
    res = run_bass_kernel_spmd(_NC_CACHE["nc"], maps, core_ids=list(range(8)))
    outs = [np.asarray(r["yT"]).T for r in res.results]
    y_prompt = np.stack(outs[:4], axis=0).astype(np.float32)
    y_sample = np.stack([o.reshape(2, 4096, D) for o in outs[4:]], axis=0).reshape(8, 4096, D).astype(np.float32)
    return (y_prompt, y_sample)
```

```python
import math
from contextlib import ExitStack
import numpy as np
import concourse.bass as bass
import concourse.mybir as mybir
from concourse.bass_utils import run_bass_kernel_spmd

F32 = mybir.dt.float32
BF16 = mybir.dt.bfloat16
AF = mybir.ActivationFunctionType
ALU = mybir.AluOpType
AX = mybir.AxisListType

D = 1024
DFF = 2816
NEG = -30000.0
B_DIL = (1, 4, 16)


def _bmask_list():
    out = []
    for g, d in enumerate(B_DIL):
        for delta in range(-4096, 4097, 128):
            q = np.arange(512)[None, :]
            k = np.arange(128)[:, None]
            diff = delta + q - k
            if np.any((diff % d == 0) & (np.abs(diff) <= 64 * d)):
                out.append((g, delta))
    return out


BMASKS = _bmask_list()


def bmask_table():
    t = np.zeros((128, len(BMASKS), 512), np.float32)
    q = np.arange(512)[None, :]
    k = np.arange(128)[:, None]
    for i, (g, delta) in enumerate(BMASKS):
        d = B_DIL[g]
        diff = delta + q - k
        t[:, i, :] = np.where((diff % d == 0) & (np.abs(diff) <= 64 * d), 1.0, 0.0)
    return t


class Stream:
    def __init__(self, name, eng, sem):
        self.name, self.eng, self.sem = name, eng, sem
        self.count = 0
        self.known = {}


class Buf:
    def __init__(self, t, name):
        self.t = t
        self.name = name
        self.w = {}
        self.r = {}
        self.isdram = False
        self.excl = False
        self.lsem = None
        self.lcount = 0
        self.ssem = None
        self.scount = 0

    def __getitem__(self, idx):
        return self.t[idx]


class KB:
    def __init__(self, nc):
        self.nc = nc
        self.top = ExitStack()
        self.sems = []
        self.free_dsems = []
        self.all_dsems = {}
        mk = lambda n, e: Stream(n, e, self.top.enter_context(nc.semaphore("s_" + n)))
        self.pe = mk("pe", nc.tensor)
        self.act = mk("act", nc.scalar)
        self.dve = mk("dve", nc.vector)
        self.pool = mk("pool", nc.gpsimd)
        import os
        self.ew2 = self.dve if os.environ.get("KPOOL", "dve") == "dve" else self.pool
        self.sp = mk("sp", nc.sync)
        self.streams = [self.pe, self.act, self.dve, self.pool, self.sp]
        self.phase = None
        self.phase_bufs = []
        self.uid = 0

    def begin_phase(self):
        self.phase = ExitStack()
        self.phase_bufs = []

    def end_phase(self):
        self.barrier()
        for b in self.phase_bufs:
            for s, c in ((b.lsem, b.lcount), (b.ssem, b.scount)):
                if s is not None:
                    self.free_dsems.append((s, c))
        self.phase.close()
        self.phase = None

    def sb(self, name, shape, dt, glob=False):
        self.uid += 1
        es = self.top if glob else self.phase
        t = es.enter_context(self.nc.sbuf_tensor(f"{name}_{self.uid}", list(shape), dt))
        b = Buf(t, name)
        if not glob:
            self.phase_bufs.append(b)
        return b

    def ps(self, name, shape, dt=F32):
        self.uid += 1
        t = self.phase.enter_context(self.nc.psum_tensor(f"{name}_{self.uid}", list(shape), dt))
        b = Buf(t, name)
        b.excl = True
        return b

    def dram(self, name, shape, dt):
        t = self.nc.dram_tensor(name, list(shape), dt, kind="Internal")
        b = Buf(t.ap(), name)
        b.isdram = True
        return b

    def _dsem(self):
        if self.free_dsems:
            return self.free_dsems.pop()
        s = self.top.enter_context(self.nc.semaphore(f"d{len(self.all_dsems)}"))
        self.all_dsems[id(s)] = s
        return (s, 0)

    def _deps(self, st, reads, writes, skip=None):
        need = {}

        def add(ev):
            s, v = ev
            if id(s) not in need or need[id(s)][1] < v:
                need[id(s)] = ev
        for b in reads:
            for ev in b.w.values():
                add(ev)
            if b.excl:
                for ev in b.r.values():
                    if ev[0] is not st.sem:
                        add(ev)
        for b in writes:
            if not b.isdram:
                for ev in b.w.values():
                    add(ev)
            for ev in b.r.values():
                add(ev)
        for s, v in need.values():
            if skip is not None and s is skip:
                continue
            if s is st.sem and st is self.pe:
                continue
            if st.known.get(id(s), 0) < v:
                st.eng.wait_ge(s, v)
                st.known[id(s)] = v

    def _mark(self, ev, reads, writes):
        for b in writes:
            if b.isdram:
                b.w[id(ev[0])] = ev
            else:
                b.w = {id(ev[0]): ev}
                b.r = {}
        for b in reads:
            if b not in writes:
                b.r[id(ev[0])] = ev

    def op(self, st, fn, reads=(), writes=()):
        reads, writes = list(reads), list(writes)
        self._deps(st, reads, writes)
        ins = fn(st.eng)
        st.count += 1
        ins.then_inc(st.sem, 1)
        self._mark((st.sem, st.count), reads, writes)
        return ins

    def load(self, sbuf, out_ap, dbuf, in_ap, st=None):
        st = st or self.sp
        if sbuf.lsem is None:
            sbuf.lsem, sbuf.lcount = self._dsem()
        self._deps(st, [dbuf], [sbuf], skip=sbuf.lsem)
        sbuf.lcount += 16
        st.eng.dma_start(out=out_ap, in_=in_ap).then_inc(sbuf.lsem, 16)
        self._mark((sbuf.lsem, sbuf.lcount), [dbuf], [sbuf])

    def store(self, dbuf, out_ap, sbuf, in_ap, st=None):
        st = st or self.pool
        if sbuf.ssem is None:
            sbuf.ssem, sbuf.scount = self._dsem()
        self._deps(st, [sbuf], [dbuf], skip=sbuf.ssem)
        sbuf.scount += 16
        st.eng.dma_start(out=out_ap, in_=in_ap).then_inc(sbuf.ssem, 16)
        self._mark((sbuf.ssem, sbuf.scount), [sbuf], [dbuf])

    def barrier(self):
        cur = [(st.sem, st.count) for st in self.streams]
        for b in self.phase_bufs:
            if b.lsem is not None:
                cur.append((b.lsem, b.lcount))
            if b.ssem is not None:
                cur.append((b.ssem, b.scount))
        for st in self.streams:
            for s, v in cur:
                if s is st.sem or v == 0:
                    continue
                if st.known.get(id(s), 0) < v:
                    st.eng.wait_ge(s, v)
                    st.known[id(s)] = v


def build(T=8192, layers=(0, 1, 2, 3)):
    nc = bass.Bass("TRN2", target_bir_lowering=False)
    kb = KB(nc)
    NT = T // 512
    NKC = T // 128
    HALF = T // 2

    def din(name, shape, dt=F32):
        b = Buf(nc.dram_tensor(name, list(shape), dt, kind="ExternalInput").ap(), name)
        b.isdram = True
        return b

    xT_in = din("xT", [D, T])
    yT_out = Buf(nc.dram_tensor("yT", [D, T], F32, kind="ExternalOutput").ap(), "yT")
    yT_out.isdram = True
    normg_in = din("normg", [128, 4 * 4 * 8])
    rope_in = din("rope", [128, 3, T])
    cmat_in = din("cmat", [128, 7, 128])
    flag_in = din("flag", [128, 2])
    abias_in = din("abias", [128, NKC * NT])
    a_wqkv = din("a_w_qkv", [2, D, 3072])
    a_lam = din("a_lam", [128, 2 * 256])
    a_sub = din("a_sub", [128, 2])
    a_wo = din("a_w_o", [2, D, D])
    b_wqkv = din("b_w_qkv", [1, D, 9216])
    b_wo = din("b_w_o", [1, D, D])
    bmask_in = din("bmask", [128, len(BMASKS), 512])
    c_win = din("c_w_in", [1, D, 5120])
    c_wo = din("c_w_o", [1, D, D])
    c_lbl = din("c_lbl", [128, 4, 8])
    c_gn = din("c_gn", [128, 1])
    m01_in = din("m01", [128, 512])
    f_win = din("f_w_in", [4, D, 2 * DFF])
    f_cw = din("f_cw", [128, 4 * 44 * 4])
    f_wout = din("f_w_out", [4, DFF, D])

    kb.begin_phase()
    ones_bf = kb.sb("ones", [128, 128], BF16, glob=True)
    ident_bf = kb.sb("ident", [128, 128], BF16, glob=True)
    rperm_bf = kb.sb("rperm", [128, 128], BF16, glob=True)
    normg = kb.sb("normg", [128, 128], F32, glob=True)
    flag = kb.sb("flag", [128, 2], F32, glob=True)
    cv = kb.sb("cv", [128, 4], F32, glob=True)
    ones_f = kb.sb("ones_f", [128, 128], F32, glob=True)
    shift_f = kb.sb("shift_f", [128, 64], F32, glob=True)
    cmask = kb.sb("cmask", [128, 2, 128], F32, glob=True)
    m01 = kb.sb("m01", [128, 512], F32, glob=True)
    cst = kb.sb("cst", [128, 7, 128], F32)
    kb.load(cst, cst[:], cmat_in, cmat_in[:])
    kb.load(normg, normg[:], normg_in, normg_in[:])
    kb.load(flag, flag[:], flag_in, flag_in[:])
    kb.op(kb.dve, lambda e: e.memset(cv[:, 0:1], 0.125), [], [cv])
    kb.op(kb.dve, lambda e: e.memset(cv[:, 1:2], 1.0), [cv], [cv])
    kb.op(kb.dve, lambda e: e.memset(cv[:, 2:3], float(128 ** -0.5)), [cv], [cv])
    kb.load(m01, m01[:], m01_in, m01_in[:])
    kb.op(kb.dve, lambda e: e.tensor_copy(out=cmask[:], in_=cst[:, 4:6, :]), [cst], [cmask])
    kb.op(kb.dve, lambda e: e.tensor_copy(out=ones_bf[:], in_=cst[:, 0, :]), [cst], [ones_bf])
    kb.op(kb.dve, lambda e: e.tensor_copy(out=ones_f[:], in_=cst[:, 0, :]), [cst], [ones_f])
    kb.op(kb.dve, lambda e: e.tensor_copy(out=shift_f[:], in_=cst[:, 6, 0:64]), [cst], [shift_f])
    kb.op(kb.dve, lambda e: e.tensor_copy(out=ident_bf[:], in_=cst[:, 1, :]), [cst], [ident_bf])
    kb.op(kb.dve, lambda e: e.tensor_copy(out=rperm_bf[:], in_=cst[:, 2, :]), [cst], [rperm_bf])
    kb.end_phase()

    def gcol(layer, n, c):
        i = (layer * 4 + n) * 8 + c
        return normg[:, i:i + 1]

    cast_rr = [0]

    def load_cast_w(wdst, dst_ap_fn, src_buf, src_ap_fn, ncols, stg, kc=8, blk=512):
        for c0 in range(0, ncols, blk):
            c1 = min(ncols, c0 + blk)
            s = stg[cast_rr[0] % len(stg)]
            kb.load(s, s[:, :kc, :c1 - c0], src_buf, src_ap_fn(c0, c1))
            st = kb.dve if cast_rr[0] % 2 == 0 else kb.act
            if st is kb.dve:
                kb.op(st, lambda e: e.tensor_copy(out=dst_ap_fn(c0, c1), in_=s[:, :kc, :c1 - c0]), [s], [wdst])
            else:
                kb.op(st, lambda e: e.activation(out=dst_ap_fn(c0, c1), in_=s[:, :kc, :c1 - c0], func=AF.Copy), [s], [wdst])
            cast_rr[0] += 1

    def rstd_from(ps_stat, n, nfeat, eps, rstd, tmp):
        kb.op(kb.act, lambda e: e.activation(out=tmp[:, :n], in_=ps_stat[:, :n], func=AF.Sqrt,
                                             bias=float(eps), scale=1.0 / nfeat), [ps_stat], [tmp])
        kb.op(kb.dve, lambda e: e.reciprocal(out=rstd[:, :n], in_=tmp[:, :n]), [tmp], [rstd])

    def phase_proj_A(layer, wbuf, wap, x_d, QT_d, KT_d, V_d):
        kb.begin_phase()
        W = kb.sb("Wqkv", [128, 8, 3072], BF16)
        stg = [kb.sb(f"stg{i}", [128, 8, 256], F32) for i in range(2)]
        wsrc = wap.rearrange("(c p) n -> p c n", p=128)
        load_cast_w(W, lambda a, b: W[:, :, a:b], wbuf, lambda a, b: wsrc[:, :, a:b], 3072, stg, blk=256)
        xs = [kb.sb(f"xs{i}", [128, 8, 512], F32) for i in range(2)]
        rp = [kb.sb(f"rp{i}", [128, 3, 512], F32) for i in range(2)]
        hT = [kb.sb(f"hT{i}", [128, 8, 512], BF16) for i in range(2)]
        sq = [kb.sb(f"sq{i}", [128, 512], BF16) for i in range(2)]
        tmp = kb.sb("tmp", [128, 512], F32)
        rstd = kb.sb("rstd", [128, 512], F32)
        qsb = [kb.sb(f"qsb{i}", [128, 512], BF16) for i in range(2)]
        t1 = [kb.sb(f"t1{i}", [128, 512], F32) for i in range(2)]
        t2 = [kb.sb(f"t2{i}", [128, 512], F32) for i in range(2)]
        qo = [kb.sb(f"qo{i}", [128, 8, 512], BF16) for i in range(2)]
        ko = [kb.sb(f"ko{i}", [128, 8, 512], BF16) for i in range(2)]
        vo = [kb.sb(f"vo{i}", [128, 4, 1024], BF16) for i in range(2)]
        pst = kb.ps("pst", [128, 512])
        pq = [kb.ps(f"pq{i}", [128, 512]) for i in range(3)]
        pr = [kb.ps(f"pr{i}", [128, 512]) for i in range(2)]
        xv = x_d[:].rearrange("(c p) t -> p c t", p=128)
        Vv = V_d[:].rearrange("(n p) f -> p n f", p=128)
        QTv = QT_d[:].rearrange("(c p) t -> p c t", p=128)
        KTv = KT_d[:].rearrange("(c p) t -> p c t", p=128)

        def ld(tt):
            s = tt % 2
            kb.load(xs[s], xs[s][:], x_d, xv[:, :, tt * 512:(tt + 1) * 512])
            kb.load(rp[s], rp[s][:], rope_in, rope_in[:, :, tt * 512:(tt + 1) * 512])

        import os
        KSUB = int(os.environ.get("KSUB", "99"))
        if KSUB < 1:
            kb.end_phase()
            return
        ld(0)
        cnt = 0
        for tt in range(NT):
            s = tt % 2
            if tt + 1 < NT:
                ld(tt + 1)
            x, h = xs[s], hT[s]
            if KSUB < 2:
                continue
            for c in range(8):
                q = sq[c % 2]
                kb.op(kb.act, lambda e: e.activation(out=q[:], in_=x[:, c, :], func=AF.Square), [x], [q])
                kb.op(kb.pe, lambda e: e.matmul(pst[:], lhsT=ones_bf[:], rhs=q[:], start=(c == 0), stop=(c == 7)),
                      [ones_bf, q], [pst])
            if KSUB < 3:
                continue
            rstd_from(pst, 512, D, 1e-6, rstd, tmp)
            if KSUB < 4:
                continue
            for c in range(8):
                kb.op(kb.dve, lambda e: e.scalar_tensor_tensor(out=h[:, c, :], in0=x[:, c, :], scalar=gcol(layer, 0, c),
                                                                 in1=rstd[:], op0=ALU.mult, op1=ALU.mult),
                      [x, rstd, normg], [h])
            if KSUB < 5:
                continue
            for m in range(16):
                p = pq[cnt % 3]
                r_ = pr[cnt % 2]
                qs, a1, a2 = qsb[cnt % 2], t1[cnt % 2], t2[cnt % 2]
                dst = (qo if m < 8 else ko)[s]
                scale = 0.125 if m < 8 else 1.0
                cnt += 1

                def mm(e):
                    for k in range(8):
                        i = e.matmul(p[:], lhsT=W[:, k, m * 128:(m + 1) * 128], rhs=h[:, k, :], start=(k == 0), stop=(k == 7))
                    return i
                KQ = int(os.environ.get("KQ", "99"))
                kb.op(kb.pe, mm, [W, h], [p])
                if KQ < 2:
                    continue
                kb.op(kb.act, lambda e: e.activation(out=qs[:], in_=p[:], func=AF.Copy, scale=scale), [p], [qs])
                if KQ < 3:
                    continue
                kb.op(kb.pe, lambda e: e.matmul(r_[:], lhsT=rperm_bf[:], rhs=qs[:], start=True, stop=True), [rperm_bf, qs], [r_])
                if KQ < 4:
                    continue
                kb.op(kb.dve, lambda e: e.tensor_tensor(out=a1[:], in0=p[:], in1=rp[s][:, (2 if m < 8 else 0), :], op=ALU.mult), [p, rp[s]], [a1])
                if KQ < 5:
                    continue
                kb.op(kb.dve, lambda e: e.tensor_tensor(out=a2[:], in0=r_[:], in1=rp[s][:, 1, :], op=ALU.mult), [r_, rp[s]], [a2])
                if KQ < 6:
                    continue
                kb.op(kb.ew2, lambda e: e.tensor_tensor(out=dst[:, m % 8, :], in0=a1[:], in1=a2[:], op=ALU.add), [a1, a2], [dst])
            if KSUB < 6:
                continue
            kb.store(QT_d, QTv[:, :, tt * 512:(tt + 1) * 512], qo[s], qo[s][:])
            kb.store(KT_d, KTv[:, :, tt * 512:(tt + 1) * 512], ko[s], ko[s][:])
            if KSUB < 7:
                continue
            for tb in range(4):
                for nb in range(2):
                    p = pq[cnt % 3]
                    cnt += 1

                    def mm(e):
                        for k in range(8):
                            i = e.matmul(p[:], lhsT=h[:, k, tb * 128:(tb + 1) * 128],
                                         rhs=W[:, k, 2048 + nb * 512:2048 + (nb + 1) * 512], start=(k == 0), stop=(k == 7))
                        return i
                    kb.op(kb.pe, mm, [W, h], [p])
                    kb.op(kb.act, lambda e: e.activation(out=vo[s][:, tb, nb * 512:(nb + 1) * 512], in_=p[:], func=AF.Copy), [p], [vo[s]])
            kb.store(V_d, Vv[:, tt * 4:(tt + 1) * 4, :], vo[s], vo[s][:])
        kb.end_phase()

    def phase_attn_A(layer, j, QT_d, KT_d, V_d, oT_d):
        kb.begin_phase()
        lam_init = 0.8 - 0.6 * math.exp(-0.3 * layer)
        NQT = NT
        QTh = [kb.sb(f"QTh{i}", [128, T], BF16) for i in range(2)]
        KTh = [kb.sb(f"KTh{i}", [128, T], BF16) for i in range(2)]
        Vh = [kb.sb(f"Vh{i}", [128, NKC, 128], BF16) for i in range(2)]
        P = [kb.sb(f"P{i}", [128, 1024], BF16) for i in range(3)]
        abias = kb.sb("abias", [128, NKC * NT], F32)
        lamt = kb.sb("lamt", [128, 256], F32)
        lamp = kb.sb("lamp", [128, 128], F32)
        lams = kb.sb("lams", [128, 8], F32)
        gsub = kb.sb("gsub", [128, 2], F32)
        r0 = kb.sb("r0", [128, 512], F32)
        r1 = kb.sb("r1", [128, 512], F32)
        n0 = kb.sb("n0", [128, 512], F32)
        n1 = kb.sb("n1", [128, 512], F32)
        sqb = kb.sb("sqb", [128, 512], BF16)
        tmp = kb.sb("tmp", [128, 512], F32)
        rstd = kb.sb("rstd", [128, 512], F32)
        ob = [kb.sb(f"ob{i}", [128, 512], BF16) for i in range(2)]
        S = [kb.ps(f"S{i}", [128, 1024]) for i in range(2)]
        O0, O1 = kb.ps("O0", [128, 512]), kb.ps("O1", [128, 512])
        L0, L1 = kb.ps("L0", [128, 512]), kb.ps("L1", [128, 512])
        kb.load(abias, abias[:], abias_in, abias_in[:])
        kb.load(lamt, lamt[:], a_lam, a_lam[:, j * 256:(j + 1) * 256])
        kb.load(gsub, gsub[:], a_sub, a_sub[:])
        kb.op(kb.dve, lambda e: e.tensor_tensor(out=lamp[:, 0:64], in0=lamt[:, 0:64], in1=lamt[:, 64:128], op=ALU.mult), [lamt], [lamp])
        kb.op(kb.dve, lambda e: e.tensor_tensor(out=lamp[:, 64:128], in0=lamt[:, 128:192], in1=lamt[:, 192:256], op=ALU.mult), [lamt, lamp], [lamp])
        kb.op(kb.dve, lambda e: e.reduce_sum(out=lams[:, 0:1], in_=lamp[:, 0:64], axis=AX.X), [lamp], [lams])
        kb.op(kb.dve, lambda e: e.reduce_sum(out=lams[:, 1:2], in_=lamp[:, 64:128], axis=AX.X), [lamp, lams], [lams])
        kb.op(kb.act, lambda e: e.activation(out=lams[:, 2:4], in_=lams[:, 0:2], func=AF.Exp), [lams], [lams])
        kb.op(kb.dve, lambda e: e.tensor_tensor(out=lams[:, 4:5], in0=lams[:, 3:4], in1=lams[:, 2:3], op=ALU.subtract), [lams], [lams])
        kb.op(kb.dve, lambda e: e.tensor_scalar_add(out=lams[:, 5:6], in0=lams[:, 4:5], scalar1=-lam_init), [lams], [lams])
        kb.op(kb.dve, lambda e: e.tensor_scalar_mul(out=lams[:, 6:7], in0=gsub[:, j:j + 1], scalar1=1.0 - lam_init), [gsub, lams], [lams])
        neglam = lams[:, 5:6]
        gs = lams[:, 6:7]
        QTv, KTv = QT_d[:], KT_d[:]
        Vv = V_d[:].rearrange("(n p) f -> p n f", p=128)

        def ldh(h):
            s = h % 2
            kb.load(QTh[s], QTh[s][:], QT_d, QTv[h * 128:(h + 1) * 128, :])
            kb.load(KTh[s], KTh[s][:], KT_d, KTv[h * 128:(h + 1) * 128, :])
            step = max(1, NKC // 4)
            for c0 in range(0, NKC, step):
                kb.load(Vh[s], Vh[s][:, c0:c0 + step, :], V_d, Vv[:, c0:c0 + step, h * 128:(h + 1) * 128])

        acc0 = [kb.sb(f"acc0{i}", [128, 512], F32) for i in range(2)]
        acc1 = [kb.sb(f"acc1{i}", [128, 512], F32) for i in range(2)]
        ai = 0
        ldh(0)
        it = 0
        oi = 0
        for h in range(8):
            s = h % 2
            if h + 1 < 8:
                ldh(h + 1)
            Q, K, V = QTh[s], KTh[s], Vh[s]
            for qt in range(NQT):
                def qk(kc, slot):
                    Sx = S[slot]

                    def f(e):
                        e.matmul(Sx[:, 0:512], lhsT=K[0:64, kc * 128:(kc + 1) * 128], rhs=Q[0:64, qt * 512:(qt + 1) * 512],
                                 start=True, stop=True, tile_position=(0, 0))
                        return e.matmul(Sx[:, 512:1024], lhsT=K[64:128, kc * 128:(kc + 1) * 128], rhs=Q[64:128, qt * 512:(qt + 1) * 512],
                                        start=True, stop=True, tile_position=(64, 0))
                    kb.op(kb.pe, f, [K, Q], [Sx])

                qk(0, it % 2)
                for kc in range(NKC):
                    slot = it % 2
                    Px = P[it % 3]
                    Sx = S[slot]
                    it += 1
                    if kc + 1 < NKC:
                        qk(kc + 1, it % 2)
                    bcol = abias[:, kc * NT + qt:kc * NT + qt + 1]
                    kb.op(kb.act, lambda e: e.activation(out=Px[:], in_=Sx[:], func=AF.Exp, bias=bcol), [Sx, abias], [Px])

                    a0 = acc0[ai % 2]
                    if kc == 0:
                        kb.op(kb.dve, lambda e: e.tensor_copy(out=a0[:], in_=Px[:, 0:512]), [Px], [a0])
                    else:
                        kb.op(kb.dve, lambda e: e.tensor_tensor(out=a0[:], in0=Px[:, 0:512], in1=a0[:], op=ALU.add), [Px, a0], [a0])

                    def pv(e):
                        e.matmul(O0[:], lhsT=V[:, kc, :], rhs=Px[:, 0:512], start=(kc == 0), stop=(kc == NKC - 1))
                        e.matmul(O1[:], lhsT=V[:, kc, :], rhs=Px[:, 512:1024], start=(kc == 0), stop=(kc == NKC - 1))
                        return e.matmul(L1[:], lhsT=ones_bf[:], rhs=Px[:, 512:1024], start=(kc == 0), stop=(kc == NKC - 1))
                    kb.op(kb.pe, pv, [V, Px, ones_bf], [O0, O1, L1])
                a0 = acc0[ai % 2]
                ai += 1
                kb.op(kb.pe, lambda e: e.matmul(L0[:], lhsT=ones_f[:], rhs=a0[:], start=True, stop=True), [ones_f, a0], [L0])
                kb.op(kb.dve, lambda e: e.reciprocal(out=r0[:], in_=L0[:]), [L0], [r0])
                kb.op(kb.dve, lambda e: e.reciprocal(out=r1[:], in_=L1[:]), [L1], [r1])
                kb.op(kb.dve, lambda e: e.tensor_tensor(out=n0[:], in0=O0[:], in1=r0[:], op=ALU.mult), [O0, r0], [n0])
                kb.op(kb.dve, lambda e: e.tensor_tensor(out=n1[:], in0=O1[:], in1=r1[:], op=ALU.mult), [O1, r1], [n1])
                kb.op(kb.dve, lambda e: e.scalar_tensor_tensor(out=n0[:], in0=n1[:], scalar=neglam, in1=n0[:], op0=ALU.mult, op1=ALU.add),
                      [n1, n0, lams], [n0])
                kb.op(kb.act, lambda e: e.activation(out=sqb[:], in_=n0[:], func=AF.Square), [n0], [sqb])
                kb.op(kb.pe, lambda e: e.matmul(L0[:], lhsT=ones_bf[:], rhs=sqb[:], start=True, stop=True), [ones_bf, sqb], [L0])
                rstd_from(L0, 512, 128, 1e-5, rstd, tmp)
                o = ob[oi % 2]
                oi += 1
                kb.op(kb.dve, lambda e: e.scalar_tensor_tensor(out=o[:], in0=n0[:], scalar=gs, in1=rstd[:], op0=ALU.mult, op1=ALU.mult),
                      [n0, rstd, lams], [o])
                kb.store(oT_d, oT_d[h * 128:(h + 1) * 128, qt * 512:(qt + 1) * 512], o, o[:])
        kb.end_phase()

    def phase_attn_B(layer, QTb, KTb, Vb, oT_d):
        kb.begin_phase()
        NM = len(BMASKS)
        midx = {gd: i for i, gd in enumerate(BMASKS)}
        Qg = [kb.sb(f"Qg{g}", [128, T], BF16) for g in range(3)]
        Kg = [kb.sb(f"Kg{g}", [128, T], BF16) for g in range(3)]
        Vg = [kb.sb(f"Vg{g}", [128, NKC, 128], BF16) for g in range(3)]
        masks = kb.sb("masks", [128, NM, 512], BF16)
        stg = [kb.sb(f"mstg{i}", [128, 1, 512], F32) for i in range(1)]
        abias = kb.sb("abias", [128, NKC * NT], F32)
        P = [kb.sb(f"P{i}", [128, 1024], BF16) for i in range(2)]
        PM = [kb.sb(f"PM{i}", [128, 1024], BF16) for i in range(3)]
        rr = kb.sb("rr", [128, 512], F32)
        rbs = kb.sb("rbs", [128, 512], F32)
        ob = [kb.sb(f"ob{i}", [128, 512], BF16) for i in range(2)]
        S = [kb.ps(f"S{i}", [128, 1024]) for i in range(2)]
        OX = [kb.ps("OA", [128, 512]), kb.ps("OB", [128, 512])]
        RB = kb.ps("RB", [128, 512])
        kb.load(abias, abias[:], abias_in, abias_in[:])
        kb.op(kb.dve, lambda e: e.memset(rr[:], 0.0), [], [rr])
        for i0_ in range(NM):
            sg = stg[0]
            kb.load(sg, sg[:], bmask_in, bmask_in[:, i0_:i0_ + 1, :])
            kb.op(kb.dve, lambda e: e.tensor_copy(out=masks[:, i0_:i0_ + 1, :], in_=sg[:]), [sg], [masks])
        it = 0
        oi = 0
        for hp in range(8):
            for g in range(3):
                kb.load(Qg[g], Qg[g][:], QTb[g], QTb[g][hp * 128:(hp + 1) * 128, :])
                kb.load(Kg[g], Kg[g][:], KTb[g], KTb[g][hp * 128:(hp + 1) * 128, :])
                Vv = Vb[g][:].rearrange("(n p) f -> p n f", p=128)
                step = max(1, NKC // 4)
                for c0 in range(0, NKC, step):
                    kb.load(Vg[g], Vg[g][:, c0:c0 + step, :], Vb[g], Vv[:, c0:c0 + step, hp * 128:(hp + 1) * 128])
            for qt in range(NT):
                tiles = [(g, kc) for g in range(3) for kc in range(NKC) if (g, qt * 512 - kc * 128) in midx]
                qs_ = slice(qt * 512, (qt + 1) * 512)

                def qkB(ti, slot):
                    g, kc = tiles[ti]
                    Sx = S[slot]
                    ks_ = slice(kc * 128, (kc + 1) * 128)

                    def f(e):
                        e.matmul(Sx[:, 0:512], lhsT=Kg[g][0:64, ks_], rhs=Qg[g][0:64, qs_], start=True, stop=True, tile_position=(0, 0))
                        return e.matmul(Sx[:, 512:1024], lhsT=Kg[g][64:128, ks_], rhs=Qg[g][64:128, qs_], start=True, stop=True,
                                        tile_position=(64, 0))
                    kb.op(kb.pe, f, [Kg[g], Qg[g]], [Sx])
                qkB(0, it % 2)
                nt_ = len(tiles)
                for ti, (g, kc) in enumerate(tiles):
                    Sx = S[it % 2]
                    Px = P[it % 2]
                    Pm = PM[it % 3]
                    it += 1
                    if ti + 1 < nt_:
                        qkB(ti + 1, it % 2)
                    mi = midx[(g, qt * 512 - kc * 128)]
                    bcol = abias[:, kc * NT + qt:kc * NT + qt + 1]
                    kb.op(kb.act, lambda e: e.activation(out=Px[:], in_=Sx[:], func=AF.Exp, bias=bcol), [Sx, abias], [Px])
                    kb.op(kb.dve, lambda e: e.tensor_tensor(out=Pm[:].rearrange("p (a q) -> p a q", a=2),
                                                            in0=Px[:].rearrange("p (a q) -> p a q", a=2),
                                                            in1=masks[:, mi:mi + 1, :].to_broadcast([128, 2, 512]), op=ALU.mult), [Px, masks], [Pm])

                    def pvB(e):
                        for a in range(2):
                            e.matmul(OX[a][0:64, :], lhsT=Vg[g][:, kc, a * 64:(a + 1) * 64], rhs=Pm[:, a * 512:(a + 1) * 512],
                                     start=(ti == 0), stop=(ti == nt_ - 1), tile_position=(0, 0))
                            i = e.matmul(OX[a][64:128, :], lhsT=ones_bf[:, 0:64], rhs=Pm[:, a * 512:(a + 1) * 512],
                                         start=(ti == 0), stop=(ti == nt_ - 1), tile_position=(0, 64))
                        return i
                    kb.op(kb.pe, pvB, [Vg[g], Pm, ones_bf], [OX[0], OX[1]])
                for a in range(2):
                    kb.op(kb.dve, lambda e: e.reciprocal(out=rr[64:128, :], in_=OX[a][64:128, :]), [OX[a]], [rr])
                    kb.op(kb.pe, lambda e: e.matmul(RB[0:64, :], lhsT=shift_f[:], rhs=rr[:], start=True, stop=True), [shift_f, rr], [RB])
                    kb.op(kb.act, lambda e: e.activation(out=rbs[0:64, :], in_=RB[0:64, :], func=AF.Copy), [RB], [rbs])
                    o = ob[oi % 2]
                    oi += 1
                    kb.op(kb.dve, lambda e: e.tensor_tensor(out=o[0:64, :], in0=OX[a][0:64, :], in1=rbs[0:64, :], op=ALU.mult), [OX[a], rbs], [o])
                    r_0 = a * 64
                    kb.store(oT_d, oT_d[hp * 128 + r_0:hp * 128 + r_0 + 64, qs_], o, o[0:64, :])
        kb.end_phase()

    def phase_proj_C(layer, x_d, qt_d, kt_d, kh_d, et_d, V_d, gT_d):
        kb.begin_phase()
        W = kb.sb("Wc", [128, 8, 5120], BF16)
        stg = [kb.sb(f"stg{i}", [128, 8, 256], F32) for i in range(2)]
        wsrc = c_win[0].rearrange("(c p) n -> p c n", p=128)
        load_cast_w(W, lambda a, b: W[:, :, a:b], c_win, lambda a, b: wsrc[:, :, a:b], 5120, stg, blk=256)
        lbl = kb.sb("lbl", [128, 4, 8], F32)
        lbe = kb.sb("lbe", [128, 4, 8], F32)
        lbs = kb.sb("lbs", [128, 4, 8], F32)
        kb.load(lbl, lbl[:], c_lbl, c_lbl[:])
        kb.op(kb.act, lambda e: e.activation(out=lbe[:], in_=lbl[:], func=AF.Exp), [lbl], [lbe])
        kb.op(kb.dve, lambda e: e.tensor_tensor(out=lbs[:, 0, :], in0=lbe[:, 0, :], in1=lbe[:, 1, :], op=ALU.add), [lbe], [lbs])
        kb.op(kb.dve, lambda e: e.tensor_tensor(out=lbs[:, 0, :], in0=lbs[:, 0, :], in1=lbe[:, 2, :], op=ALU.add), [lbe, lbs], [lbs])
        kb.op(kb.dve, lambda e: e.tensor_tensor(out=lbs[:, 0, :], in0=lbs[:, 0, :], in1=lbe[:, 3, :], op=ALU.add), [lbe, lbs], [lbs])
        kb.op(kb.dve, lambda e: e.tensor_copy(out=lbs[:, 1, :], in_=lbe[:, 1, :]), [lbe, lbs], [lbs])
        for i in range(2, layer + 1):
            kb.op(kb.dve, lambda e: e.tensor_tensor(out=lbs[:, 1, :], in0=lbs[:, 1, :], in1=lbe[:, i, :], op=ALU.add), [lbe, lbs], [lbs])
        kb.op(kb.dve, lambda e: e.reciprocal(out=lbs[:, 2, :], in_=lbs[:, 0, :]), [lbs], [lbs])
        kb.op(kb.dve, lambda e: e.tensor_tensor(out=lbs[:, 3, :], in0=lbs[:, 0, :], in1=lbs[:, 1, :], op=ALU.subtract), [lbs], [lbs])
        kb.op(kb.dve, lambda e: e.tensor_tensor(out=lbs[:, 3, :], in0=lbs[:, 3, :], in1=lbs[:, 2, :], op=ALU.mult), [lbs], [lbs])
        xs = [kb.sb(f"xs{i}", [128, 8, 512], F32) for i in range(2)]
        hT = kb.sb("hT", [128, 8, 512], BF16)
        sq = [kb.sb(f"sq{i}", [128, 512], BF16) for i in range(2)]
        tmp = kb.sb("tmp", [128, 512], F32)
        rstd = kb.sb("rstd", [128, 512], F32)
        F = lambda n: kb.sb(n, [128, 512], F32)
        qs, sg, kk, lf, G, Gd, Ek, eG, enG, eK = [F(n) for n in ("qs", "sg", "kk", "lf", "G", "Gd", "Ek", "eG", "enG", "eK")]
        gate = [kb.sb(f"gate{i}", [128, 512], BF16) for i in range(2)]
        qo = [kb.sb(f"qo{i}", [128, 512], BF16) for i in range(2)]
        ko = [kb.sb(f"ko{i}", [128, 512], BF16) for i in range(2)]
        kh = [kb.sb(f"kh{i}", [128, 512], BF16) for i in range(2)]
        kht = [kb.sb(f"kht{i}", [128, 4, 128], BF16) for i in range(2)]
        eto = [kb.sb(f"eto{i}", [128, 8], F32) for i in range(2)]
        vo = kb.sb("vo", [128, 4, 1024], BF16)
        pst = kb.ps("pst", [128, 512])
        pq = [kb.ps(f"pq{i}", [128, 512]) for i in range(3)]
        ptr = [kb.ps(f"ptr{i}", [128, 128], BF16) for i in range(2)]
        xv = x_d[:].rearrange("(c p) t -> p c t", p=128)
        Vv = V_d[:].rearrange("(n p) f -> p n f", p=128)
        khv = [kh_d[d][:].rearrange("(n p) f -> p n f", p=128) for d in range(2)]

        def ld(tt):
            kb.load(xs[tt % 2], xs[tt % 2][:], x_d, xv[:, :, tt * 512:(tt + 1) * 512])
        ld(0)
        cnt = 0
        oc = 0
        for tt in range(NT):
            if tt + 1 < NT:
                ld(tt + 1)
            x, h = xs[tt % 2], hT
            for c in range(8):
                q = sq[c % 2]
                kb.op(kb.act, lambda e: e.activation(out=q[:], in_=x[:, c, :], func=AF.Square), [x], [q])
                kb.op(kb.pe, lambda e: e.matmul(pst[:], lhsT=ones_bf[:], rhs=q[:], start=(c == 0), stop=(c == 7)), [ones_bf, q], [pst])
            rstd_from(pst, 512, D, 1e-6, rstd, tmp)
            for c in range(8):
                kb.op(kb.dve, lambda e: e.scalar_tensor_tensor(out=h[:, c, :], in0=x[:, c, :], scalar=gcol(layer, 0, c),
                                                                 in1=rstd[:], op0=ALU.mult, op1=ALU.mult), [x, rstd, normg], [h])

            def proj(col0):
                nonlocal cnt
                p = pq[cnt % 3]
                cnt += 1

                def mm(e):
                    for k in range(8):
                        i = e.matmul(p[:], lhsT=W[:, k, col0:col0 + 128], rhs=h[:, k, :], start=(k == 0), stop=(k == 7))
                    return i
                kb.op(kb.pe, mm, [W, h], [p])
                return p
            csl = slice(tt * 512, (tt + 1) * 512)
            for hd in range(8):
                rows = slice(hd * 128, (hd + 1) * 128)
                p = proj(hd * 128)
                kb.op(kb.act, lambda e: e.activation(out=qs[:], in_=p[:], func=AF.Silu), [p], [qs])
                p = proj(4096 + hd * 128)
                gt = gate[oc % 2]
                kb.op(kb.act, lambda e: e.activation(out=gt[:], in_=p[:], func=AF.Silu), [p], [gt])
                kb.store(gT_d, gT_d[rows, csl], gt, gt[:])
                for d in range(2):
                    qo_, ko_, kh_, kht_, eto_ = qo[oc % 2], ko[oc % 2], kh[oc % 2], kht[oc % 2], eto[oc % 2]
                    oc += 1
                    p = proj(1024 + d * 1024 + hd * 128)
                    kb.op(kb.act, lambda e: e.activation(out=sg[:], in_=p[:], func=AF.Sigmoid, scale=-1.0), [p], [sg])
                    kb.op(kb.dve, lambda e: e.tensor_scalar_mul(out=kk[:], in0=sg[:], scalar1=lbs[:, 3, hd:hd + 1]), [sg, lbs], [kk])
                    kb.op(kb.act, lambda e: e.activation(out=lf[:], in_=kk[:], func=AF.Ln, scale=-1.0, bias=1.0), [kk], [lf])
                    kb.op(kb.dve, lambda e: e.tensor_tensor_scan(out=G[:], data0=m01[:], data1=lf[:], initial=0.0, op0=ALU.mult, op1=ALU.add),
                          [m01, lf], [G])
                    G3 = G[:].rearrange("p (c t) -> p c t", t=64)
                    TOTb = G3[:, :, 63:64].to_broadcast([128, 8, 64])
                    r3 = lambda b: b[:].rearrange("p (c t) -> p c t", t=64)
                    if d == 0:
                        Gdb = G
                        kb.op(kb.dve, lambda e: e.tensor_tensor(out=r3(Ek), in0=TOTb, in1=G3, op=ALU.subtract), [G], [Ek])
                    else:
                        Gdb = Gd
                        kb.op(kb.dve, lambda e: e.tensor_tensor(out=Ek[:], in0=G[:], in1=lf[:], op=ALU.subtract), [G, lf], [Ek])
                        kb.op(kb.dve, lambda e: e.tensor_tensor(out=r3(Gd), in0=TOTb, in1=r3(Ek), op=ALU.subtract), [G, Ek], [Gd])
                    kb.op(kb.act, lambda e: e.activation(out=eG[:], in_=Gdb[:], func=AF.Exp), [Gdb], [eG])
                    kb.op(kb.act, lambda e: e.activation(out=enG[:], in_=Gdb[:], func=AF.Exp, scale=-1.0), [Gdb], [enG])
                    kb.op(kb.act, lambda e: e.activation(out=eK[:], in_=Ek[:], func=AF.Exp), [Ek], [eK])
                    kb.op(kb.act, lambda e: e.activation(out=eto_[:], in_=G3[:, :, 63], func=AF.Exp), [G], [eto_])
                    kb.op(kb.dve, lambda e: e.scalar_tensor_tensor(out=qo_[:], in0=qs[:], scalar=cv[:, 2:3], in1=eG[:], op0=ALU.mult, op1=ALU.mult),
                          [qs, cv, eG], [qo_])
                    kb.op(kb.dve, lambda e: e.tensor_tensor(out=ko_[:], in0=kk[:], in1=enG[:], op=ALU.mult), [kk, enG], [ko_])
                    kb.op(kb.dve, lambda e: e.tensor_tensor(out=kh_[:], in0=kk[:], in1=eK[:], op=ALU.mult), [kk, eK], [kh_])
                    for tb in range(4):
                        pt = ptr[tb % 2]
                        kb.op(kb.pe, lambda e: e.transpose(pt[:], kh_[:, tb * 128:(tb + 1) * 128], ident_bf[:]), [kh_, ident_bf], [pt])
                        kb.op(kb.act, lambda e: e.activation(out=kht_[:, tb, :], in_=pt[:], func=AF.Copy), [pt], [kht_])
                    kb.store(qt_d[d], qt_d[d][rows, csl], qo_, qo_[:])
                    kb.store(kt_d[d], kt_d[d][rows, csl], ko_, ko_[:])
                    kb.store(kh_d[d], khv[d][:, tt * 4:(tt + 1) * 4, rows], kht_, kht_[:])
                    kb.store(et_d[d], et_d[d][rows, tt * 8:(tt + 1) * 8], eto_, eto_[:])
            for tb in range(4):
                for nb in range(2):
                    p = pq[cnt % 3]
                    cnt += 1

                    def mm(e):
                        for k in range(8):
                            i = e.matmul(p[:], lhsT=h[:, k, tb * 128:(tb + 1) * 128],
                                         rhs=W[:, k, 3072 + nb * 512:3072 + (nb + 1) * 512], start=(k == 0), stop=(k == 7))
                        return i
                    kb.op(kb.pe, mm, [W, h], [p])
                    kb.op(kb.act, lambda e: e.activation(out=vo[:, tb, nb * 512:(nb + 1) * 512], in_=p[:], func=AF.Copy), [p], [vo])
            kb.store(V_d, Vv[:, tt * 4:(tt + 1) * 4, :], vo, vo[:])
        kb.end_phase()

    def phase_scan_C(layer, qt_d, kt_d, kh_d, et_d, V_d, gT_d, oT_d):
        kb.begin_phase()
        NC = T // 64
        Qt = [kb.sb(f"Qt{i}", [128, T], BF16) for i in range(2)]
        Kt = [kb.sb(f"Kt{i}", [128, T], BF16) for i in range(2)]
        Kh = [kb.sb(f"Kh{i}", [128, NKC, 128], BF16) for i in range(2)]
        Et = [kb.sb(f"Et{i}", [128, NC], F32) for i in range(2)]
        Vh = kb.sb("Vh", [128, NKC, 128], BF16)
        Gt = kb.sb("Gt", [128, T], BF16)
        ofw = kb.sb("ofw", [128, T], F32)
        Sf = kb.sb("Sf", [128, 128], F32)
        Sb = kb.sb("Sb", [128, 128], BF16)
        sc = [kb.sb(f"sc{i}", [128, 128], BF16) for i in range(2)]
        gn = kb.sb("gn", [128, 1], F32)
        sqb = kb.sb("sqb", [128, 512], BF16)
        tmp = kb.sb("tmp", [128, 512], F32)
        rstd = kb.sb("rstd", [128, 512], F32)
        on = kb.sb("on", [128, 512], F32)
        ob = [kb.sb(f"ob{i}", [128, 512], BF16) for i in range(2)]
        scp = [kb.ps(f"scp{i}", [128, 128]) for i in range(2)]
        op_ = [kb.ps(f"op{i}", [128, 128]) for i in range(2)]
        dsp = [kb.ps(f"dsp{i}", [128, 128]) for i in range(2)]
        pst = kb.ps("pst", [128, 512])
        kb.load(gn, gn[:], c_gn, c_gn[:])
        Vv = V_d[:].rearrange("(n p) f -> p n f", p=128)
        khv = [kh_d[d][:].rearrange("(n p) f -> p n f", p=128) for d in range(2)]
        step = max(1, NKC // 4)
        it = 0
        oi = 0
        ci = 0
        for hd in range(8):
            rows = slice(hd * 128, (hd + 1) * 128)
            for c0 in range(0, NKC, step):
                kb.load(Vh, Vh[:, c0:c0 + step, :], V_d, Vv[:, c0:c0 + step, rows])
            kb.load(Gt, Gt[:], gT_d, gT_d[rows, :])
            for d in range(2):
                kb.load(Qt[d], Qt[d][:], qt_d[d], qt_d[d][rows, :])
                kb.load(Kt[d], Kt[d][:], kt_d[d], kt_d[d][rows, :])
                kb.load(Et[d], Et[d][:], et_d[d], et_d[d][rows, :])
                for c0 in range(0, NKC, step):
                    kb.load(Kh[d], Kh[d][:, c0:c0 + step, :], kh_d[d], khv[d][:, c0:c0 + step, rows])
            for d in range(2):
                Q, K, KH, ET = Qt[d], Kt[d], Kh[d], Et[d]
                kb.op(kb.dve, lambda e: e.memset(Sf[:], 0.0), [], [Sf])
                kb.op(kb.dve, lambda e: e.memset(Sb[:], 0.0), [], [Sb])
                blocks = range(NKC) if d == 0 else range(NKC - 1, -1, -1)
                for b in blocks:
                    bs = slice(b * 128, (b + 1) * 128)
                    scp_, sc_, o_ = scp[it % 2], sc[it % 2], op_[it % 2]
                    it += 1
                    kb.op(kb.pe, lambda e: e.matmul(scp_[:], lhsT=K[:, bs], rhs=Q[:, bs], start=True, stop=True), [K, Q], [scp_])
                    kb.op(kb.dve, lambda e: e.tensor_tensor(out=sc_[:], in0=scp_[:], in1=cmask[:, d, :], op=ALU.mult), [scp_, cmask], [sc_])
                    kb.op(kb.pe, lambda e: e.matmul(o_[:], lhsT=Vh[:, b, :], rhs=sc_[:], start=True, stop=False), [Vh, sc_], [o_])
                    chunks = (2 * b, 2 * b + 1) if d == 0 else (2 * b + 1, 2 * b)
                    for n_, c in enumerate(chunks):
                        p0 = (c % 2) * 64
                        first_of_other = (c == NC // 2) if d == 0 else (c == NC // 2 - 1)
                        if first_of_other:
                            kb.op(kb.dve, lambda e: e.tensor_scalar_mul(out=Sf[:], in0=Sf[:], scalar1=flag[:, 0:1]), [Sf, flag], [Sf])
                            kb.op(kb.act, lambda e: e.activation(out=Sb[:], in_=Sf[:], func=AF.Copy), [Sf], [Sb])
                        kb.op(kb.pe, lambda e: e.matmul(o_[:, p0:p0 + 64], lhsT=Sb[:], rhs=Q[:, c * 64:(c + 1) * 64], start=False, stop=(n_ == 1)),
                              [Sb, Q], [o_])
                        ds_ = dsp[ci % 2]
                        ci += 1
                        kb.op(kb.pe, lambda e: e.matmul(ds_[:], lhsT=KH[p0:p0 + 64, b, :], rhs=Vh[p0:p0 + 64, b, :], start=True, stop=True,
                                                        tile_position=(p0, 0)), [KH, Vh], [ds_])
                        kb.op(kb.dve, lambda e: e.scalar_tensor_tensor(out=Sf[:], in0=Sf[:], scalar=ET[:, c:c + 1], in1=ds_[:],
                                                                         op0=ALU.mult, op1=ALU.add), [Sf, ET, ds_], [Sf])
                        kb.op(kb.act, lambda e: e.activation(out=Sb[:], in_=Sf[:], func=AF.Copy), [Sf], [Sb])
                    if d == 0:
                        kb.op(kb.act, lambda e: e.activation(out=ofw[:, bs], in_=o_[:], func=AF.Copy), [o_], [ofw])
                    else:
                        kb.op(kb.dve, lambda e: e.tensor_tensor(out=ofw[:, bs], in0=o_[:], in1=ofw[:, bs], op=ALU.add), [o_, ofw], [ofw])
            for tt in range(NT):
                csl = slice(tt * 512, (tt + 1) * 512)
                kb.op(kb.act, lambda e: e.activation(out=sqb[:], in_=ofw[:, csl], func=AF.Square), [ofw], [sqb])
                kb.op(kb.pe, lambda e: e.matmul(pst[:], lhsT=ones_bf[:], rhs=sqb[:], start=True, stop=True), [ones_bf, sqb], [pst])
                rstd_from(pst, 512, 128, 1e-6, rstd, tmp)
                kb.op(kb.dve, lambda e: e.scalar_tensor_tensor(out=on[:], in0=ofw[:, csl], scalar=gn[:, 0:1], in1=rstd[:], op0=ALU.mult, op1=ALU.mult),
                      [ofw, gn, rstd], [on])
                o = ob[oi % 2]
                oi += 1
                kb.op(kb.dve, lambda e: e.tensor_tensor(out=o[:], in0=on[:], in1=Gt[:, csl], op=ALU.mult), [on, Gt], [o])
                kb.store(oT_d, oT_d[rows, csl], o, o[:])
        kb.end_phase()

    def phase_wo(layer, wo_buf, wo_ap, oT_d, x_d, x1_d):
        kb.begin_phase()
        Wo = kb.sb("Wo", [128, 8, 1024], BF16)
        stg = [kb.sb(f"stg{i}", [128, 8, 512], F32) for i in range(2)]
        wsrc = wo_ap.rearrange("(c p) n -> p c n", p=128)
        load_cast_w(Wo, lambda a, b: Wo[:, :, a:b], wo_buf, lambda a, b: wsrc[:, :, a:b], 1024, stg)
        xs = [kb.sb(f"xs{i}", [128, 8, 512], F32) for i in range(2)]
        os_ = [kb.sb(f"os{i}", [128, 8, 512], BF16) for i in range(2)]
        y = kb.sb("y", [128, 8, 512], F32)
        sq = [kb.sb(f"sq{i}", [128, 512], BF16) for i in range(2)]
        tmp = kb.sb("tmp", [128, 512], F32)
        rstd = kb.sb("rstd", [128, 512], F32)
        xo = [kb.sb(f"xo{i}", [128, 8, 512], F32) for i in range(2)]
        pst = kb.ps("pst", [128, 512])
        pq = [kb.ps(f"pq{i}", [128, 512]) for i in range(3)]
        xv = x_d[:].rearrange("(c p) t -> p c t", p=128)
        x1v = x1_d[:].rearrange("(c p) t -> p c t", p=128)
        ov = oT_d[:].rearrange("(c p) t -> p c t", p=128)

        def ld(tt):
            s = tt % 2
            kb.load(xs[s], xs[s][:], x_d, xv[:, :, tt * 512:(tt + 1) * 512])
            kb.load(os_[s], os_[s][:], oT_d, ov[:, :, tt * 512:(tt + 1) * 512])
        ld(0)
        cnt = 0
        for tt in range(NT):
            s = tt % 2
            if tt + 1 < NT:
                ld(tt + 1)
            x, o = xs[s], os_[s]
            for m in range(8):
                p = pq[cnt % 3]
                q = sq[cnt % 2]
                cnt += 1

                def mm(e):
                    for k in range(8):
                        i = e.matmul(p[:], lhsT=Wo[:, k, m * 128:(m + 1) * 128], rhs=o[:, k, :], start=(k == 0), stop=(k == 7))
                    return i
                kb.op(kb.pe, mm, [Wo, o], [p])
                kb.op(kb.act, lambda e: e.activation(out=q[:], in_=p[:], func=AF.Square), [p], [q])
                kb.op(kb.dve, lambda e: e.tensor_copy(out=y[:, m, :], in_=p[:]), [p], [y])
                kb.op(kb.pe, lambda e: e.matmul(pst[:], lhsT=ones_bf[:], rhs=q[:], start=(m == 0), stop=(m == 7)), [ones_bf, q], [pst])
            rstd_from(pst, 512, D, 1e-6, rstd, tmp)
            for m in range(8):
                kb.op(kb.dve, lambda e: e.scalar_tensor_tensor(out=y[:, m, :], in0=y[:, m, :], scalar=gcol(layer, 1, m), in1=rstd[:],
                                                                 op0=ALU.mult, op1=ALU.mult), [y, rstd, normg], [y])
                kb.op(kb.ew2, lambda e: e.tensor_tensor(out=xo[s][:, m, :], in0=y[:, m, :], in1=x[:, m, :], op=ALU.add), [y, x], [xo[s]])
            kb.store(x1_d, x1v[:, :, tt * 512:(tt + 1) * 512], xo[s], xo[s][:])
        kb.end_phase()

    def phase_ffn(layer, x1_d, x2_d):
        kb.begin_phase()
        NV = 256
        NW = NV + 2
        Win = kb.sb("Win", [128, 8, 2 * DFF], BF16)
        Wout = kb.sb("Wout", [128, 22, 1024], BF16)
        stg = [kb.sb(f"stg{i}", [128, 8, 128], F32) for i in range(1)]
        wsrc = f_win[layer].rearrange("(c p) n -> p c n", p=128)
        load_cast_w(Win, lambda a, b: Win[:, :, a:b], f_win, lambda a, b: wsrc[:, :, a:b], 2 * DFF, stg, blk=128)
        wsrc2 = f_wout[layer].rearrange("(c p) n -> p c n", p=128)
        for c0 in range(0, 22, 8):
            c1 = min(22, c0 + 8)
            for n0_ in range(0, 1024, 128):
                sg = stg[cast_rr[0] % len(stg)]
                cast_rr[0] += 1
                kb.load(sg, sg[:, :c1 - c0, :], f_wout, wsrc2[:, c0:c1, n0_:n0_ + 128])
                kb.op(kb.dve, lambda e: e.tensor_copy(out=Wout[:, c0:c1, n0_:n0_ + 128], in_=sg[:, :c1 - c0, :]), [sg], [Wout])
        cw = kb.sb("cw", [128, 44, 4], F32)
        kb.load(cw, cw[:], f_cw, f_cw[:, layer * 176:(layer + 1) * 176].rearrange("p (c f) -> p c f", f=4))
        xw = [kb.sb(f"xw{i}", [128, 8, NW], F32) for i in range(2)]
        h = kb.sb("h", [128, 8, NW], BF16)
        sq = [kb.sb(f"sq{i}", [128, NW], BF16) for i in range(2)]
        tmp = kb.sb("tmp", [128, NW], F32)
        rstd = kb.sb("rstd", [128, NW], F32)
        ta = [kb.sb(f"ta{i}", [128, NV], F32) for i in range(2)]
        tb_ = [kb.sb(f"tb{i}", [128, NV], F32) for i in range(2)]
        ga = [kb.sb(f"ga{i}", [128, NV], F32) for i in range(2)]
        gg = kb.sb("gg", [128, 22, NV], BF16)
        y = kb.sb("y", [128, 8, NV], F32)
        xo = [kb.sb(f"xo{i}", [128, 8, NV], F32) for i in range(2)]
        pst = kb.ps("pst", [128, 512])
        pa = [kb.ps(f"pa{i}", [128, 512]) for i in range(2)]
        pb = [kb.ps(f"pb{i}", [128, 512]) for i in range(2)]
        po = [kb.ps(f"po{i}", [128, 512]) for i in range(2)]
        xv = x1_d[:].rearrange("(c p) t -> p c t", p=128)
        x2v = x2_d[:].rearrange("(c p) t -> p c t", p=128)
        wins = list(range(0, T, NV))

        def ld(wi):
            s0 = wins[wi]
            b = xw[wi % 2]
            lo, hi = s0 - 1, s0 + NV + 1
            clo, chi = max(lo, 0), min(hi, T)
            if clo > lo:
                kb.op(kb.pool, lambda e: e.memset(b[:, :, 0:1], 0.0), [], [b])
            if chi < hi:
                kb.op(kb.pool, lambda e: e.memset(b[:, :, NW - 1:NW], 0.0), [], [b])
            kb.load(b, b[:, :, clo - lo:NW - (hi - chi)], x1_d, xv[:, :, clo:chi])
        ld(0)
        cnt = 0
        for wi, s0 in enumerate(wins):
            if wi + 1 < len(wins):
                ld(wi + 1)
            x = xw[wi % 2]
            for c in range(8):
                q = sq[c % 2]
                kb.op(kb.act, lambda e: e.activation(out=q[:], in_=x[:, c, :], func=AF.Square), [x], [q])
                kb.op(kb.pe, lambda e: e.matmul(pst[:, :NW], lhsT=ones_bf[:], rhs=q[:], start=(c == 0), stop=(c == 7)), [ones_bf, q], [pst])
            rstd_from(pst, NW, D, 1e-6, rstd, tmp)
            for c in range(8):
                kb.op(kb.dve, lambda e: e.scalar_tensor_tensor(out=h[:, c, :], in0=x[:, c, :], scalar=gcol(layer, 2, c), in1=rstd[:],
                                                                 op0=ALU.mult, op1=ALU.mult), [x, rstd, normg], [h])
            if s0 == HALF:
                kb.op(kb.dve, lambda e: e.tensor_scalar_mul(out=h[:, :, 0:1], in0=h[:, :, 0:1], scalar1=flag[:, 0:1]), [h, flag], [h])
            if s0 + NV == HALF:
                kb.op(kb.dve, lambda e: e.tensor_scalar_mul(out=h[:, :, NW - 1:NW], in0=h[:, :, NW - 1:NW], scalar1=flag[:, 0:1]), [h, flag], [h])
            for jj in range(22):
                A, B = pa[jj % 2], pb[jj % 2]
                a_, b_, g_ = ta[jj % 2], tb_[jj % 2], ga[jj % 2]

                def mma(e):
                    for k in range(8):
                        i = e.matmul(A[:, :NW], lhsT=Win[:, k, jj * 128:(jj + 1) * 128], rhs=h[:, k, :], start=(k == 0), stop=(k == 7))
                    return i

                def mmb(e):
                    for k in range(8):
                        i = e.matmul(B[:, :NW], lhsT=Win[:, k, DFF + jj * 128:DFF + (jj + 1) * 128], rhs=h[:, k, :], start=(k == 0), stop=(k == 7))
                    return i
                kb.op(kb.pe, mma, [Win, h], [A])
                kb.op(kb.pe, mmb, [Win, h], [B])
                for (Pp, tt_, ci, eng2) in ((A, a_, jj, kb.dve), (B, b_, 22 + jj, kb.dve)):
                    kb.op(kb.act, lambda e: e.activation(out=tt_[:], in_=Pp[:, 1:NV + 1], func=AF.Identity,
                                                         scale=cw[:, ci, 1:2], bias=cw[:, ci, 3:4]), [Pp, cw], [tt_])
                    kb.op(eng2, lambda e: e.scalar_tensor_tensor(out=tt_[:], in0=Pp[:, 0:NV], scalar=cw[:, ci, 0:1], in1=tt_[:],
                                                                  op0=ALU.mult, op1=ALU.add), [Pp, cw, tt_], [tt_])
                    kb.op(eng2, lambda e: e.scalar_tensor_tensor(out=tt_[:], in0=Pp[:, 2:NV + 2], scalar=cw[:, ci, 2:3], in1=tt_[:],
                                                                  op0=ALU.mult, op1=ALU.add), [Pp, cw, tt_], [tt_])
                kb.op(kb.act, lambda e: e.activation(out=g_[:], in_=a_[:], func=AF.Gelu_apprx_tanh), [a_], [g_])
                kb.op(kb.ew2, lambda e: e.tensor_tensor(out=gg[:, jj, :], in0=g_[:], in1=b_[:], op=ALU.mult), [g_, b_], [gg])
            for m in range(8):
                p = po[m % 2]
                q = sq[m % 2]

                def mm(e):
                    for k in range(22):
                        i = e.matmul(p[:, :NV], lhsT=Wout[:, k, m * 128:(m + 1) * 128], rhs=gg[:, k, :], start=(k == 0), stop=(k == 21))
                    return i
                kb.op(kb.pe, mm, [Wout, gg], [p])
                kb.op(kb.act, lambda e: e.activation(out=q[:, :NV], in_=p[:, :NV], func=AF.Square), [p], [q])
                kb.op(kb.dve, lambda e: e.tensor_copy(out=y[:, m, :], in_=p[:, :NV]), [p], [y])
                kb.op(kb.pe, lambda e: e.matmul(pst[:, :NV], lhsT=ones_bf[:], rhs=q[:, :NV], start=(m == 0), stop=(m == 7)), [ones_bf, q], [pst])
            rstd_from(pst, NV, D, 1e-6, rstd, tmp)
            xo_ = xo[wi % 2]
            for m in range(8):
                kb.op(kb.dve, lambda e: e.scalar_tensor_tensor(out=y[:, m, :], in0=y[:, m, :], scalar=gcol(layer, 3, m), in1=rstd[:, :NV],
                                                                 op0=ALU.mult, op1=ALU.mult), [y, rstd, normg], [y])
                kb.op(kb.ew2, lambda e: e.tensor_tensor(out=xo_[:, m, :], in0=y[:, m, :], in1=x[:, m, 1:NV + 1], op=ALU.add), [y, x], [xo_])
            kb.store(x2_d, x2v[:, :, s0:s0 + NV], xo_, xo_[:])
        kb.end_phase()

    QTb = [kb.dram(f"QTb{g}", [D, T], BF16) for g in range(3)]
    KTb = [kb.dram(f"KTb{g}", [D, T], BF16) for g in range(3)]
    Vb = [kb.dram(f"Vb{g}", [T, D], BF16) for g in range(3)]
    cq_d = [kb.dram(f"cq{d}", [D, T], BF16) for d in range(2)]
    ck_d = [kb.dram(f"ck{d}", [D, T], BF16) for d in range(2)]
    ckh_d = [kb.dram(f"ckh{d}", [T, D], BF16) for d in range(2)]
    cet_d = [kb.dram(f"cet{d}", [D, T // 64], F32) for d in range(2)]
    QT_d = kb.dram("QT_d", [D, T], BF16)
    KT_d = kb.dram("KT_d", [D, T], BF16)
    V_d = kb.dram("V_d", [T, D], BF16)
    oT_d = kb.dram("oT_d", [D, T], BF16)
    x1_d = kb.dram("x1_d", [D, T], F32)
    xa_d = kb.dram("xa_d", [D, T], F32)
    xb_d = kb.dram("xb_d", [D, T], F32)
    cur = xT_in
    for li, layer in enumerate(layers):
        last = li == len(layers) - 1
        nxt = yT_out if last else (xa_d if li % 2 == 0 else xb_d)
        kind = layer % 3
        j = layer // 3
        import os
        stop = int(os.environ.get("KSTOP", "99"))
        if kind == 0:
            if stop >= 1:
                phase_proj_A(layer, a_wqkv, a_wqkv[j], cur, QT_d, KT_d, V_d)
            if stop >= 2:
                phase_attn_A(layer, j, QT_d, KT_d, V_d, oT_d)
            if stop >= 3:
                phase_wo(layer, a_wo, a_wo[j], oT_d, cur, x1_d)
        elif kind == 1:
            for g in range(3):
                phase_proj_A(layer, b_wqkv, b_wqkv[0][:, g * 3072:(g + 1) * 3072], cur, QTb[g], KTb[g], Vb[g])
            phase_attn_B(layer, QTb, KTb, Vb, oT_d)
            phase_wo(layer, b_wo, b_wo[0], oT_d, cur, x1_d)
        else:
            phase_proj_C(layer, cur, cq_d, ck_d, ckh_d, cet_d, V_d, KT_d)
            phase_scan_C(layer, cq_d, ck_d, ckh_d, cet_d, V_d, KT_d, oT_d)
            phase_wo(layer, c_wo, c_wo[0], oT_d, cur, x1_d)
        if stop >= 4:
            phase_ffn(layer, x1_d, nxt)
        cur = nxt
    kb.begin_phase()
    kb.end_phase()
    return nc


def rope_tables(pos):
    half = 8
    inv = (500000.0 ** (-np.arange(half, dtype=np.float32) / half)).astype(np.float32)
    ang = pos.astype(np.float32)[:, None] * inv[None, :]
    cos, sin = np.cos(ang).astype(np.float32), np.sin(ang).astype(np.float32)
    T = pos.shape[0]
    tab = np.zeros((128, 3, T), np.float32)
    tab[:, 0, :] = 1.0
    for hd in range(2):
        b = hd * 64
        tab[b:b + 8, 0, :] = cos.T
        tab[b + 8:b + 16, 0, :] = cos.T
        tab[b:b + 8, 1, :] = -sin.T
        tab[b + 8:b + 16, 1, :] = sin.T
    tab[:, 2, :] = tab[:, 0, :] * np.float32(0.125)
    return tab


def const_mats():
    c = np.zeros((128, 7, 128), np.float32)
    for m in range(64):
        c[64 + m, 6, m] = 1.0
    c[:, 0, :] = 1.0
    c[:, 1, :] = np.eye(128, dtype=np.float32)
    for hd in range(2):
        for i in range(8):
            a, b = hd * 64 + i, hd * 64 + i + 8
            c[a, 2, b] = 1.0
            c[b, 2, a] = 1.0
    s = np.arange(128)
    c[:, 3, :] = (s[:, None] <= s[None, :]).astype(np.float32)
    same = (s[:, None] // 64) == (s[None, :] // 64)
    c[:, 4, :] = ((s[:, None] <= s[None, :]) & same).astype(np.float32)
    c[:, 5, :] = ((s[:, None] >= s[None, :]) & same).astype(np.float32)
    return c


def make_in_maps(inp, T, seqs_per_core):
    maps = []
    NT, NKC = T // 512, T // 128
    normg = np.ascontiguousarray(inp["norm_g"].reshape(4, 4, 8, 128).transpose(3, 0, 1, 2).reshape(128, 128))
    a_lam = np.ascontiguousarray(np.broadcast_to(inp["a_lambda"].reshape(1, -1), (128, 512)))
    a_sub = np.ascontiguousarray(inp["a_subln_g"].T)
    cwt = np.concatenate([inp["f_conv_w"], inp["f_conv_b"][:, None, :]], axis=1)
    f_cw = np.ascontiguousarray(cwt.reshape(4, 4, 44, 128).transpose(3, 0, 2, 1).reshape(128, 4 * 44 * 4))
    cm = const_mats()
    bm = bmask_table()
    c_lbl = np.ascontiguousarray(inp["c_lb_logits"].reshape(4, 8, 128).transpose(2, 0, 1))
    c_gn = np.ascontiguousarray(inp["c_gnorm_g"].reshape(1, 128).T)
    m01h = np.ones((128, 512), np.float32)
    m01h[:, ::64] = 0.0
    for seqs in seqs_per_core:
        xT = np.ascontiguousarray(np.concatenate(seqs, axis=0).T)
        pos = np.concatenate([np.arange(s.shape[0]) for s in seqs])
        sid = np.concatenate([np.full(s.shape[0], i) for i, s in enumerate(seqs)])
        ksid = sid[::128][:, None]
        qsid = sid[::512][None, :]
        ab = np.where(ksid == qsid, 0.0, NEG).astype(np.float32).reshape(1, NKC * NT)
        flag = np.zeros((128, 2), np.float32)
        flag[:, 0] = 1.0 if len(seqs) == 1 else 0.0
        m = {
            "xT": xT, "normg": normg, "rope": rope_tables(pos), "cmat": cm, "flag": flag,
            "abias": np.ascontiguousarray(np.broadcast_to(ab, (128, NKC * NT))),
            "a_w_qkv": inp["a_w_qkv"], "a_lam": a_lam, "a_sub": a_sub, "a_w_o": inp["a_w_o"],
            "c_w_in": inp["c_w_in"], "c_w_o": inp["c_w_o"], "c_lbl": c_lbl, "c_gn": c_gn, "m01": m01h,
            "b_w_qkv": inp["b_w_qkv"], "b_w_o": inp["b_w_o"], "bmask": bm,
            "f_w_in": inp["f_w_in"], "f_cw": f_cw, "f_w_out": inp["f_w_out"],
        }
        maps.append(m)
    return maps


_NC_CACHE = {}


def kernel(**inputs):
    inp = {k: np.asarray(v) for k, v in inputs.items()}
    T = 8192
    xp, xs = inp["x_prompt"], inp["x_sample"]
    seqs = [[xp[b]] for b in range(4)] + [[xs[2 * c], xs[2 * c + 1]] for c in range(4)]
    maps = make_in_maps(inp, T, seqs)
    if "nc" not in _NC_CACHE:
        _NC_CACHE["nc"] = build(T)
    res = run_bass_kernel_spmd(_NC_CACHE["nc"], maps, core_ids=list(range(8)))
    outs = [np.asarray(r["yT"]).T for r in res.results]
    y_prompt = np.stack(outs[:4], axis=0).astype(np.float32)
    y_sample = np.stack([o.reshape(2, 4096, D) for o in outs[4:]], axis=0).reshape(8, 4096, D).astype(np.float32)
    return (y_prompt, y_sample)
```

```python
import math
from contextlib import ExitStack
import numpy as np
import concourse.bass as bass
import concourse.mybir as mybir
from concourse.bass_utils import run_bass_kernel_spmd

F32 = mybir.dt.float32
BF16 = mybir.dt.bfloat16
AF = mybir.ActivationFunctionType
ALU = mybir.AluOpType
AX = mybir.AxisListType

D = 1024
DFF = 2816
NEG = -30000.0
B_DIL = (1, 4, 16)


def _bmask_list():
    out = []
    for g, d in enumerate(B_DIL):
        for delta in range(-4096, 4097, 128):
            q = np.arange(512)[None, :]
            k = np.arange(128)[:, None]
            diff = delta + q - k
            if np.any((diff % d == 0) & (np.abs(diff) <= 64 * d)):
                out.append((g, delta))
    return out


BMASKS = _bmask_list()


def bmask_table():
    t = np.zeros((128, len(BMASKS), 512), np.float32)
    q = np.arange(512)[None, :]
    k = np.arange(128)[:, None]
    for i, (g, delta) in enumerate(BMASKS):
        d = B_DIL[g]
        diff = delta + q - k
        t[:, i, :] = np.where((diff % d == 0) & (np.abs(diff) <= 64 * d), 1.0, 0.0)
    return t


class Stream:
    def __init__(self, name, eng, sem):
        self.name, self.eng, self.sem = name, eng, sem
        self.count = 0
        self.known = {}


class Buf:
    def __init__(self, t, name):
        self.t = t
        self.name = name
        self.w = {}
        self.r = {}
        self.isdram = False
        self.excl = False
        self.c = None
        self.lsem = None
        self.lcount = 0
        self.ssem = None
        self.scount = 0

    def __getitem__(self, idx):
        return self.t[idx]


class KB:
    def __init__(self, nc):
        self.nc = nc
        self.top = ExitStack()
        self.sems = []
        self.free_dsems = []
        self.all_dsems = {}
        mk = lambda n, e: Stream(n, e, self.top.enter_context(nc.semaphore("s_" + n)))
        self.pe = mk("pe", nc.tensor)
        self.act = mk("act", nc.scalar)
        self.dve = mk("dve", nc.vector)
        self.pool = mk("pool", nc.gpsimd)
        import os
        self.ew2 = self.dve if os.environ.get("KPOOL", "dve") == "dve" else self.pool
        self.sp = mk("sp", nc.sync)
        self.streams = [self.pe, self.act, self.dve, self.pool, self.sp]
        self.phase = None
        self.phase_bufs = []
        self.uid = 0

    def begin_phase(self):
        self.phase = ExitStack()
        self.phase_bufs = []

    def end_phase(self):
        self.barrier()
        for b in self.phase_bufs:
            for s, c in ((b.lsem, b.lcount), (b.ssem, b.scount)):
                if s is not None:
                    self.free_dsems.append((s, c))
        self.phase.close()
        self.phase = None

    def sb(self, name, shape, dt, glob=False):
        self.uid += 1
        es = self.top if glob else self.phase
        t = es.enter_context(self.nc.sbuf_tensor(f"{name}_{self.uid}", list(shape), dt))
        b = Buf(t, name)
        if not glob:
            self.phase_bufs.append(b)
        return b

    def ps(self, name, shape, dt=F32):
        self.uid += 1
        t = self.phase.enter_context(self.nc.psum_tensor(f"{name}_{self.uid}", list(shape), dt))
        b = Buf(t, name)
        b.excl = True
        return b

    def dram(self, name, shape, dt):
        t = self.nc.dram_tensor(name, list(shape), dt, kind="Internal")
        b = Buf(t.ap(), name)
        b.isdram = True
        return b

    def _dsem(self):
        if self.free_dsems:
            return self.free_dsems.pop()
        s = self.top.enter_context(self.nc.semaphore(f"d{len(self.all_dsems)}"))
        self.all_dsems[id(s)] = s
        return (s, 0)

    @staticmethod
    def _expand(bufs):
        out = []
        for b in bufs:
            if b.c is not None:
                out.extend(b.c)
            else:
                out.append(b)
        return out

    def split(self, buf, n):
        buf.c = [Buf(buf.t, f"{buf.name}.{i}") for i in range(n)]
        for ch in buf.c:
            ch.excl = buf.excl
            ch.isdram = buf.isdram
        return buf

    def _deps(self, st, reads, writes, skip=None):
        reads, writes = self._expand(reads), self._expand(writes)
        need = {}

        def add(ev):
            s, v = ev
            if id(s) not in need or need[id(s)][1] < v:
                need[id(s)] = ev
        for b in reads:
            for ev in b.w.values():
                add(ev)
            if b.excl:
                for ev in b.r.values():
                    if ev[0] is not st.sem:
                        add(ev)
        for b in writes:
            if not b.isdram:
                for ev in b.w.values():
                    add(ev)
            for ev in b.r.values():
                add(ev)
        for s, v in need.values():
            if skip is not None and s is skip:
                continue
            if s is st.sem and st is self.pe:
                continue
            if st.known.get(id(s), 0) < v:
                st.eng.wait_ge(s, v)
                st.known[id(s)] = v

    def _mark(self, ev, reads, writes):
        reads, writes = self._expand(reads), self._expand(writes)
        for b in writes:
            if b.isdram:
                b.w[id(ev[0])] = ev
            else:
                b.w = {id(ev[0]): ev}
                b.r = {}
        for b in reads:
            if b not in writes:
                b.r[id(ev[0])] = ev

    def op(self, st, fn, reads=(), writes=()):
        reads, writes = list(reads), list(writes)
        self._deps(st, reads, writes)
        ins = fn(st.eng)
        st.count += 1
        ins.then_inc(st.sem, 1)
        self._mark((st.sem, st.count), reads, writes)
        return ins

    def load(self, sbuf, out_ap, dbuf, in_ap, st=None):
        st = st or self.sp
        if sbuf.lsem is None:
            sbuf.lsem, sbuf.lcount = self._dsem()
        self._deps(st, [dbuf], [sbuf], skip=sbuf.lsem)
        sbuf.lcount += 16
        st.eng.dma_start(out=out_ap, in_=in_ap).then_inc(sbuf.lsem, 16)
        self._mark((sbuf.lsem, sbuf.lcount), [dbuf], [sbuf])

    def store(self, dbuf, out_ap, sbuf, in_ap, st=None):
        st = st or self.pool
        if sbuf.ssem is None:
            sbuf.ssem, sbuf.scount = self._dsem()
        self._deps(st, [sbuf], [dbuf], skip=sbuf.ssem)
        sbuf.scount += 16
        st.eng.dma_start(out=out_ap, in_=in_ap).then_inc(sbuf.ssem, 16)
        self._mark((sbuf.ssem, sbuf.scount), [sbuf], [dbuf])

    def barrier(self):
        cur = [(st.sem, st.count) for st in self.streams]
        for b in self.phase_bufs:
            if b.lsem is not None:
                cur.append((b.lsem, b.lcount))
            if b.ssem is not None:
                cur.append((b.ssem, b.scount))
        for st in self.streams:
            for s, v in cur:
                if s is st.sem or v == 0:
                    continue
                if st.known.get(id(s), 0) < v:
                    st.eng.wait_ge(s, v)
                    st.known[id(s)] = v


def build(T=8192, layers=(0, 1, 2, 3)):
    nc = bass.Bass("TRN2", target_bir_lowering=False)
    kb = KB(nc)
    NT = T // 512
    NKC = T // 128
    HALF = T // 2

    def din(name, shape, dt=F32):
        b = Buf(nc.dram_tensor(name, list(shape), dt, kind="ExternalInput").ap(), name)
        b.isdram = True
        return b

    xT_in = din("xT", [D, T])
    yT_out = Buf(nc.dram_tensor("yT", [D, T], F32, kind="ExternalOutput").ap(), "yT")
    yT_out.isdram = True
    normg_in = din("normg", [128, 4 * 4 * 8])
    rope_in = din("rope", [128, 3, T])
    cmat_in = din("cmat", [128, 7, 128])
    flag_in = din("flag", [128, 2])
    abias_in = din("abias", [128, NKC * NT])
    a_wqkv = din("a_w_qkv", [2, D, 3072])
    a_lam = din("a_lam", [128, 2 * 256])
    a_sub = din("a_sub", [128, 2])
    a_wo = din("a_w_o", [2, D, D])
    b_wqkv = din("b_w_qkv", [1, D, 9216])
    b_wo = din("b_w_o", [1, D, D])
    bmask_in = din("bmask", [128, len(BMASKS), 512])
    c_win = din("c_w_in", [1, D, 5120])
    c_wo = din("c_w_o", [1, D, D])
    c_lbl = din("c_lbl", [128, 4, 8])
    c_gn = din("c_gn", [128, 1])
    m01_in = din("m01", [128, 512])
    f_win = din("f_w_in", [4, D, 2 * DFF])
    f_cw = din("f_cw", [128, 4 * 44 * 4])
    f_wout = din("f_w_out", [4, DFF, D])

    kb.begin_phase()
    ones_bf = kb.sb("ones", [128, 128], BF16, glob=True)
    ident_bf = kb.sb("ident", [128, 128], BF16, glob=True)
    rperm_bf = kb.sb("rperm", [128, 128], BF16, glob=True)
    normg = kb.sb("normg", [128, 128], F32, glob=True)
    flag = kb.sb("flag", [128, 2], F32, glob=True)
    cv = kb.sb("cv", [128, 4], F32, glob=True)
    ones_f = kb.sb("ones_f", [128, 128], F32, glob=True)
    shift_f = kb.sb("shift_f", [128, 64], F32, glob=True)
    cmask = kb.sb("cmask", [128, 2, 128], F32, glob=True)
    m01 = kb.sb("m01", [128, 512], F32, glob=True)
    cst = kb.sb("cst", [128, 7, 128], F32)
    kb.load(cst, cst[:], cmat_in, cmat_in[:])
    kb.load(normg, normg[:], normg_in, normg_in[:])
    kb.load(flag, flag[:], flag_in, flag_in[:])
    kb.op(kb.dve, lambda e: e.memset(cv[:, 0:1], 0.125), [], [cv])
    kb.op(kb.dve, lambda e: e.memset(cv[:, 1:2], 1.0), [cv], [cv])
    kb.op(kb.dve, lambda e: e.memset(cv[:, 2:3], float(128 ** -0.5)), [cv], [cv])
    kb.load(m01, m01[:], m01_in, m01_in[:])
    kb.op(kb.dve, lambda e: e.tensor_copy(out=cmask[:], in_=cst[:, 4:6, :]), [cst], [cmask])
    kb.op(kb.dve, lambda e: e.tensor_copy(out=ones_bf[:], in_=cst[:, 0, :]), [cst], [ones_bf])
    kb.op(kb.dve, lambda e: e.tensor_copy(out=ones_f[:], in_=cst[:, 0, :]), [cst], [ones_f])
    kb.op(kb.dve, lambda e: e.tensor_copy(out=shift_f[:], in_=cst[:, 6, 0:64]), [cst], [shift_f])
    kb.op(kb.dve, lambda e: e.tensor_copy(out=ident_bf[:], in_=cst[:, 1, :]), [cst], [ident_bf])
    kb.op(kb.dve, lambda e: e.tensor_copy(out=rperm_bf[:], in_=cst[:, 2, :]), [cst], [rperm_bf])
    kb.end_phase()

    def gcol(layer, n, c):
        i = (layer * 4 + n) * 8 + c
        return normg[:, i:i + 1]

    cast_rr = [0]

    def load_cast_w(wdst, dst_ap_fn, src_buf, src_ap_fn, ncols, stg, kc=8, blk=512):
        for c0 in range(0, ncols, blk):
            c1 = min(ncols, c0 + blk)
            s = stg[cast_rr[0] % len(stg)]
            kb.load(s, s[:, :kc, :c1 - c0], src_buf, src_ap_fn(c0, c1))
            st = kb.dve if cast_rr[0] % 2 == 0 else kb.act
            if st is kb.dve:
                kb.op(st, lambda e: e.tensor_copy(out=dst_ap_fn(c0, c1), in_=s[:, :kc, :c1 - c0]), [s], [wdst])
            else:
                kb.op(st, lambda e: e.activation(out=dst_ap_fn(c0, c1), in_=s[:, :kc, :c1 - c0], func=AF.Copy), [s], [wdst])
            cast_rr[0] += 1

    def rstd_from(ps_stat, n, nfeat, eps, rstd, tmp):
        kb.op(kb.act, lambda e: e.activation(out=tmp[:, :n], in_=ps_stat[:, :n], func=AF.Ln,
                                             bias=float(eps), scale=1.0 / nfeat), [ps_stat], [tmp])
        kb.op(kb.act, lambda e: e.activation(out=rstd[:, :n], in_=tmp[:, :n], func=AF.Exp, scale=-0.5), [tmp], [rstd])

    def phase_proj_A(layer, wbuf, wap, x_d, QT_d, KT_d, V_d):
        kb.begin_phase()
        W = kb.sb("Wqkv", [128, 8, 3072], BF16)
        stg = [kb.sb(f"stg{i}", [128, 8, 256], F32) for i in range(2)]
        wsrc = wap.rearrange("(c p) n -> p c n", p=128)
        load_cast_w(W, lambda a, b: W[:, :, a:b], wbuf, lambda a, b: wsrc[:, :, a:b], 3072, stg, blk=256)
        xs = [kb.sb(f"xs{i}", [128, 8, 512], F32) for i in range(2)]
        rp = [kb.sb(f"rp{i}", [128, 3, 512], F32) for i in range(2)]
        hT = [kb.split(kb.sb(f"hT{i}", [128, 8, 512], BF16), 8) for i in range(2)]
        sq = [kb.sb(f"sq{i}", [128, 512], BF16) for i in range(2)]
        tmp = kb.sb("tmp", [128, 512], F32)
        rstd = kb.sb("rstd", [128, 512], F32)
        qsb = [kb.sb(f"qsb{i}", [128, 512], BF16) for i in range(2)]
        t1 = [kb.sb(f"t1{i}", [128, 512], F32) for i in range(2)]
        t2 = [kb.sb(f"t2{i}", [128, 512], F32) for i in range(2)]
        qo = [kb.split(kb.sb(f"qo{i}", [128, 8, 512], BF16), 8) for i in range(2)]
        ko = [kb.split(kb.sb(f"ko{i}", [128, 8, 512], BF16), 8) for i in range(2)]
        vo = [kb.split(kb.sb(f"vo{i}", [128, 4, 1024], BF16), 8) for i in range(2)]
        pst = kb.ps("pst", [128, 512])
        pq = [kb.ps(f"pq{i}", [128, 512]) for i in range(3)]
        pr = [kb.ps(f"pr{i}", [128, 512]) for i in range(2)]
        xv = x_d[:].rearrange("(c p) t -> p c t", p=128)
        Vv = V_d[:].rearrange("(n p) f -> p n f", p=128)
        QTv = QT_d[:].rearrange("(c p) t -> p c t", p=128)
        KTv = KT_d[:].rearrange("(c p) t -> p c t", p=128)

        def ld(tt):
            s = tt % 2
            kb.load(xs[s], xs[s][:], x_d, xv[:, :, tt * 512:(tt + 1) * 512])
            kb.load(rp[s], rp[s][:], rope_in, rope_in[:, :, tt * 512:(tt + 1) * 512])

        import os
        KSUB = int(os.environ.get("KSUB", "99"))
        if KSUB < 1:
            kb.end_phase()
            return
        ld(0)
        cnt = 0
        for tt in range(NT):
            s = tt % 2
            if tt + 1 < NT:
                ld(tt + 1)
            x, h = xs[s], hT[s]
            if KSUB < 2:
                continue
            for c in range(8):
                q = sq[c % 2]
                kb.op(kb.act, lambda e: e.activation(out=q[:], in_=x[:, c, :], func=AF.Square), [x], [q])
                kb.op(kb.pe, lambda e: e.matmul(pst[:], lhsT=ones_bf[:], rhs=q[:], start=(c == 0), stop=(c == 7)),
                      [ones_bf, q], [pst])
            if KSUB < 3:
                continue
            rstd_from(pst, 512, D, 1e-6, rstd, tmp)
            if KSUB < 4:
                continue
            for c in range(8):
                kb.op(kb.dve, lambda e: e.scalar_tensor_tensor(out=h[:, c, :], in0=x[:, c, :], scalar=gcol(layer, 0, c),
                                                                 in1=rstd[:], op0=ALU.mult, op1=ALU.mult),
                      [x, rstd, normg], [h.c[c]])
            if KSUB < 5:
                continue
            for m in range(16):
                p = pq[cnt % 3]
                r_ = pr[cnt % 2]
                qs, a1, a2 = qsb[cnt % 2], t1[cnt % 2], t2[cnt % 2]
                dst = (qo if m < 8 else ko)[s]
                scale = 0.125 if m < 8 else 1.0
                cnt += 1

                def mm(e):
                    for k in range(8):
                        i = e.matmul(p[:], lhsT=W[:, k, m * 128:(m + 1) * 128], rhs=h[:, k, :], start=(k == 0), stop=(k == 7))
                    return i
                KQ = int(os.environ.get("KQ", "99"))
                kb.op(kb.pe, mm, [W, h], [p])
                if KQ < 2:
                    continue
                kb.op(kb.act, lambda e: e.activation(out=qs[:], in_=p[:], func=AF.Copy, scale=scale), [p], [qs])
                if KQ < 3:
                    continue
                kb.op(kb.pe, lambda e: e.matmul(r_[:], lhsT=rperm_bf[:], rhs=qs[:], start=True, stop=True), [rperm_bf, qs], [r_])
                if KQ < 4:
                    continue
                kb.op(kb.dve, lambda e: e.tensor_tensor(out=a1[:], in0=p[:], in1=rp[s][:, (2 if m < 8 else 0), :], op=ALU.mult), [p, rp[s]], [a1])
                if KQ < 5:
                    continue
                kb.op(kb.dve, lambda e: e.tensor_tensor(out=a2[:], in0=r_[:], in1=rp[s][:, 1, :], op=ALU.mult), [r_, rp[s]], [a2])
                if KQ < 6:
                    continue
                kb.op(kb.ew2, lambda e: e.tensor_tensor(out=dst[:, m % 8, :], in0=a1[:], in1=a2[:], op=ALU.add), [a1, a2], [dst.c[m % 8]])
            if KSUB < 6:
                continue
            kb.store(QT_d, QTv[:, :, tt * 512:(tt + 1) * 512], qo[s], qo[s][:])
            kb.store(KT_d, KTv[:, :, tt * 512:(tt + 1) * 512], ko[s], ko[s][:])
            if KSUB < 7:
                continue
            for tb in range(4):
                for nb in range(2):
                    p = pq[cnt % 3]
                    cnt += 1

                    def mm(e):
                        for k in range(8):
                            i = e.matmul(p[:], lhsT=h[:, k, tb * 128:(tb + 1) * 128],
                                         rhs=W[:, k, 2048 + nb * 512:2048 + (nb + 1) * 512], start=(k == 0), stop=(k == 7))
                        return i
                    kb.op(kb.pe, mm, [W, h], [p])
                    kb.op(kb.act, lambda e: e.activation(out=vo[s][:, tb, nb * 512:(nb + 1) * 512], in_=p[:], func=AF.Copy), [p], [vo[s].c[tb * 2 + nb]])
            kb.store(V_d, Vv[:, tt * 4:(tt + 1) * 4, :], vo[s], vo[s][:])
        kb.end_phase()

    def phase_attn_A(layer, j, QT_d, KT_d, V_d, oT_d):
        kb.begin_phase()
        lam_init = 0.8 - 0.6 * math.exp(-0.3 * layer)
        NQT = NT
        QTh = [kb.sb(f"QTh{i}", [128, T], BF16) for i in range(2)]
        KTh = [kb.sb(f"KTh{i}", [128, T], BF16) for i in range(2)]
        Vh = [kb.sb(f"Vh{i}", [128, NKC, 128], BF16) for i in range(2)]
        P = [kb.sb(f"P{i}", [128, 1024], BF16) for i in range(3)]
        abias = kb.sb("abias", [128, NKC * NT], F32)
        lamt = kb.sb("lamt", [128, 256], F32)
        lamp = kb.sb("lamp", [128, 128], F32)
        lams = kb.sb("lams", [128, 8], F32)
        gsub = kb.sb("gsub", [128, 2], F32)
        r0 = kb.sb("r0", [128, 512], F32)
        r1 = kb.sb("r1", [128, 512], F32)
        n0 = kb.sb("n0", [128, 512], F32)
        n1 = kb.sb("n1", [128, 512], F32)
        sqb = kb.sb("sqb", [128, 512], BF16)
        tmp = kb.sb("tmp", [128, 512], F32)
        rstd = kb.sb("rstd", [128, 512], F32)
        ob = [kb.sb(f"ob{i}", [128, 512], BF16) for i in range(2)]
        S = [kb.ps(f"S{i}", [128, 1024]) for i in range(2)]
        O0, O1 = kb.ps("O0", [128, 512]), kb.ps("O1", [128, 512])
        L0, L1 = kb.ps("L0", [128, 512]), kb.ps("L1", [128, 512])
        kb.load(abias, abias[:], abias_in, abias_in[:])
        kb.load(lamt, lamt[:], a_lam, a_lam[:, j * 256:(j + 1) * 256])
        kb.load(gsub, gsub[:], a_sub, a_sub[:])
        kb.op(kb.dve, lambda e: e.tensor_tensor(out=lamp[:, 0:64], in0=lamt[:, 0:64], in1=lamt[:, 64:128], op=ALU.mult), [lamt], [lamp])
        kb.op(kb.dve, lambda e: e.tensor_tensor(out=lamp[:, 64:128], in0=lamt[:, 128:192], in1=lamt[:, 192:256], op=ALU.mult), [lamt, lamp], [lamp])
        kb.op(kb.dve, lambda e: e.reduce_sum(out=lams[:, 0:1], in_=lamp[:, 0:64], axis=AX.X), [lamp], [lams])
        kb.op(kb.dve, lambda e: e.reduce_sum(out=lams[:, 1:2], in_=lamp[:, 64:128], axis=AX.X), [lamp, lams], [lams])
        kb.op(kb.act, lambda e: e.activation(out=lams[:, 2:4], in_=lams[:, 0:2], func=AF.Exp), [lams], [lams])
        kb.op(kb.dve, lambda e: e.tensor_tensor(out=lams[:, 4:5], in0=lams[:, 3:4], in1=lams[:, 2:3], op=ALU.subtract), [lams], [lams])
        kb.op(kb.dve, lambda e: e.tensor_scalar_add(out=lams[:, 5:6], in0=lams[:, 4:5], scalar1=-lam_init), [lams], [lams])
        kb.op(kb.dve, lambda e: e.tensor_scalar_mul(out=lams[:, 6:7], in0=gsub[:, j:j + 1], scalar1=1.0 - lam_init), [gsub, lams], [lams])
        neglam = lams[:, 5:6]
        gs = lams[:, 6:7]
        QTv, KTv = QT_d[:], KT_d[:]
        Vv = V_d[:].rearrange("(n p) f -> p n f", p=128)

        def ldh(h):
            s = h % 2
            kb.load(QTh[s], QTh[s][:], QT_d, QTv[h * 128:(h + 1) * 128, :])
            kb.load(KTh[s], KTh[s][:], KT_d, KTv[h * 128:(h + 1) * 128, :])
            step = max(1, NKC // 4)
            for c0 in range(0, NKC, step):
                kb.load(Vh[s], Vh[s][:, c0:c0 + step, :], V_d, Vv[:, c0:c0 + step, h * 128:(h + 1) * 128])

        acc0 = [kb.sb(f"acc0{i}", [128, 512], F32) for i in range(2)]
        acc1 = [kb.sb(f"acc1{i}", [128, 512], F32) for i in range(2)]
        ai = 0
        ldh(0)
        it = 0
        oi = 0
        for h in range(8):
            s = h % 2
            if h + 1 < 8:
                ldh(h + 1)
            Q, K, V = QTh[s], KTh[s], Vh[s]
            for qt in range(NQT):
                def qk(kc, slot):
                    Sx = S[slot]

                    def f(e):
                        e.matmul(Sx[:, 0:512], lhsT=K[0:64, kc * 128:(kc + 1) * 128], rhs=Q[0:64, qt * 512:(qt + 1) * 512],
                                 start=True, stop=True, tile_position=(0, 0))
                        return e.matmul(Sx[:, 512:1024], lhsT=K[64:128, kc * 128:(kc + 1) * 128], rhs=Q[64:128, qt * 512:(qt + 1) * 512],
                                        start=True, stop=True, tile_position=(64, 0))
                    kb.op(kb.pe, f, [K, Q], [Sx])

                qk(0, it % 2)
                Pxs = {}

                def pvA(kc):
                    Px = Pxs[kc]

                    def pv(e):
                        e.matmul(O0[:], lhsT=V[:, kc, :], rhs=Px[:, 0:512], start=(kc == 0), stop=(kc == NKC - 1))
                        e.matmul(O1[:], lhsT=V[:, kc, :], rhs=Px[:, 512:1024], start=(kc == 0), stop=(kc == NKC - 1))
                        return e.matmul(L1[:], lhsT=ones_bf[:], rhs=Px[:, 512:1024], start=(kc == 0), stop=(kc == NKC - 1))
                    kb.op(kb.pe, pv, [V, Px, ones_bf], [O0, O1, L1])
                for kc in range(NKC):
                    slot = it % 2
                    Px = P[it % 3]
                    Pxs[kc] = Px
                    Sx = S[slot]
                    it += 1
                    if kc + 1 < NKC:
                        qk(kc + 1, it % 2)
                    bcol = abias[:, kc * NT + qt:kc * NT + qt + 1]
                    kb.op(kb.act, lambda e: e.activation(out=Px[:], in_=Sx[:], func=AF.Exp, bias=bcol), [Sx, abias], [Px])
                    a0 = acc0[ai % 2]
                    if kc == 0:
                        kb.op(kb.dve, lambda e: e.tensor_copy(out=a0[:], in_=Px[:, 0:512]), [Px], [a0])
                    else:
                        kb.op(kb.dve, lambda e: e.tensor_tensor(out=a0[:], in0=Px[:, 0:512], in1=a0[:], op=ALU.add), [Px, a0], [a0])
                    if kc >= 1:
                        pvA(kc - 1)
                pvA(NKC - 1)
                a0 = acc0[ai % 2]
                ai += 1
                kb.op(kb.pe, lambda e: e.matmul(L0[:], lhsT=ones_f[:], rhs=a0[:], start=True, stop=True), [ones_f, a0], [L0])
                kb.op(kb.act, lambda e: e.activation(out=n1[:], in_=L0[:], func=AF.Ln), [L0], [n1])
                kb.op(kb.act, lambda e: e.activation(out=r0[:], in_=n1[:], func=AF.Exp, scale=-1.0), [n1], [r0])
                kb.op(kb.act, lambda e: e.activation(out=n1[:], in_=L1[:], func=AF.Ln), [L1, r0], [n1])
                kb.op(kb.act, lambda e: e.activation(out=r1[:], in_=n1[:], func=AF.Exp, scale=-1.0), [n1], [r1])
                kb.op(kb.dve, lambda e: e.tensor_tensor(out=n0[:], in0=O0[:], in1=r0[:], op=ALU.mult), [O0, r0], [n0])
                kb.op(kb.dve, lambda e: e.tensor_tensor(out=n1[:], in0=O1[:], in1=r1[:], op=ALU.mult), [O1, r1], [n1])
                kb.op(kb.dve, lambda e: e.scalar_tensor_tensor(out=n0[:], in0=n1[:], scalar=neglam, in1=n0[:], op0=ALU.mult, op1=ALU.add),
                      [n1, n0, lams], [n0])
                kb.op(kb.act, lambda e: e.activation(out=sqb[:], in_=n0[:], func=AF.Square), [n0], [sqb])
                kb.op(kb.pe, lambda e: e.matmul(L0[:], lhsT=ones_bf[:], rhs=sqb[:], start=True, stop=True), [ones_bf, sqb], [L0])
                rstd_from(L0, 512, 128, 1e-5, rstd, tmp)
                o = ob[oi % 2]
                oi += 1
                kb.op(kb.dve, lambda e: e.scalar_tensor_tensor(out=o[:], in0=n0[:], scalar=gs, in1=rstd[:], op0=ALU.mult, op1=ALU.mult),
                      [n0, rstd, lams], [o])
                kb.store(oT_d, oT_d[h * 128:(h + 1) * 128, qt * 512:(qt + 1) * 512], o, o[:])
        kb.end_phase()

    def phase_attn_B(layer, QTb, KTb, Vb, oT_d):
        kb.begin_phase()
        NM = len(BMASKS)
        midx = {gd: i for i, gd in enumerate(BMASKS)}
        Qg = [kb.sb(f"Qg{g}", [128, T], BF16) for g in range(3)]
        Kg = [kb.sb(f"Kg{g}", [128, T], BF16) for g in range(3)]
        Vg = [kb.sb(f"Vg{g}", [128, NKC, 128], BF16) for g in range(3)]
        masks = kb.sb("masks", [128, NM, 512], BF16)
        stg = [kb.sb(f"mstg{i}", [128, 1, 512], F32) for i in range(1)]
        abias = kb.sb("abias", [128, NKC * NT], F32)
        P = [kb.sb(f"P{i}", [128, 1024], BF16) for i in range(2)]
        PM = [kb.sb(f"PM{i}", [128, 1024], BF16) for i in range(3)]
        rr = kb.sb("rr", [128, 512], F32)
        rbs = kb.sb("rbs", [128, 512], F32)
        ob = [kb.sb(f"ob{i}", [128, 512], BF16) for i in range(2)]
        S = [kb.ps(f"S{i}", [128, 1024]) for i in range(2)]
        OX = [kb.ps("OA", [128, 512]), kb.ps("OB", [128, 512])]
        RB = kb.ps("RB", [128, 512])
        kb.load(abias, abias[:], abias_in, abias_in[:])
        kb.op(kb.dve, lambda e: e.memset(rr[:], 0.0), [], [rr])
        for i0_ in range(NM):
            sg = stg[0]
            kb.load(sg, sg[:], bmask_in, bmask_in[:, i0_:i0_ + 1, :])
            kb.op(kb.dve, lambda e: e.tensor_copy(out=masks[:, i0_:i0_ + 1, :], in_=sg[:]), [sg], [masks])
        it = 0
        oi = 0
        for hp in range(8):
            for g in range(3):
                kb.load(Qg[g], Qg[g][:], QTb[g], QTb[g][hp * 128:(hp + 1) * 128, :])
                kb.load(Kg[g], Kg[g][:], KTb[g], KTb[g][hp * 128:(hp + 1) * 128, :])
                Vv = Vb[g][:].rearrange("(n p) f -> p n f", p=128)
                step = max(1, NKC // 4)
                for c0 in range(0, NKC, step):
                    kb.load(Vg[g], Vg[g][:, c0:c0 + step, :], Vb[g], Vv[:, c0:c0 + step, hp * 128:(hp + 1) * 128])
            for qt in range(NT):
                tiles = [(g, kc) for g in range(3) for kc in range(NKC) if (g, qt * 512 - kc * 128) in midx]
                qs_ = slice(qt * 512, (qt + 1) * 512)

                def qkB(ti, slot):
                    g, kc = tiles[ti]
                    Sx = S[slot]
                    ks_ = slice(kc * 128, (kc + 1) * 128)

                    def f(e):
                        e.matmul(Sx[:, 0:512], lhsT=Kg[g][0:64, ks_], rhs=Qg[g][0:64, qs_], start=True, stop=True, tile_position=(0, 0))
                        return e.matmul(Sx[:, 512:1024], lhsT=Kg[g][64:128, ks_], rhs=Qg[g][64:128, qs_], start=True, stop=True,
                                        tile_position=(64, 0))
                    kb.op(kb.pe, f, [Kg[g], Qg[g]], [Sx])
                qkB(0, it % 2)
                nt_ = len(tiles)
                Pms = {}

                def pvBt(ti):
                    g, kc = tiles[ti]
                    Pm = Pms[ti]

                    def pvB(e):
                        for a in range(2):
                            e.matmul(OX[a][0:64, :], lhsT=Vg[g][:, kc, a * 64:(a + 1) * 64], rhs=Pm[:, a * 512:(a + 1) * 512],
                                     start=(ti == 0), stop=(ti == nt_ - 1), tile_position=(0, 0))
                            i = e.matmul(OX[a][64:128, :], lhsT=ones_bf[:, 0:64], rhs=Pm[:, a * 512:(a + 1) * 512],
                                         start=(ti == 0), stop=(ti == nt_ - 1), tile_position=(0, 64))
                        return i
                    kb.op(kb.pe, pvB, [Vg[g], Pm, ones_bf], [OX[0], OX[1]])
                for ti, (g, kc) in enumerate(tiles):
                    Sx = S[it % 2]
                    Px = P[it % 2]
                    Pm = PM[it % 3]
                    Pms[ti] = Pm
                    it += 1
                    if ti + 1 < nt_:
                        qkB(ti + 1, it % 2)
                    mi = midx[(g, qt * 512 - kc * 128)]
                    bcol = abias[:, kc * NT + qt:kc * NT + qt + 1]
                    kb.op(kb.act, lambda e: e.activation(out=Px[:], in_=Sx[:], func=AF.Exp, bias=bcol), [Sx, abias], [Px])
                    kb.op(kb.dve, lambda e: e.tensor_tensor(out=Pm[:].rearrange("p (a q) -> p a q", a=2),
                                                            in0=Px[:].rearrange("p (a q) -> p a q", a=2),
                                                            in1=masks[:, mi:mi + 1, :].to_broadcast([128, 2, 512]), op=ALU.mult), [Px, masks], [Pm])
                    if ti >= 1:
                        pvBt(ti - 1)
                pvBt(nt_ - 1)
                for a in range(2):
                    kb.op(kb.dve, lambda e: e.reciprocal(out=rr[64:128, :], in_=OX[a][64:128, :]), [OX[a]], [rr])
                    kb.op(kb.pe, lambda e: e.matmul(RB[0:64, :], lhsT=shift_f[:], rhs=rr[:], start=True, stop=True), [shift_f, rr], [RB])
                    kb.op(kb.act, lambda e: e.activation(out=rbs[0:64, :], in_=RB[0:64, :], func=AF.Copy), [RB], [rbs])
                    o = ob[oi % 2]
                    oi += 1
                    kb.op(kb.dve, lambda e: e.tensor_tensor(out=o[0:64, :], in0=OX[a][0:64, :], in1=rbs[0:64, :], op=ALU.mult), [OX[a], rbs], [o])
                    r_0 = a * 64
                    kb.store(oT_d, oT_d[hp * 128 + r_0:hp * 128 + r_0 + 64, qs_], o, o[0:64, :])
        kb.end_phase()

    def phase_proj_C(layer, x_d, qt_d, kt_d, kh_d, et_d, V_d, gT_d):
        kb.begin_phase()
        W = kb.sb("Wc", [128, 8, 5120], BF16)
        stg = [kb.sb(f"stg{i}", [128, 8, 256], F32) for i in range(2)]
        wsrc = c_win[0].rearrange("(c p) n -> p c n", p=128)
        load_cast_w(W, lambda a, b: W[:, :, a:b], c_win, lambda a, b: wsrc[:, :, a:b], 5120, stg, blk=256)
        lbl = kb.sb("lbl", [128, 4, 8], F32)
        lbe = kb.sb("lbe", [128, 4, 8], F32)
        lbs = kb.sb("lbs", [128, 4, 8], F32)
        kb.load(lbl, lbl[:], c_lbl, c_lbl[:])
        kb.op(kb.act, lambda e: e.activation(out=lbe[:], in_=lbl[:], func=AF.Exp), [lbl], [lbe])
        kb.op(kb.dve, lambda e: e.tensor_tensor(out=lbs[:, 0, :], in0=lbe[:, 0, :], in1=lbe[:, 1, :], op=ALU.add), [lbe], [lbs])
        kb.op(kb.dve, lambda e: e.tensor_tensor(out=lbs[:, 0, :], in0=lbs[:, 0, :], in1=lbe[:, 2, :], op=ALU.add), [lbe, lbs], [lbs])
        kb.op(kb.dve, lambda e: e.tensor_tensor(out=lbs[:, 0, :], in0=lbs[:, 0, :], in1=lbe[:, 3, :], op=ALU.add), [lbe, lbs], [lbs])
        kb.op(kb.dve, lambda e: e.tensor_copy(out=lbs[:, 1, :], in_=lbe[:, 1, :]), [lbe, lbs], [lbs])
        for i in range(2, layer + 1):
            kb.op(kb.dve, lambda e: e.tensor_tensor(out=lbs[:, 1, :], in0=lbs[:, 1, :], in1=lbe[:, i, :], op=ALU.add), [lbe, lbs], [lbs])
        kb.op(kb.dve, lambda e: e.reciprocal(out=lbs[:, 2, :], in_=lbs[:, 0, :]), [lbs], [lbs])
        kb.op(kb.dve, lambda e: e.tensor_tensor(out=lbs[:, 3, :], in0=lbs[:, 0, :], in1=lbs[:, 1, :], op=ALU.subtract), [lbs], [lbs])
        kb.op(kb.dve, lambda e: e.tensor_tensor(out=lbs[:, 3, :], in0=lbs[:, 3, :], in1=lbs[:, 2, :], op=ALU.mult), [lbs], [lbs])
        xs = [kb.sb(f"xs{i}", [128, 8, 512], F32) for i in range(2)]
        hT = kb.sb("hT", [128, 8, 512], BF16)
        sq = [kb.sb(f"sq{i}", [128, 512], BF16) for i in range(2)]
        tmp = kb.sb("tmp", [128, 512], F32)
        rstd = kb.sb("rstd", [128, 512], F32)
        F = lambda n: kb.sb(n, [128, 512], F32)
        qs, sg, kk, lf, G, Gd, Ek, eG, enG, eK = [F(n) for n in ("qs", "sg", "kk", "lf", "G", "Gd", "Ek", "eG", "enG", "eK")]
        gate = [kb.sb(f"gate{i}", [128, 512], BF16) for i in range(2)]
        qo = [kb.sb(f"qo{i}", [128, 512], BF16) for i in range(2)]
        ko = [kb.sb(f"ko{i}", [128, 512], BF16) for i in range(2)]
        kh = [kb.sb(f"kh{i}", [128, 512], BF16) for i in range(2)]
        kht = [kb.sb(f"kht{i}", [128, 4, 128], BF16) for i in range(2)]
        eto = [kb.sb(f"eto{i}", [128, 8], F32) for i in range(2)]
        vo = kb.sb("vo", [128, 4, 1024], BF16)
        pst = kb.ps("pst", [128, 512])
        pq = [kb.ps(f"pq{i}", [128, 512]) for i in range(3)]
        ptr = [kb.ps(f"ptr{i}", [128, 128], BF16) for i in range(2)]
        xv = x_d[:].rearrange("(c p) t -> p c t", p=128)
        Vv = V_d[:].rearrange("(n p) f -> p n f", p=128)
        khv = [kh_d[d][:].rearrange("(n p) f -> p n f", p=128) for d in range(2)]

        def ld(tt):
            kb.load(xs[tt % 2], xs[tt % 2][:], x_d, xv[:, :, tt * 512:(tt + 1) * 512])
        ld(0)
        cnt = 0
        oc = 0
        for tt in range(NT):
            if tt + 1 < NT:
                ld(tt + 1)
            x, h = xs[tt % 2], hT
            for c in range(8):
                q = sq[c % 2]
                kb.op(kb.act, lambda e: e.activation(out=q[:], in_=x[:, c, :], func=AF.Square), [x], [q])
                kb.op(kb.pe, lambda e: e.matmul(pst[:], lhsT=ones_bf[:], rhs=q[:], start=(c == 0), stop=(c == 7)), [ones_bf, q], [pst])
            rstd_from(pst, 512, D, 1e-6, rstd, tmp)
            for c in range(8):
                kb.op(kb.dve, lambda e: e.scalar_tensor_tensor(out=h[:, c, :], in0=x[:, c, :], scalar=gcol(layer, 0, c),
                                                                 in1=rstd[:], op0=ALU.mult, op1=ALU.mult), [x, rstd, normg], [h])

            def proj(col0):
                nonlocal cnt
                p = pq[cnt % 3]
                cnt += 1

                def mm(e):
                    for k in range(8):
                        i = e.matmul(p[:], lhsT=W[:, k, col0:col0 + 128], rhs=h[:, k, :], start=(k == 0), stop=(k == 7))
                    return i
                kb.op(kb.pe, mm, [W, h], [p])
                return p
            csl = slice(tt * 512, (tt + 1) * 512)
            for hd in range(8):
                rows = slice(hd * 128, (hd + 1) * 128)
                p = proj(hd * 128)
                kb.op(kb.act, lambda e: e.activation(out=qs[:], in_=p[:], func=AF.Silu), [p], [qs])
                p = proj(4096 + hd * 128)
                gt = gate[oc % 2]
                kb.op(kb.act, lambda e: e.activation(out=gt[:], in_=p[:], func=AF.Silu), [p], [gt])
                kb.store(gT_d, gT_d[rows, csl], gt, gt[:])
                for d in range(2):
                    qo_, ko_, kh_, kht_, eto_ = qo[oc % 2], ko[oc % 2], kh[oc % 2], kht[oc % 2], eto[oc % 2]
                    oc += 1
                    p = proj(1024 + d * 1024 + hd * 128)
                    kb.op(kb.act, lambda e: e.activation(out=sg[:], in_=p[:], func=AF.Sigmoid, scale=-1.0), [p], [sg])
                    kb.op(kb.dve, lambda e: e.tensor_scalar_mul(out=kk[:], in0=sg[:], scalar1=lbs[:, 3, hd:hd + 1]), [sg, lbs], [kk])
                    kb.op(kb.act, lambda e: e.activation(out=lf[:], in_=kk[:], func=AF.Ln, scale=-1.0, bias=1.0), [kk], [lf])
                    kb.op(kb.dve, lambda e: e.tensor_tensor_scan(out=G[:], data0=m01[:], data1=lf[:], initial=0.0, op0=ALU.mult, op1=ALU.add),
                          [m01, lf], [G])
                    G3 = G[:].rearrange("p (c t) -> p c t", t=64)
                    TOTb = G3[:, :, 63:64].to_broadcast([128, 8, 64])
                    r3 = lambda b: b[:].rearrange("p (c t) -> p c t", t=64)
                    if d == 0:
                        Gdb = G
                        kb.op(kb.dve, lambda e: e.tensor_tensor(out=r3(Ek), in0=TOTb, in1=G3, op=ALU.subtract), [G], [Ek])
                    else:
                        Gdb = Gd
                        kb.op(kb.dve, lambda e: e.tensor_tensor(out=Ek[:], in0=G[:], in1=lf[:], op=ALU.subtract), [G, lf], [Ek])
                        kb.op(kb.dve, lambda e: e.tensor_tensor(out=r3(Gd), in0=TOTb, in1=r3(Ek), op=ALU.subtract), [G, Ek], [Gd])
                    kb.op(kb.act, lambda e: e.activation(out=eG[:], in_=Gdb[:], func=AF.Exp), [Gdb], [eG])
                    kb.op(kb.act, lambda e: e.activation(out=enG[:], in_=Gdb[:], func=AF.Exp, scale=-1.0), [Gdb], [enG])
                    kb.op(kb.act, lambda e: e.activation(out=eK[:], in_=Ek[:], func=AF.Exp), [Ek], [eK])
                    kb.op(kb.act, lambda e: e.activation(out=eto_[:], in_=G3[:, :, 63], func=AF.Exp), [G], [eto_])
                    kb.op(kb.dve, lambda e: e.scalar_tensor_tensor(out=qo_[:], in0=qs[:], scalar=cv[:, 2:3], in1=eG[:], op0=ALU.mult, op1=ALU.mult),
                          [qs, cv, eG], [qo_])
                    kb.op(kb.dve, lambda e: e.tensor_tensor(out=ko_[:], in0=kk[:], in1=enG[:], op=ALU.mult), [kk, enG], [ko_])
                    kb.op(kb.dve, lambda e: e.tensor_tensor(out=kh_[:], in0=kk[:], in1=eK[:], op=ALU.mult), [kk, eK], [kh_])
                    for tb in range(4):
                        pt = ptr[tb % 2]
                        kb.op(kb.pe, lambda e: e.transpose(pt[:], kh_[:, tb * 128:(tb + 1) * 128], ident_bf[:]), [kh_, ident_bf], [pt])
                        kb.op(kb.act, lambda e: e.activation(out=kht_[:, tb, :], in_=pt[:], func=AF.Copy), [pt], [kht_])
                    kb.store(qt_d[d], qt_d[d][rows, csl], qo_, qo_[:])
                    kb.store(kt_d[d], kt_d[d][rows, csl], ko_, ko_[:])
                    kb.store(kh_d[d], khv[d][:, tt * 4:(tt + 1) * 4, rows], kht_, kht_[:])
                    kb.store(et_d[d], et_d[d][rows, tt * 8:(tt + 1) * 8], eto_, eto_[:])
            for tb in range(4):
                for nb in range(2):
                    p = pq[cnt % 3]
                    cnt += 1

                    def mm(e):
                        for k in range(8):
                            i = e.matmul(p[:], lhsT=h[:, k, tb * 128:(tb + 1) * 128],
                                         rhs=W[:, k, 3072 + nb * 512:3072 + (nb + 1) * 512], start=(k == 0), stop=(k == 7))
                        return i
                    kb.op(kb.pe, mm, [W, h], [p])
                    kb.op(kb.act, lambda e: e.activation(out=vo[:, tb, nb * 512:(nb + 1) * 512], in_=p[:], func=AF.Copy), [p], [vo])
            kb.store(V_d, Vv[:, tt * 4:(tt + 1) * 4, :], vo, vo[:])
        kb.end_phase()

    def phase_scan_C(layer, qt_d, kt_d, kh_d, et_d, V_d, gT_d, oT_d):
        kb.begin_phase()
        NC = T // 64
        Qt = [kb.sb(f"Qt{i}", [128, T], BF16) for i in range(2)]
        Kt = [kb.sb(f"Kt{i}", [128, T], BF16) for i in range(2)]
        Kh = [kb.sb(f"Kh{i}", [128, NKC, 128], BF16) for i in range(2)]
        Et = [kb.sb(f"Et{i}", [128, NC], F32) for i in range(2)]
        Vh = kb.sb("Vh", [128, NKC, 128], BF16)
        Gt = kb.sb("Gt", [128, T], BF16)
        ofw = kb.sb("ofw", [128, T], F32)
        Sf = kb.sb("Sf", [128, 128], F32)
        Sb = kb.sb("Sb", [128, 128], BF16)
        sc = [kb.sb(f"sc{i}", [128, 128], BF16) for i in range(2)]
        gn = kb.sb("gn", [128, 1], F32)
        sqb = kb.sb("sqb", [128, 512], BF16)
        tmp = kb.sb("tmp", [128, 512], F32)
        rstd = kb.sb("rstd", [128, 512], F32)
        on = kb.sb("on", [128, 512], F32)
        ob = [kb.sb(f"ob{i}", [128, 512], BF16) for i in range(2)]
        scp = [kb.ps(f"scp{i}", [128, 128]) for i in range(2)]
        op_ = [kb.ps(f"op{i}", [128, 128]) for i in range(2)]
        dsp = [kb.ps(f"dsp{i}", [128, 128]) for i in range(2)]
        pst = kb.ps("pst", [128, 512])
        kb.load(gn, gn[:], c_gn, c_gn[:])
        Vv = V_d[:].rearrange("(n p) f -> p n f", p=128)
        khv = [kh_d[d][:].rearrange("(n p) f -> p n f", p=128) for d in range(2)]
        step = max(1, NKC // 4)
        it = 0
        oi = 0
        ci = 0
        for hd in range(8):
            rows = slice(hd * 128, (hd + 1) * 128)
            for c0 in range(0, NKC, step):
                kb.load(Vh, Vh[:, c0:c0 + step, :], V_d, Vv[:, c0:c0 + step, rows])
            kb.load(Gt, Gt[:], gT_d, gT_d[rows, :])
            for d in range(2):
                kb.load(Qt[d], Qt[d][:], qt_d[d], qt_d[d][rows, :])
                kb.load(Kt[d], Kt[d][:], kt_d[d], kt_d[d][rows, :])
                kb.load(Et[d], Et[d][:], et_d[d], et_d[d][rows, :])
                for c0 in range(0, NKC, step):
                    kb.load(Kh[d], Kh[d][:, c0:c0 + step, :], kh_d[d], khv[d][:, c0:c0 + step, rows])
            for d in range(2):
                Q, K, KH, ET = Qt[d], Kt[d], Kh[d], Et[d]
                kb.op(kb.dve, lambda e: e.memset(Sf[:], 0.0), [], [Sf])
                kb.op(kb.dve, lambda e: e.memset(Sb[:], 0.0), [], [Sb])
                blocks = range(NKC) if d == 0 else range(NKC - 1, -1, -1)
                for b in blocks:
                    bs = slice(b * 128, (b + 1) * 128)
                    scp_, sc_, o_ = scp[it % 2], sc[it % 2], op_[it % 2]
                    it += 1
                    kb.op(kb.pe, lambda e: e.matmul(scp_[:], lhsT=K[:, bs], rhs=Q[:, bs], start=True, stop=True), [K, Q], [scp_])
                    kb.op(kb.dve, lambda e: e.tensor_tensor(out=sc_[:], in0=scp_[:], in1=cmask[:, d, :], op=ALU.mult), [scp_, cmask], [sc_])
                    kb.op(kb.pe, lambda e: e.matmul(o_[:], lhsT=Vh[:, b, :], rhs=sc_[:], start=True, stop=False), [Vh, sc_], [o_])
                    chunks = (2 * b, 2 * b + 1) if d == 0 else (2 * b + 1, 2 * b)
                    for n_, c in enumerate(chunks):
                        p0 = (c % 2) * 64
                        first_of_other = (c == NC // 2) if d == 0 else (c == NC // 2 - 1)
                        if first_of_other:
                            kb.op(kb.dve, lambda e: e.tensor_scalar_mul(out=Sf[:], in0=Sf[:], scalar1=flag[:, 0:1]), [Sf, flag], [Sf])
                            kb.op(kb.act, lambda e: e.activation(out=Sb[:], in_=Sf[:], func=AF.Copy), [Sf], [Sb])
                        kb.op(kb.pe, lambda e: e.matmul(o_[:, p0:p0 + 64], lhsT=Sb[:], rhs=Q[:, c * 64:(c + 1) * 64], start=False, stop=(n_ == 1)),
                              [Sb, Q], [o_])
                        ds_ = dsp[ci % 2]
                        ci += 1
                        kb.op(kb.pe, lambda e: e.matmul(ds_[:], lhsT=KH[p0:p0 + 64, b, :], rhs=Vh[p0:p0 + 64, b, :], start=True, stop=True,
                                                        tile_position=(p0, 0)), [KH, Vh], [ds_])
                        kb.op(kb.dve, lambda e: e.scalar_tensor_tensor(out=Sf[:], in0=Sf[:], scalar=ET[:, c:c + 1], in1=ds_[:],
                                                                         op0=ALU.mult, op1=ALU.add), [Sf, ET, ds_], [Sf])
                        kb.op(kb.act, lambda e: e.activation(out=Sb[:], in_=Sf[:], func=AF.Copy), [Sf], [Sb])
                    if d == 0:
                        kb.op(kb.act, lambda e: e.activation(out=ofw[:, bs], in_=o_[:], func=AF.Copy), [o_], [ofw])
                    else:
                        kb.op(kb.dve, lambda e: e.tensor_tensor(out=ofw[:, bs], in0=o_[:], in1=ofw[:, bs], op=ALU.add), [o_, ofw], [ofw])
            for tt in range(NT):
                csl = slice(tt * 512, (tt + 1) * 512)
                kb.op(kb.act, lambda e: e.activation(out=sqb[:], in_=ofw[:, csl], func=AF.Square), [ofw], [sqb])
                kb.op(kb.pe, lambda e: e.matmul(pst[:], lhsT=ones_bf[:], rhs=sqb[:], start=True, stop=True), [ones_bf, sqb], [pst])
                rstd_from(pst, 512, 128, 1e-6, rstd, tmp)
                kb.op(kb.dve, lambda e: e.scalar_tensor_tensor(out=on[:], in0=ofw[:, csl], scalar=gn[:, 0:1], in1=rstd[:], op0=ALU.mult, op1=ALU.mult),
                      [ofw, gn, rstd], [on])
                o = ob[oi % 2]
                oi += 1
                kb.op(kb.dve, lambda e: e.tensor_tensor(out=o[:], in0=on[:], in1=Gt[:, csl], op=ALU.mult), [on, Gt], [o])
                kb.store(oT_d, oT_d[rows, csl], o, o[:])
        kb.end_phase()

    def phase_wo(layer, wo_buf, wo_ap, oT_d, x_d, x1_d):
        kb.begin_phase()
        Wo = kb.sb("Wo", [128, 8, 1024], BF16)
        stg = [kb.sb(f"stg{i}", [128, 8, 512], F32) for i in range(2)]
        wsrc = wo_ap.rearrange("(c p) n -> p c n", p=128)
        load_cast_w(Wo, lambda a, b: Wo[:, :, a:b], wo_buf, lambda a, b: wsrc[:, :, a:b], 1024, stg)
        xs = [kb.sb(f"xs{i}", [128, 8, 512], F32) for i in range(2)]
        os_ = [kb.sb(f"os{i}", [128, 8, 512], BF16) for i in range(2)]
        y = kb.split(kb.sb("y", [128, 8, 512], F32), 8)
        sq = [kb.sb(f"sq{i}", [128, 512], BF16) for i in range(2)]
        tmp = kb.sb("tmp", [128, 512], F32)
        rstd = kb.sb("rstd", [128, 512], F32)
        xo = [kb.split(kb.sb(f"xo{i}", [128, 8, 512], F32), 8) for i in range(2)]
        pst = kb.ps("pst", [128, 512])
        pq = [kb.ps(f"pq{i}", [128, 512]) for i in range(3)]
        xv = x_d[:].rearrange("(c p) t -> p c t", p=128)
        x1v = x1_d[:].rearrange("(c p) t -> p c t", p=128)
        ov = oT_d[:].rearrange("(c p) t -> p c t", p=128)

        def ld(tt):
            s = tt % 2
            kb.load(xs[s], xs[s][:], x_d, xv[:, :, tt * 512:(tt + 1) * 512])
            kb.load(os_[s], os_[s][:], oT_d, ov[:, :, tt * 512:(tt + 1) * 512])
        ld(0)
        cnt = 0
        for tt in range(NT):
            s = tt % 2
            if tt + 1 < NT:
                ld(tt + 1)
            x, o = xs[s], os_[s]
            for m in range(8):
                p = pq[cnt % 3]
                q = sq[cnt % 2]
                cnt += 1

                def mm(e):
                    for k in range(8):
                        i = e.matmul(p[:], lhsT=Wo[:, k, m * 128:(m + 1) * 128], rhs=o[:, k, :], start=(k == 0), stop=(k == 7))
                    return i
                kb.op(kb.pe, mm, [Wo, o], [p])
                kb.op(kb.act, lambda e: e.activation(out=q[:], in_=p[:], func=AF.Square), [p], [q])
                kb.op(kb.dve, lambda e: e.tensor_copy(out=y[:, m, :], in_=p[:]), [p], [y.c[m]])
                kb.op(kb.pe, lambda e: e.matmul(pst[:], lhsT=ones_bf[:], rhs=q[:], start=(m == 0), stop=(m == 7)), [ones_bf, q], [pst])
            rstd_from(pst, 512, D, 1e-6, rstd, tmp)
            for m in range(8):
                kb.op(kb.dve, lambda e: e.scalar_tensor_tensor(out=y[:, m, :], in0=y[:, m, :], scalar=gcol(layer, 1, m), in1=rstd[:],
                                                                 op0=ALU.mult, op1=ALU.mult), [y.c[m], rstd, normg], [y.c[m]])
                kb.op(kb.ew2, lambda e: e.tensor_tensor(out=xo[s][:, m, :], in0=y[:, m, :], in1=x[:, m, :], op=ALU.add), [y.c[m], x], [xo[s].c[m]])
            kb.store(x1_d, x1v[:, :, tt * 512:(tt + 1) * 512], xo[s], xo[s][:])
        kb.end_phase()

    def phase_ffn(layer, x1_d, x2_d):
        kb.begin_phase()
        NV = 256
        NW = NV + 2
        Win = kb.sb("Win", [128, 8, 2 * DFF], BF16)
        Wout = kb.sb("Wout", [128, 22, 1024], BF16)
        stg = [kb.sb(f"stg{i}", [128, 8, 128], F32) for i in range(1)]
        wsrc = f_win[layer].rearrange("(c p) n -> p c n", p=128)
        load_cast_w(Win, lambda a, b: Win[:, :, a:b], f_win, lambda a, b: wsrc[:, :, a:b], 2 * DFF, stg, blk=128)
        wsrc2 = f_wout[layer].rearrange("(c p) n -> p c n", p=128)
        for c0 in range(0, 22, 8):
            c1 = min(22, c0 + 8)
            for n0_ in range(0, 1024, 128):
                sg = stg[cast_rr[0] % len(stg)]
                cast_rr[0] += 1
                kb.load(sg, sg[:, :c1 - c0, :], f_wout, wsrc2[:, c0:c1, n0_:n0_ + 128])
                kb.op(kb.dve, lambda e: e.tensor_copy(out=Wout[:, c0:c1, n0_:n0_ + 128], in_=sg[:, :c1 - c0, :]), [sg], [Wout])
        cw = kb.sb("cw", [128, 44, 4], F32)
        kb.load(cw, cw[:], f_cw, f_cw[:, layer * 176:(layer + 1) * 176].rearrange("p (c f) -> p c f", f=4))
        xw = [kb.sb(f"xw{i}", [128, 8, NW], F32) for i in range(2)]
        h = kb.split(kb.sb("h", [128, 8, NW], BF16), 8)
        sq = [kb.sb(f"sq{i}", [128, NW], BF16) for i in range(2)]
        tmp = kb.sb("tmp", [128, NW], F32)
        rstd = kb.sb("rstd", [128, NW], F32)
        ta = [kb.sb(f"ta{i}", [128, NV], F32) for i in range(2)]
        tb_ = [kb.sb(f"tb{i}", [128, NV], F32) for i in range(2)]
        ga = [kb.sb(f"ga{i}", [128, NV], F32) for i in range(2)]
        gg = kb.split(kb.sb("gg", [128, 22, NV], BF16), 22)
        y = kb.split(kb.sb("y", [128, 8, NV], F32), 8)
        xo = [kb.split(kb.sb(f"xo{i}", [128, 8, NV], F32), 8) for i in range(2)]
        pst = kb.ps("pst", [128, 512])
        pa = [kb.ps(f"pa{i}", [128, 512]) for i in range(2)]
        pb = [kb.ps(f"pb{i}", [128, 512]) for i in range(2)]
        po = [kb.ps(f"po{i}", [128, 512]) for i in range(2)]
        xv = x1_d[:].rearrange("(c p) t -> p c t", p=128)
        x2v = x2_d[:].rearrange("(c p) t -> p c t", p=128)
        wins = list(range(0, T, NV))

        def ld(wi):
            s0 = wins[wi]
            b = xw[wi % 2]
            lo, hi = s0 - 1, s0 + NV + 1
            clo, chi = max(lo, 0), min(hi, T)
            if clo > lo:
                kb.op(kb.pool, lambda e: e.memset(b[:, :, 0:1], 0.0), [], [b])
            if chi < hi:
                kb.op(kb.pool, lambda e: e.memset(b[:, :, NW - 1:NW], 0.0), [], [b])
            kb.load(b, b[:, :, clo - lo:NW - (hi - chi)], x1_d, xv[:, :, clo:chi])
        ld(0)
        cnt = 0
        for wi, s0 in enumerate(wins):
            if wi + 1 < len(wins):
                ld(wi + 1)
            x = xw[wi % 2]
            for c in range(8):
                q = sq[c % 2]
                kb.op(kb.act, lambda e: e.activation(out=q[:], in_=x[:, c, :], func=AF.Square), [x], [q])
                kb.op(kb.pe, lambda e: e.matmul(pst[:, :NW], lhsT=ones_bf[:], rhs=q[:], start=(c == 0), stop=(c == 7)), [ones_bf, q], [pst])
            rstd_from(pst, NW, D, 1e-6, rstd, tmp)
            for c in range(8):
                kb.op(kb.dve, lambda e: e.scalar_tensor_tensor(out=h[:, c, :], in0=x[:, c, :], scalar=gcol(layer, 2, c), in1=rstd[:],
                                                                 op0=ALU.mult, op1=ALU.mult), [x, rstd, normg], [h.c[c]])
            if s0 == HALF:
                kb.op(kb.dve, lambda e: e.tensor_scalar_mul(out=h[:, :, 0:1], in0=h[:, :, 0:1], scalar1=flag[:, 0:1]), [h, flag], [h])
            if s0 + NV == HALF:
                kb.op(kb.dve, lambda e: e.tensor_scalar_mul(out=h[:, :, NW - 1:NW], in0=h[:, :, NW - 1:NW], scalar1=flag[:, 0:1]), [h, flag], [h])
            for jj in range(22):
                A, B = pa[jj % 2], pb[jj % 2]
                a_, b_, g_ = ta[jj % 2], tb_[jj % 2], ga[jj % 2]

                def mma(e):
                    for k in range(8):
                        i = e.matmul(A[:, :NW], lhsT=Win[:, k, jj * 128:(jj + 1) * 128], rhs=h[:, k, :], start=(k == 0), stop=(k == 7))
                    return i

                def mmb(e):
                    for k in range(8):
                        i = e.matmul(B[:, :NW], lhsT=Win[:, k, DFF + jj * 128:DFF + (jj + 1) * 128], rhs=h[:, k, :], start=(k == 0), stop=(k == 7))
                    return i
                kb.op(kb.pe, mma, [Win, h], [A])
                kb.op(kb.pe, mmb, [Win, h], [B])
                for (Pp, tt_, ci, eng2) in ((A, a_, jj, kb.dve), (B, b_, 22 + jj, kb.dve)):
                    kb.op(kb.act, lambda e: e.activation(out=tt_[:], in_=Pp[:, 1:NV + 1], func=AF.Identity,
                                                         scale=cw[:, ci, 1:2], bias=cw[:, ci, 3:4]), [Pp, cw], [tt_])
                    kb.op(eng2, lambda e: e.scalar_tensor_tensor(out=tt_[:], in0=Pp[:, 0:NV], scalar=cw[:, ci, 0:1], in1=tt_[:],
                                                                  op0=ALU.mult, op1=ALU.add), [Pp, cw, tt_], [tt_])
                    kb.op(eng2, lambda e: e.scalar_tensor_tensor(out=tt_[:], in0=Pp[:, 2:NV + 2], scalar=cw[:, ci, 2:3], in1=tt_[:],
                                                                  op0=ALU.mult, op1=ALU.add), [Pp, cw, tt_], [tt_])
                kb.op(kb.act, lambda e: e.activation(out=g_[:], in_=a_[:], func=AF.Gelu_apprx_tanh), [a_], [g_])
                kb.op(kb.ew2, lambda e: e.tensor_tensor(out=gg[:, jj, :], in0=g_[:], in1=b_[:], op=ALU.mult), [g_, b_], [gg.c[jj]])
            for m in range(8):
                p = po[m % 2]
                q = sq[m % 2]

                def mm(e):
                    for k in range(22):
                        i = e.matmul(p[:, :NV], lhsT=Wout[:, k, m * 128:(m + 1) * 128], rhs=gg[:, k, :], start=(k == 0), stop=(k == 21))
                    return i
                kb.op(kb.pe, mm, [Wout, gg], [p])
                kb.op(kb.act, lambda e: e.activation(out=q[:, :NV], in_=p[:, :NV], func=AF.Square), [p], [q])
                kb.op(kb.dve, lambda e: e.tensor_copy(out=y[:, m, :], in_=p[:, :NV]), [p], [y.c[m]])
                kb.op(kb.pe, lambda e: e.matmul(pst[:, :NV], lhsT=ones_bf[:], rhs=q[:, :NV], start=(m == 0), stop=(m == 7)), [ones_bf, q], [pst])
            rstd_from(pst, NV, D, 1e-6, rstd, tmp)
            xo_ = xo[wi % 2]
            for m in range(8):
                kb.op(kb.dve, lambda e: e.scalar_tensor_tensor(out=y[:, m, :], in0=y[:, m, :], scalar=gcol(layer, 3, m), in1=rstd[:, :NV],
                                                                 op0=ALU.mult, op1=ALU.mult), [y.c[m], rstd, normg], [y.c[m]])
                kb.op(kb.ew2, lambda e: e.tensor_tensor(out=xo_[:, m, :], in0=y[:, m, :], in1=x[:, m, 1:NV + 1], op=ALU.add), [y.c[m], x], [xo_.c[m]])
            kb.store(x2_d, x2v[:, :, s0:s0 + NV], xo_, xo_[:])
        kb.end_phase()

    QTb = [kb.dram(f"QTb{g}", [D, T], BF16) for g in range(3)]
    KTb = [kb.dram(f"KTb{g}", [D, T], BF16) for g in range(3)]
    Vb = [kb.dram(f"Vb{g}", [T, D], BF16) for g in range(3)]
    cq_d = [kb.dram(f"cq{d}", [D, T], BF16) for d in range(2)]
    ck_d = [kb.dram(f"ck{d}", [D, T], BF16) for d in range(2)]
    ckh_d = [kb.dram(f"ckh{d}", [T, D], BF16) for d in range(2)]
    cet_d = [kb.dram(f"cet{d}", [D, T // 64], F32) for d in range(2)]
    QT_d = kb.dram("QT_d", [D, T], BF16)
    KT_d = kb.dram("KT_d", [D, T], BF16)
    V_d = kb.dram("V_d", [T, D], BF16)
    oT_d = kb.dram("oT_d", [D, T], BF16)
    x1_d = kb.dram("x1_d", [D, T], F32)
    xa_d = kb.dram("xa_d", [D, T], F32)
    xb_d = kb.dram("xb_d", [D, T], F32)
    cur = xT_in
    for li, layer in enumerate(layers):
        last = li == len(layers) - 1
        nxt = yT_out if last else (xa_d if li % 2 == 0 else xb_d)
        kind = layer % 3
        j = layer // 3
        import os
        stop = int(os.environ.get("KSTOP", "99"))
        if kind == 0:
            if stop >= 1:
                phase_proj_A(layer, a_wqkv, a_wqkv[j], cur, QT_d, KT_d, V_d)
            if stop >= 2:
                phase_attn_A(layer, j, QT_d, KT_d, V_d, oT_d)
            if stop >= 3:
                phase_wo(layer, a_wo, a_wo[j], oT_d, cur, x1_d)
        elif kind == 1:
            for g in range(3):
                phase_proj_A(layer, b_wqkv, b_wqkv[0][:, g * 3072:(g + 1) * 3072], cur, QTb[g], KTb[g], Vb[g])
            phase_attn_B(layer, QTb, KTb, Vb, oT_d)
            phase_wo(layer, b_wo, b_wo[0], oT_d, cur, x1_d)
        else:
            phase_proj_C(layer, cur, cq_d, ck_d, ckh_d, cet_d, V_d, KT_d)
            phase_scan_C(layer, cq_d, ck_d, ckh_d, cet_d, V_d, KT_d, oT_d)
            phase_wo(layer, c_wo, c_wo[0], oT_d, cur, x1_d)
        if stop >= 4:
            phase_ffn(layer, x1_d, nxt)
        cur = nxt
    kb.begin_phase()
    kb.end_phase()
    return nc


def rope_tables(pos):
    half = 8
    inv = (500000.0 ** (-np.arange(half, dtype=np.float32) / half)).astype(np.float32)
    ang = pos.astype(np.float32)[:, None] * inv[None, :]
    cos, sin = np.cos(ang).astype(np.float32), np.sin(ang).astype(np.float32)
    T = pos.shape[0]
    tab = np.zeros((128, 3, T), np.float32)
    tab[:, 0, :] = 1.0
    for hd in range(2):
        b = hd * 64
        tab[b:b + 8, 0, :] = cos.T
        tab[b + 8:b + 16, 0, :] = cos.T
        tab[b:b + 8, 1, :] = -sin.T
        tab[b + 8:b + 16, 1, :] = sin.T
    tab[:, 2, :] = tab[:, 0, :] * np.float32(0.125)
    return tab


def const_mats():
    c = np.zeros((128, 7, 128), np.float32)
    for m in range(64):
        c[64 + m, 6, m] = 1.0
    c[:, 0, :] = 1.0
    c[:, 1, :] = np.eye(128, dtype=np.float32)
    for hd in range(2):
        for i in range(8):
            a, b = hd * 64 + i, hd * 64 + i + 8
            c[a, 2, b] = 1.0
            c[b, 2, a] = 1.0
    s = np.arange(128)
    c[:, 3, :] = (s[:, None] <= s[None, :]).astype(np.float32)
    same = (s[:, None] // 64) == (s[None, :] // 64)
    c[:, 4, :] = ((s[:, None] <= s[None, :]) & same).astype(np.float32)
    c[:, 5, :] = ((s[:, None] >= s[None, :]) & same).astype(np.float32)
    return c


def make_in_maps(inp, T, seqs_per_core):
    maps = []
    NT, NKC = T // 512, T // 128
    normg = np.ascontiguousarray(inp["norm_g"].reshape(4, 4, 8, 128).transpose(3, 0, 1, 2).reshape(128, 128))
    a_lam = np.ascontiguousarray(np.broadcast_to(inp["a_lambda"].reshape(1, -1), (128, 512)))
    a_sub = np.ascontiguousarray(inp["a_subln_g"].T)
    cwt = np.concatenate([inp["f_conv_w"], inp["f_conv_b"][:, None, :]], axis=1)
    f_cw = np.ascontiguousarray(cwt.reshape(4, 4, 44, 128).transpose(3, 0, 2, 1).reshape(128, 4 * 44 * 4))
    cm = const_mats()
    bm = bmask_table()
    c_lbl = np.ascontiguousarray(inp["c_lb_logits"].reshape(4, 8, 128).transpose(2, 0, 1))
    c_gn = np.ascontiguousarray(inp["c_gnorm_g"].reshape(1, 128).T)
    m01h = np.ones((128, 512), np.float32)
    m01h[:, ::64] = 0.0
    for seqs in seqs_per_core:
        xT = np.ascontiguousarray(np.concatenate(seqs, axis=0).T)
        pos = np.concatenate([np.arange(s.shape[0]) for s in seqs])
        sid = np.concatenate([np.full(s.shape[0], i) for i, s in enumerate(seqs)])
        ksid = sid[::128][:, None]
        qsid = sid[::512][None, :]
        ab = np.where(ksid == qsid, 0.0, NEG).astype(np.float32).reshape(1, NKC * NT)
        flag = np.zeros((128, 2), np.float32)
        flag[:, 0] = 1.0 if len(seqs) == 1 else 0.0
        m = {
            "xT": xT, "normg": normg, "rope": rope_tables(pos), "cmat": cm, "flag": flag,
            "abias": np.ascontiguousarray(np.broadcast_to(ab, (128, NKC * NT))),
            "a_w_qkv": inp["a_w_qkv"], "a_lam": a_lam, "a_sub": a_sub, "a_w_o": inp["a_w_o"],
            "c_w_in": inp["c_w_in"], "c_w_o": inp["c_w_o"], "c_lbl": c_lbl, "c_gn": c_gn, "m01": m01h,
            "b_w_qkv": inp["b_w_qkv"], "b_w_o": inp["b_w_o"], "bmask": bm,
            "f_w_in": inp["f_w_in"], "f_cw": f_cw, "f_w_out": inp["f_w_out"],
        }
        maps.append(m)
    return maps


_NC_CACHE = {}


def kernel(**inputs):
    inp = {k: np.asarray(v) for k, v in inputs.items()}
    T = 8192
    xp, xs = inp["x_prompt"], inp["x_sample"]
    seqs = [[xp[b]] for b in range(4)] + [[xs[2 * c], xs[2 * c + 1]] for c in range(4)]
    maps = make_in_maps(inp, T, seqs)
    if "nc" not in _NC_CACHE:
        _NC_CACHE["nc"] = build(T)
    res = run_bass_kernel_spmd(_NC_CACHE["nc"], maps, core_ids=list(range(8)))
    outs = [np.asarray(r["yT"]).T for r in res.results]
    y_prompt = np.stack(outs[:4], axis=0).astype(np.float32)
    y_sample = np.stack([o.reshape(2, 4096, D) for o in outs[4:]], axis=0).reshape(8, 4096, D).astype(np.float32)
    return (y_prompt, y_sample)
```

```python
import math
from contextlib import ExitStack
import numpy as np
import concourse.bass as bass
import concourse.mybir as mybir
from concourse.bass_utils import run_bass_kernel_spmd

F32 = mybir.dt.float32
BF16 = mybir.dt.bfloat16
AF = mybir.ActivationFunctionType
ALU = mybir.AluOpType
AX = mybir.AxisListType

D = 1024
DFF = 2816
NEG = -30000.0
B_DIL = (1, 4, 16)


def _bmask_list():
    out = []
    for g, d in enumerate(B_DIL):
        for delta in range(-4096, 4097, 128):
            q = np.arange(512)[None, :]
            k = np.arange(128)[:, None]
            diff = delta + q - k
            if np.any((diff % d == 0) & (np.abs(diff) <= 64 * d)):
                out.append((g, delta))
    return out


BMASKS = _bmask_list()


def bmask_table():
    t = np.zeros((128, len(BMASKS), 512), np.float32)
    q = np.arange(512)[None, :]
    k = np.arange(128)[:, None]
    for i, (g, delta) in enumerate(BMASKS):
        d = B_DIL[g]
        diff = delta + q - k
        t[:, i, :] = np.where((diff % d == 0) & (np.abs(diff) <= 64 * d), 1.0, 0.0)
    return t


class Stream:
    def __init__(self, name, eng, sem):
        self.name, self.eng, self.sem = name, eng, sem
        self.count = 0
        self.known = {}


class Buf:
    def __init__(self, t, name):
        self.t = t
        self.name = name
        self.w = {}
        self.r = {}
        self.isdram = False
        self.excl = False
        self.c = None
        self.lsem = None
        self.lcount = 0
        self.ssem = None
        self.scount = 0

    def __getitem__(self, idx):
        return self.t[idx]


class KB:
    def __init__(self, nc):
        self.nc = nc
        self.top = ExitStack()
        self.sems = []
        self.free_dsems = []
        self.all_dsems = {}
        mk = lambda n, e: Stream(n, e, self.top.enter_context(nc.semaphore("s_" + n)))
        self.pe = mk("pe", nc.tensor)
        self.act = mk("act", nc.scalar)
        self.dve = mk("dve", nc.vector)
        self.pool = mk("pool", nc.gpsimd)
        import os
        self.ew2 = self.dve if os.environ.get("KPOOL", "dve") == "dve" else self.pool
        self.sp = mk("sp", nc.sync)
        self.streams = [self.pe, self.act, self.dve, self.pool, self.sp]
        self.phase = None
        self.phase_bufs = []
        self.uid = 0

    def begin_phase(self):
        self.phase = ExitStack()
        self.phase_bufs = []

    def end_phase(self):
        self.barrier()
        for b in self.phase_bufs:
            for s, c in ((b.lsem, b.lcount), (b.ssem, b.scount)):
                if s is not None:
                    self.free_dsems.append((s, c))
        self.phase.close()
        self.phase = None

    def sb(self, name, shape, dt, glob=False):
        self.uid += 1
        es = self.top if glob else self.phase
        t = es.enter_context(self.nc.sbuf_tensor(f"{name}_{self.uid}", list(shape), dt))
        b = Buf(t, name)
        if not glob:
            self.phase_bufs.append(b)
        return b

    def ps(self, name, shape, dt=F32):
        self.uid += 1
        t = self.phase.enter_context(self.nc.psum_tensor(f"{name}_{self.uid}", list(shape), dt))
        b = Buf(t, name)
        b.excl = True
        return b

    def dram(self, name, shape, dt):
        t = self.nc.dram_tensor(name, list(shape), dt, kind="Internal")
        b = Buf(t.ap(), name)
        b.isdram = True
        return b

    def _dsem(self):
        if self.free_dsems:
            return self.free_dsems.pop()
        s = self.top.enter_context(self.nc.semaphore(f"d{len(self.all_dsems)}"))
        self.all_dsems[id(s)] = s
        return (s, 0)

    @staticmethod
    def _expand(bufs):
        out = []
        for b in bufs:
            if b.c is not None:
                out.extend(b.c)
            else:
                out.append(b)
        return out

    def split(self, buf, n):
        buf.c = [Buf(buf.t, f"{buf.name}.{i}") for i in range(n)]
        for ch in buf.c:
            ch.excl = buf.excl
            ch.isdram = buf.isdram
        return buf

    def _deps(self, st, reads, writes, skip=None):
        reads, writes = self._expand(reads), self._expand(writes)
        need = {}

        def add(ev):
            s, v = ev
            if id(s) not in need or need[id(s)][1] < v:
                need[id(s)] = ev
        for b in reads:
            for ev in b.w.values():
                add(ev)
            if b.excl:
                for ev in b.r.values():
                    if ev[0] is not st.sem:
                        add(ev)
        for b in writes:
            if not b.isdram:
                for ev in b.w.values():
                    add(ev)
            for ev in b.r.values():
                add(ev)
        for s, v in need.values():
            if skip is not None and s is skip:
                continue
            if s is st.sem and st is self.pe:
                continue
            if st.known.get(id(s), 0) < v:
                st.eng.wait_ge(s, v)
                st.known[id(s)] = v

    def _mark(self, ev, reads, writes):
        reads, writes = self._expand(reads), self._expand(writes)
        for b in writes:
            if b.isdram:
                b.w[id(ev[0])] = ev
            else:
                b.w = {id(ev[0]): ev}
                b.r = {}
        for b in reads:
            if b not in writes:
                b.r[id(ev[0])] = ev

    def op(self, st, fn, reads=(), writes=()):
        reads, writes = list(reads), list(writes)
        self._deps(st, reads, writes)
        ins = fn(st.eng)
        st.count += 1
        ins.then_inc(st.sem, 1)
        self._mark((st.sem, st.count), reads, writes)
        return ins

    def load(self, sbuf, out_ap, dbuf, in_ap, st=None):
        st = st or self.sp
        if sbuf.lsem is None:
            sbuf.lsem, sbuf.lcount = self._dsem()
        self._deps(st, [dbuf], [sbuf], skip=sbuf.lsem)
        sbuf.lcount += 16
        st.eng.dma_start(out=out_ap, in_=in_ap).then_inc(sbuf.lsem, 16)
        self._mark((sbuf.lsem, sbuf.lcount), [dbuf], [sbuf])

    def store(self, dbuf, out_ap, sbuf, in_ap, st=None):
        st = st or self.pool
        if sbuf.ssem is None:
            sbuf.ssem, sbuf.scount = self._dsem()
        self._deps(st, [sbuf], [dbuf], skip=sbuf.ssem)
        sbuf.scount += 16
        st.eng.dma_start(out=out_ap, in_=in_ap).then_inc(sbuf.ssem, 16)
        self._mark((sbuf.ssem, sbuf.scount), [sbuf], [dbuf])

    def barrier(self):
        cur = [(st.sem, st.count) for st in self.streams]
        for b in self.phase_bufs:
            if b.lsem is not None:
                cur.append((b.lsem, b.lcount))
            if b.ssem is not None:
                cur.append((b.ssem, b.scount))
        for st in self.streams:
            for s, v in cur:
                if s is st.sem or v == 0:
                    continue
                if st.known.get(id(s), 0) < v:
                    st.eng.wait_ge(s, v)
                    st.known[id(s)] = v


def build(T=8192, layers=(0, 1, 2, 3)):
    nc = bass.Bass("TRN2", target_bir_lowering=False)
    kb = KB(nc)
    NT = T // 512
    NKC = T // 128
    HALF = T // 2

    def din(name, shape, dt=F32):
        b = Buf(nc.dram_tensor(name, list(shape), dt, kind="ExternalInput").ap(), name)
        b.isdram = True
        return b

    xT_in = din("xT", [D, T])
    yT_out = Buf(nc.dram_tensor("yT", [D, T], F32, kind="ExternalOutput").ap(), "yT")
    yT_out.isdram = True
    normg_in = din("normg", [128, 4 * 4 * 8])
    rope_in = din("rope", [128, 3, T])
    cmat_in = din("cmat", [128, 7, 128])
    flag_in = din("flag", [128, 2])
    abias_in = din("abias", [128, NKC * NT])
    a_wqkv = din("a_w_qkv", [2, D, 3072])
    a_lam = din("a_lam", [128, 2 * 256])
    a_sub = din("a_sub", [128, 2])
    a_wo = din("a_w_o", [2, D, D])
    b_wqkv = din("b_w_qkv", [1, D, 9216])
    b_wo = din("b_w_o", [1, D, D])
    bmask_in = din("bmask", [128, len(BMASKS), 512])
    c_win = din("c_w_in", [1, D, 5120])
    c_wo = din("c_w_o", [1, D, D])
    c_lbl = din("c_lbl", [128, 4, 8])
    c_gn = din("c_gn", [128, 1])
    m01_in = din("m01", [128, 512])
    f_win = din("f_w_in", [4, D, 2 * DFF])
    f_cw = din("f_cw", [128, 4 * 44 * 4])
    f_wout = din("f_w_out", [4, DFF, D])

    kb.begin_phase()
    ones_bf = kb.sb("ones", [128, 128], BF16, glob=True)
    ident_bf = kb.sb("ident", [128, 128], BF16, glob=True)
    rperm_bf = kb.sb("rperm", [128, 128], BF16, glob=True)
    normg = kb.sb("normg", [128, 128], F32, glob=True)
    flag = kb.sb("flag", [128, 2], F32, glob=True)
    cv = kb.sb("cv", [128, 4], F32, glob=True)
    ones_f = kb.sb("ones_f", [128, 128], F32, glob=True)
    shift_f = kb.sb("shift_f", [128, 64], F32, glob=True)
    cmask = kb.sb("cmask", [128, 2, 128], F32, glob=True)
    m01 = kb.sb("m01", [128, 512], F32, glob=True)
    cst = kb.sb("cst", [128, 7, 128], F32)
    kb.load(cst, cst[:], cmat_in, cmat_in[:])
    kb.load(normg, normg[:], normg_in, normg_in[:])
    kb.load(flag, flag[:], flag_in, flag_in[:])
    kb.op(kb.dve, lambda e: e.memset(cv[:, 0:1], 0.125), [], [cv])
    kb.op(kb.dve, lambda e: e.memset(cv[:, 1:2], 1.0), [cv], [cv])
    kb.op(kb.dve, lambda e: e.memset(cv[:, 2:3], float(128 ** -0.5)), [cv], [cv])
    kb.load(m01, m01[:], m01_in, m01_in[:])
    kb.op(kb.dve, lambda e: e.tensor_copy(out=cmask[:], in_=cst[:, 4:6, :]), [cst], [cmask])
    kb.op(kb.dve, lambda e: e.tensor_copy(out=ones_bf[:], in_=cst[:, 0, :]), [cst], [ones_bf])
    kb.op(kb.dve, lambda e: e.tensor_copy(out=ones_f[:], in_=cst[:, 0, :]), [cst], [ones_f])
    kb.op(kb.dve, lambda e: e.tensor_copy(out=shift_f[:], in_=cst[:, 6, 0:64]), [cst], [shift_f])
    kb.op(kb.dve, lambda e: e.tensor_copy(out=ident_bf[:], in_=cst[:, 1, :]), [cst], [ident_bf])
    kb.op(kb.dve, lambda e: e.tensor_copy(out=rperm_bf[:], in_=cst[:, 2, :]), [cst], [rperm_bf])
    kb.end_phase()

    def gcol(layer, n, c):
        i = (layer * 4 + n) * 8 + c
        return normg[:, i:i + 1]

    cast_rr = [0]

    def load_cast_w(wdst, dst_ap_fn, src_buf, src_ap_fn, ncols, stg, kc=8, blk=512):
        for c0 in range(0, ncols, blk):
            c1 = min(ncols, c0 + blk)
            s = stg[cast_rr[0] % len(stg)]
            kb.load(s, s[:, :kc, :c1 - c0], src_buf, src_ap_fn(c0, c1))
            st = kb.dve if cast_rr[0] % 2 == 0 else kb.act
            if st is kb.dve:
                kb.op(st, lambda e: e.tensor_copy(out=dst_ap_fn(c0, c1), in_=s[:, :kc, :c1 - c0]), [s], [wdst])
            else:
                kb.op(st, lambda e: e.activation(out=dst_ap_fn(c0, c1), in_=s[:, :kc, :c1 - c0], func=AF.Copy), [s], [wdst])
            cast_rr[0] += 1

    def rstd_from(ps_stat, n, nfeat, eps, rstd, tmp):
        kb.op(kb.act, lambda e: e.activation(out=tmp[:, :n], in_=ps_stat[:, :n], func=AF.Ln,
                                             bias=float(eps), scale=1.0 / nfeat), [ps_stat], [tmp])
        kb.op(kb.act, lambda e: e.activation(out=rstd[:, :n], in_=tmp[:, :n], func=AF.Exp, scale=-0.5), [tmp], [rstd])

    def phase_proj_A(layer, wbuf, wap, x_d, QT_d, KT_d, V_d):
        kb.begin_phase()
        W = kb.sb("Wqkv", [128, 8, 3072], BF16)
        stg = [kb.sb(f"stg{i}", [128, 8, 256], F32) for i in range(2)]
        wsrc = wap.rearrange("(c p) n -> p c n", p=128)
        load_cast_w(W, lambda a, b: W[:, :, a:b], wbuf, lambda a, b: wsrc[:, :, a:b], 3072, stg, blk=256)
        xs = [kb.sb(f"xs{i}", [128, 8, 512], F32) for i in range(2)]
        rp = [kb.sb(f"rp{i}", [128, 3, 512], F32) for i in range(2)]
        hT = [kb.split(kb.sb(f"hT{i}", [128, 8, 512], BF16), 8) for i in range(2)]
        sq = [kb.sb(f"sq{i}", [128, 512], BF16) for i in range(2)]
        tmp = kb.sb("tmp", [128, 512], F32)
        rstd = kb.sb("rstd", [128, 512], F32)
        qsb = [kb.sb(f"qsb{i}", [128, 512], BF16) for i in range(2)]
        t1 = [kb.sb(f"t1{i}", [128, 512], F32) for i in range(2)]
        t2 = [kb.sb(f"t2{i}", [128, 512], F32) for i in range(2)]
        qo = [kb.split(kb.sb(f"qo{i}", [128, 8, 512], BF16), 8) for i in range(2)]
        ko = [kb.split(kb.sb(f"ko{i}", [128, 8, 512], BF16), 8) for i in range(2)]
        vo = [kb.split(kb.sb(f"vo{i}", [128, 4, 1024], BF16), 8) for i in range(2)]
        pst = kb.ps("pst", [128, 512])
        pq = [kb.ps(f"pq{i}", [128, 512]) for i in range(3)]
        pr = [kb.ps(f"pr{i}", [128, 512]) for i in range(2)]
        xv = x_d[:].rearrange("(c p) t -> p c t", p=128)
        Vv = V_d[:].rearrange("(n p) f -> p n f", p=128)
        QTv = QT_d[:].rearrange("(c p) t -> p c t", p=128)
        KTv = KT_d[:].rearrange("(c p) t -> p c t", p=128)

        def ld(tt):
            s = tt % 2
            kb.load(xs[s], xs[s][:], x_d, xv[:, :, tt * 512:(tt + 1) * 512])
            kb.load(rp[s], rp[s][:], rope_in, rope_in[:, :, tt * 512:(tt + 1) * 512])

        import os
        KSUB = int(os.environ.get("KSUB", "99"))
        if KSUB < 1:
            kb.end_phase()
            return
        ld(0)
        cnt = 0
        for tt in range(NT):
            s = tt % 2
            if tt + 1 < NT:
                ld(tt + 1)
            x, h = xs[s], hT[s]
            if KSUB < 2:
                continue
            for c in range(8):
                q = sq[c % 2]
                kb.op(kb.act, lambda e: e.activation(out=q[:], in_=x[:, c, :], func=AF.Square), [x], [q])
                kb.op(kb.pe, lambda e: e.matmul(pst[:], lhsT=ones_bf[:], rhs=q[:], start=(c == 0), stop=(c == 7)),
                      [ones_bf, q], [pst])
            if KSUB < 3:
                continue
            rstd_from(pst, 512, D, 1e-6, rstd, tmp)
            if KSUB < 4:
                continue
            for c in range(8):
                kb.op(kb.dve, lambda e: e.scalar_tensor_tensor(out=h[:, c, :], in0=x[:, c, :], scalar=gcol(layer, 0, c),
                                                                 in1=rstd[:], op0=ALU.mult, op1=ALU.mult),
                      [x, rstd, normg], [h.c[c]])
            if KSUB < 5:
                continue
            for m in range(16):
                p = pq[cnt % 3]
                r_ = pr[cnt % 2]
                qs, a1, a2 = qsb[cnt % 2], t1[cnt % 2], t2[cnt % 2]
                dst = (qo if m < 8 else ko)[s]
                scale = 0.125 if m < 8 else 1.0
                cnt += 1

                def mm(e):
                    for k in range(8):
                        i = e.matmul(p[:], lhsT=W[:, k, m * 128:(m + 1) * 128], rhs=h[:, k, :], start=(k == 0), stop=(k == 7))
                    return i
                KQ = int(os.environ.get("KQ", "99"))
                kb.op(kb.pe, mm, [W, h], [p])
                if KQ < 2:
                    continue
                kb.op(kb.act, lambda e: e.activation(out=qs[:], in_=p[:], func=AF.Copy, scale=scale), [p], [qs])
                if KQ < 3:
                    continue
                kb.op(kb.pe, lambda e: e.matmul(r_[:], lhsT=rperm_bf[:], rhs=qs[:], start=True, stop=True), [rperm_bf, qs], [r_])
                if KQ < 4:
                    continue
                kb.op(kb.dve, lambda e: e.tensor_tensor(out=a1[:], in0=p[:], in1=rp[s][:, (2 if m < 8 else 0), :], op=ALU.mult), [p, rp[s]], [a1])
                if KQ < 5:
                    continue
                kb.op(kb.dve, lambda e: e.tensor_tensor(out=a2[:], in0=r_[:], in1=rp[s][:, 1, :], op=ALU.mult), [r_, rp[s]], [a2])
                if KQ < 6:
                    continue
                kb.op(kb.ew2, lambda e: e.tensor_tensor(out=dst[:, m % 8, :], in0=a1[:], in1=a2[:], op=ALU.add), [a1, a2], [dst.c[m % 8]])
            if KSUB < 6:
                continue
            kb.store(QT_d, QTv[:, :, tt * 512:(tt + 1) * 512], qo[s], qo[s][:])
            kb.store(KT_d, KTv[:, :, tt * 512:(tt + 1) * 512], ko[s], ko[s][:])
            if KSUB < 7:
                continue
            for tb in range(4):
                for nb in range(2):
                    p = pq[cnt % 3]
                    cnt += 1

                    def mm(e):
                        for k in range(8):
                            i = e.matmul(p[:], lhsT=h[:, k, tb * 128:(tb + 1) * 128],
                                         rhs=W[:, k, 2048 + nb * 512:2048 + (nb + 1) * 512], start=(k == 0), stop=(k == 7))
                        return i
                    kb.op(kb.pe, mm, [W, h], [p])
                    kb.op(kb.act, lambda e: e.activation(out=vo[s][:, tb, nb * 512:(nb + 1) * 512], in_=p[:], func=AF.Copy), [p], [vo[s].c[tb * 2 + nb]])
            kb.store(V_d, Vv[:, tt * 4:(tt + 1) * 4, :], vo[s], vo[s][:])
        kb.end_phase()

    def phase_attn_A(layer, j, QT_d, KT_d, V_d, oT_d):
        kb.begin_phase()
        lam_init = 0.8 - 0.6 * math.exp(-0.3 * layer)
        NQT = NT
        QTh = [kb.sb(f"QTh{i}", [128, T], BF16) for i in range(2)]
        KTh = [kb.sb(f"KTh{i}", [128, T], BF16) for i in range(2)]
        Vh = [kb.sb(f"Vh{i}", [128, NKC, 128], BF16) for i in range(2)]
        P = [kb.sb(f"P{i}", [128, 1024], BF16) for i in range(3)]
        abias = kb.sb("abias", [128, NKC * NT], F32)
        lamt = kb.sb("lamt", [128, 256], F32)
        lamp = kb.sb("lamp", [128, 128], F32)
        lams = kb.sb("lams", [128, 8], F32)
        gsub = kb.sb("gsub", [128, 2], F32)
        r0 = kb.sb("r0", [128, 512], F32)
        r1 = kb.sb("r1", [128, 512], F32)
        n0 = kb.sb("n0", [128, 512], F32)
        n1 = kb.sb("n1", [128, 512], F32)
        sqb = kb.sb("sqb", [128, 512], BF16)
        tmp = kb.sb("tmp", [128, 512], F32)
        rstd = kb.sb("rstd", [128, 512], F32)
        ob = [kb.sb(f"ob{i}", [128, 512], BF16) for i in range(2)]
        S = [kb.ps(f"S{i}", [128, 1024]) for i in range(2)]
        O0, O1 = kb.ps("O0", [128, 512]), kb.ps("O1", [128, 512])
        L0, L1 = kb.ps("L0", [128, 512]), kb.ps("L1", [128, 512])
        kb.load(abias, abias[:], abias_in, abias_in[:])
        kb.load(lamt, lamt[:], a_lam, a_lam[:, j * 256:(j + 1) * 256])
        kb.load(gsub, gsub[:], a_sub, a_sub[:])
        kb.op(kb.dve, lambda e: e.tensor_tensor(out=lamp[:, 0:64], in0=lamt[:, 0:64], in1=lamt[:, 64:128], op=ALU.mult), [lamt], [lamp])
        kb.op(kb.dve, lambda e: e.tensor_tensor(out=lamp[:, 64:128], in0=lamt[:, 128:192], in1=lamt[:, 192:256], op=ALU.mult), [lamt, lamp], [lamp])
        kb.op(kb.dve, lambda e: e.reduce_sum(out=lams[:, 0:1], in_=lamp[:, 0:64], axis=AX.X), [lamp], [lams])
        kb.op(kb.dve, lambda e: e.reduce_sum(out=lams[:, 1:2], in_=lamp[:, 64:128], axis=AX.X), [lamp, lams], [lams])
        kb.op(kb.act, lambda e: e.activation(out=lams[:, 2:4], in_=lams[:, 0:2], func=AF.Exp), [lams], [lams])
        kb.op(kb.dve, lambda e: e.tensor_tensor(out=lams[:, 4:5], in0=lams[:, 3:4], in1=lams[:, 2:3], op=ALU.subtract), [lams], [lams])
        kb.op(kb.dve, lambda e: e.tensor_scalar_add(out=lams[:, 5:6], in0=lams[:, 4:5], scalar1=-lam_init), [lams], [lams])
        kb.op(kb.dve, lambda e: e.tensor_scalar_mul(out=lams[:, 6:7], in0=gsub[:, j:j + 1], scalar1=1.0 - lam_init), [gsub, lams], [lams])
        neglam = lams[:, 5:6]
        gs = lams[:, 6:7]
        QTv, KTv = QT_d[:], KT_d[:]
        Vv = V_d[:].rearrange("(n p) f -> p n f", p=128)

        def ldh(h):
            s = h % 2
            kb.load(QTh[s], QTh[s][:], QT_d, QTv[h * 128:(h + 1) * 128, :])
            kb.load(KTh[s], KTh[s][:], KT_d, KTv[h * 128:(h + 1) * 128, :])
            step = max(1, NKC // 4)
            for c0 in range(0, NKC, step):
                kb.load(Vh[s], Vh[s][:, c0:c0 + step, :], V_d, Vv[:, c0:c0 + step, h * 128:(h + 1) * 128])

        acc0 = [kb.sb(f"acc0{i}", [128, 512], F32) for i in range(2)]
        acc1 = [kb.sb(f"acc1{i}", [128, 512], F32) for i in range(2)]
        ai = 0
        ldh(0)
        it = 0
        oi = 0
        for h in range(8):
            s = h % 2
            if h + 1 < 8:
                ldh(h + 1)
            Q, K, V = QTh[s], KTh[s], Vh[s]
            for qt in range(NQT):
                def qk(kc, slot):
                    Sx = S[slot]

                    def f(e):
                        e.matmul(Sx[:, 0:512], lhsT=K[0:64, kc * 128:(kc + 1) * 128], rhs=Q[0:64, qt * 512:(qt + 1) * 512],
                                 start=True, stop=True, tile_position=(0, 0))
                        return e.matmul(Sx[:, 512:1024], lhsT=K[64:128, kc * 128:(kc + 1) * 128], rhs=Q[64:128, qt * 512:(qt + 1) * 512],
                                        start=True, stop=True, tile_position=(64, 0))
                    kb.op(kb.pe, f, [K, Q], [Sx])

                qk(0, it % 2)
                Pxs = {}

                def pvA(kc):
                    Px = Pxs[kc]

                    def pv(e):
                        e.matmul(O0[:], lhsT=V[:, kc, :], rhs=Px[:, 0:512], start=(kc == 0), stop=(kc == NKC - 1))
                        e.matmul(O1[:], lhsT=V[:, kc, :], rhs=Px[:, 512:1024], start=(kc == 0), stop=(kc == NKC - 1))
                        return e.matmul(L1[:], lhsT=ones_bf[:], rhs=Px[:, 512:1024], start=(kc == 0), stop=(kc == NKC - 1))
                    kb.op(kb.pe, pv, [V, Px, ones_bf], [O0, O1, L1])
                for kc in range(NKC):
                    slot = it % 2
                    Px = P[it % 3]
                    Pxs[kc] = Px
                    Sx = S[slot]
                    it += 1
                    if kc + 1 < NKC:
                        qk(kc + 1, it % 2)
                    bcol = abias[:, kc * NT + qt:kc * NT + qt + 1]
                    kb.op(kb.act, lambda e: e.activation(out=Px[:], in_=Sx[:], func=AF.Exp, bias=bcol), [Sx, abias], [Px])
                    a0 = acc0[ai % 2]
                    if kc == 0:
                        kb.op(kb.dve, lambda e: e.tensor_copy(out=a0[:], in_=Px[:, 0:512]), [Px], [a0])
                    else:
                        kb.op(kb.dve, lambda e: e.tensor_tensor(out=a0[:], in0=Px[:, 0:512], in1=a0[:], op=ALU.add), [Px, a0], [a0])
                    if kc >= 1:
                        pvA(kc - 1)
                pvA(NKC - 1)
                a0 = acc0[ai % 2]
                ai += 1
                kb.op(kb.pe, lambda e: e.matmul(L0[:], lhsT=ones_f[:], rhs=a0[:], start=True, stop=True), [ones_f, a0], [L0])
                kb.op(kb.act, lambda e: e.activation(out=n1[:], in_=L0[:], func=AF.Ln), [L0], [n1])
                kb.op(kb.act, lambda e: e.activation(out=r0[:], in_=n1[:], func=AF.Exp, scale=-1.0), [n1], [r0])
                kb.op(kb.act, lambda e: e.activation(out=n1[:], in_=L1[:], func=AF.Ln), [L1, r0], [n1])
                kb.op(kb.act, lambda e: e.activation(out=r1[:], in_=n1[:], func=AF.Exp, scale=-1.0), [n1], [r1])
                kb.op(kb.dve, lambda e: e.tensor_tensor(out=n0[:], in0=O0[:], in1=r0[:], op=ALU.mult), [O0, r0], [n0])
                kb.op(kb.dve, lambda e: e.tensor_tensor(out=n1[:], in0=O1[:], in1=r1[:], op=ALU.mult), [O1, r1], [n1])
                kb.op(kb.dve, lambda e: e.scalar_tensor_tensor(out=n0[:], in0=n1[:], scalar=neglam, in1=n0[:], op0=ALU.mult, op1=ALU.add),
                      [n1, n0, lams], [n0])
                kb.op(kb.act, lambda e: e.activation(out=sqb[:], in_=n0[:], func=AF.Square), [n0], [sqb])
                kb.op(kb.pe, lambda e: e.matmul(L0[:], lhsT=ones_bf[:], rhs=sqb[:], start=True, stop=True), [ones_bf, sqb], [L0])
                rstd_from(L0, 512, 128, 1e-5, rstd, tmp)
                o = ob[oi % 2]
                oi += 1
                kb.op(kb.dve, lambda e: e.scalar_tensor_tensor(out=o[:], in0=n0[:], scalar=gs, in1=rstd[:], op0=ALU.mult, op1=ALU.mult),
                      [n0, rstd, lams], [o])
                kb.store(oT_d, oT_d[h * 128:(h + 1) * 128, qt * 512:(qt + 1) * 512], o, o[:])
        kb.end_phase()

    def phase_attn_B(layer, QTb, KTb, Vb, oT_d):
        kb.begin_phase()
        NM = len(BMASKS)
        midx = {gd: i for i, gd in enumerate(BMASKS)}
        Qg = [kb.sb(f"Qg{g}", [128, T], BF16) for g in range(3)]
        Kg = [kb.sb(f"Kg{g}", [128, T], BF16) for g in range(3)]
        Vg = [kb.sb(f"Vg{g}", [128, NKC, 128], BF16) for g in range(3)]
        masks = kb.sb("masks", [128, NM, 512], BF16)
        stg = [kb.sb(f"mstg{i}", [128, 1, 512], F32) for i in range(1)]
        abias = kb.sb("abias", [128, NKC * NT], F32)
        P = [kb.sb(f"P{i}", [128, 1024], BF16) for i in range(2)]
        PM = [kb.sb(f"PM{i}", [128, 1024], BF16) for i in range(3)]
        rr = kb.sb("rr", [128, 512], F32)
        rbs = kb.sb("rbs", [128, 512], F32)
        ob = [kb.sb(f"ob{i}", [128, 512], BF16) for i in range(2)]
        S = [kb.ps(f"S{i}", [128, 1024]) for i in range(2)]
        OX = [kb.ps("OA", [128, 512]), kb.ps("OB", [128, 512])]
        RB = kb.ps("RB", [128, 512])
        kb.load(abias, abias[:], abias_in, abias_in[:])
        kb.op(kb.dve, lambda e: e.memset(rr[:], 0.0), [], [rr])
        for i0_ in range(NM):
            sg = stg[0]
            kb.load(sg, sg[:], bmask_in, bmask_in[:, i0_:i0_ + 1, :])
            kb.op(kb.dve, lambda e: e.tensor_copy(out=masks[:, i0_:i0_ + 1, :], in_=sg[:]), [sg], [masks])
        it = 0
        oi = 0
        for hp in range(8):
            for g in range(3):
                kb.load(Qg[g], Qg[g][:], QTb[g], QTb[g][hp * 128:(hp + 1) * 128, :])
                kb.load(Kg[g], Kg[g][:], KTb[g], KTb[g][hp * 128:(hp + 1) * 128, :])
                Vv = Vb[g][:].rearrange("(n p) f -> p n f", p=128)
                step = max(1, NKC // 4)
                for c0 in range(0, NKC, step):
                    kb.load(Vg[g], Vg[g][:, c0:c0 + step, :], Vb[g], Vv[:, c0:c0 + step, hp * 128:(hp + 1) * 128])
            for qt in range(NT):
                tiles = [(g, kc) for g in range(3) for kc in range(NKC) if (g, qt * 512 - kc * 128) in midx]
                qs_ = slice(qt * 512, (qt + 1) * 512)

                def qkB(ti, slot):
                    g, kc = tiles[ti]
                    Sx = S[slot]
                    ks_ = slice(kc * 128, (kc + 1) * 128)

                    def f(e):
                        e.matmul(Sx[:, 0:512], lhsT=Kg[g][0:64, ks_], rhs=Qg[g][0:64, qs_], start=True, stop=True, tile_position=(0, 0))
                        return e.matmul(Sx[:, 512:1024], lhsT=Kg[g][64:128, ks_], rhs=Qg[g][64:128, qs_], start=True, stop=True,
                                        tile_position=(64, 0))
                    kb.op(kb.pe, f, [Kg[g], Qg[g]], [Sx])
                qkB(0, it % 2)
                nt_ = len(tiles)
                Pms = {}

                def pvBt(ti):
                    g, kc = tiles[ti]
                    Pm = Pms[ti]

                    def pvB(e):
                        for a in range(2):
                            e.matmul(OX[a][0:64, :], lhsT=Vg[g][:, kc, a * 64:(a + 1) * 64], rhs=Pm[:, a * 512:(a + 1) * 512],
                                     start=(ti == 0), stop=(ti == nt_ - 1), tile_position=(0, 0))
                            i = e.matmul(OX[a][64:128, :], lhsT=ones_bf[:, 0:64], rhs=Pm[:, a * 512:(a + 1) * 512],
                                         start=(ti == 0), stop=(ti == nt_ - 1), tile_position=(0, 64))
                        return i
                    kb.op(kb.pe, pvB, [Vg[g], Pm, ones_bf], [OX[0], OX[1]])
                for ti, (g, kc) in enumerate(tiles):
                    Sx = S[it % 2]
                    Px = P[it % 2]
                    Pm = PM[it % 3]
                    Pms[ti] = Pm
                    it += 1
                    if ti + 1 < nt_:
                        qkB(ti + 1, it % 2)
                    mi = midx[(g, qt * 512 - kc * 128)]
                    bcol = abias[:, kc * NT + qt:kc * NT + qt + 1]
                    kb.op(kb.act, lambda e: e.activation(out=Px[:], in_=Sx[:], func=AF.Exp, bias=bcol), [Sx, abias], [Px])
                    kb.op(kb.dve, lambda e: e.tensor_tensor(out=Pm[:].rearrange("p (a q) -> p a q", a=2),
                                                            in0=Px[:].rearrange("p (a q) -> p a q", a=2),
                                                            in1=masks[:, mi:mi + 1, :].to_broadcast([128, 2, 512]), op=ALU.mult), [Px, masks], [Pm])
                    if ti >= 1:
                        pvBt(ti - 1)
                pvBt(nt_ - 1)
                for a in range(2):
                    kb.op(kb.dve, lambda e: e.reciprocal(out=rr[64:128, :], in_=OX[a][64:128, :]), [OX[a]], [rr])
                    kb.op(kb.pe, lambda e: e.matmul(RB[0:64, :], lhsT=shift_f[:], rhs=rr[:], start=True, stop=True), [shift_f, rr], [RB])
                    kb.op(kb.act, lambda e: e.activation(out=rbs[0:64, :], in_=RB[0:64, :], func=AF.Copy), [RB], [rbs])
                    o = ob[oi % 2]
                    oi += 1
                    kb.op(kb.dve, lambda e: e.tensor_tensor(out=o[0:64, :], in0=OX[a][0:64, :], in1=rbs[0:64, :], op=ALU.mult), [OX[a], rbs], [o])
                    r_0 = a * 64
                    kb.store(oT_d, oT_d[hp * 128 + r_0:hp * 128 + r_0 + 64, qs_], o, o[0:64, :])
        kb.end_phase()

    def phase_proj_C(layer, x_d, qt_d, kt_d, kh_d, et_d, V_d, gT_d):
        kb.begin_phase()
        W = kb.sb("Wc", [128, 8, 5120], BF16)
        stg = [kb.sb(f"stg{i}", [128, 8, 256], F32) for i in range(2)]
        wsrc = c_win[0].rearrange("(c p) n -> p c n", p=128)
        load_cast_w(W, lambda a, b: W[:, :, a:b], c_win, lambda a, b: wsrc[:, :, a:b], 5120, stg, blk=256)
        lbl = kb.sb("lbl", [128, 4, 8], F32)
        lbe = kb.sb("lbe", [128, 4, 8], F32)
        lbs = kb.sb("lbs", [128, 4, 8], F32)
        kb.load(lbl, lbl[:], c_lbl, c_lbl[:])
        kb.op(kb.act, lambda e: e.activation(out=lbe[:], in_=lbl[:], func=AF.Exp), [lbl], [lbe])
        kb.op(kb.dve, lambda e: e.tensor_tensor(out=lbs[:, 0, :], in0=lbe[:, 0, :], in1=lbe[:, 1, :], op=ALU.add), [lbe], [lbs])
        kb.op(kb.dve, lambda e: e.tensor_tensor(out=lbs[:, 0, :], in0=lbs[:, 0, :], in1=lbe[:, 2, :], op=ALU.add), [lbe, lbs], [lbs])
        kb.op(kb.dve, lambda e: e.tensor_tensor(out=lbs[:, 0, :], in0=lbs[:, 0, :], in1=lbe[:, 3, :], op=ALU.add), [lbe, lbs], [lbs])
        kb.op(kb.dve, lambda e: e.tensor_copy(out=lbs[:, 1, :], in_=lbe[:, 1, :]), [lbe, lbs], [lbs])
        for i in range(2, layer + 1):
            kb.op(kb.dve, lambda e: e.tensor_tensor(out=lbs[:, 1, :], in0=lbs[:, 1, :], in1=lbe[:, i, :], op=ALU.add), [lbe, lbs], [lbs])
        kb.op(kb.dve, lambda e: e.reciprocal(out=lbs[:, 2, :], in_=lbs[:, 0, :]), [lbs], [lbs])
        kb.op(kb.dve, lambda e: e.tensor_tensor(out=lbs[:, 3, :], in0=lbs[:, 0, :], in1=lbs[:, 1, :], op=ALU.subtract), [lbs], [lbs])
        kb.op(kb.dve, lambda e: e.tensor_tensor(out=lbs[:, 3, :], in0=lbs[:, 3, :], in1=lbs[:, 2, :], op=ALU.mult), [lbs], [lbs])
        xs = [kb.sb(f"xs{i}", [128, 8, 512], F32) for i in range(2)]
        hT = kb.sb("hT", [128, 8, 512], BF16)
        sq = [kb.sb(f"sq{i}", [128, 512], BF16) for i in range(2)]
        tmp = kb.sb("tmp", [128, 512], F32)
        rstd = kb.sb("rstd", [128, 512], F32)
        F = lambda n: kb.sb(n, [128, 512], F32)
        tsets = [[F(n + str(i)) for n in ("sg", "kk", "lf", "G", "Gd", "Ek", "eG", "enG", "eK")] for i in range(2)]
        qsets = [F("qs0"), F("qs1")]
        gate = [kb.sb(f"gate{i}", [128, 512], BF16) for i in range(2)]
        qo = [kb.sb(f"qo{i}", [128, 512], BF16) for i in range(2)]
        ko = [kb.sb(f"ko{i}", [128, 512], BF16) for i in range(2)]
        kh = [kb.sb(f"kh{i}", [128, 512], BF16) for i in range(2)]
        kht = [kb.sb(f"kht{i}", [128, 4, 128], BF16) for i in range(2)]
        eto = [kb.sb(f"eto{i}", [128, 8], F32) for i in range(2)]
        vo = kb.sb("vo", [128, 4, 1024], BF16)
        pst = kb.ps("pst", [128, 512])
        pq = [kb.ps(f"pq{i}", [128, 512]) for i in range(3)]
        ptr = [kb.ps(f"ptr{i}", [128, 128], BF16) for i in range(2)]
        xv = x_d[:].rearrange("(c p) t -> p c t", p=128)
        Vv = V_d[:].rearrange("(n p) f -> p n f", p=128)
        khv = [kh_d[d][:].rearrange("(n p) f -> p n f", p=128) for d in range(2)]

        def ld(tt):
            kb.load(xs[tt % 2], xs[tt % 2][:], x_d, xv[:, :, tt * 512:(tt + 1) * 512])
        ld(0)
        cnt = 0
        oc = 0
        for tt in range(NT):
            if tt + 1 < NT:
                ld(tt + 1)
            x, h = xs[tt % 2], hT
            for c in range(8):
                q = sq[c % 2]
                kb.op(kb.act, lambda e: e.activation(out=q[:], in_=x[:, c, :], func=AF.Square), [x], [q])
                kb.op(kb.pe, lambda e: e.matmul(pst[:], lhsT=ones_bf[:], rhs=q[:], start=(c == 0), stop=(c == 7)), [ones_bf, q], [pst])
            rstd_from(pst, 512, D, 1e-6, rstd, tmp)
            for c in range(8):
                kb.op(kb.dve, lambda e: e.scalar_tensor_tensor(out=h[:, c, :], in0=x[:, c, :], scalar=gcol(layer, 0, c),
                                                                 in1=rstd[:], op0=ALU.mult, op1=ALU.mult), [x, rstd, normg], [h])

            def proj(col0):
                nonlocal cnt
                p = pq[cnt % 3]
                cnt += 1

                def mm(e):
                    for k in range(8):
                        i = e.matmul(p[:], lhsT=W[:, k, col0:col0 + 128], rhs=h[:, k, :], start=(k == 0), stop=(k == 7))
                    return i
                kb.op(kb.pe, mm, [W, h], [p])
                return p
            csl = slice(tt * 512, (tt + 1) * 512)
            for hd in range(8):
                rows = slice(hd * 128, (hd + 1) * 128)
                p = proj(hd * 128)
                qs = qsets[hd % 2]
                kb.op(kb.act, lambda e: e.activation(out=qs[:], in_=p[:], func=AF.Silu), [p], [qs])
                p = proj(4096 + hd * 128)
                gt = gate[oc % 2]
                kb.op(kb.act, lambda e: e.activation(out=gt[:], in_=p[:], func=AF.Silu), [p], [gt])
                kb.store(gT_d, gT_d[rows, csl], gt, gt[:])
                for d in range(2):
                    qo_, ko_, kh_, kht_, eto_ = qo[oc % 2], ko[oc % 2], kh[oc % 2], kht[oc % 2], eto[oc % 2]
                    sg, kk, lf, G, Gd, Ek, eG, enG, eK = tsets[oc % 2]
                    oc += 1
                    p = proj(1024 + d * 1024 + hd * 128)
                    kb.op(kb.act, lambda e: e.activation(out=sg[:], in_=p[:], func=AF.Sigmoid, scale=-1.0), [p], [sg])
                    kb.op(kb.dve, lambda e: e.tensor_scalar_mul(out=kk[:], in0=sg[:], scalar1=lbs[:, 3, hd:hd + 1]), [sg, lbs], [kk])
                    kb.op(kb.act, lambda e: e.activation(out=lf[:], in_=kk[:], func=AF.Ln, scale=-1.0, bias=1.0), [kk], [lf])
                    kb.op(kb.dve, lambda e: e.tensor_tensor_scan(out=G[:], data0=m01[:], data1=lf[:], initial=0.0, op0=ALU.mult, op1=ALU.add),
                          [m01, lf], [G])
                    G3 = G[:].rearrange("p (c t) -> p c t", t=64)
                    TOTb = G3[:, :, 63:64].to_broadcast([128, 8, 64])
                    r3 = lambda b: b[:].rearrange("p (c t) -> p c t", t=64)
                    if d == 0:
                        Gdb = G
                        kb.op(kb.dve, lambda e: e.tensor_tensor(out=r3(Ek), in0=TOTb, in1=G3, op=ALU.subtract), [G], [Ek])
                    else:
                        Gdb = Gd
                        kb.op(kb.dve, lambda e: e.tensor_tensor(out=Ek[:], in0=G[:], in1=lf[:], op=ALU.subtract), [G, lf], [Ek])
                        kb.op(kb.dve, lambda e: e.tensor_tensor(out=r3(Gd), in0=TOTb, in1=r3(Ek), op=ALU.subtract), [G, Ek], [Gd])
                    kb.op(kb.act, lambda e: e.activation(out=eG[:], in_=Gdb[:], func=AF.Exp), [Gdb], [eG])
                    kb.op(kb.act, lambda e: e.activation(out=enG[:], in_=Gdb[:], func=AF.Exp, scale=-1.0), [Gdb], [enG])
                    kb.op(kb.act, lambda e: e.activation(out=eK[:], in_=Ek[:], func=AF.Exp), [Ek], [eK])
                    kb.op(kb.act, lambda e: e.activation(out=eto_[:], in_=G3[:, :, 63], func=AF.Exp), [G], [eto_])
                    kb.op(kb.dve, lambda e: e.scalar_tensor_tensor(out=qo_[:], in0=qs[:], scalar=cv[:, 2:3], in1=eG[:], op0=ALU.mult, op1=ALU.mult),
                          [qs, cv, eG], [qo_])
                    kb.op(kb.dve, lambda e: e.tensor_tensor(out=ko_[:], in0=kk[:], in1=enG[:], op=ALU.mult), [kk, enG], [ko_])
                    kb.op(kb.dve, lambda e: e.tensor_tensor(out=kh_[:], in0=kk[:], in1=eK[:], op=ALU.mult), [kk, eK], [kh_])
                    for tb in range(4):
                        pt = ptr[tb % 2]
                        kb.op(kb.pe, lambda e: e.transpose(pt[:], kh_[:, tb * 128:(tb + 1) * 128], ident_bf[:]), [kh_, ident_bf], [pt])
                        kb.op(kb.act, lambda e: e.activation(out=kht_[:, tb, :], in_=pt[:], func=AF.Copy), [pt], [kht_])
                    kb.store(qt_d[d], qt_d[d][rows, csl], qo_, qo_[:])
                    kb.store(kt_d[d], kt_d[d][rows, csl], ko_, ko_[:])
                    kb.store(kh_d[d], khv[d][:, tt * 4:(tt + 1) * 4, rows], kht_, kht_[:])
                    kb.store(et_d[d], et_d[d][rows, tt * 8:(tt + 1) * 8], eto_, eto_[:])
            for tb in range(4):
                for nb in range(2):
                    p = pq[cnt % 3]
                    cnt += 1

                    def mm(e):
                        for k in range(8):
                            i = e.matmul(p[:], lhsT=h[:, k, tb * 128:(tb + 1) * 128],
                                         rhs=W[:, k, 3072 + nb * 512:3072 + (nb + 1) * 512], start=(k == 0), stop=(k == 7))
                        return i
                    kb.op(kb.pe, mm, [W, h], [p])
                    kb.op(kb.act, lambda e: e.activation(out=vo[:, tb, nb * 512:(nb + 1) * 512], in_=p[:], func=AF.Copy), [p], [vo])
            kb.store(V_d, Vv[:, tt * 4:(tt + 1) * 4, :], vo, vo[:])
        kb.end_phase()

    def phase_scan_C(layer, qt_d, kt_d, kh_d, et_d, V_d, gT_d, oT_d):
        kb.begin_phase()
        NC = T // 64
        Qt = [kb.sb(f"Qt{i}", [128, T], BF16) for i in range(2)]
        Kt = [kb.sb(f"Kt{i}", [128, T], BF16) for i in range(2)]
        Kh = [kb.sb(f"Kh{i}", [128, NKC, 128], BF16) for i in range(2)]
        Et = [kb.sb(f"Et{i}", [128, NC], F32) for i in range(2)]
        Vh = kb.sb("Vh", [128, NKC, 128], BF16)
        Gt = kb.sb("Gt", [128, T], BF16)
        ofw = kb.split(kb.sb("ofw", [128, T], F32), NT)
        obw = kb.split(kb.sb("obw", [128, T], F32), NT)
        Sfs = [kb.sb(f"Sf{i}", [128, 128], F32) for i in range(2)]
        Sbs = [kb.sb(f"Sb{i}", [128, 128], BF16) for i in range(2)]
        sc = [kb.sb(f"sc{i}", [128, 128], BF16) for i in range(2)]
        gn = kb.sb("gn", [128, 1], F32)
        sqb = kb.sb("sqb", [128, 512], BF16)
        tmp = kb.sb("tmp", [128, 512], F32)
        rstd = kb.sb("rstd", [128, 512], F32)
        ob = [kb.sb(f"ob{i}", [128, 512], BF16) for i in range(2)]
        scp = [kb.ps(f"scp{i}", [128, 128]) for i in range(2)]
        op_ = [kb.ps(f"op{i}", [128, 128]) for i in range(2)]
        dsp = [kb.ps(f"dsp{i}", [128, 128]) for i in range(2)]
        pst = kb.ps("pst", [128, 512])
        kb.load(gn, gn[:], c_gn, c_gn[:])
        Vv = V_d[:].rearrange("(n p) f -> p n f", p=128)
        khv = [kh_d[d][:].rearrange("(n p) f -> p n f", p=128) for d in range(2)]
        step = max(1, NKC // 4)
        it = 0
        oi = 0
        ci = 0
        for hd in range(8):
            rows = slice(hd * 128, (hd + 1) * 128)
            for c0 in range(0, NKC, step):
                kb.load(Vh, Vh[:, c0:c0 + step, :], V_d, Vv[:, c0:c0 + step, rows])
            kb.load(Gt, Gt[:], gT_d, gT_d[rows, :])
            for d in range(2):
                kb.load(Qt[d], Qt[d][:], qt_d[d], qt_d[d][rows, :])
                kb.load(Kt[d], Kt[d][:], kt_d[d], kt_d[d][rows, :])
                kb.load(Et[d], Et[d][:], et_d[d], et_d[d][rows, :])
                for c0 in range(0, NKC, step):
                    kb.load(Kh[d], Kh[d][:, c0:c0 + step, :], kh_d[d], khv[d][:, c0:c0 + step, rows])
            for d in range(2):
                kb.op(kb.dve, lambda e: e.memset(Sfs[d][:], 0.0), [], [Sfs[d]])
                kb.op(kb.dve, lambda e: e.memset(Sbs[d][:], 0.0), [], [Sbs[d]])
            for step_ in range(NKC):
                for d in range(2):
                    Q, K, KH, ET = Qt[d], Kt[d], Kh[d], Et[d]
                    Sf, Sb = Sfs[d], Sbs[d]
                    b = step_ if d == 0 else NKC - 1 - step_
                    bs = slice(b * 128, (b + 1) * 128)
                    scp_, sc_, o_ = scp[it % 2], sc[it % 2], op_[it % 2]
                    it += 1
                    kb.op(kb.pe, lambda e: e.matmul(scp_[:], lhsT=K[:, bs], rhs=Q[:, bs], start=True, stop=True), [K, Q], [scp_])
                    kb.op(kb.dve, lambda e: e.tensor_tensor(out=sc_[:], in0=scp_[:], in1=cmask[:, d, :], op=ALU.mult), [scp_, cmask], [sc_])
                    kb.op(kb.pe, lambda e: e.matmul(o_[:], lhsT=Vh[:, b, :], rhs=sc_[:], start=True, stop=False), [Vh, sc_], [o_])
                    chunks = (2 * b, 2 * b + 1) if d == 0 else (2 * b + 1, 2 * b)
                    for n_, c in enumerate(chunks):
                        p0 = (c % 2) * 64
                        first_of_other = (c == NC // 2) if d == 0 else (c == NC // 2 - 1)
                        if first_of_other:
                            kb.op(kb.dve, lambda e: e.tensor_scalar_mul(out=Sf[:], in0=Sf[:], scalar1=flag[:, 0:1]), [Sf, flag], [Sf])
                            kb.op(kb.act, lambda e: e.activation(out=Sb[:], in_=Sf[:], func=AF.Copy), [Sf], [Sb])
                        kb.op(kb.pe, lambda e: e.matmul(o_[:, p0:p0 + 64], lhsT=Sb[:], rhs=Q[:, c * 64:(c + 1) * 64], start=False, stop=(n_ == 1)),
                              [Sb, Q], [o_])
                        ds_ = dsp[ci % 2]
                        ci += 1
                        kb.op(kb.pe, lambda e: e.matmul(ds_[:], lhsT=KH[p0:p0 + 64, b, :], rhs=Vh[p0:p0 + 64, b, :], start=True, stop=True,
                                                        tile_position=(p0, 0)), [KH, Vh], [ds_])
                        kb.op(kb.dve, lambda e: e.scalar_tensor_tensor(out=Sf[:], in0=Sf[:], scalar=ET[:, c:c + 1], in1=ds_[:],
                                                                         op0=ALU.mult, op1=ALU.add), [Sf, ET, ds_], [Sf])
                        kb.op(kb.act, lambda e: e.activation(out=Sb[:], in_=Sf[:], func=AF.Copy), [Sf], [Sb])
                    dst_ = ofw if d == 0 else obw
                    kb.op(kb.act, lambda e: e.activation(out=dst_[:, bs], in_=o_[:], func=AF.Copy), [o_], [dst_.c[b // 4]])
            for tt in range(NT):
                csl = slice(tt * 512, (tt + 1) * 512)
                kb.op(kb.dve, lambda e: e.tensor_tensor(out=ofw[:, csl], in0=ofw[:, csl], in1=obw[:, csl], op=ALU.add),
                      [ofw.c[tt], obw.c[tt]], [ofw.c[tt]])
            for tt in range(NT):
                csl = slice(tt * 512, (tt + 1) * 512)
                kb.op(kb.act, lambda e: e.activation(out=sqb[:], in_=ofw[:, csl], func=AF.Square), [ofw], [sqb])
                kb.op(kb.pe, lambda e: e.matmul(pst[:], lhsT=ones_bf[:], rhs=sqb[:], start=True, stop=True), [ones_bf, sqb], [pst])
                rstd_from(pst, 512, 128, 1e-6, rstd, tmp)
                kb.op(kb.dve, lambda e: e.scalar_tensor_tensor(out=tmp[:], in0=ofw[:, csl], scalar=gn[:, 0:1], in1=rstd[:], op0=ALU.mult, op1=ALU.mult),
                      [ofw, gn, rstd], [tmp])
                o = ob[oi % 2]
                oi += 1
                kb.op(kb.dve, lambda e: e.tensor_tensor(out=o[:], in0=tmp[:], in1=Gt[:, csl], op=ALU.mult), [tmp, Gt], [o])
                kb.store(oT_d, oT_d[rows, csl], o, o[:])
        kb.end_phase()

    def phase_wo(layer, wo_buf, wo_ap, oT_d, x_d, x1_d):
        kb.begin_phase()
        Wo = kb.sb("Wo", [128, 8, 1024], BF16)
        stg = [kb.sb(f"stg{i}", [128, 8, 512], F32) for i in range(2)]
        wsrc = wo_ap.rearrange("(c p) n -> p c n", p=128)
        load_cast_w(Wo, lambda a, b: Wo[:, :, a:b], wo_buf, lambda a, b: wsrc[:, :, a:b], 1024, stg)
        xs = [kb.sb(f"xs{i}", [128, 8, 512], F32) for i in range(2)]
        os_ = [kb.sb(f"os{i}", [128, 8, 512], BF16) for i in range(2)]
        y = kb.split(kb.sb("y", [128, 8, 512], F32), 8)
        sq = [kb.sb(f"sq{i}", [128, 512], BF16) for i in range(2)]
        tmp = kb.sb("tmp", [128, 512], F32)
        rstd = kb.sb("rstd", [128, 512], F32)
        xo = [kb.split(kb.sb(f"xo{i}", [128, 8, 512], F32), 8) for i in range(2)]
        pst = kb.ps("pst", [128, 512])
        pq = [kb.ps(f"pq{i}", [128, 512]) for i in range(3)]
        xv = x_d[:].rearrange("(c p) t -> p c t", p=128)
        x1v = x1_d[:].rearrange("(c p) t -> p c t", p=128)
        ov = oT_d[:].rearrange("(c p) t -> p c t", p=128)

        def ld(tt):
            s = tt % 2
            kb.load(xs[s], xs[s][:], x_d, xv[:, :, tt * 512:(tt + 1) * 512])
            kb.load(os_[s], os_[s][:], oT_d, ov[:, :, tt * 512:(tt + 1) * 512])
        ld(0)
        cnt = 0
        for tt in range(NT):
            s = tt % 2
            if tt + 1 < NT:
                ld(tt + 1)
            x, o = xs[s], os_[s]
            for m in range(8):
                p = pq[cnt % 3]
                q = sq[cnt % 2]
                cnt += 1

                def mm(e):
                    for k in range(8):
                        i = e.matmul(p[:], lhsT=Wo[:, k, m * 128:(m + 1) * 128], rhs=o[:, k, :], start=(k == 0), stop=(k == 7))
                    return i
                kb.op(kb.pe, mm, [Wo, o], [p])
                kb.op(kb.act, lambda e: e.activation(out=q[:], in_=p[:], func=AF.Square), [p], [q])
                kb.op(kb.dve, lambda e: e.tensor_copy(out=y[:, m, :], in_=p[:]), [p], [y.c[m]])
                kb.op(kb.pe, lambda e: e.matmul(pst[:], lhsT=ones_bf[:], rhs=q[:], start=(m == 0), stop=(m == 7)), [ones_bf, q], [pst])
            rstd_from(pst, 512, D, 1e-6, rstd, tmp)
            for m in range(8):
                kb.op(kb.dve, lambda e: e.scalar_tensor_tensor(out=y[:, m, :], in0=y[:, m, :], scalar=gcol(layer, 1, m), in1=rstd[:],
                                                                 op0=ALU.mult, op1=ALU.mult), [y.c[m], rstd, normg], [y.c[m]])
                kb.op(kb.ew2, lambda e: e.tensor_tensor(out=xo[s][:, m, :], in0=y[:, m, :], in1=x[:, m, :], op=ALU.add), [y.c[m], x], [xo[s].c[m]])
            kb.store(x1_d, x1v[:, :, tt * 512:(tt + 1) * 512], xo[s], xo[s][:])
        kb.end_phase()

    def phase_ffn(layer, x1_d, x2_d):
        kb.begin_phase()
        NV = 256
        NW = NV + 2
        Win = kb.sb("Win", [128, 8, 2 * DFF], BF16)
        Wout = kb.sb("Wout", [128, 22, 1024], BF16)
        stg = [kb.sb(f"stg{i}", [128, 8, 128], F32) for i in range(1)]
        wsrc = f_win[layer].rearrange("(c p) n -> p c n", p=128)
        load_cast_w(Win, lambda a, b: Win[:, :, a:b], f_win, lambda a, b: wsrc[:, :, a:b], 2 * DFF, stg, blk=128)
        wsrc2 = f_wout[layer].rearrange("(c p) n -> p c n", p=128)
        for c0 in range(0, 22, 8):
            c1 = min(22, c0 + 8)
            for n0_ in range(0, 1024, 128):
                sg = stg[cast_rr[0] % len(stg)]
                cast_rr[0] += 1
                kb.load(sg, sg[:, :c1 - c0, :], f_wout, wsrc2[:, c0:c1, n0_:n0_ + 128])
                kb.op(kb.dve, lambda e: e.tensor_copy(out=Wout[:, c0:c1, n0_:n0_ + 128], in_=sg[:, :c1 - c0, :]), [sg], [Wout])
        cw = kb.sb("cw", [128, 44, 4], F32)
        kb.load(cw, cw[:], f_cw, f_cw[:, layer * 176:(layer + 1) * 176].rearrange("p (c f) -> p c f", f=4))
        xw = [kb.sb(f"xw{i}", [128, 8, NW], F32) for i in range(2)]
        h = kb.split(kb.sb("h", [128, 8, NW], BF16), 8)
        sq = [kb.sb(f"sq{i}", [128, NW], BF16) for i in range(2)]
        tmp = kb.sb("tmp", [128, NW], F32)
        rstd = kb.sb("rstd", [128, NW], F32)
        ta = [kb.sb(f"ta{i}", [128, NV], F32) for i in range(2)]
        tb_ = [kb.sb(f"tb{i}", [128, NV], F32) for i in range(2)]
        ga = [kb.sb(f"ga{i}", [128, NV], F32) for i in range(2)]
        gg = kb.split(kb.sb("gg", [128, 22, NV], BF16), 22)
        y = kb.split(kb.sb("y", [128, 8, NV], F32), 8)
        xo = [kb.split(kb.sb(f"xo{i}", [128, 8, NV], F32), 8) for i in range(2)]
        pst = kb.ps("pst", [128, 512])
        pa = [kb.ps(f"pa{i}", [128, 512]) for i in range(2)]
        pb = [kb.ps(f"pb{i}", [128, 512]) for i in range(2)]
        po = [kb.ps(f"po{i}", [128, 512]) for i in range(2)]
        xv = x1_d[:].rearrange("(c p) t -> p c t", p=128)
        x2v = x2_d[:].rearrange("(c p) t -> p c t", p=128)
        wins = list(range(0, T, NV))

        def ld(wi):
            s0 = wins[wi]
            b = xw[wi % 2]
            lo, hi = s0 - 1, s0 + NV + 1
            clo, chi = max(lo, 0), min(hi, T)
            if clo > lo:
                kb.op(kb.pool, lambda e: e.memset(b[:, :, 0:1], 0.0), [], [b])
            if chi < hi:
                kb.op(kb.pool, lambda e: e.memset(b[:, :, NW - 1:NW], 0.0), [], [b])
            kb.load(b, b[:, :, clo - lo:NW - (hi - chi)], x1_d, xv[:, :, clo:chi])
        ld(0)
        cnt = 0
        for wi, s0 in enumerate(wins):
            if wi + 1 < len(wins):
                ld(wi + 1)
            x = xw[wi % 2]
            for c in range(8):
                q = sq[c % 2]
                kb.op(kb.act, lambda e: e.activation(out=q[:], in_=x[:, c, :], func=AF.Square), [x], [q])
                kb.op(kb.pe, lambda e: e.matmul(pst[:, :NW], lhsT=ones_bf[:], rhs=q[:], start=(c == 0), stop=(c == 7)), [ones_bf, q], [pst])
            rstd_from(pst, NW, D, 1e-6, rstd, tmp)
            for c in range(8):
                kb.op(kb.dve, lambda e: e.scalar_tensor_tensor(out=h[:, c, :], in0=x[:, c, :], scalar=gcol(layer, 2, c), in1=rstd[:],
                                                                 op0=ALU.mult, op1=ALU.mult), [x, rstd, normg], [h.c[c]])
            if s0 == HALF:
                kb.op(kb.dve, lambda e: e.tensor_scalar_mul(out=h[:, :, 0:1], in0=h[:, :, 0:1], scalar1=flag[:, 0:1]), [h, flag], [h])
            if s0 + NV == HALF:
                kb.op(kb.dve, lambda e: e.tensor_scalar_mul(out=h[:, :, NW - 1:NW], in0=h[:, :, NW - 1:NW], scalar1=flag[:, 0:1]), [h, flag], [h])
            for jj in range(22):
                A, B = pa[jj % 2], pb[jj % 2]
                a_, b_, g_ = ta[jj % 2], tb_[jj % 2], ga[jj % 2]

                def mma(e):
                    for k in range(8):
                        i = e.matmul(A[:, :NW], lhsT=Win[:, k, jj * 128:(jj + 1) * 128], rhs=h[:, k, :], start=(k == 0), stop=(k == 7))
                    return i

                def mmb(e):
                    for k in range(8):
                        i = e.matmul(B[:, :NW], lhsT=Win[:, k, DFF + jj * 128:DFF + (jj + 1) * 128], rhs=h[:, k, :], start=(k == 0), stop=(k == 7))
                    return i
                kb.op(kb.pe, mma, [Win, h], [A])
                kb.op(kb.pe, mmb, [Win, h], [B])
                for (Pp, tt_, ci, eng2) in ((A, a_, jj, kb.dve), (B, b_, 22 + jj, kb.dve)):
                    kb.op(kb.act, lambda e: e.activation(out=tt_[:], in_=Pp[:, 1:NV + 1], func=AF.Identity,
                                                         scale=cw[:, ci, 1:2], bias=cw[:, ci, 3:4]), [Pp, cw], [tt_])
                    kb.op(eng2, lambda e: e.scalar_tensor_tensor(out=tt_[:], in0=Pp[:, 0:NV], scalar=cw[:, ci, 0:1], in1=tt_[:],
                                                                  op0=ALU.mult, op1=ALU.add), [Pp, cw, tt_], [tt_])
                    kb.op(eng2, lambda e: e.scalar_tensor_tensor(out=tt_[:], in0=Pp[:, 2:NV + 2], scalar=cw[:, ci, 2:3], in1=tt_[:],
                                                                  op0=ALU.mult, op1=ALU.add), [Pp, cw, tt_], [tt_])
                kb.op(kb.act, lambda e: e.activation(out=g_[:], in_=a_[:], func=AF.Gelu_apprx_tanh), [a_], [g_])
                kb.op(kb.ew2, lambda e: e.tensor_tensor(out=gg[:, jj, :], in0=g_[:], in1=b_[:], op=ALU.mult), [g_, b_], [gg.c[jj]])
            for m in range(8):
                p = po[m % 2]
                q = sq[m % 2]

                def mm(e):
                    for k in range(22):
                        i = e.matmul(p[:, :NV], lhsT=Wout[:, k, m * 128:(m + 1) * 128], rhs=gg[:, k, :], start=(k == 0), stop=(k == 21))
                    return i
                kb.op(kb.pe, mm, [Wout, gg], [p])
                kb.op(kb.act, lambda e: e.activation(out=q[:, :NV], in_=p[:, :NV], func=AF.Square), [p], [q])
                kb.op(kb.dve, lambda e: e.tensor_copy(out=y[:, m, :], in_=p[:, :NV]), [p], [y.c[m]])
                kb.op(kb.pe, lambda e: e.matmul(pst[:, :NV], lhsT=ones_bf[:], rhs=q[:, :NV], start=(m == 0), stop=(m == 7)), [ones_bf, q], [pst])
            rstd_from(pst, NV, D, 1e-6, rstd, tmp)
            xo_ = xo[wi % 2]
            for m in range(8):
                kb.op(kb.dve, lambda e: e.scalar_tensor_tensor(out=y[:, m, :], in0=y[:, m, :], scalar=gcol(layer, 3, m), in1=rstd[:, :NV],
                                                                 op0=ALU.mult, op1=ALU.mult), [y.c[m], rstd, normg], [y.c[m]])
                kb.op(kb.ew2, lambda e: e.tensor_tensor(out=xo_[:, m, :], in0=y[:, m, :], in1=x[:, m, 1:NV + 1], op=ALU.add), [y.c[m], x], [xo_.c[m]])
            kb.store(x2_d, x2v[:, :, s0:s0 + NV], xo_, xo_[:])
        kb.end_phase()

    QTb = [kb.dram(f"QTb{g}", [D, T], BF16) for g in range(3)]
    KTb = [kb.dram(f"KTb{g}", [D, T], BF16) for g in range(3)]
    Vb = [kb.dram(f"Vb{g}", [T, D], BF16) for g in range(3)]
    cq_d = [kb.dram(f"cq{d}", [D, T], BF16) for d in range(2)]
    ck_d = [kb.dram(f"ck{d}", [D, T], BF16) for d in range(2)]
    ckh_d = [kb.dram(f"ckh{d}", [T, D], BF16) for d in range(2)]
    cet_d = [kb.dram(f"cet{d}", [D, T // 64], F32) for d in range(2)]
    QT_d = kb.dram("QT_d", [D, T], BF16)
    KT_d = kb.dram("KT_d", [D, T], BF16)
    V_d = kb.dram("V_d", [T, D], BF16)
    oT_d = kb.dram("oT_d", [D, T], BF16)
    x1_d = kb.dram("x1_d", [D, T], F32)
    xa_d = kb.dram("xa_d", [D, T], F32)
    xb_d = kb.dram("xb_d", [D, T], F32)
    cur = xT_in
    for li, layer in enumerate(layers):
        last = li == len(layers) - 1
        nxt = yT_out if last else (xa_d if li % 2 == 0 else xb_d)
        kind = layer % 3
        j = layer // 3
        import os
        stop = int(os.environ.get("KSTOP", "99"))
        if kind == 0:
            if stop >= 1:
                phase_proj_A(layer, a_wqkv, a_wqkv[j], cur, QT_d, KT_d, V_d)
            if stop >= 2:
                phase_attn_A(layer, j, QT_d, KT_d, V_d, oT_d)
            if stop >= 3:
                phase_wo(layer, a_wo, a_wo[j], oT_d, cur, x1_d)
        elif kind == 1:
            for g in range(3):
                phase_proj_A(layer, b_wqkv, b_wqkv[0][:, g * 3072:(g + 1) * 3072], cur, QTb[g], KTb[g], Vb[g])
            phase_attn_B(layer, QTb, KTb, Vb, oT_d)
            phase_wo(layer, b_wo, b_wo[0], oT_d, cur, x1_d)
        else:
            phase_proj_C(layer, cur, cq_d, ck_d, ckh_d, cet_d, V_d, KT_d)
            phase_scan_C(layer, cq_d, ck_d, ckh_d, cet_d, V_d, KT_d, oT_d)
            phase_wo(layer, c_wo, c_wo[0], oT_d, cur, x1_d)
        if stop >= 4:
            phase_ffn(layer, x1_d, nxt)
        cur = nxt
    kb.begin_phase()
    kb.end_phase()
    return nc


def rope_tables(pos):
    half = 8
    inv = (500000.0 ** (-np.arange(half, dtype=np.float32) / half)).astype(np.float32)
    ang = pos.astype(np.float32)[:, None] * inv[None, :]
    cos, sin = np.cos(ang).astype(np.float32), np.sin(ang).astype(np.float32)
    T = pos.shape[0]
    tab = np.zeros((128, 3, T), np.float32)
    tab[:, 0, :] = 1.0
    for hd in range(2):
        b = hd * 64
        tab[b:b + 8, 0, :] = cos.T
        tab[b + 8:b + 16, 0, :] = cos.T
        tab[b:b + 8, 1, :] = -sin.T
        tab[b + 8:b + 16, 1, :] = sin.T
    tab[:, 2, :] = tab[:, 0, :] * np.float32(0.125)
    return tab


def const_mats():
    c = np.zeros((128, 7, 128), np.float32)
    for m in range(64):
        c[64 + m, 6, m] = 1.0
    c[:, 0, :] = 1.0
    c[:, 1, :] = np.eye(128, dtype=np.float32)
    for hd in range(2):
        for i in range(8):
            a, b = hd * 64 + i, hd * 64 + i + 8
            c[a, 2, b] = 1.0
            c[b, 2, a] = 1.0
    s = np.arange(128)
    c[:, 3, :] = (s[:, None] <= s[None, :]).astype(np.float32)
    same = (s[:, None] // 64) == (s[None, :] // 64)
    c[:, 4, :] = ((s[:, None] <= s[None, :]) & same).astype(np.float32)
    c[:, 5, :] = ((s[:, None] >= s[None, :]) & same).astype(np.float32)
    return c


def make_in_maps(inp, T, seqs_per_core):
    maps = []
    NT, NKC = T // 512, T // 128
    normg = np.ascontiguousarray(inp["norm_g"].reshape(4, 4, 8, 128).transpose(3, 0, 1, 2).reshape(128, 128))
    a_lam = np.ascontiguousarray(np.broadcast_to(inp["a_lambda"].reshape(1, -1), (128, 512)))
    a_sub = np.ascontiguousarray(inp["a_subln_g"].T)
    cwt = np.concatenate([inp["f_conv_w"], inp["f_conv_b"][:, None, :]], axis=1)
    f_cw = np.ascontiguousarray(cwt.reshape(4, 4, 44, 128).transpose(3, 0, 2, 1).reshape(128, 4 * 44 * 4))
    cm = const_mats()
    bm = bmask_table()
    c_lbl = np.ascontiguousarray(inp["c_lb_logits"].reshape(4, 8, 128).transpose(2, 0, 1))
    c_gn = np.ascontiguousarray(inp["c_gnorm_g"].reshape(1, 128).T)
    m01h = np.ones((128, 512), np.float32)
    m01h[:, ::64] = 0.0
    for seqs in seqs_per_core:
        xT = np.ascontiguousarray(np.concatenate(seqs, axis=0).T)
        pos = np.concatenate([np.arange(s.shape[0]) for s in seqs])
        sid = np.concatenate([np.full(s.shape[0], i) for i, s in enumerate(seqs)])
        ksid = sid[::128][:, None]
        qsid = sid[::512][None, :]
        ab = np.where(ksid == qsid, 0.0, NEG).astype(np.float32).reshape(1, NKC * NT)
        flag = np.zeros((128, 2), np.float32)
        flag[:, 0] = 1.0 if len(seqs) == 1 else 0.0
        m = {
            "xT": xT, "normg": normg, "rope": rope_tables(pos), "cmat": cm, "flag": flag,
            "abias": np.ascontiguousarray(np.broadcast_to(ab, (128, NKC * NT))),
            "a_w_qkv": inp["a_w_qkv"], "a_lam": a_lam, "a_sub": a_sub, "a_w_o": inp["a_w_o"],
            "c_w_in": inp["c_w_in"], "c_w_o": inp["c_w_o"], "c_lbl": c_lbl, "c_gn": c_gn, "m01": m01h,
            "b_w_qkv": inp["b_w_qkv"], "b_w_o": inp["b_w_o"], "bmask": bm,
            "f_w_in": inp["f_w_in"], "f_cw": f_cw, "f_w_out": inp["f_w_out"],
        }
        maps.append(m)
    return maps


_NC_CACHE = {}


def kernel(**inputs):
    inp = {k: np.asarray(v) for k, v in inputs.items()}
    T = 8192
    xp, xs = inp["x_prompt"], inp["x_sample"]
    seqs = [[xp[b]] for b in range(4)] + [[xs[2 * c], xs[2 * c + 1]] for c in range(4)]
    maps = make_in_maps(inp, T, seqs)
    if "nc" not in _NC_CACHE:
        _NC_CACHE["nc"] = build(T)
    res = run_bass_kernel_spmd(_NC_CACHE["nc"], maps, core_ids=list(range(8)))
    outs = [np.asarray(r["yT"]).T for r in res.results]
    y_prompt = np.stack(outs[:4], axis=0).astype(np.float32)
    y_sample = np.stack([o.reshape(2, 4096, D) for o in outs[4:]], axis=0).reshape(8, 4096, D).astype(np.float32)
    return (y_prompt, y_sample)
```

```python
import math
from contextlib import ExitStack
import numpy as np
import concourse.bass as bass
import concourse.mybir as mybir
from concourse.bass_utils import run_bass_kernel_spmd

F32 = mybir.dt.float32
BF16 = mybir.dt.bfloat16
AF = mybir.ActivationFunctionType
ALU = mybir.AluOpType
AX = mybir.AxisListType

D = 1024
DFF = 2816
NEG = -30000.0
B_DIL = (1, 4, 16)


def _bmask_list():
    out = []
    for g, d in enumerate(B_DIL):
        for delta in range(-4096, 4097, 128):
            q = np.arange(512)[None, :]
            k = np.arange(128)[:, None]
            diff = delta + q - k
            if np.any((diff % d == 0) & (np.abs(diff) <= 64 * d)):
                out.append((g, delta))
    return out


BMASKS = _bmask_list()


def bmask_table():
    t = np.zeros((128, len(BMASKS), 512), np.float32)
    q = np.arange(512)[None, :]
    k = np.arange(128)[:, None]
    for i, (g, delta) in enumerate(BMASKS):
        d = B_DIL[g]
        diff = delta + q - k
        t[:, i, :] = np.where((diff % d == 0) & (np.abs(diff) <= 64 * d), 1.0, 0.0)
    return t


class Stream:
    def __init__(self, name, eng, sem):
        self.name, self.eng, self.sem = name, eng, sem
        self.count = 0
        self.known = {}


class Buf:
    def __init__(self, t, name):
        self.t = t
        self.name = name
        self.w = {}
        self.r = {}
        self.isdram = False
        self.excl = False
        self.c = None
        self.lsem = None
        self.lcount = 0
        self.ssem = None
        self.scount = 0

    def __getitem__(self, idx):
        return self.t[idx]


class KB:
    def __init__(self, nc):
        self.nc = nc
        self.top = ExitStack()
        self.sems = []
        self.free_dsems = []
        self.all_dsems = {}
        mk = lambda n, e: Stream(n, e, self.top.enter_context(nc.semaphore("s_" + n)))
        self.pe = mk("pe", nc.tensor)
        self.act = mk("act", nc.scalar)
        self.dve = mk("dve", nc.vector)
        self.pool = mk("pool", nc.gpsimd)
        import os
        self.ew2 = self.dve if os.environ.get("KPOOL", "dve") == "dve" else self.pool
        self.sp = mk("sp", nc.sync)
        self.streams = [self.pe, self.act, self.dve, self.pool, self.sp]
        self.phase = None
        self.phase_bufs = []
        self.uid = 0

    def begin_phase(self):
        self.phase = ExitStack()
        self.phase_bufs = []

    def end_phase(self):
        self.barrier()
        for b in self.phase_bufs:
            for s, c in ((b.lsem, b.lcount), (b.ssem, b.scount)):
                if s is not None:
                    self.free_dsems.append((s, c))
        self.phase.close()
        self.phase = None

    def sb(self, name, shape, dt, glob=False):
        self.uid += 1
        es = self.top if glob else self.phase
        t = es.enter_context(self.nc.sbuf_tensor(f"{name}_{self.uid}", list(shape), dt))
        b = Buf(t, name)
        if not glob:
            self.phase_bufs.append(b)
        return b

    def ps(self, name, shape, dt=F32):
        self.uid += 1
        t = self.phase.enter_context(self.nc.psum_tensor(f"{name}_{self.uid}", list(shape), dt))
        b = Buf(t, name)
        b.excl = True
        return b

    def dram(self, name, shape, dt):
        t = self.nc.dram_tensor(name, list(shape), dt, kind="Internal")
        b = Buf(t.ap(), name)
        b.isdram = True
        return b

    def _dsem(self):
        if self.free_dsems:
            return self.free_dsems.pop()
        s = self.top.enter_context(self.nc.semaphore(f"d{len(self.all_dsems)}"))
        self.all_dsems[id(s)] = s
        return (s, 0)

    @staticmethod
    def _expand(bufs):
        out = []
        for b in bufs:
            if b.c is not None:
                out.extend(b.c)
            else:
                out.append(b)
        return out

    def split(self, buf, n):
        buf.c = [Buf(buf.t, f"{buf.name}.{i}") for i in range(n)]
        for ch in buf.c:
            ch.excl = buf.excl
            ch.isdram = buf.isdram
        return buf

    def _deps(self, st, reads, writes, skip=None):
        reads, writes = self._expand(reads), self._expand(writes)
        need = {}

        def add(ev):
            s, v = ev
            if id(s) not in need or need[id(s)][1] < v:
                need[id(s)] = ev
        for b in reads:
            for ev in b.w.values():
                add(ev)
            if b.excl:
                for ev in b.r.values():
                    if ev[0] is not st.sem:
                        add(ev)
        for b in writes:
            if not b.isdram:
                for ev in b.w.values():
                    add(ev)
            for ev in b.r.values():
                add(ev)
        for s, v in need.values():
            if skip is not None and s is skip:
                continue
            if s is st.sem and st is self.pe:
                continue
            if st.known.get(id(s), 0) < v:
                st.eng.wait_ge(s, v)
                st.known[id(s)] = v

    def _mark(self, ev, reads, writes):
        reads, writes = self._expand(reads), self._expand(writes)
        for b in writes:
            if b.isdram:
                b.w[id(ev[0])] = ev
            else:
                b.w = {id(ev[0]): ev}
                b.r = {}
        for b in reads:
            if b not in writes:
                b.r[id(ev[0])] = ev

    def op(self, st, fn, reads=(), writes=()):
        reads, writes = list(reads), list(writes)
        self._deps(st, reads, writes)
        ins = fn(st.eng)
        st.count += 1
        ins.then_inc(st.sem, 1)
        self._mark((st.sem, st.count), reads, writes)
        return ins

    def load(self, sbuf, out_ap, dbuf, in_ap, st=None):
        st = st or self.sp
        if sbuf.lsem is None:
            sbuf.lsem, sbuf.lcount = self._dsem()
        self._deps(st, [dbuf], [sbuf], skip=sbuf.lsem)
        sbuf.lcount += 16
        st.eng.dma_start(out=out_ap, in_=in_ap).then_inc(sbuf.lsem, 16)
        self._mark((sbuf.lsem, sbuf.lcount), [dbuf], [sbuf])

    def store(self, dbuf, out_ap, sbuf, in_ap, st=None):
        st = st or self.pool
        if sbuf.ssem is None:
            sbuf.ssem, sbuf.scount = self._dsem()
        self._deps(st, [sbuf], [dbuf], skip=sbuf.ssem)
        sbuf.scount += 16
        st.eng.dma_start(out=out_ap, in_=in_ap).then_inc(sbuf.ssem, 16)
        self._mark((sbuf.ssem, sbuf.scount), [sbuf], [dbuf])

    def barrier(self):
        cur = [(st.sem, st.count) for st in self.streams]
        for b in self.phase_bufs:
            if b.lsem is not None:
                cur.append((b.lsem, b.lcount))
            if b.ssem is not None:
                cur.append((b.ssem, b.scount))
        for st in self.streams:
            for s, v in cur:
                if s is st.sem or v == 0:
                    continue
                if st.known.get(id(s), 0) < v:
                    st.eng.wait_ge(s, v)
                    st.known[id(s)] = v


def build(T=8192, layers=(0, 1, 2, 3)):
    nc = bass.Bass("TRN2", target_bir_lowering=False)
    kb = KB(nc)
    NT = T // 512
    NKC = T // 128
    HALF = T // 2

    def din(name, shape, dt=F32):
        b = Buf(nc.dram_tensor(name, list(shape), dt, kind="ExternalInput").ap(), name)
        b.isdram = True
        return b

    xT_in = din("xT", [D, T])
    yT_out = Buf(nc.dram_tensor("yT", [D, T], F32, kind="ExternalOutput").ap(), "yT")
    yT_out.isdram = True
    normg_in = din("normg", [128, 4 * 4 * 8])
    rope_in = din("rope", [128, 3, T])
    cmat_in = din("cmat", [128, 7, 128])
    flag_in = din("flag", [128, 2])
    abias_in = din("abias", [128, NKC * NT])
    a_wqkv = din("a_w_qkv", [2, D, 3072])
    a_lam = din("a_lam", [128, 2 * 256])
    a_sub = din("a_sub", [128, 2])
    a_wo = din("a_w_o", [2, D, D])
    b_wqkv = din("b_w_qkv", [1, D, 9216])
    b_wo = din("b_w_o", [1, D, D])
    bmask_in = din("bmask", [128, len(BMASKS), 512])
    c_win = din("c_w_in", [1, D, 5120])
    c_wo = din("c_w_o", [1, D, D])
    c_lbl = din("c_lbl", [128, 4, 8])
    c_gn = din("c_gn", [128, 1])
    m01_in = din("m01", [128, 512])
    f_win = din("f_w_in", [4, D, 2 * DFF])
    f_cw = din("f_cw", [128, 4 * 44 * 4])
    f_wout = din("f_w_out", [4, DFF, D])

    kb.begin_phase()
    ones_bf = kb.sb("ones", [128, 128], BF16, glob=True)
    ident_bf = kb.sb("ident", [128, 128], BF16, glob=True)
    rperm_bf = kb.sb("rperm", [128, 128], BF16, glob=True)
    normg = kb.sb("normg", [128, 128], F32, glob=True)
    flag = kb.sb("flag", [128, 2], F32, glob=True)
    cv = kb.sb("cv", [128, 4], F32, glob=True)
    ones_f = kb.sb("ones_f", [128, 128], F32, glob=True)
    shift_f = kb.sb("shift_f", [128, 64], F32, glob=True)
    cmask = kb.sb("cmask", [128, 2, 128], F32, glob=True)
    m01 = kb.sb("m01", [128, 512], F32, glob=True)
    cst = kb.sb("cst", [128, 7, 128], F32)
    kb.load(cst, cst[:], cmat_in, cmat_in[:])
    kb.load(normg, normg[:], normg_in, normg_in[:])
    kb.load(flag, flag[:], flag_in, flag_in[:])
    kb.op(kb.dve, lambda e: e.memset(cv[:, 0:1], 0.125), [], [cv])
    kb.op(kb.dve, lambda e: e.memset(cv[:, 1:2], 1.0), [cv], [cv])
    kb.op(kb.dve, lambda e: e.memset(cv[:, 2:3], float(128 ** -0.5)), [cv], [cv])
    kb.load(m01, m01[:], m01_in, m01_in[:])
    kb.op(kb.dve, lambda e: e.tensor_copy(out=cmask[:], in_=cst[:, 4:6, :]), [cst], [cmask])
    kb.op(kb.dve, lambda e: e.tensor_copy(out=ones_bf[:], in_=cst[:, 0, :]), [cst], [ones_bf])
    kb.op(kb.dve, lambda e: e.tensor_copy(out=ones_f[:], in_=cst[:, 0, :]), [cst], [ones_f])
    kb.op(kb.dve, lambda e: e.tensor_copy(out=shift_f[:], in_=cst[:, 6, 0:64]), [cst], [shift_f])
    kb.op(kb.dve, lambda e: e.tensor_copy(out=ident_bf[:], in_=cst[:, 1, :]), [cst], [ident_bf])
    kb.op(kb.dve, lambda e: e.tensor_copy(out=rperm_bf[:], in_=cst[:, 2, :]), [cst], [rperm_bf])
    kb.end_phase()

    def gcol(layer, n, c):
        i = (layer * 4 + n) * 8 + c
        return normg[:, i:i + 1]

    cast_rr = [0]

    def load_cast_w(wdst, dst_ap_fn, src_buf, src_ap_fn, ncols, stg=None, kc=8, blk=512):
        for c0 in range(0, ncols, 512):
            c1 = min(ncols, c0 + 512)
            kb.load(wdst, dst_ap_fn(c0, c1), src_buf, src_ap_fn(c0, c1), st=kb.pool)

    def rstd_from(ps_stat, n, nfeat, eps, rstd, tmp):
        kb.op(kb.act, lambda e: e.activation(out=tmp[:, :n], in_=ps_stat[:, :n], func=AF.Ln,
                                             bias=float(eps), scale=1.0 / nfeat), [ps_stat], [tmp])
        kb.op(kb.act, lambda e: e.activation(out=rstd[:, :n], in_=tmp[:, :n], func=AF.Exp, scale=-0.5), [tmp], [rstd])

    def phase_proj_A(layer, wbuf, wap, x_d, QT_d, KT_d, V_d):
        kb.begin_phase()
        W = kb.sb("Wqkv", [128, 8, 3072], BF16)
        wsrc = wap.rearrange("(c p) n -> p c n", p=128)
        load_cast_w(W, lambda a, b: W[:, :, a:b], wbuf, lambda a, b: wsrc[:, :, a:b], 3072)
        xs = [kb.sb(f"xs{i}", [128, 8, 512], F32) for i in range(2)]
        rp = [kb.sb(f"rp{i}", [128, 3, 512], F32) for i in range(2)]
        hT = [kb.split(kb.sb(f"hT{i}", [128, 8, 512], BF16), 8) for i in range(2)]
        sq = [kb.sb(f"sq{i}", [128, 512], BF16) for i in range(2)]
        tmp = kb.sb("tmp", [128, 512], F32)
        rstd = kb.sb("rstd", [128, 512], F32)
        qsb = [kb.sb(f"qsb{i}", [128, 512], BF16) for i in range(2)]
        t1 = [kb.sb(f"t1{i}", [128, 512], F32) for i in range(2)]
        t2 = [kb.sb(f"t2{i}", [128, 512], F32) for i in range(2)]
        qo = [kb.split(kb.sb(f"qo{i}", [128, 8, 512], BF16), 8) for i in range(2)]
        ko = [kb.split(kb.sb(f"ko{i}", [128, 8, 512], BF16), 8) for i in range(2)]
        vo = [kb.split(kb.sb(f"vo{i}", [128, 4, 1024], BF16), 8) for i in range(2)]
        pst = kb.ps("pst", [128, 512])
        pq = [kb.ps(f"pq{i}", [128, 512]) for i in range(3)]
        pr = [kb.ps(f"pr{i}", [128, 512]) for i in range(2)]
        xv = x_d[:].rearrange("(c p) t -> p c t", p=128)
        Vv = V_d[:].rearrange("(n p) f -> p n f", p=128)
        QTv = QT_d[:].rearrange("(c p) t -> p c t", p=128)
        KTv = KT_d[:].rearrange("(c p) t -> p c t", p=128)

        def ld(tt):
            s = tt % 2
            kb.load(xs[s], xs[s][:], x_d, xv[:, :, tt * 512:(tt + 1) * 512])
            kb.load(rp[s], rp[s][:], rope_in, rope_in[:, :, tt * 512:(tt + 1) * 512])

        import os
        KSUB = int(os.environ.get("KSUB", "99"))
        if KSUB < 1:
            kb.end_phase()
            return
        ld(0)
        cnt = 0
        for tt in range(NT):
            s = tt % 2
            if tt + 1 < NT:
                ld(tt + 1)
            x, h = xs[s], hT[s]
            if KSUB < 2:
                continue
            for c in range(8):
                q = sq[c % 2]
                kb.op(kb.act, lambda e: e.activation(out=q[:], in_=x[:, c, :], func=AF.Square), [x], [q])
                kb.op(kb.pe, lambda e: e.matmul(pst[:], lhsT=ones_bf[:], rhs=q[:], start=(c == 0), stop=(c == 7)),
                      [ones_bf, q], [pst])
            if KSUB < 3:
                continue
            rstd_from(pst, 512, D, 1e-6, rstd, tmp)
            if KSUB < 4:
                continue
            for c in range(8):
                kb.op(kb.dve, lambda e: e.scalar_tensor_tensor(out=h[:, c, :], in0=x[:, c, :], scalar=gcol(layer, 0, c),
                                                                 in1=rstd[:], op0=ALU.mult, op1=ALU.mult),
                      [x, rstd, normg], [h.c[c]])
            if KSUB < 5:
                continue
            for m in range(16):
                p = pq[cnt % 3]
                r_ = pr[cnt % 2]
                qs, a1, a2 = qsb[cnt % 2], t1[cnt % 2], t2[cnt % 2]
                dst = (qo if m < 8 else ko)[s]
                scale = 0.125 if m < 8 else 1.0
                cnt += 1

                def mm(e):
                    for k in range(8):
                        i = e.matmul(p[:], lhsT=W[:, k, m * 128:(m + 1) * 128], rhs=h[:, k, :], start=(k == 0), stop=(k == 7))
                    return i
                KQ = int(os.environ.get("KQ", "99"))
                kb.op(kb.pe, mm, [W, h], [p])
                if KQ < 2:
                    continue
                kb.op(kb.act, lambda e: e.activation(out=qs[:], in_=p[:], func=AF.Copy, scale=scale), [p], [qs])
                if KQ < 3:
                    continue
                kb.op(kb.pe, lambda e: e.matmul(r_[:], lhsT=rperm_bf[:], rhs=qs[:], start=True, stop=True), [rperm_bf, qs], [r_])
                if KQ < 4:
                    continue
                kb.op(kb.dve, lambda e: e.tensor_tensor(out=a1[:], in0=p[:], in1=rp[s][:, (2 if m < 8 else 0), :], op=ALU.mult), [p, rp[s]], [a1])
                if KQ < 5:
                    continue
                kb.op(kb.dve, lambda e: e.tensor_tensor(out=a2[:], in0=r_[:], in1=rp[s][:, 1, :], op=ALU.mult), [r_, rp[s]], [a2])
                if KQ < 6:
                    continue
                kb.op(kb.ew2, lambda e: e.tensor_tensor(out=dst[:, m % 8, :], in0=a1[:], in1=a2[:], op=ALU.add), [a1, a2], [dst.c[m % 8]])
            if KSUB < 6:
                continue
            kb.store(QT_d, QTv[:, :, tt * 512:(tt + 1) * 512], qo[s], qo[s][:])
            kb.store(KT_d, KTv[:, :, tt * 512:(tt + 1) * 512], ko[s], ko[s][:])
            if KSUB < 7:
                continue
            for tb in range(4):
                for nb in range(2):
                    p = pq[cnt % 3]
                    cnt += 1

                    def mm(e):
                        for k in range(8):
                            i = e.matmul(p[:], lhsT=h[:, k, tb * 128:(tb + 1) * 128],
                                         rhs=W[:, k, 2048 + nb * 512:2048 + (nb + 1) * 512], start=(k == 0), stop=(k == 7))
                        return i
                    kb.op(kb.pe, mm, [W, h], [p])
                    kb.op(kb.act, lambda e: e.activation(out=vo[s][:, tb, nb * 512:(nb + 1) * 512], in_=p[:], func=AF.Copy), [p], [vo[s].c[tb * 2 + nb]])
            kb.store(V_d, Vv[:, tt * 4:(tt + 1) * 4, :], vo[s], vo[s][:])
        kb.end_phase()

    def phase_attn_A(layer, j, QT_d, KT_d, V_d, oT_d):
        kb.begin_phase()
        lam_init = 0.8 - 0.6 * math.exp(-0.3 * layer)
        NQT = NT
        QTh = [kb.sb(f"QTh{i}", [128, T], BF16) for i in range(2)]
        KTh = [kb.sb(f"KTh{i}", [128, T], BF16) for i in range(2)]
        Vh = [kb.sb(f"Vh{i}", [128, NKC, 128], BF16) for i in range(2)]
        P = [kb.sb(f"P{i}", [128, 1024], BF16) for i in range(3)]
        abias = kb.sb("abias", [128, NKC * NT], F32)
        lamt = kb.sb("lamt", [128, 256], F32)
        lamp = kb.sb("lamp", [128, 128], F32)
        lams = kb.sb("lams", [128, 8], F32)
        gsub = kb.sb("gsub", [128, 2], F32)
        r0 = kb.sb("r0", [128, 512], F32)
        r1 = kb.sb("r1", [128, 512], F32)
        n0 = kb.sb("n0", [128, 512], F32)
        n1 = kb.sb("n1", [128, 512], F32)
        sqb = kb.sb("sqb", [128, 512], BF16)
        tmp = kb.sb("tmp", [128, 512], F32)
        rstd = kb.sb("rstd", [128, 512], F32)
        ob = [kb.sb(f"ob{i}", [128, 512], BF16) for i in range(2)]
        S = [kb.ps(f"S{i}", [128, 1024]) for i in range(2)]
        O0, O1 = kb.ps("O0", [128, 512]), kb.ps("O1", [128, 512])
        L0, L1 = kb.ps("L0", [128, 512]), kb.ps("L1", [128, 512])
        kb.load(abias, abias[:], abias_in, abias_in[:])
        kb.load(lamt, lamt[:], a_lam, a_lam[:, j * 256:(j + 1) * 256])
        kb.load(gsub, gsub[:], a_sub, a_sub[:])
        kb.op(kb.dve, lambda e: e.tensor_tensor(out=lamp[:, 0:64], in0=lamt[:, 0:64], in1=lamt[:, 64:128], op=ALU.mult), [lamt], [lamp])
        kb.op(kb.dve, lambda e: e.tensor_tensor(out=lamp[:, 64:128], in0=lamt[:, 128:192], in1=lamt[:, 192:256], op=ALU.mult), [lamt, lamp], [lamp])
        kb.op(kb.dve, lambda e: e.reduce_sum(out=lams[:, 0:1], in_=lamp[:, 0:64], axis=AX.X), [lamp], [lams])
        kb.op(kb.dve, lambda e: e.reduce_sum(out=lams[:, 1:2], in_=lamp[:, 64:128], axis=AX.X), [lamp, lams], [lams])
        kb.op(kb.act, lambda e: e.activation(out=lams[:, 2:4], in_=lams[:, 0:2], func=AF.Exp), [lams], [lams])
        kb.op(kb.dve, lambda e: e.tensor_tensor(out=lams[:, 4:5], in0=lams[:, 3:4], in1=lams[:, 2:3], op=ALU.subtract), [lams], [lams])
        kb.op(kb.dve, lambda e: e.tensor_scalar_add(out=lams[:, 5:6], in0=lams[:, 4:5], scalar1=-lam_init), [lams], [lams])
        kb.op(kb.dve, lambda e: e.tensor_scalar_mul(out=lams[:, 6:7], in0=gsub[:, j:j + 1], scalar1=1.0 - lam_init), [gsub, lams], [lams])
        neglam = lams[:, 5:6]
        gs = lams[:, 6:7]
        QTv, KTv = QT_d[:], KT_d[:]
        Vv = V_d[:].rearrange("(n p) f -> p n f", p=128)

        def ldh(h):
            s = h % 2
            kb.load(QTh[s], QTh[s][:], QT_d, QTv[h * 128:(h + 1) * 128, :])
            kb.load(KTh[s], KTh[s][:], KT_d, KTv[h * 128:(h + 1) * 128, :])
            step = max(1, NKC // 4)
            for c0 in range(0, NKC, step):
                kb.load(Vh[s], Vh[s][:, c0:c0 + step, :], V_d, Vv[:, c0:c0 + step, h * 128:(h + 1) * 128])

        acc0 = [kb.sb(f"acc0{i}", [128, 512], F32) for i in range(2)]
        acc1 = [kb.sb(f"acc1{i}", [128, 512], F32) for i in range(2)]
        ai = 0
        ldh(0)
        it = 0
        oi = 0
        for h in range(8):
            s = h % 2
            if h + 1 < 8:
                ldh(h + 1)
            Q, K, V = QTh[s], KTh[s], Vh[s]
            for qt in range(NQT):
                def qk(kc, slot):
                    Sx = S[slot]

                    def f(e):
                        e.matmul(Sx[:, 0:512], lhsT=K[0:64, kc * 128:(kc + 1) * 128], rhs=Q[0:64, qt * 512:(qt + 1) * 512],
                                 start=True, stop=True, tile_position=(0, 0))
                        return e.matmul(Sx[:, 512:1024], lhsT=K[64:128, kc * 128:(kc + 1) * 128], rhs=Q[64:128, qt * 512:(qt + 1) * 512],
                                        start=True, stop=True, tile_position=(64, 0))
                    kb.op(kb.pe, f, [K, Q], [Sx])

                qk(0, it % 2)
                Pxs = {}

                def pvA(kc):
                    Px = Pxs[kc]

                    def pv(e):
                        e.matmul(O0[:], lhsT=V[:, kc, :], rhs=Px[:, 0:512], start=(kc == 0), stop=(kc == NKC - 1))
                        e.matmul(O1[:], lhsT=V[:, kc, :], rhs=Px[:, 512:1024], start=(kc == 0), stop=(kc == NKC - 1))
                        return e.matmul(L1[:], lhsT=ones_bf[:], rhs=Px[:, 512:1024], start=(kc == 0), stop=(kc == NKC - 1))
                    kb.op(kb.pe, pv, [V, Px, ones_bf], [O0, O1, L1])
                for kc in range(NKC):
                    slot = it % 2
                    Px = P[it % 3]
                    Pxs[kc] = Px
                    Sx = S[slot]
                    it += 1
                    if kc + 1 < NKC:
                        qk(kc + 1, it % 2)
                    bcol = abias[:, kc * NT + qt:kc * NT + qt + 1]
                    kb.op(kb.act, lambda e: e.activation(out=Px[:], in_=Sx[:], func=AF.Exp, bias=bcol), [Sx, abias], [Px])
                    a0 = acc0[ai % 2]
                    if kc == 0:
                        kb.op(kb.dve, lambda e: e.tensor_copy(out=a0[:], in_=Px[:, 0:512]), [Px], [a0])
                    else:
                        kb.op(kb.dve, lambda e: e.tensor_tensor(out=a0[:], in0=Px[:, 0:512], in1=a0[:], op=ALU.add), [Px, a0], [a0])
                    if kc >= 1:
                        pvA(kc - 1)
                pvA(NKC - 1)
                a0 = acc0[ai % 2]
                ai += 1
                kb.op(kb.pe, lambda e: e.matmul(L0[:], lhsT=ones_f[:], rhs=a0[:], start=True, stop=True), [ones_f, a0], [L0])
                kb.op(kb.act, lambda e: e.activation(out=n1[:], in_=L0[:], func=AF.Ln), [L0], [n1])
                kb.op(kb.act, lambda e: e.activation(out=r0[:], in_=n1[:], func=AF.Exp, scale=-1.0), [n1], [r0])
                kb.op(kb.act, lambda e: e.activation(out=n1[:], in_=L1[:], func=AF.Ln), [L1, r0], [n1])
                kb.op(kb.act, lambda e: e.activation(out=r1[:], in_=n1[:], func=AF.Exp, scale=-1.0), [n1], [r1])
                kb.op(kb.dve, lambda e: e.tensor_tensor(out=n0[:], in0=O0[:], in1=r0[:], op=ALU.mult), [O0, r0], [n0])
                kb.op(kb.dve, lambda e: e.tensor_tensor(out=n1[:], in0=O1[:], in1=r1[:], op=ALU.mult), [O1, r1], [n1])
                kb.op(kb.dve, lambda e: e.scalar_tensor_tensor(out=n0[:], in0=n1[:], scalar=neglam, in1=n0[:], op0=ALU.mult, op1=ALU.add),
                      [n1, n0, lams], [n0])
                kb.op(kb.act, lambda e: e.activation(out=sqb[:], in_=n0[:], func=AF.Square), [n0], [sqb])
                kb.op(kb.pe, lambda e: e.matmul(L0[:], lhsT=ones_bf[:], rhs=sqb[:], start=True, stop=True), [ones_bf, sqb], [L0])
                rstd_from(L0, 512, 128, 1e-5, rstd, tmp)
                o = ob[oi % 2]
                oi += 1
                kb.op(kb.dve, lambda e: e.scalar_tensor_tensor(out=o[:], in0=n0[:], scalar=gs, in1=rstd[:], op0=ALU.mult, op1=ALU.mult),
                      [n0, rstd, lams], [o])
                kb.store(oT_d, oT_d[h * 128:(h + 1) * 128, qt * 512:(qt + 1) * 512], o, o[:])
        kb.end_phase()

    def phase_attn_B(layer, QTb, KTb, Vb, oT_d):
        kb.begin_phase()
        NM = len(BMASKS)
        midx = {gd: i for i, gd in enumerate(BMASKS)}
        Qg = [kb.sb(f"Qg{g}", [128, T], BF16) for g in range(3)]
        Kg = [kb.sb(f"Kg{g}", [128, T], BF16) for g in range(3)]
        Vg = [kb.sb(f"Vg{g}", [128, NKC, 128], BF16) for g in range(3)]
        masks = kb.sb("masks", [128, NM, 512], BF16)
        abias = kb.sb("abias", [128, NKC * NT], F32)
        P = [kb.sb(f"P{i}", [128, 1024], BF16) for i in range(2)]
        PM = [kb.sb(f"PM{i}", [128, 1024], BF16) for i in range(3)]
        rr = kb.sb("rr", [128, 512], F32)
        rbs = kb.sb("rbs", [128, 512], F32)
        ob = [kb.sb(f"ob{i}", [128, 512], BF16) for i in range(2)]
        S = [kb.ps(f"S{i}", [128, 1024]) for i in range(2)]
        OX = [kb.ps("OA", [128, 512]), kb.ps("OB", [128, 512])]
        RB = kb.ps("RB", [128, 512])
        kb.load(abias, abias[:], abias_in, abias_in[:])
        kb.op(kb.dve, lambda e: e.memset(rr[:], 0.0), [], [rr])
        for i0_ in range(0, NM, 4):
            i1_ = min(NM, i0_ + 4)
            kb.load(masks, masks[:, i0_:i1_, :], bmask_in, bmask_in[:, i0_:i1_, :], st=kb.pool)
        it = 0
        oi = 0
        for hp in range(8):
            for g in range(3):
                kb.load(Qg[g], Qg[g][:], QTb[g], QTb[g][hp * 128:(hp + 1) * 128, :])
                kb.load(Kg[g], Kg[g][:], KTb[g], KTb[g][hp * 128:(hp + 1) * 128, :])
                Vv = Vb[g][:].rearrange("(n p) f -> p n f", p=128)
                step = max(1, NKC // 4)
                for c0 in range(0, NKC, step):
                    kb.load(Vg[g], Vg[g][:, c0:c0 + step, :], Vb[g], Vv[:, c0:c0 + step, hp * 128:(hp + 1) * 128])
            for qt in range(NT):
                tiles = [(g, kc) for g in range(3) for kc in range(NKC) if (g, qt * 512 - kc * 128) in midx]
                qs_ = slice(qt * 512, (qt + 1) * 512)

                def qkB(ti, slot):
                    g, kc = tiles[ti]
                    Sx = S[slot]
                    ks_ = slice(kc * 128, (kc + 1) * 128)

                    def f(e):
                        e.matmul(Sx[:, 0:512], lhsT=Kg[g][0:64, ks_], rhs=Qg[g][0:64, qs_], start=True, stop=True, tile_position=(0, 0))
                        return e.matmul(Sx[:, 512:1024], lhsT=Kg[g][64:128, ks_], rhs=Qg[g][64:128, qs_], start=True, stop=True,
                                        tile_position=(64, 0))
                    kb.op(kb.pe, f, [Kg[g], Qg[g]], [Sx])
                qkB(0, it % 2)
                nt_ = len(tiles)
                Pms = {}

                def pvBt(ti):
                    g, kc = tiles[ti]
                    Pm = Pms[ti]

                    def pvB(e):
                        for a in range(2):
                            e.matmul(OX[a][0:64, :], lhsT=Vg[g][:, kc, a * 64:(a + 1) * 64], rhs=Pm[:, a * 512:(a + 1) * 512],
                                     start=(ti == 0), stop=(ti == nt_ - 1), tile_position=(0, 0))
                            i = e.matmul(OX[a][64:128, :], lhsT=ones_bf[:, 0:64], rhs=Pm[:, a * 512:(a + 1) * 512],
                                         start=(ti == 0), stop=(ti == nt_ - 1), tile_position=(0, 64))
                        return i
                    kb.op(kb.pe, pvB, [Vg[g], Pm, ones_bf], [OX[0], OX[1]])
                for ti, (g, kc) in enumerate(tiles):
                    Sx = S[it % 2]
                    Px = P[it % 2]
                    Pm = PM[it % 3]
                    Pms[ti] = Pm
                    it += 1
                    if ti + 1 < nt_:
                        qkB(ti + 1, it % 2)
                    mi = midx[(g, qt * 512 - kc * 128)]
                    bcol = abias[:, kc * NT + qt:kc * NT + qt + 1]
                    kb.op(kb.act, lambda e: e.activation(out=Px[:], in_=Sx[:], func=AF.Exp, bias=bcol), [Sx, abias], [Px])
                    kb.op(kb.dve, lambda e: e.tensor_tensor(out=Pm[:].rearrange("p (a q) -> p a q", a=2),
                                                            in0=Px[:].rearrange("p (a q) -> p a q", a=2),
                                                            in1=masks[:, mi:mi + 1, :].to_broadcast([128, 2, 512]), op=ALU.mult), [Px, masks], [Pm])
                    if ti >= 1:
                        pvBt(ti - 1)
                pvBt(nt_ - 1)
                for a in range(2):
                    kb.op(kb.dve, lambda e: e.reciprocal(out=rr[64:128, :], in_=OX[a][64:128, :]), [OX[a]], [rr])
                    kb.op(kb.pe, lambda e: e.matmul(RB[0:64, :], lhsT=shift_f[:], rhs=rr[:], start=True, stop=True), [shift_f, rr], [RB])
                    kb.op(kb.act, lambda e: e.activation(out=rbs[0:64, :], in_=RB[0:64, :], func=AF.Copy), [RB], [rbs])
                    o = ob[oi % 2]
                    oi += 1
                    kb.op(kb.dve, lambda e: e.tensor_tensor(out=o[0:64, :], in0=OX[a][0:64, :], in1=rbs[0:64, :], op=ALU.mult), [OX[a], rbs], [o])
                    r_0 = a * 64
                    kb.store(oT_d, oT_d[hp * 128 + r_0:hp * 128 + r_0 + 64, qs_], o, o[0:64, :])
        kb.end_phase()

    def phase_proj_C(layer, x_d, qt_d, kt_d, kh_d, et_d, V_d, gT_d):
        kb.begin_phase()
        W = kb.sb("Wc", [128, 8, 5120], BF16)
        wsrc = c_win[0].rearrange("(c p) n -> p c n", p=128)
        load_cast_w(W, lambda a, b: W[:, :, a:b], c_win, lambda a, b: wsrc[:, :, a:b], 5120)
        lbl = kb.sb("lbl", [128, 4, 8], F32)
        lbe = kb.sb("lbe", [128, 4, 8], F32)
        lbs = kb.sb("lbs", [128, 4, 8], F32)
        kb.load(lbl, lbl[:], c_lbl, c_lbl[:])
        kb.op(kb.act, lambda e: e.activation(out=lbe[:], in_=lbl[:], func=AF.Exp), [lbl], [lbe])
        kb.op(kb.dve, lambda e: e.tensor_tensor(out=lbs[:, 0, :], in0=lbe[:, 0, :], in1=lbe[:, 1, :], op=ALU.add), [lbe], [lbs])
        kb.op(kb.dve, lambda e: e.tensor_tensor(out=lbs[:, 0, :], in0=lbs[:, 0, :], in1=lbe[:, 2, :], op=ALU.add), [lbe, lbs], [lbs])
        kb.op(kb.dve, lambda e: e.tensor_tensor(out=lbs[:, 0, :], in0=lbs[:, 0, :], in1=lbe[:, 3, :], op=ALU.add), [lbe, lbs], [lbs])
        kb.op(kb.dve, lambda e: e.tensor_copy(out=lbs[:, 1, :], in_=lbe[:, 1, :]), [lbe, lbs], [lbs])
        for i in range(2, layer + 1):
            kb.op(kb.dve, lambda e: e.tensor_tensor(out=lbs[:, 1, :], in0=lbs[:, 1, :], in1=lbe[:, i, :], op=ALU.add), [lbe, lbs], [lbs])
        kb.op(kb.dve, lambda e: e.reciprocal(out=lbs[:, 2, :], in_=lbs[:, 0, :]), [lbs], [lbs])
        kb.op(kb.dve, lambda e: e.tensor_tensor(out=lbs[:, 3, :], in0=lbs[:, 0, :], in1=lbs[:, 1, :], op=ALU.subtract), [lbs], [lbs])
        kb.op(kb.dve, lambda e: e.tensor_tensor(out=lbs[:, 3, :], in0=lbs[:, 3, :], in1=lbs[:, 2, :], op=ALU.mult), [lbs], [lbs])
        xs = [kb.sb(f"xs{i}", [128, 8, 512], F32) for i in range(2)]
        hT = kb.sb("hT", [128, 8, 512], BF16)
        sq = [kb.sb(f"sq{i}", [128, 512], BF16) for i in range(2)]
        tmp = kb.sb("tmp", [128, 512], F32)
        rstd = kb.sb("rstd", [128, 512], F32)
        F = lambda n: kb.sb(n, [128, 512], F32)
        tsets = [[F(n + str(i)) for n in ("sg", "kk", "lf", "G", "Gd", "Ek", "eG", "enG", "eK")] for i in range(2)]
        qsets = [F("qs0"), F("qs1")]
        gate = [kb.sb(f"gate{i}", [128, 512], BF16) for i in range(2)]
        qo = [kb.sb(f"qo{i}", [128, 512], BF16) for i in range(2)]
        ko = [kb.sb(f"ko{i}", [128, 512], BF16) for i in range(2)]
        kh = [kb.sb(f"kh{i}", [128, 512], BF16) for i in range(2)]
        kht = [kb.sb(f"kht{i}", [128, 4, 128], BF16) for i in range(2)]
        eto = [kb.sb(f"eto{i}", [128, 8], F32) for i in range(2)]
        vo = kb.sb("vo", [128, 4, 1024], BF16)
        pst = kb.ps("pst", [128, 512])
        pq = [kb.ps(f"pq{i}", [128, 512]) for i in range(3)]
        ptr = [kb.ps(f"ptr{i}", [128, 128], BF16) for i in range(2)]
        xv = x_d[:].rearrange("(c p) t -> p c t", p=128)
        Vv = V_d[:].rearrange("(n p) f -> p n f", p=128)
        khv = [kh_d[d][:].rearrange("(n p) f -> p n f", p=128) for d in range(2)]

        def ld(tt):
            kb.load(xs[tt % 2], xs[tt % 2][:], x_d, xv[:, :, tt * 512:(tt + 1) * 512])
        ld(0)
        cnt = 0
        oc = 0
        for tt in range(NT):
            if tt + 1 < NT:
                ld(tt + 1)
            x, h = xs[tt % 2], hT
            for c in range(8):
                q = sq[c % 2]
                kb.op(kb.act, lambda e: e.activation(out=q[:], in_=x[:, c, :], func=AF.Square), [x], [q])
                kb.op(kb.pe, lambda e: e.matmul(pst[:], lhsT=ones_bf[:], rhs=q[:], start=(c == 0), stop=(c == 7)), [ones_bf, q], [pst])
            rstd_from(pst, 512, D, 1e-6, rstd, tmp)
            for c in range(8):
                kb.op(kb.dve, lambda e: e.scalar_tensor_tensor(out=h[:, c, :], in0=x[:, c, :], scalar=gcol(layer, 0, c),
                                                                 in1=rstd[:], op0=ALU.mult, op1=ALU.mult), [x, rstd, normg], [h])

            def proj(col0):
                nonlocal cnt
                p = pq[cnt % 3]
                cnt += 1

                def mm(e):
                    for k in range(8):
                        i = e.matmul(p[:], lhsT=W[:, k, col0:col0 + 128], rhs=h[:, k, :], start=(k == 0), stop=(k == 7))
                    return i
                kb.op(kb.pe, mm, [W, h], [p])
                return p
            csl = slice(tt * 512, (tt + 1) * 512)
            for hd in range(8):
                rows = slice(hd * 128, (hd + 1) * 128)
                p = proj(hd * 128)
                qs = qsets[hd % 2]
                kb.op(kb.act, lambda e: e.activation(out=qs[:], in_=p[:], func=AF.Silu), [p], [qs])
                p = proj(4096 + hd * 128)
                gt = gate[oc % 2]
                kb.op(kb.act, lambda e: e.activation(out=gt[:], in_=p[:], func=AF.Silu), [p], [gt])
                kb.store(gT_d, gT_d[rows, csl], gt, gt[:])
                for d in range(2):
                    qo_, ko_, kh_, kht_, eto_ = qo[oc % 2], ko[oc % 2], kh[oc % 2], kht[oc % 2], eto[oc % 2]
                    sg, kk, lf, G, Gd, Ek, eG, enG, eK = tsets[oc % 2]
                    oc += 1
                    p = proj(1024 + d * 1024 + hd * 128)
                    kb.op(kb.act, lambda e: e.activation(out=sg[:], in_=p[:], func=AF.Sigmoid, scale=-1.0), [p], [sg])
                    kb.op(kb.dve, lambda e: e.tensor_scalar_mul(out=kk[:], in0=sg[:], scalar1=lbs[:, 3, hd:hd + 1]), [sg, lbs], [kk])
                    kb.op(kb.act, lambda e: e.activation(out=lf[:], in_=kk[:], func=AF.Ln, scale=-1.0, bias=1.0), [kk], [lf])
                    kb.op(kb.dve, lambda e: e.tensor_tensor_scan(out=G[:], data0=m01[:], data1=lf[:], initial=0.0, op0=ALU.mult, op1=ALU.add),
                          [m01, lf], [G])
                    G3 = G[:].rearrange("p (c t) -> p c t", t=64)
                    TOTb = G3[:, :, 63:64].to_broadcast([128, 8, 64])
                    r3 = lambda b: b[:].rearrange("p (c t) -> p c t", t=64)
                    if d == 0:
                        Gdb = G
                        kb.op(kb.dve, lambda e: e.tensor_tensor(out=r3(Ek), in0=TOTb, in1=G3, op=ALU.subtract), [G], [Ek])
                    else:
                        Gdb = Gd
                        kb.op(kb.dve, lambda e: e.tensor_tensor(out=Ek[:], in0=G[:], in1=lf[:], op=ALU.subtract), [G, lf], [Ek])
                        kb.op(kb.dve, lambda e: e.tensor_tensor(out=r3(Gd), in0=TOTb, in1=r3(Ek), op=ALU.subtract), [G, Ek], [Gd])
                    kb.op(kb.act, lambda e: e.activation(out=eG[:], in_=Gdb[:], func=AF.Exp), [Gdb], [eG])
                    kb.op(kb.act, lambda e: e.activation(out=enG[:], in_=Gdb[:], func=AF.Exp, scale=-1.0), [Gdb], [enG])
                    kb.op(kb.act, lambda e: e.activation(out=eK[:], in_=Ek[:], func=AF.Exp), [Ek], [eK])
                    kb.op(kb.act, lambda e: e.activation(out=eto_[:], in_=G3[:, :, 63], func=AF.Exp), [G], [eto_])
                    kb.op(kb.dve, lambda e: e.scalar_tensor_tensor(out=qo_[:], in0=qs[:], scalar=cv[:, 2:3], in1=eG[:], op0=ALU.mult, op1=ALU.mult),
                          [qs, cv, eG], [qo_])
                    kb.op(kb.dve, lambda e: e.tensor_tensor(out=ko_[:], in0=kk[:], in1=enG[:], op=ALU.mult), [kk, enG], [ko_])
                    kb.op(kb.dve, lambda e: e.tensor_tensor(out=kh_[:], in0=kk[:], in1=eK[:], op=ALU.mult), [kk, eK], [kh_])
                    for tb in range(4):
                        pt = ptr[tb % 2]
                        kb.op(kb.pe, lambda e: e.transpose(pt[:], kh_[:, tb * 128:(tb + 1) * 128], ident_bf[:]), [kh_, ident_bf], [pt])
                        kb.op(kb.act, lambda e: e.activation(out=kht_[:, tb, :], in_=pt[:], func=AF.Copy), [pt], [kht_])
                    kb.store(qt_d[d], qt_d[d][rows, csl], qo_, qo_[:])
                    kb.store(kt_d[d], kt_d[d][rows, csl], ko_, ko_[:])
                    kb.store(kh_d[d], khv[d][:, tt * 4:(tt + 1) * 4, rows], kht_, kht_[:])
                    kb.store(et_d[d], et_d[d][rows, tt * 8:(tt + 1) * 8], eto_, eto_[:])
            for tb in range(4):
                for nb in range(2):
                    p = pq[cnt % 3]
                    cnt += 1

                    def mm(e):
                        for k in range(8):
                            i = e.matmul(p[:], lhsT=h[:, k, tb * 128:(tb + 1) * 128],
                                         rhs=W[:, k, 3072 + nb * 512:3072 + (nb + 1) * 512], start=(k == 0), stop=(k == 7))
                        return i
                    kb.op(kb.pe, mm, [W, h], [p])
                    kb.op(kb.act, lambda e: e.activation(out=vo[:, tb, nb * 512:(nb + 1) * 512], in_=p[:], func=AF.Copy), [p], [vo])
            kb.store(V_d, Vv[:, tt * 4:(tt + 1) * 4, :], vo, vo[:])
        kb.end_phase()

    def phase_scan_C(layer, qt_d, kt_d, kh_d, et_d, V_d, gT_d, oT_d):
        kb.begin_phase()
        NC = T // 64
        Qt = [kb.sb(f"Qt{i}", [128, T], BF16) for i in range(2)]
        Kt = [kb.sb(f"Kt{i}", [128, T], BF16) for i in range(2)]
        Kh = [kb.sb(f"Kh{i}", [128, NKC, 128], BF16) for i in range(2)]
        Et = [kb.sb(f"Et{i}", [128, NC], F32) for i in range(2)]
        Vh = kb.sb("Vh", [128, NKC, 128], BF16)
        Gt = kb.sb("Gt", [128, T], BF16)
        ofw = kb.split(kb.sb("ofw", [128, T], F32), NT)
        obw = kb.split(kb.sb("obw", [128, T], F32), NT)
        Sfs = [kb.sb(f"Sf{i}", [128, 128], F32) for i in range(2)]
        Sbs = [kb.sb(f"Sb{i}", [128, 128], BF16) for i in range(2)]
        sc = [kb.sb(f"sc{i}", [128, 128], BF16) for i in range(2)]
        gn = kb.sb("gn", [128, 1], F32)
        sqb = kb.sb("sqb", [128, 512], BF16)
        tmp = kb.sb("tmp", [128, 512], F32)
        rstd = kb.sb("rstd", [128, 512], F32)
        ob = [kb.sb(f"ob{i}", [128, 512], BF16) for i in range(2)]
        scp = [kb.ps(f"scp{i}", [128, 128]) for i in range(2)]
        op_ = [kb.ps(f"op{i}", [128, 128]) for i in range(2)]
        dsp = [kb.ps(f"dsp{i}", [128, 128]) for i in range(2)]
        pst = kb.ps("pst", [128, 512])
        kb.load(gn, gn[:], c_gn, c_gn[:])
        Vv = V_d[:].rearrange("(n p) f -> p n f", p=128)
        khv = [kh_d[d][:].rearrange("(n p) f -> p n f", p=128) for d in range(2)]
        step = max(1, NKC // 4)
        it = 0
        oi = 0
        ci = 0
        for hd in range(8):
            rows = slice(hd * 128, (hd + 1) * 128)
            for c0 in range(0, NKC, step):
                kb.load(Vh, Vh[:, c0:c0 + step, :], V_d, Vv[:, c0:c0 + step, rows])
            kb.load(Gt, Gt[:], gT_d, gT_d[rows, :])
            for d in range(2):
                kb.load(Qt[d], Qt[d][:], qt_d[d], qt_d[d][rows, :])
                kb.load(Kt[d], Kt[d][:], kt_d[d], kt_d[d][rows, :])
                kb.load(Et[d], Et[d][:], et_d[d], et_d[d][rows, :])
                for c0 in range(0, NKC, step):
                    kb.load(Kh[d], Kh[d][:, c0:c0 + step, :], kh_d[d], khv[d][:, c0:c0 + step, rows])
            for d in range(2):
                kb.op(kb.dve, lambda e: e.memset(Sfs[d][:], 0.0), [], [Sfs[d]])
                kb.op(kb.dve, lambda e: e.memset(Sbs[d][:], 0.0), [], [Sbs[d]])
            for step_ in range(NKC):
                for d in range(2):
                    Q, K, KH, ET = Qt[d], Kt[d], Kh[d], Et[d]
                    Sf, Sb = Sfs[d], Sbs[d]
                    b = step_ if d == 0 else NKC - 1 - step_
                    bs = slice(b * 128, (b + 1) * 128)
                    scp_, sc_, o_ = scp[it % 2], sc[it % 2], op_[it % 2]
                    it += 1
                    kb.op(kb.pe, lambda e: e.matmul(scp_[:], lhsT=K[:, bs], rhs=Q[:, bs], start=True, stop=True), [K, Q], [scp_])
                    kb.op(kb.dve, lambda e: e.tensor_tensor(out=sc_[:], in0=scp_[:], in1=cmask[:, d, :], op=ALU.mult), [scp_, cmask], [sc_])
                    kb.op(kb.pe, lambda e: e.matmul(o_[:], lhsT=Vh[:, b, :], rhs=sc_[:], start=True, stop=False), [Vh, sc_], [o_])
                    chunks = (2 * b, 2 * b + 1) if d == 0 else (2 * b + 1, 2 * b)
                    for n_, c in enumerate(chunks):
                        p0 = (c % 2) * 64
                        first_of_other = (c == NC // 2) if d == 0 else (c == NC // 2 - 1)
                        if first_of_other:
                            kb.op(kb.dve, lambda e: e.tensor_scalar_mul(out=Sf[:], in0=Sf[:], scalar1=flag[:, 0:1]), [Sf, flag], [Sf])
                            kb.op(kb.act, lambda e: e.activation(out=Sb[:], in_=Sf[:], func=AF.Copy), [Sf], [Sb])
                        kb.op(kb.pe, lambda e: e.matmul(o_[:, p0:p0 + 64], lhsT=Sb[:], rhs=Q[:, c * 64:(c + 1) * 64], start=False, stop=(n_ == 1)),
                              [Sb, Q], [o_])
                        ds_ = dsp[ci % 2]
                        ci += 1
                        kb.op(kb.pe, lambda e: e.matmul(ds_[:], lhsT=KH[p0:p0 + 64, b, :], rhs=Vh[p0:p0 + 64, b, :], start=True, stop=True,
                                                        tile_position=(p0, 0)), [KH, Vh], [ds_])
                        kb.op(kb.dve, lambda e: e.scalar_tensor_tensor(out=Sf[:], in0=Sf[:], scalar=ET[:, c:c + 1], in1=ds_[:],
                                                                         op0=ALU.mult, op1=ALU.add), [Sf, ET, ds_], [Sf])
                        kb.op(kb.act, lambda e: e.activation(out=Sb[:], in_=Sf[:], func=AF.Copy), [Sf], [Sb])
                    dst_ = ofw if d == 0 else obw
                    kb.op(kb.act, lambda e: e.activation(out=dst_[:, bs], in_=o_[:], func=AF.Copy), [o_], [dst_.c[b // 4]])
            for tt in range(NT):
                csl = slice(tt * 512, (tt + 1) * 512)
                kb.op(kb.dve, lambda e: e.tensor_tensor(out=ofw[:, csl], in0=ofw[:, csl], in1=obw[:, csl], op=ALU.add),
                      [ofw.c[tt], obw.c[tt]], [ofw.c[tt]])
            for tt in range(NT):
                csl = slice(tt * 512, (tt + 1) * 512)
                kb.op(kb.act, lambda e: e.activation(out=sqb[:], in_=ofw[:, csl], func=AF.Square), [ofw], [sqb])
                kb.op(kb.pe, lambda e: e.matmul(pst[:], lhsT=ones_bf[:], rhs=sqb[:], start=True, stop=True), [ones_bf, sqb], [pst])
                rstd_from(pst, 512, 128, 1e-6, rstd, tmp)
                kb.op(kb.dve, lambda e: e.scalar_tensor_tensor(out=tmp[:], in0=ofw[:, csl], scalar=gn[:, 0:1], in1=rstd[:], op0=ALU.mult, op1=ALU.mult),
                      [ofw, gn, rstd], [tmp])
                o = ob[oi % 2]
                oi += 1
                kb.op(kb.dve, lambda e: e.tensor_tensor(out=o[:], in0=tmp[:], in1=Gt[:, csl], op=ALU.mult), [tmp, Gt], [o])
                kb.store(oT_d, oT_d[rows, csl], o, o[:])
        kb.end_phase()

    def phase_wo(layer, wo_buf, wo_ap, oT_d, x_d, x1_d):
        kb.begin_phase()
        Wo = kb.sb("Wo", [128, 8, 1024], BF16)
        wsrc = wo_ap.rearrange("(c p) n -> p c n", p=128)
        load_cast_w(Wo, lambda a, b: Wo[:, :, a:b], wo_buf, lambda a, b: wsrc[:, :, a:b], 1024)
        xs = [kb.sb(f"xs{i}", [128, 8, 512], F32) for i in range(2)]
        os_ = [kb.sb(f"os{i}", [128, 8, 512], BF16) for i in range(2)]
        y = kb.split(kb.sb("y", [128, 8, 512], F32), 8)
        sq = [kb.sb(f"sq{i}", [128, 512], BF16) for i in range(2)]
        tmp = kb.sb("tmp", [128, 512], F32)
        rstd = kb.sb("rstd", [128, 512], F32)
        xo = [kb.split(kb.sb(f"xo{i}", [128, 8, 512], F32), 8) for i in range(2)]
        pst = kb.ps("pst", [128, 512])
        pq = [kb.ps(f"pq{i}", [128, 512]) for i in range(3)]
        xv = x_d[:].rearrange("(c p) t -> p c t", p=128)
        x1v = x1_d[:].rearrange("(c p) t -> p c t", p=128)
        ov = oT_d[:].rearrange("(c p) t -> p c t", p=128)

        def ld(tt):
            s = tt % 2
            kb.load(xs[s], xs[s][:], x_d, xv[:, :, tt * 512:(tt + 1) * 512])
            kb.load(os_[s], os_[s][:], oT_d, ov[:, :, tt * 512:(tt + 1) * 512])
        ld(0)
        cnt = 0
        for tt in range(NT):
            s = tt % 2
            if tt + 1 < NT:
                ld(tt + 1)
            x, o = xs[s], os_[s]
            for m in range(8):
                p = pq[cnt % 3]
                q = sq[cnt % 2]
                cnt += 1

                def mm(e):
                    for k in range(8):
                        i = e.matmul(p[:], lhsT=Wo[:, k, m * 128:(m + 1) * 128], rhs=o[:, k, :], start=(k == 0), stop=(k == 7))
                    return i
                kb.op(kb.pe, mm, [Wo, o], [p])
                kb.op(kb.act, lambda e: e.activation(out=q[:], in_=p[:], func=AF.Square), [p], [q])
                kb.op(kb.dve, lambda e: e.tensor_copy(out=y[:, m, :], in_=p[:]), [p], [y.c[m]])
                kb.op(kb.pe, lambda e: e.matmul(pst[:], lhsT=ones_bf[:], rhs=q[:], start=(m == 0), stop=(m == 7)), [ones_bf, q], [pst])
            rstd_from(pst, 512, D, 1e-6, rstd, tmp)
            for m in range(8):
                kb.op(kb.dve, lambda e: e.scalar_tensor_tensor(out=y[:, m, :], in0=y[:, m, :], scalar=gcol(layer, 1, m), in1=rstd[:],
                                                                 op0=ALU.mult, op1=ALU.mult), [y.c[m], rstd, normg], [y.c[m]])
                kb.op(kb.ew2, lambda e: e.tensor_tensor(out=xo[s][:, m, :], in0=y[:, m, :], in1=x[:, m, :], op=ALU.add), [y.c[m], x], [xo[s].c[m]])
            kb.store(x1_d, x1v[:, :, tt * 512:(tt + 1) * 512], xo[s], xo[s][:])
        kb.end_phase()

    def phase_ffn(layer, x1_d, x2_d):
        kb.begin_phase()
        NV = 256
        NW = NV + 2
        Win = kb.sb("Win", [128, 8, 2 * DFF], BF16)
        Wout = kb.sb("Wout", [128, 22, 1024], BF16)
        wsrc = f_win[layer].rearrange("(c p) n -> p c n", p=128)
        load_cast_w(Win, lambda a, b: Win[:, :, a:b], f_win, lambda a, b: wsrc[:, :, a:b], 2 * DFF)
        wsrc2 = f_wout[layer].rearrange("(c p) n -> p c n", p=128)
        for c0 in range(0, 22, 6):
            c1 = min(22, c0 + 6)
            kb.load(Wout, Wout[:, c0:c1, :], f_wout, wsrc2[:, c0:c1, :], st=kb.pool)
        cw = kb.sb("cw", [128, 44, 4], F32)
        kb.load(cw, cw[:], f_cw, f_cw[:, layer * 176:(layer + 1) * 176].rearrange("p (c f) -> p c f", f=4))
        xw = [kb.sb(f"xw{i}", [128, 8, NW], F32) for i in range(2)]
        h = kb.split(kb.sb("h", [128, 8, NW], BF16), 8)
        sq = [kb.sb(f"sq{i}", [128, NW], BF16) for i in range(2)]
        tmp = kb.sb("tmp", [128, NW], F32)
        rstd = kb.sb("rstd", [128, NW], F32)
        ta = [kb.sb(f"ta{i}", [128, NV], F32) for i in range(2)]
        tb_ = [kb.sb(f"tb{i}", [128, NV], F32) for i in range(2)]
        ga = [kb.sb(f"ga{i}", [128, NV], F32) for i in range(2)]
        gg = kb.split(kb.sb("gg", [128, 22, NV], BF16), 22)
        y = kb.split(kb.sb("y", [128, 8, NV], F32), 8)
        xo = [kb.split(kb.sb(f"xo{i}", [128, 8, NV], F32), 8) for i in range(2)]
        pst = kb.ps("pst", [128, 512])
        pa = [kb.ps(f"pa{i}", [128, 512]) for i in range(2)]
        pb = [kb.ps(f"pb{i}", [128, 512]) for i in range(2)]
        po = [kb.ps(f"po{i}", [128, 512]) for i in range(2)]
        xv = x1_d[:].rearrange("(c p) t -> p c t", p=128)
        x2v = x2_d[:].rearrange("(c p) t -> p c t", p=128)
        wins = list(range(0, T, NV))

        def ld(wi):
            s0 = wins[wi]
            b = xw[wi % 2]
            lo, hi = s0 - 1, s0 + NV + 1
            clo, chi = max(lo, 0), min(hi, T)
            if clo > lo:
                kb.op(kb.pool, lambda e: e.memset(b[:, :, 0:1], 0.0), [], [b])
            if chi < hi:
                kb.op(kb.pool, lambda e: e.memset(b[:, :, NW - 1:NW], 0.0), [], [b])
            kb.load(b, b[:, :, clo - lo:NW - (hi - chi)], x1_d, xv[:, :, clo:chi])
        ld(0)
        cnt = 0
        for wi, s0 in enumerate(wins):
            if wi + 1 < len(wins):
                ld(wi + 1)
            x = xw[wi % 2]
            for c in range(8):
                q = sq[c % 2]
                kb.op(kb.act, lambda e: e.activation(out=q[:], in_=x[:, c, :], func=AF.Square), [x], [q])
                kb.op(kb.pe, lambda e: e.matmul(pst[:, :NW], lhsT=ones_bf[:], rhs=q[:], start=(c == 0), stop=(c == 7)), [ones_bf, q], [pst])
            rstd_from(pst, NW, D, 1e-6, rstd, tmp)
            for c in range(8):
                kb.op(kb.dve, lambda e: e.scalar_tensor_tensor(out=h[:, c, :], in0=x[:, c, :], scalar=gcol(layer, 2, c), in1=rstd[:],
                                                                 op0=ALU.mult, op1=ALU.mult), [x, rstd, normg], [h.c[c]])
            if s0 == HALF:
                kb.op(kb.dve, lambda e: e.tensor_scalar_mul(out=h[:, :, 0:1], in0=h[:, :, 0:1], scalar1=flag[:, 0:1]), [h, flag], [h])
            if s0 + NV == HALF:
                kb.op(kb.dve, lambda e: e.tensor_scalar_mul(out=h[:, :, NW - 1:NW], in0=h[:, :, NW - 1:NW], scalar1=flag[:, 0:1]), [h, flag], [h])
            for jj in range(22):
                A, B = pa[jj % 2], pb[jj % 2]
                a_, b_, g_ = ta[jj % 2], tb_[jj % 2], ga[jj % 2]

                def mma(e):
                    for k in range(8):
                        i = e.matmul(A[:, :NW], lhsT=Win[:, k, jj * 128:(jj + 1) * 128], rhs=h[:, k, :], start=(k == 0), stop=(k == 7))
                    return i

                def mmb(e):
                    for k in range(8):
                        i = e.matmul(B[:, :NW], lhsT=Win[:, k, DFF + jj * 128:DFF + (jj + 1) * 128], rhs=h[:, k, :], start=(k == 0), stop=(k == 7))
                    return i
                kb.op(kb.pe, mma, [Win, h], [A])
                kb.op(kb.pe, mmb, [Win, h], [B])
                for (Pp, tt_, ci, eng2) in ((A, a_, jj, kb.dve), (B, b_, 22 + jj, kb.dve)):
                    kb.op(kb.act, lambda e: e.activation(out=tt_[:], in_=Pp[:, 1:NV + 1], func=AF.Identity,
                                                         scale=cw[:, ci, 1:2], bias=cw[:, ci, 3:4]), [Pp, cw], [tt_])
                    kb.op(eng2, lambda e: e.scalar_tensor_tensor(out=tt_[:], in0=Pp[:, 0:NV], scalar=cw[:, ci, 0:1], in1=tt_[:],
                                                                  op0=ALU.mult, op1=ALU.add), [Pp, cw, tt_], [tt_])
                    kb.op(eng2, lambda e: e.scalar_tensor_tensor(out=tt_[:], in0=Pp[:, 2:NV + 2], scalar=cw[:, ci, 2:3], in1=tt_[:],
                                                                  op0=ALU.mult, op1=ALU.add), [Pp, cw, tt_], [tt_])
                kb.op(kb.act, lambda e: e.activation(out=g_[:], in_=a_[:], func=AF.Gelu_apprx_tanh), [a_], [g_])
                kb.op(kb.ew2, lambda e: e.tensor_tensor(out=gg[:, jj, :], in0=g_[:], in1=b_[:], op=ALU.mult), [g_, b_], [gg.c[jj]])
            for m in range(8):
                p = po[m % 2]
                q = sq[m % 2]

                def mm(e):
                    for k in range(22):
                        i = e.matmul(p[:, :NV], lhsT=Wout[:, k, m * 128:(m + 1) * 128], rhs=gg[:, k, :], start=(k == 0), stop=(k == 21))
                    return i
                kb.op(kb.pe, mm, [Wout, gg], [p])
                kb.op(kb.act, lambda e: e.activation(out=q[:, :NV], in_=p[:, :NV], func=AF.Square), [p], [q])
                kb.op(kb.dve, lambda e: e.tensor_copy(out=y[:, m, :], in_=p[:, :NV]), [p], [y.c[m]])
                kb.op(kb.pe, lambda e: e.matmul(pst[:, :NV], lhsT=ones_bf[:], rhs=q[:, :NV], start=(m == 0), stop=(m == 7)), [ones_bf, q], [pst])
            rstd_from(pst, NV, D, 1e-6, rstd, tmp)
            xo_ = xo[wi % 2]
            for m in range(8):
                kb.op(kb.dve, lambda e: e.scalar_tensor_tensor(out=y[:, m, :], in0=y[:, m, :], scalar=gcol(layer, 3, m), in1=rstd[:, :NV],
                                                                 op0=ALU.mult, op1=ALU.mult), [y.c[m], rstd, normg], [y.c[m]])
                kb.op(kb.ew2, lambda e: e.tensor_tensor(out=xo_[:, m, :], in0=y[:, m, :], in1=x[:, m, 1:NV + 1], op=ALU.add), [y.c[m], x], [xo_.c[m]])
            kb.store(x2_d, x2v[:, :, s0:s0 + NV], xo_, xo_[:])
        kb.end_phase()

    QTb = [kb.dram(f"QTb{g}", [D, T], BF16) for g in range(3)]
    KTb = [kb.dram(f"KTb{g}", [D, T], BF16) for g in range(3)]
    Vb = [kb.dram(f"Vb{g}", [T, D], BF16) for g in range(3)]
    cq_d = [kb.dram(f"cq{d}", [D, T], BF16) for d in range(2)]
    ck_d = [kb.dram(f"ck{d}", [D, T], BF16) for d in range(2)]
    ckh_d = [kb.dram(f"ckh{d}", [T, D], BF16) for d in range(2)]
    cet_d = [kb.dram(f"cet{d}", [D, T // 64], F32) for d in range(2)]
    QT_d = kb.dram("QT_d", [D, T], BF16)
    KT_d = kb.dram("KT_d", [D, T], BF16)
    V_d = kb.dram("V_d", [T, D], BF16)
    oT_d = kb.dram("oT_d", [D, T], BF16)
    x1_d = kb.dram("x1_d", [D, T], F32)
    xa_d = kb.dram("xa_d", [D, T], F32)
    xb_d = kb.dram("xb_d", [D, T], F32)
    cur = xT_in
    for li, layer in enumerate(layers):
        last = li == len(layers) - 1
        nxt = yT_out if last else (xa_d if li % 2 == 0 else xb_d)
        kind = layer % 3
        j = layer // 3
        import os
        stop = int(os.environ.get("KSTOP", "99"))
        if kind == 0:
            if stop >= 1:
                phase_proj_A(layer, a_wqkv, a_wqkv[j], cur, QT_d, KT_d, V_d)
            if stop >= 2:
                phase_attn_A(layer, j, QT_d, KT_d, V_d, oT_d)
            if stop >= 3:
                phase_wo(layer, a_wo, a_wo[j], oT_d, cur, x1_d)
        elif kind == 1:
            for g in range(3):
                phase_proj_A(layer, b_wqkv, b_wqkv[0][:, g * 3072:(g + 1) * 3072], cur, QTb[g], KTb[g], Vb[g])
            phase_attn_B(layer, QTb, KTb, Vb, oT_d)
            phase_wo(layer, b_wo, b_wo[0], oT_d, cur, x1_d)
        else:
            phase_proj_C(layer, cur, cq_d, ck_d, ckh_d, cet_d, V_d, KT_d)
            phase_scan_C(layer, cq_d, ck_d, ckh_d, cet_d, V_d, KT_d, oT_d)
            phase_wo(layer, c_wo, c_wo[0], oT_d, cur, x1_d)
        if stop >= 4:
            phase_ffn(layer, x1_d, nxt)
        cur = nxt
    kb.begin_phase()
    kb.end_phase()
    return nc


def rope_tables(pos):
    half = 8
    inv = (500000.0 ** (-np.arange(half, dtype=np.float32) / half)).astype(np.float32)
    ang = pos.astype(np.float32)[:, None] * inv[None, :]
    cos, sin = np.cos(ang).astype(np.float32), np.sin(ang).astype(np.float32)
    T = pos.shape[0]
    tab = np.zeros((128, 3, T), np.float32)
    tab[:, 0, :] = 1.0
    for hd in range(2):
        b = hd * 64
        tab[b:b + 8, 0, :] = cos.T
        tab[b + 8:b + 16, 0, :] = cos.T
        tab[b:b + 8, 1, :] = -sin.T
        tab[b + 8:b + 16, 1, :] = sin.T
    tab[:, 2, :] = tab[:, 0, :] * np.float32(0.125)
    return tab


def const_mats():
    c = np.zeros((128, 7, 128), np.float32)
    for m in range(64):
        c[64 + m, 6, m] = 1.0
    c[:, 0, :] = 1.0
    c[:, 1, :] = np.eye(128, dtype=np.float32)
    for hd in range(2):
        for i in range(8):
            a, b = hd * 64 + i, hd * 64 + i + 8
            c[a, 2, b] = 1.0
            c[b, 2, a] = 1.0
    s = np.arange(128)
    c[:, 3, :] = (s[:, None] <= s[None, :]).astype(np.float32)
    same = (s[:, None] // 64) == (s[None, :] // 64)
    c[:, 4, :] = ((s[:, None] <= s[None, :]) & same).astype(np.float32)
    c[:, 5, :] = ((s[:, None] >= s[None, :]) & same).astype(np.float32)
    return c


def make_in_maps(inp, T, seqs_per_core):
    maps = []
    NT, NKC = T // 512, T // 128
    normg = np.ascontiguousarray(inp["norm_g"].reshape(4, 4, 8, 128).transpose(3, 0, 1, 2).reshape(128, 128))
    a_lam = np.ascontiguousarray(np.broadcast_to(inp["a_lambda"].reshape(1, -1), (128, 512)))
    a_sub = np.ascontiguousarray(inp["a_subln_g"].T)
    cwt = np.concatenate([inp["f_conv_w"], inp["f_conv_b"][:, None, :]], axis=1)
    f_cw = np.ascontiguousarray(cwt.reshape(4, 4, 44, 128).transpose(3, 0, 2, 1).reshape(128, 4 * 44 * 4))
    cm = const_mats()
    bm = bmask_table()
    c_lbl = np.ascontiguousarray(inp["c_lb_logits"].reshape(4, 8, 128).transpose(2, 0, 1))
    c_gn = np.ascontiguousarray(inp["c_gnorm_g"].reshape(1, 128).T)
    m01h = np.ones((128, 512), np.float32)
    m01h[:, ::64] = 0.0
    for seqs in seqs_per_core:
        xT = np.ascontiguousarray(np.concatenate(seqs, axis=0).T)
        pos = np.concatenate([np.arange(s.shape[0]) for s in seqs])
        sid = np.concatenate([np.full(s.shape[0], i) for i, s in enumerate(seqs)])
        ksid = sid[::128][:, None]
        qsid = sid[::512][None, :]
        ab = np.where(ksid == qsid, 0.0, NEG).astype(np.float32).reshape(1, NKC * NT)
        flag = np.zeros((128, 2), np.float32)
        flag[:, 0] = 1.0 if len(seqs) == 1 else 0.0
        m = {
            "xT": xT, "normg": normg, "rope": rope_tables(pos), "cmat": cm, "flag": flag,
            "abias": np.ascontiguousarray(np.broadcast_to(ab, (128, NKC * NT))),
            "a_w_qkv": inp["a_w_qkv"], "a_lam": a_lam, "a_sub": a_sub, "a_w_o": inp["a_w_o"],
            "c_w_in": inp["c_w_in"], "c_w_o": inp["c_w_o"], "c_lbl": c_lbl, "c_gn": c_gn, "m01": m01h,
            "b_w_qkv": inp["b_w_qkv"], "b_w_o": inp["b_w_o"], "bmask": bm,
            "f_w_in": inp["f_w_in"], "f_cw": f_cw, "f_w_out": inp["f_w_out"],
        }
        maps.append(m)
    return maps


_NC_CACHE = {}


def kernel(**inputs):
    inp = {k: np.asarray(v) for k, v in inputs.items()}
    T = 8192
    xp, xs = inp["x_prompt"], inp["x_sample"]
    seqs = [[xp[b]] for b in range(4)] + [[xs[2 * c], xs[2 * c + 1]] for c in range(4)]
    maps = make_in_maps(inp, T, seqs)
    if "nc" not in _NC_CACHE:
        _NC_CACHE["nc"] = build(T)
    res = run_bass_kernel_spmd(_NC_CACHE["nc"], maps, core_ids=list(range(8)))
    outs = [np.asarray(r["yT"]).T for r in res.results]
    y_prompt = np.stack(outs[:4], axis=0).astype(np.float32)
    y_sample = np.stack([o.reshape(2, 4096, D) for o in outs[4:]], axis=0).reshape(8, 4096, D).astype(np.float32)
    return (y_prompt, y_sample)
```

```python
import math
from contextlib import ExitStack
import numpy as np
import concourse.bass as bass
import concourse.mybir as mybir
from concourse.bass_utils import run_bass_kernel_spmd

F32 = mybir.dt.float32
BF16 = mybir.dt.bfloat16
AF = mybir.ActivationFunctionType
ALU = mybir.AluOpType
AX = mybir.AxisListType

D = 1024
DFF = 2816
NEG = -30000.0
B_DIL = (1, 4, 16)


def _bmask_list():
    out = []
    for g, d in enumerate(B_DIL):
        for delta in range(-4096, 4097, 128):
            q = np.arange(512)[None, :]
            k = np.arange(128)[:, None]
            diff = delta + q - k
            if np.any((diff % d == 0) & (np.abs(diff) <= 64 * d)):
                out.append((g, delta))
    return out


BMASKS = _bmask_list()


def bmask_table():
    t = np.zeros((128, len(BMASKS), 512), np.float32)
    q = np.arange(512)[None, :]
    k = np.arange(128)[:, None]
    for i, (g, delta) in enumerate(BMASKS):
        d = B_DIL[g]
        diff = delta + q - k
        t[:, i, :] = np.where((diff % d == 0) & (np.abs(diff) <= 64 * d), 1.0, 0.0)
    return t


class Stream:
    def __init__(self, name, eng, sem):
        self.name, self.eng, self.sem = name, eng, sem
        self.count = 0
        self.known = {}


class Buf:
    def __init__(self, t, name):
        self.t = t
        self.name = name
        self.w = {}
        self.r = {}
        self.isdram = False
        self.excl = False
        self.c = None
        self.lsem = None
        self.lcount = 0
        self.ssem = None
        self.scount = 0

    def __getitem__(self, idx):
        return self.t[idx]


class KB:
    def __init__(self, nc):
        self.nc = nc
        self.top = ExitStack()
        self.sems = []
        self.free_dsems = []
        self.all_dsems = {}
        mk = lambda n, e: Stream(n, e, self.top.enter_context(nc.semaphore("s_" + n)))
        self.pe = mk("pe", nc.tensor)
        self.act = mk("act", nc.scalar)
        self.dve = mk("dve", nc.vector)
        self.pool = mk("pool", nc.gpsimd)
        import os
        self.ew2 = self.dve if os.environ.get("KPOOL", "dve") == "dve" else self.pool
        self.sp = mk("sp", nc.sync)
        self.streams = [self.pe, self.act, self.dve, self.pool, self.sp]
        self.phase = None
        self.phase_bufs = []
        self.uid = 0

    def begin_phase(self):
        self.phase = ExitStack()
        self.phase_bufs = []

    def end_phase(self):
        self.barrier()
        for b in self.phase_bufs:
            for s, c in ((b.lsem, b.lcount), (b.ssem, b.scount)):
                if s is not None:
                    self.free_dsems.append((s, c))
        self.phase.close()
        self.phase = None

    def sb(self, name, shape, dt, glob=False):
        self.uid += 1
        es = self.top if glob else self.phase
        t = es.enter_context(self.nc.sbuf_tensor(f"{name}_{self.uid}", list(shape), dt))
        b = Buf(t, name)
        if not glob:
            self.phase_bufs.append(b)
        return b

    def ps(self, name, shape, dt=F32):
        self.uid += 1
        t = self.phase.enter_context(self.nc.psum_tensor(f"{name}_{self.uid}", list(shape), dt))
        b = Buf(t, name)
        b.excl = True
        return b

    def dram(self, name, shape, dt):
        t = self.nc.dram_tensor(name, list(shape), dt, kind="Internal")
        b = Buf(t.ap(), name)
        b.isdram = True
        return b

    def _dsem(self):
        if self.free_dsems:
            return self.free_dsems.pop()
        s = self.top.enter_context(self.nc.semaphore(f"d{len(self.all_dsems)}"))
        self.all_dsems[id(s)] = s
        return (s, 0)

    @staticmethod
    def _expand(bufs):
        out = []
        for b in bufs:
            if b.c is not None:
                out.extend(b.c)
            else:
                out.append(b)
        return out

    def split(self, buf, n):
        buf.c = [Buf(buf.t, f"{buf.name}.{i}") for i in range(n)]
        for ch in buf.c:
            ch.excl = buf.excl
            ch.isdram = buf.isdram
        return buf

    def _deps(self, st, reads, writes, skip=None):
        reads, writes = self._expand(reads), self._expand(writes)
        need = {}

        def add(ev):
            s, v = ev
            if id(s) not in need or need[id(s)][1] < v:
                need[id(s)] = ev
        for b in reads:
            for ev in b.w.values():
                add(ev)
            if b.excl:
                for ev in b.r.values():
                    if ev[0] is not st.sem:
                        add(ev)
        for b in writes:
            if not b.isdram:
                for ev in b.w.values():
                    add(ev)
            for ev in b.r.values():
                add(ev)
        for s, v in need.values():
            if skip is not None and s is skip:
                continue
            if s is st.sem and st is self.pe:
                continue
            if st.known.get(id(s), 0) < v:
                st.eng.wait_ge(s, v)
                st.known[id(s)] = v

    def _mark(self, ev, reads, writes):
        reads, writes = self._expand(reads), self._expand(writes)
        for b in writes:
            if b.isdram:
                b.w[id(ev[0])] = ev
            else:
                b.w = {id(ev[0]): ev}
                b.r = {}
        for b in reads:
            if b not in writes:
                b.r[id(ev[0])] = ev

    def op(self, st, fn, reads=(), writes=()):
        reads, writes = list(reads), list(writes)
        self._deps(st, reads, writes)
        ins = fn(st.eng)
        st.count += 1
        ins.then_inc(st.sem, 1)
        self._mark((st.sem, st.count), reads, writes)
        return ins

    def load(self, sbuf, out_ap, dbuf, in_ap, st=None):
        st = st or self.sp
        if sbuf.lsem is None:
            sbuf.lsem, sbuf.lcount = self._dsem()
        self._deps(st, [dbuf], [sbuf], skip=sbuf.lsem)
        sbuf.lcount += 16
        st.eng.dma_start(out=out_ap, in_=in_ap).then_inc(sbuf.lsem, 16)
        self._mark((sbuf.lsem, sbuf.lcount), [dbuf], [sbuf])

    def store(self, dbuf, out_ap, sbuf, in_ap, st=None):
        st = st or self.pool
        if sbuf.ssem is None:
            sbuf.ssem, sbuf.scount = self._dsem()
        self._deps(st, [sbuf], [dbuf], skip=sbuf.ssem)
        sbuf.scount += 16
        st.eng.dma_start(out=out_ap, in_=in_ap).then_inc(sbuf.ssem, 16)
        self._mark((sbuf.ssem, sbuf.scount), [sbuf], [dbuf])

    def barrier(self):
        cur = [(st.sem, st.count) for st in self.streams]
        for b in self.phase_bufs:
            if b.lsem is not None:
                cur.append((b.lsem, b.lcount))
            if b.ssem is not None:
                cur.append((b.ssem, b.scount))
        for st in self.streams:
            for s, v in cur:
                if s is st.sem or v == 0:
                    continue
                if st.known.get(id(s), 0) < v:
                    st.eng.wait_ge(s, v)
                    st.known[id(s)] = v


def build(T=8192, layers=(0, 1, 2, 3)):
    nc = bass.Bass("TRN2", target_bir_lowering=False)
    kb = KB(nc)
    NT = T // 512
    NKC = T // 128
    HALF = T // 2

    def din(name, shape, dt=F32):
        b = Buf(nc.dram_tensor(name, list(shape), dt, kind="ExternalInput").ap(), name)
        b.isdram = True
        return b

    xT_in = din("xT", [D, T])
    yT_out = Buf(nc.dram_tensor("yT", [D, T], F32, kind="ExternalOutput").ap(), "yT")
    yT_out.isdram = True
    normg_in = din("normg", [128, 4 * 4 * 8])
    rope_in = din("rope", [128, 3, T])
    cmat_in = din("cmat", [128, 7, 128])
    flag_in = din("flag", [128, 2])
    abias_in = din("abias", [128, NKC * NT])
    a_wqkv = din("a_w_qkv", [2, D, 3072])
    a_lam = din("a_lam", [128, 2 * 256])
    a_sub = din("a_sub", [128, 2])
    a_wo = din("a_w_o", [2, D, D])
    b_wqkv = din("b_w_qkv", [1, D, 9216])
    b_wo = din("b_w_o", [1, D, D])
    bmask_in = din("bmask", [128, len(BMASKS), 512])
    c_win = din("c_w_in", [1, D, 5120])
    c_wo = din("c_w_o", [1, D, D])
    c_lbl = din("c_lbl", [128, 4, 8])
    c_gn = din("c_gn", [128, 1])
    m01_in = din("m01", [128, 512])
    f_win = din("f_w_in", [4, D, 2 * DFF])
    f_cw = din("f_cw", [128, 4 * 44 * 4])
    f_wout = din("f_w_out", [4, DFF, D])

    kb.begin_phase()
    ones_bf = kb.sb("ones", [128, 128], BF16, glob=True)
    ident_bf = kb.sb("ident", [128, 128], BF16, glob=True)
    rperm_bf = kb.sb("rperm", [128, 128], BF16, glob=True)
    normg = kb.sb("normg", [128, 128], F32, glob=True)
    flag = kb.sb("flag", [128, 2], F32, glob=True)
    cv = kb.sb("cv", [128, 4], F32, glob=True)
    ones_f = kb.sb("ones_f", [128, 128], F32, glob=True)
    shift_f = kb.sb("shift_f", [128, 64], F32, glob=True)
    cmask = kb.sb("cmask", [128, 2, 128], F32, glob=True)
    m01 = kb.sb("m01", [128, 512], F32, glob=True)
    cst = kb.sb("cst", [128, 7, 128], F32)
    kb.load(cst, cst[:], cmat_in, cmat_in[:])
    kb.load(normg, normg[:], normg_in, normg_in[:])
    kb.load(flag, flag[:], flag_in, flag_in[:])
    kb.op(kb.dve, lambda e: e.memset(cv[:, 0:1], 0.125), [], [cv])
    kb.op(kb.dve, lambda e: e.memset(cv[:, 1:2], 1.0), [cv], [cv])
    kb.op(kb.dve, lambda e: e.memset(cv[:, 2:3], float(128 ** -0.5)), [cv], [cv])
    kb.load(m01, m01[:], m01_in, m01_in[:])
    kb.op(kb.dve, lambda e: e.tensor_copy(out=cmask[:], in_=cst[:, 4:6, :]), [cst], [cmask])
    kb.op(kb.dve, lambda e: e.tensor_copy(out=ones_bf[:], in_=cst[:, 0, :]), [cst], [ones_bf])
    kb.op(kb.dve, lambda e: e.tensor_copy(out=ones_f[:], in_=cst[:, 0, :]), [cst], [ones_f])
    kb.op(kb.dve, lambda e: e.tensor_copy(out=shift_f[:], in_=cst[:, 6, 0:64]), [cst], [shift_f])
    kb.op(kb.dve, lambda e: e.tensor_copy(out=ident_bf[:], in_=cst[:, 1, :]), [cst], [ident_bf])
    kb.op(kb.dve, lambda e: e.tensor_copy(out=rperm_bf[:], in_=cst[:, 2, :]), [cst], [rperm_bf])
    kb.end_phase()

    def gcol(layer, n, c):
        i = (layer * 4 + n) * 8 + c
        return normg[:, i:i + 1]

    cast_rr = [0]

    def load_cast_w(wdst, dst_ap_fn, src_buf, src_ap_fn, ncols, stg=None, kc=8, blk=512):
        for c0 in range(0, ncols, 512):
            c1 = min(ncols, c0 + 512)
            kb.load(wdst, dst_ap_fn(c0, c1), src_buf, src_ap_fn(c0, c1), st=kb.pool)

    def rstd_from(ps_stat, n, nfeat, eps, rstd, tmp):
        kb.op(kb.act, lambda e: e.activation(out=tmp[:, :n], in_=ps_stat[:, :n], func=AF.Ln,
                                             bias=float(eps), scale=1.0 / nfeat), [ps_stat], [tmp])
        kb.op(kb.act, lambda e: e.activation(out=rstd[:, :n], in_=tmp[:, :n], func=AF.Exp, scale=-0.5), [tmp], [rstd])

    def phase_proj_A(layer, wbuf, wap, x_d, QT_d, KT_d, V_d):
        kb.begin_phase()
        W = kb.sb("Wqkv", [128, 8, 3072], BF16)
        wsrc = wap.rearrange("(c p) n -> p c n", p=128)
        load_cast_w(W, lambda a, b: W[:, :, a:b], wbuf, lambda a, b: wsrc[:, :, a:b], 3072)
        xs = [kb.sb(f"xs{i}", [128, 8, 512], F32) for i in range(2)]
        rp = [kb.sb(f"rp{i}", [128, 3, 512], F32) for i in range(2)]
        hT = [kb.split(kb.sb(f"hT{i}", [128, 8, 512], BF16), 8) for i in range(2)]
        sq = [kb.sb(f"sq{i}", [128, 512], BF16) for i in range(2)]
        tmp = kb.sb("tmp", [128, 512], F32)
        rstd = kb.sb("rstd", [128, 512], F32)
        qsb = [kb.sb(f"qsb{i}", [128, 512], BF16) for i in range(2)]
        t1 = [kb.sb(f"t1{i}", [128, 512], F32) for i in range(2)]
        t2 = [kb.sb(f"t2{i}", [128, 512], F32) for i in range(2)]
        qo = [kb.split(kb.sb(f"qo{i}", [128, 8, 512], BF16), 8) for i in range(2)]
        ko = [kb.split(kb.sb(f"ko{i}", [128, 8, 512], BF16), 8) for i in range(2)]
        vo = [kb.split(kb.sb(f"vo{i}", [128, 4, 1024], BF16), 8) for i in range(2)]
        pst = kb.ps("pst", [128, 512])
        pq = [kb.ps(f"pq{i}", [128, 512]) for i in range(3)]
        pr = [kb.ps(f"pr{i}", [128, 512]) for i in range(2)]
        xv = x_d[:].rearrange("(c p) t -> p c t", p=128)
        Vv = V_d[:].rearrange("(n p) f -> p n f", p=128)
        QTv = QT_d[:].rearrange("(c p) t -> p c t", p=128)
        KTv = KT_d[:].rearrange("(c p) t -> p c t", p=128)

        def ld(tt):
            s = tt % 2
            kb.load(xs[s], xs[s][:], x_d, xv[:, :, tt * 512:(tt + 1) * 512])
            kb.load(rp[s], rp[s][:], rope_in, rope_in[:, :, tt * 512:(tt + 1) * 512])

        import os
        KSUB = int(os.environ.get("KSUB", "99"))
        if KSUB < 1:
            kb.end_phase()
            return
        ld(0)
        cnt = 0
        for tt in range(NT):
            s = tt % 2
            if tt + 1 < NT:
                ld(tt + 1)
            x, h = xs[s], hT[s]
            if KSUB < 2:
                continue
            for c in range(8):
                q = sq[c % 2]
                kb.op(kb.act, lambda e: e.activation(out=q[:], in_=x[:, c, :], func=AF.Square), [x], [q])
                kb.op(kb.pe, lambda e: e.matmul(pst[:], lhsT=ones_bf[:], rhs=q[:], start=(c == 0), stop=(c == 7)),
                      [ones_bf, q], [pst])
            if KSUB < 3:
                continue
            rstd_from(pst, 512, D, 1e-6, rstd, tmp)
            if KSUB < 4:
                continue
            for c in range(8):
                kb.op(kb.dve, lambda e: e.scalar_tensor_tensor(out=h[:, c, :], in0=x[:, c, :], scalar=gcol(layer, 0, c),
                                                                 in1=rstd[:], op0=ALU.mult, op1=ALU.mult),
                      [x, rstd, normg], [h.c[c]])
            if KSUB < 5:
                continue
            for m in range(16):
                p = pq[cnt % 3]
                r_ = pr[cnt % 2]
                qs, a1, a2 = qsb[cnt % 2], t1[cnt % 2], t2[cnt % 2]
                dst = (qo if m < 8 else ko)[s]
                scale = 0.125 if m < 8 else 1.0
                cnt += 1

                def mm(e):
                    for k in range(8):
                        i = e.matmul(p[:], lhsT=W[:, k, m * 128:(m + 1) * 128], rhs=h[:, k, :], start=(k == 0), stop=(k == 7))
                    return i
                KQ = int(os.environ.get("KQ", "99"))
                kb.op(kb.pe, mm, [W, h], [p])
                if KQ < 2:
                    continue
                kb.op(kb.act, lambda e: e.activation(out=qs[:], in_=p[:], func=AF.Copy, scale=scale), [p], [qs])
                if KQ < 3:
                    continue
                kb.op(kb.pe, lambda e: e.matmul(r_[:], lhsT=rperm_bf[:], rhs=qs[:], start=True, stop=True), [rperm_bf, qs], [r_])
                if KQ < 4:
                    continue
                kb.op(kb.dve, lambda e: e.tensor_tensor(out=a1[:], in0=p[:], in1=rp[s][:, (2 if m < 8 else 0), :], op=ALU.mult), [p, rp[s]], [a1])
                if KQ < 5:
                    continue
                kb.op(kb.dve, lambda e: e.tensor_tensor(out=a2[:], in0=r_[:], in1=rp[s][:, 1, :], op=ALU.mult), [r_, rp[s]], [a2])
                if KQ < 6:
                    continue
                kb.op(kb.ew2, lambda e: e.tensor_tensor(out=dst[:, m % 8, :], in0=a1[:], in1=a2[:], op=ALU.add), [a1, a2], [dst.c[m % 8]])
            if KSUB < 6:
                continue
            kb.store(QT_d, QTv[:, :, tt * 512:(tt + 1) * 512], qo[s], qo[s][:])
            kb.store(KT_d, KTv[:, :, tt * 512:(tt + 1) * 512], ko[s], ko[s][:])
            if KSUB < 7:
                continue
            for tb in range(4):
                for nb in range(2):
                    p = pq[cnt % 3]
                    cnt += 1

                    def mm(e):
                        for k in range(8):
                            i = e.matmul(p[:], lhsT=h[:, k, tb * 128:(tb + 1) * 128],
                                         rhs=W[:, k, 2048 + nb * 512:2048 + (nb + 1) * 512], start=(k == 0), stop=(k == 7))
                        return i
                    kb.op(kb.pe, mm, [W, h], [p])
                    kb.op(kb.act, lambda e: e.activation(out=vo[s][:, tb, nb * 512:(nb + 1) * 512], in_=p[:], func=AF.Copy), [p], [vo[s].c[tb * 2 + nb]])
            kb.store(V_d, Vv[:, tt * 4:(tt + 1) * 4, :], vo[s], vo[s][:])
        kb.end_phase()

    def phase_attn_A(layer, j, QT_d, KT_d, V_d, oT_d):
        kb.begin_phase()
        lam_init = 0.8 - 0.6 * math.exp(-0.3 * layer)
        NQT = NT
        QTh = [kb.sb(f"QTh{i}", [128, T], BF16) for i in range(2)]
        KTh = [kb.sb(f"KTh{i}", [128, T], BF16) for i in range(2)]
        Vh = [kb.sb(f"Vh{i}", [128, NKC, 128], BF16) for i in range(2)]
        P = [kb.sb(f"P{i}", [128, 1024], BF16) for i in range(3)]
        abias = kb.sb("abias", [128, NKC * NT], F32)
        lamt = kb.sb("lamt", [128, 256], F32)
        lamp = kb.sb("lamp", [128, 128], F32)
        lams = kb.sb("lams", [128, 8], F32)
        gsub = kb.sb("gsub", [128, 2], F32)
        r0 = kb.sb("r0", [128, 512], F32)
        r1 = kb.sb("r1", [128, 512], F32)
        n0 = kb.sb("n0", [128, 512], F32)
        n1 = kb.sb("n1", [128, 512], F32)
        sqb = kb.sb("sqb", [128, 512], BF16)
        tmp = kb.sb("tmp", [128, 512], F32)
        rstd = kb.sb("rstd", [128, 512], F32)
        ob = [kb.sb(f"ob{i}", [128, 512], BF16) for i in range(2)]
        S = [kb.ps(f"S{i}", [128, 1024]) for i in range(2)]
        O0, O1 = kb.ps("O0", [128, 512]), kb.ps("O1", [128, 512])
        L0, L1 = kb.ps("L0", [128, 512]), kb.ps("L1", [128, 512])
        kb.load(abias, abias[:], abias_in, abias_in[:])
        kb.load(lamt, lamt[:], a_lam, a_lam[:, j * 256:(j + 1) * 256])
        kb.load(gsub, gsub[:], a_sub, a_sub[:])
        kb.op(kb.dve, lambda e: e.tensor_tensor(out=lamp[:, 0:64], in0=lamt[:, 0:64], in1=lamt[:, 64:128], op=ALU.mult), [lamt], [lamp])
        kb.op(kb.dve, lambda e: e.tensor_tensor(out=lamp[:, 64:128], in0=lamt[:, 128:192], in1=lamt[:, 192:256], op=ALU.mult), [lamt, lamp], [lamp])
        kb.op(kb.dve, lambda e: e.reduce_sum(out=lams[:, 0:1], in_=lamp[:, 0:64], axis=AX.X), [lamp], [lams])
        kb.op(kb.dve, lambda e: e.reduce_sum(out=lams[:, 1:2], in_=lamp[:, 64:128], axis=AX.X), [lamp, lams], [lams])
        kb.op(kb.act, lambda e: e.activation(out=lams[:, 2:4], in_=lams[:, 0:2], func=AF.Exp), [lams], [lams])
        kb.op(kb.dve, lambda e: e.tensor_tensor(out=lams[:, 4:5], in0=lams[:, 3:4], in1=lams[:, 2:3], op=ALU.subtract), [lams], [lams])
        kb.op(kb.dve, lambda e: e.tensor_scalar_add(out=lams[:, 5:6], in0=lams[:, 4:5], scalar1=-lam_init), [lams], [lams])
        kb.op(kb.dve, lambda e: e.tensor_scalar_mul(out=lams[:, 6:7], in0=gsub[:, j:j + 1], scalar1=1.0 - lam_init), [gsub, lams], [lams])
        neglam = lams[:, 5:6]
        gs = lams[:, 6:7]
        QTv, KTv = QT_d[:], KT_d[:]
        Vv = V_d[:].rearrange("(n p) f -> p n f", p=128)

        def ldh(h):
            s = h % 2
            kb.load(QTh[s], QTh[s][:], QT_d, QTv[h * 128:(h + 1) * 128, :])
            kb.load(KTh[s], KTh[s][:], KT_d, KTv[h * 128:(h + 1) * 128, :])
            step = max(1, NKC // 4)
            for c0 in range(0, NKC, step):
                kb.load(Vh[s], Vh[s][:, c0:c0 + step, :], V_d, Vv[:, c0:c0 + step, h * 128:(h + 1) * 128])

        acc0 = [kb.sb(f"acc0{i}", [128, 512], F32) for i in range(2)]
        acc1 = [kb.sb(f"acc1{i}", [128, 512], F32) for i in range(2)]
        ai = 0
        ldh(0)
        it = 0
        oi = 0
        for h in range(8):
            s = h % 2
            if h + 1 < 8:
                ldh(h + 1)
            Q, K, V = QTh[s], KTh[s], Vh[s]
            for qt in range(NQT):
                def qk(kc, slot):
                    Sx = S[slot]

                    def f(e):
                        e.matmul(Sx[:, 0:512], lhsT=K[0:64, kc * 128:(kc + 1) * 128], rhs=Q[0:64, qt * 512:(qt + 1) * 512],
                                 start=True, stop=True, tile_position=(0, 0))
                        return e.matmul(Sx[:, 512:1024], lhsT=K[64:128, kc * 128:(kc + 1) * 128], rhs=Q[64:128, qt * 512:(qt + 1) * 512],
                                        start=True, stop=True, tile_position=(64, 0))
                    kb.op(kb.pe, f, [K, Q], [Sx])

                qk(0, it % 2)
                Pxs = {}

                def pvA(kc):
                    Px = Pxs[kc]

                    def pv(e):
                        e.matmul(O0[:], lhsT=V[:, kc, :], rhs=Px[:, 0:512], start=(kc == 0), stop=(kc == NKC - 1))
                        e.matmul(O1[:], lhsT=V[:, kc, :], rhs=Px[:, 512:1024], start=(kc == 0), stop=(kc == NKC - 1))
                        return e.matmul(L1[:], lhsT=ones_bf[:], rhs=Px[:, 512:1024], start=(kc == 0), stop=(kc == NKC - 1))
                    kb.op(kb.pe, pv, [V, Px, ones_bf], [O0, O1, L1])
                for kc in range(NKC):
                    slot = it % 2
                    Px = P[it % 3]
                    Pxs[kc] = Px
                    Sx = S[slot]
                    it += 1
                    if kc + 1 < NKC:
                        qk(kc + 1, it % 2)
                    bcol = abias[:, kc * NT + qt:kc * NT + qt + 1]
                    kb.op(kb.act, lambda e: e.activation(out=Px[:], in_=Sx[:], func=AF.Exp, bias=bcol), [Sx, abias], [Px])
                    a0 = acc0[ai % 2]
                    if kc == 0:
                        kb.op(kb.dve, lambda e: e.tensor_copy(out=a0[:], in_=Px[:, 0:512]), [Px], [a0])
                    else:
                        kb.op(kb.dve, lambda e: e.tensor_tensor(out=a0[:], in0=Px[:, 0:512], in1=a0[:], op=ALU.add), [Px, a0], [a0])
                    if kc >= 1:
                        pvA(kc - 1)
                pvA(NKC - 1)
                a0 = acc0[ai % 2]
                ai += 1
                kb.op(kb.pe, lambda e: e.matmul(L0[:], lhsT=ones_f[:], rhs=a0[:], start=True, stop=True), [ones_f, a0], [L0])
                kb.op(kb.act, lambda e: e.activation(out=n1[:], in_=L0[:], func=AF.Ln), [L0], [n1])
                kb.op(kb.act, lambda e: e.activation(out=r0[:], in_=n1[:], func=AF.Exp, scale=-1.0), [n1], [r0])
                kb.op(kb.act, lambda e: e.activation(out=n1[:], in_=L1[:], func=AF.Ln), [L1, r0], [n1])
                kb.op(kb.act, lambda e: e.activation(out=r1[:], in_=n1[:], func=AF.Exp, scale=-1.0), [n1], [r1])
                kb.op(kb.dve, lambda e: e.tensor_tensor(out=n0[:], in0=O0[:], in1=r0[:], op=ALU.mult), [O0, r0], [n0])
                kb.op(kb.dve, lambda e: e.tensor_tensor(out=n1[:], in0=O1[:], in1=r1[:], op=ALU.mult), [O1, r1], [n1])
                kb.op(kb.dve, lambda e: e.scalar_tensor_tensor(out=n0[:], in0=n1[:], scalar=neglam, in1=n0[:], op0=ALU.mult, op1=ALU.add),
                      [n1, n0, lams], [n0])
                kb.op(kb.act, lambda e: e.activation(out=sqb[:], in_=n0[:], func=AF.Square), [n0], [sqb])
                kb.op(kb.pe, lambda e: e.matmul(L0[:], lhsT=ones_bf[:], rhs=sqb[:], start=True, stop=True), [ones_bf, sqb], [L0])
                rstd_from(L0, 512, 128, 1e-5, rstd, tmp)
                o = ob[oi % 2]
                oi += 1
                kb.op(kb.dve, lambda e: e.scalar_tensor_tensor(out=o[:], in0=n0[:], scalar=gs, in1=rstd[:], op0=ALU.mult, op1=ALU.mult),
                      [n0, rstd, lams], [o])
                kb.store(oT_d, oT_d[h * 128:(h + 1) * 128, qt * 512:(qt + 1) * 512], o, o[:])
        kb.end_phase()

    def phase_attn_B(layer, QTb, KTb, Vb, oT_d):
        kb.begin_phase()
        NM = len(BMASKS)
        midx = {gd: i for i, gd in enumerate(BMASKS)}
        Qg = [kb.sb(f"Qg{g}", [128, T], BF16) for g in range(3)]
        Kg = [kb.sb(f"Kg{g}", [128, T], BF16) for g in range(3)]
        Vg = [kb.sb(f"Vg{g}", [128, NKC, 128], BF16) for g in range(3)]
        masks = kb.sb("masks", [128, NM, 512], BF16)
        abias = kb.sb("abias", [128, NKC * NT], F32)
        P = [kb.sb(f"P{i}", [128, 1024], BF16) for i in range(2)]
        PM = [kb.sb(f"PM{i}", [128, 1024], BF16) for i in range(3)]
        rr = kb.sb("rr", [128, 512], F32)
        rbs = kb.sb("rbs", [128, 512], F32)
        ob = [kb.sb(f"ob{i}", [128, 512], BF16) for i in range(2)]
        S = [kb.ps(f"S{i}", [128, 1024]) for i in range(2)]
        OX = [kb.ps("OA", [128, 512]), kb.ps("OB", [128, 512])]
        RB = kb.ps("RB", [128, 512])
        kb.load(abias, abias[:], abias_in, abias_in[:])
        kb.op(kb.dve, lambda e: e.memset(rr[:], 0.0), [], [rr])
        for i0_ in range(0, NM, 4):
            i1_ = min(NM, i0_ + 4)
            kb.load(masks, masks[:, i0_:i1_, :], bmask_in, bmask_in[:, i0_:i1_, :], st=kb.pool)
        it = 0
        oi = 0
        for hp in range(8):
            for g in range(3):
                kb.load(Qg[g], Qg[g][:], QTb[g], QTb[g][hp * 128:(hp + 1) * 128, :])
                kb.load(Kg[g], Kg[g][:], KTb[g], KTb[g][hp * 128:(hp + 1) * 128, :])
                Vv = Vb[g][:].rearrange("(n p) f -> p n f", p=128)
                step = max(1, NKC // 4)
                for c0 in range(0, NKC, step):
                    kb.load(Vg[g], Vg[g][:, c0:c0 + step, :], Vb[g], Vv[:, c0:c0 + step, hp * 128:(hp + 1) * 128])
            for qt in range(NT):
                tiles = [(g, kc) for g in range(3) for kc in range(NKC) if (g, qt * 512 - kc * 128) in midx]
                qs_ = slice(qt * 512, (qt + 1) * 512)

                def qkB(ti, slot):
                    g, kc = tiles[ti]
                    Sx = S[slot]
                    ks_ = slice(kc * 128, (kc + 1) * 128)

                    def f(e):
                        e.matmul(Sx[:, 0:512], lhsT=Kg[g][0:64, ks_], rhs=Qg[g][0:64, qs_], start=True, stop=True, tile_position=(0, 0))
                        return e.matmul(Sx[:, 512:1024], lhsT=Kg[g][64:128, ks_], rhs=Qg[g][64:128, qs_], start=True, stop=True,
                                        tile_position=(64, 0))
                    kb.op(kb.pe, f, [Kg[g], Qg[g]], [Sx])
                qkB(0, it % 2)
                nt_ = len(tiles)
                Pms = {}

                def pvBt(ti):
                    g, kc = tiles[ti]
                    Pm = Pms[ti]

                    def pvB(e):
                        for a in range(2):
                            e.matmul(OX[a][0:64, :], lhsT=Vg[g][:, kc, a * 64:(a + 1) * 64], rhs=Pm[:, a * 512:(a + 1) * 512],
                                     start=(ti == 0), stop=(ti == nt_ - 1), tile_position=(0, 0))
                            i = e.matmul(OX[a][64:128, :], lhsT=ones_bf[:, 0:64], rhs=Pm[:, a * 512:(a + 1) * 512],
                                         start=(ti == 0), stop=(ti == nt_ - 1), tile_position=(0, 64))
                        return i
                    kb.op(kb.pe, pvB, [Vg[g], Pm, ones_bf], [OX[0], OX[1]])
                for ti, (g, kc) in enumerate(tiles):
                    Sx = S[it % 2]
                    Px = P[it % 2]
                    Pm = PM[it % 3]
                    Pms[ti] = Pm
                    it += 1
                    if ti + 1 < nt_:
                        qkB(ti + 1, it % 2)
                    mi = midx[(g, qt * 512 - kc * 128)]
                    bcol = abias[:, kc * NT + qt:kc * NT + qt + 1]
                    kb.op(kb.act, lambda e: e.activation(out=Px[:], in_=Sx[:], func=AF.Exp, bias=bcol), [Sx, abias], [Px])
                    kb.op(kb.dve, lambda e: e.tensor_tensor(out=Pm[:].rearrange("p (a q) -> p a q", a=2),
                                                            in0=Px[:].rearrange("p (a q) -> p a q", a=2),
                                                            in1=masks[:, mi:mi + 1, :].to_broadcast([128, 2, 512]), op=ALU.mult), [Px, masks], [Pm])
                    if ti >= 1:
                        pvBt(ti - 1)
                pvBt(nt_ - 1)
                for a in range(2):
                    kb.op(kb.dve, lambda e: e.reciprocal(out=rr[64:128, :], in_=OX[a][64:128, :]), [OX[a]], [rr])
                    kb.op(kb.pe, lambda e: e.matmul(RB[0:64, :], lhsT=shift_f[:], rhs=rr[:], start=True, stop=True), [shift_f, rr], [RB])
                    kb.op(kb.act, lambda e: e.activation(out=rbs[0:64, :], in_=RB[0:64, :], func=AF.Copy), [RB], [rbs])
                    o = ob[oi % 2]
                    oi += 1
                    kb.op(kb.dve, lambda e: e.tensor_tensor(out=o[0:64, :], in0=OX[a][0:64, :], in1=rbs[0:64, :], op=ALU.mult), [OX[a], rbs], [o])
                    r_0 = a * 64
                    kb.store(oT_d, oT_d[hp * 128 + r_0:hp * 128 + r_0 + 64, qs_], o, o[0:64, :])
        kb.end_phase()

    def phase_proj_C(layer, x_d, qt_d, kt_d, kh_d, et_d, V_d, gT_d):
        kb.begin_phase()
        W = kb.sb("Wc", [128, 8, 5120], BF16)
        wsrc = c_win[0].rearrange("(c p) n -> p c n", p=128)
        load_cast_w(W, lambda a, b: W[:, :, a:b], c_win, lambda a, b: wsrc[:, :, a:b], 5120)
        lbl = kb.sb("lbl", [128, 4, 8], F32)
        lbe = kb.sb("lbe", [128, 4, 8], F32)
        lbs = kb.sb("lbs", [128, 4, 8], F32)
        kb.load(lbl, lbl[:], c_lbl, c_lbl[:])
        kb.op(kb.act, lambda e: e.activation(out=lbe[:], in_=lbl[:], func=AF.Exp), [lbl], [lbe])
        kb.op(kb.dve, lambda e: e.tensor_tensor(out=lbs[:, 0, :], in0=lbe[:, 0, :], in1=lbe[:, 1, :], op=ALU.add), [lbe], [lbs])
        kb.op(kb.dve, lambda e: e.tensor_tensor(out=lbs[:, 0, :], in0=lbs[:, 0, :], in1=lbe[:, 2, :], op=ALU.add), [lbe, lbs], [lbs])
        kb.op(kb.dve, lambda e: e.tensor_tensor(out=lbs[:, 0, :], in0=lbs[:, 0, :], in1=lbe[:, 3, :], op=ALU.add), [lbe, lbs], [lbs])
        kb.op(kb.dve, lambda e: e.tensor_copy(out=lbs[:, 1, :], in_=lbe[:, 1, :]), [lbe, lbs], [lbs])
        for i in range(2, layer + 1):
            kb.op(kb.dve, lambda e: e.tensor_tensor(out=lbs[:, 1, :], in0=lbs[:, 1, :], in1=lbe[:, i, :], op=ALU.add), [lbe, lbs], [lbs])
        kb.op(kb.dve, lambda e: e.reciprocal(out=lbs[:, 2, :], in_=lbs[:, 0, :]), [lbs], [lbs])
        kb.op(kb.dve, lambda e: e.tensor_tensor(out=lbs[:, 3, :], in0=lbs[:, 0, :], in1=lbs[:, 1, :], op=ALU.subtract), [lbs], [lbs])
        kb.op(kb.dve, lambda e: e.tensor_tensor(out=lbs[:, 3, :], in0=lbs[:, 3, :], in1=lbs[:, 2, :], op=ALU.mult), [lbs], [lbs])
        xs = [kb.sb(f"xs{i}", [128, 8, 512], F32) for i in range(2)]
        hT = kb.sb("hT", [128, 8, 512], BF16)
        sq = [kb.sb(f"sq{i}", [128, 512], BF16) for i in range(2)]
        tmp = kb.sb("tmp", [128, 512], F32)
        rstd = kb.sb("rstd", [128, 512], F32)
        F = lambda n: kb.sb(n, [128, 512], F32)
        tsets = [[F(n + str(i)) for n in ("sg", "kk", "lf", "G", "Gd", "Ek", "eG", "enG", "eK")] for i in range(2)]
        qsets = [F("qs0"), F("qs1")]
        gate = [kb.sb(f"gate{i}", [128, 512], BF16) for i in range(2)]
        qo = [kb.sb(f"qo{i}", [128, 512], BF16) for i in range(2)]
        ko = [kb.sb(f"ko{i}", [128, 512], BF16) for i in range(2)]
        kh = [kb.sb(f"kh{i}", [128, 512], BF16) for i in range(2)]
        kht = [kb.sb(f"kht{i}", [128, 4, 128], BF16) for i in range(2)]
        eto = [kb.sb(f"eto{i}", [128, 8], F32) for i in range(2)]
        vo = kb.sb("vo", [128, 4, 1024], BF16)
        pst = kb.ps("pst", [128, 512])
        pq = [kb.ps(f"pq{i}", [128, 512]) for i in range(3)]
        ptr = [kb.ps(f"ptr{i}", [128, 128], BF16) for i in range(2)]
        xv = x_d[:].rearrange("(c p) t -> p c t", p=128)
        Vv = V_d[:].rearrange("(n p) f -> p n f", p=128)
        khv = [kh_d[d][:].rearrange("(n p) f -> p n f", p=128) for d in range(2)]

        def ld(tt):
            kb.load(xs[tt % 2], xs[tt % 2][:], x_d, xv[:, :, tt * 512:(tt + 1) * 512])
        ld(0)
        cnt = 0
        oc = 0
        for tt in range(NT):
            if tt + 1 < NT:
                ld(tt + 1)
            x, h = xs[tt % 2], hT
            for c in range(8):
                q = sq[c % 2]
                kb.op(kb.act, lambda e: e.activation(out=q[:], in_=x[:, c, :], func=AF.Square), [x], [q])
                kb.op(kb.pe, lambda e: e.matmul(pst[:], lhsT=ones_bf[:], rhs=q[:], start=(c == 0), stop=(c == 7)), [ones_bf, q], [pst])
            rstd_from(pst, 512, D, 1e-6, rstd, tmp)
            for c in range(8):
                kb.op(kb.dve, lambda e: e.scalar_tensor_tensor(out=h[:, c, :], in0=x[:, c, :], scalar=gcol(layer, 0, c),
                                                                 in1=rstd[:], op0=ALU.mult, op1=ALU.mult), [x, rstd, normg], [h])

            def proj(col0):
                nonlocal cnt
                p = pq[cnt % 3]
                cnt += 1

                def mm(e):
                    for k in range(8):
                        i = e.matmul(p[:], lhsT=W[:, k, col0:col0 + 128], rhs=h[:, k, :], start=(k == 0), stop=(k == 7))
                    return i
                kb.op(kb.pe, mm, [W, h], [p])
                return p
            csl = slice(tt * 512, (tt + 1) * 512)
            for hd in range(8):
                rows = slice(hd * 128, (hd + 1) * 128)
                p = proj(hd * 128)
                qs = qsets[hd % 2]
                kb.op(kb.act, lambda e: e.activation(out=qs[:], in_=p[:], func=AF.Silu), [p], [qs])
                p = proj(4096 + hd * 128)
                gt = gate[oc % 2]
                kb.op(kb.act, lambda e: e.activation(out=gt[:], in_=p[:], func=AF.Silu), [p], [gt])
                kb.store(gT_d, gT_d[rows, csl], gt, gt[:])
                for d in range(2):
                    qo_, ko_, kh_, kht_, eto_ = qo[oc % 2], ko[oc % 2], kh[oc % 2], kht[oc % 2], eto[oc % 2]
                    sg, kk, lf, G, Gd, Ek, eG, enG, eK = tsets[oc % 2]
                    oc += 1
                    p = proj(1024 + d * 1024 + hd * 128)
                    kb.op(kb.act, lambda e: e.activation(out=sg[:], in_=p[:], func=AF.Sigmoid, scale=-1.0), [p], [sg])
                    kb.op(kb.dve, lambda e: e.tensor_scalar_mul(out=kk[:], in0=sg[:], scalar1=lbs[:, 3, hd:hd + 1]), [sg, lbs], [kk])
                    kb.op(kb.act, lambda e: e.activation(out=lf[:], in_=kk[:], func=AF.Ln, scale=-1.0, bias=1.0), [kk], [lf])
                    kb.op(kb.dve, lambda e: e.tensor_tensor_scan(out=G[:], data0=m01[:], data1=lf[:], initial=0.0, op0=ALU.mult, op1=ALU.add),
                          [m01, lf], [G])
                    G3 = G[:].rearrange("p (c t) -> p c t", t=64)
                    TOTb = G3[:, :, 63:64].to_broadcast([128, 8, 64])
                    r3 = lambda b: b[:].rearrange("p (c t) -> p c t", t=64)
                    if d == 0:
                        Gdb = G
                        kb.op(kb.dve, lambda e: e.tensor_tensor(out=r3(Ek), in0=TOTb, in1=G3, op=ALU.subtract), [G], [Ek])
                    else:
                        Gdb = Gd
                        kb.op(kb.dve, lambda e: e.tensor_tensor(out=Ek[:], in0=G[:], in1=lf[:], op=ALU.subtract), [G, lf], [Ek])
                        kb.op(kb.dve, lambda e: e.tensor_tensor(out=r3(Gd), in0=TOTb, in1=r3(Ek), op=ALU.subtract), [G, Ek], [Gd])
                    kb.op(kb.act, lambda e: e.activation(out=eG[:], in_=Gdb[:], func=AF.Exp), [Gdb], [eG])
                    kb.op(kb.act, lambda e: e.activation(out=enG[:], in_=Gdb[:], func=AF.Exp, scale=-1.0), [Gdb], [enG])
                    kb.op(kb.act, lambda e: e.activation(out=eK[:], in_=Ek[:], func=AF.Exp), [Ek], [eK])
                    kb.op(kb.act, lambda e: e.activation(out=eto_[:], in_=G3[:, :, 63], func=AF.Exp), [G], [eto_])
                    kb.op(kb.dve, lambda e: e.scalar_tensor_tensor(out=qo_[:], in0=qs[:], scalar=cv[:, 2:3], in1=eG[:], op0=ALU.mult, op1=ALU.mult),
                          [qs, cv, eG], [qo_])
                    kb.op(kb.dve, lambda e: e.tensor_tensor(out=ko_[:], in0=kk[:], in1=enG[:], op=ALU.mult), [kk, enG], [ko_])
                    kb.op(kb.dve, lambda e: e.tensor_tensor(out=kh_[:], in0=kk[:], in1=eK[:], op=ALU.mult), [kk, eK], [kh_])
                    for tb in range(4):
                        pt = ptr[tb % 2]
                        kb.op(kb.pe, lambda e: e.transpose(pt[:], kh_[:, tb * 128:(tb + 1) * 128], ident_bf[:]), [kh_, ident_bf], [pt])
                        kb.op(kb.act, lambda e: e.activation(out=kht_[:, tb, :], in_=pt[:], func=AF.Copy), [pt], [kht_])
                    kb.store(qt_d[d], qt_d[d][rows, csl], qo_, qo_[:])
                    kb.store(kt_d[d], kt_d[d][rows, csl], ko_, ko_[:])
                    kb.store(kh_d[d], khv[d][:, tt * 4:(tt + 1) * 4, rows], kht_, kht_[:])
                    kb.store(et_d[d], et_d[d][rows, tt * 8:(tt + 1) * 8], eto_, eto_[:])
            for tb in range(4):
                for nb in range(2):
                    p = pq[cnt % 3]
                    cnt += 1

                    def mm(e):
                        for k in range(8):
                            i = e.matmul(p[:], lhsT=h[:, k, tb * 128:(tb + 1) * 128],
                                         rhs=W[:, k, 3072 + nb * 512:3072 + (nb + 1) * 512], start=(k == 0), stop=(k == 7))
                        return i
                    kb.op(kb.pe, mm, [W, h], [p])
                    kb.op(kb.act, lambda e: e.activation(out=vo[:, tb, nb * 512:(nb + 1) * 512], in_=p[:], func=AF.Copy), [p], [vo])
            kb.store(V_d, Vv[:, tt * 4:(tt + 1) * 4, :], vo, vo[:])
        kb.end_phase()

    def phase_scan_C(layer, qt_d, kt_d, kh_d, et_d, V_d, gT_d, oT_d):
        kb.begin_phase()
        NC = T // 64
        Qt = [kb.sb(f"Qt{i}", [128, T], BF16) for i in range(2)]
        Kt = [kb.sb(f"Kt{i}", [128, T], BF16) for i in range(2)]
        Kh = [kb.sb(f"Kh{i}", [128, NKC, 128], BF16) for i in range(2)]
        Et = [kb.sb(f"Et{i}", [128, NC], F32) for i in range(2)]
        Vh = kb.sb("Vh", [128, NKC, 128], BF16)
        Gt = kb.sb("Gt", [128, T], BF16)
        ofw = kb.split(kb.sb("ofw", [128, T], F32), NT)
        obw = kb.split(kb.sb("obw", [128, T], F32), NT)
        Sfs = [kb.sb(f"Sf{i}", [128, 128], F32) for i in range(2)]
        Sbs = [kb.sb(f"Sb{i}", [128, 128], BF16) for i in range(2)]
        sc = [kb.sb(f"sc{i}", [128, 128], BF16) for i in range(2)]
        gn = kb.sb("gn", [128, 1], F32)
        sqb = kb.sb("sqb", [128, 512], BF16)
        tmp = kb.sb("tmp", [128, 512], F32)
        rstd = kb.sb("rstd", [128, 512], F32)
        ob = [kb.sb(f"ob{i}", [128, 512], BF16) for i in range(2)]
        scp = [kb.ps(f"scp{i}", [128, 128]) for i in range(2)]
        op_ = [kb.ps(f"op{i}", [128, 128]) for i in range(2)]
        dsp = [kb.ps(f"dsp{i}", [128, 128]) for i in range(2)]
        pst = kb.ps("pst", [128, 512])
        kb.load(gn, gn[:], c_gn, c_gn[:])
        Vv = V_d[:].rearrange("(n p) f -> p n f", p=128)
        khv = [kh_d[d][:].rearrange("(n p) f -> p n f", p=128) for d in range(2)]
        step = max(1, NKC // 4)
        it = 0
        oi = 0
        ci = 0
        for hd in range(8):
            rows = slice(hd * 128, (hd + 1) * 128)
            for c0 in range(0, NKC, step):
                kb.load(Vh, Vh[:, c0:c0 + step, :], V_d, Vv[:, c0:c0 + step, rows])
            kb.load(Gt, Gt[:], gT_d, gT_d[rows, :])
            for d in range(2):
                kb.load(Qt[d], Qt[d][:], qt_d[d], qt_d[d][rows, :])
                kb.load(Kt[d], Kt[d][:], kt_d[d], kt_d[d][rows, :])
                kb.load(Et[d], Et[d][:], et_d[d], et_d[d][rows, :])
                for c0 in range(0, NKC, step):
                    kb.load(Kh[d], Kh[d][:, c0:c0 + step, :], kh_d[d], khv[d][:, c0:c0 + step, rows])
            for d in range(2):
                kb.op(kb.dve, lambda e: e.memset(Sfs[d][:], 0.0), [], [Sfs[d]])
                kb.op(kb.dve, lambda e: e.memset(Sbs[d][:], 0.0), [], [Sbs[d]])
            for step_ in range(NKC):
                for d in range(2):
                    Q, K, KH, ET = Qt[d], Kt[d], Kh[d], Et[d]
                    Sf, Sb = Sfs[d], Sbs[d]
                    b = step_ if d == 0 else NKC - 1 - step_
                    bs = slice(b * 128, (b + 1) * 128)
                    scp_, sc_, o_ = scp[it % 2], sc[it % 2], op_[it % 2]
                    it += 1
                    kb.op(kb.pe, lambda e: e.matmul(scp_[:], lhsT=K[:, bs], rhs=Q[:, bs], start=True, stop=True), [K, Q], [scp_])
                    kb.op(kb.dve, lambda e: e.tensor_tensor(out=sc_[:], in0=scp_[:], in1=cmask[:, d, :], op=ALU.mult), [scp_, cmask], [sc_])
                    kb.op(kb.pe, lambda e: e.matmul(o_[:], lhsT=Vh[:, b, :], rhs=sc_[:], start=True, stop=False), [Vh, sc_], [o_])
                    chunks = (2 * b, 2 * b + 1) if d == 0 else (2 * b + 1, 2 * b)
                    for n_, c in enumerate(chunks):
                        p0 = (c % 2) * 64
                        first_of_other = (c == NC // 2) if d == 0 else (c == NC // 2 - 1)
                        if first_of_other:
                            kb.op(kb.dve, lambda e: e.tensor_scalar_mul(out=Sf[:], in0=Sf[:], scalar1=flag[:, 0:1]), [Sf, flag], [Sf])
                            kb.op(kb.act, lambda e: e.activation(out=Sb[:], in_=Sf[:], func=AF.Copy), [Sf], [Sb])
                        kb.op(kb.pe, lambda e: e.matmul(o_[:, p0:p0 + 64], lhsT=Sb[:], rhs=Q[:, c * 64:(c + 1) * 64], start=False, stop=(n_ == 1)),
                              [Sb, Q], [o_])
                        ds_ = dsp[ci % 2]
                        ci += 1
                        kb.op(kb.pe, lambda e: e.matmul(ds_[:], lhsT=KH[p0:p0 + 64, b, :], rhs=Vh[p0:p0 + 64, b, :], start=True, stop=True,
                                                        tile_position=(p0, 0)), [KH, Vh], [ds_])
                        kb.op(kb.dve, lambda e: e.scalar_tensor_tensor(out=Sf[:], in0=Sf[:], scalar=ET[:, c:c + 1], in1=ds_[:],
                                                                         op0=ALU.mult, op1=ALU.add), [Sf, ET, ds_], [Sf])
                        kb.op(kb.act, lambda e: e.activation(out=Sb[:], in_=Sf[:], func=AF.Copy), [Sf], [Sb])
                    dst_ = ofw if d == 0 else obw
                    kb.op(kb.act, lambda e: e.activation(out=dst_[:, bs], in_=o_[:], func=AF.Copy), [o_], [dst_.c[b // 4]])
            for tt in range(NT):
                csl = slice(tt * 512, (tt + 1) * 512)
                kb.op(kb.dve, lambda e: e.tensor_tensor(out=ofw[:, csl], in0=ofw[:, csl], in1=obw[:, csl], op=ALU.add),
                      [ofw.c[tt], obw.c[tt]], [ofw.c[tt]])
            for tt in range(NT):
                csl = slice(tt * 512, (tt + 1) * 512)
                kb.op(kb.act, lambda e: e.activation(out=sqb[:], in_=ofw[:, csl], func=AF.Square), [ofw], [sqb])
                kb.op(kb.pe, lambda e: e.matmul(pst[:], lhsT=ones_bf[:], rhs=sqb[:], start=True, stop=True), [ones_bf, sqb], [pst])
                rstd_from(pst, 512, 128, 1e-6, rstd, tmp)
                kb.op(kb.dve, lambda e: e.scalar_tensor_tensor(out=tmp[:], in0=ofw[:, csl], scalar=gn[:, 0:1], in1=rstd[:], op0=ALU.mult, op1=ALU.mult),
                      [ofw, gn, rstd], [tmp])
                o = ob[oi % 2]
                oi += 1
                kb.op(kb.dve, lambda e: e.tensor_tensor(out=o[:], in0=tmp[:], in1=Gt[:, csl], op=ALU.mult), [tmp, Gt], [o])
                kb.store(oT_d, oT_d[rows, csl], o, o[:])
        kb.end_phase()

    def phase_wo(layer, wo_buf, wo_ap, oT_d, x_d, x1_d):
        kb.begin_phase()
        Wo = kb.sb("Wo", [128, 8, 1024], BF16)
        wsrc = wo_ap.rearrange("(c p) n -> p c n", p=128)
        load_cast_w(Wo, lambda a, b: Wo[:, :, a:b], wo_buf, lambda a, b: wsrc[:, :, a:b], 1024)
        xs = [kb.sb(f"xs{i}", [128, 8, 512], F32) for i in range(2)]
        os_ = [kb.sb(f"os{i}", [128, 8, 512], BF16) for i in range(2)]
        y = kb.split(kb.sb("y", [128, 8, 512], F32), 8)
        sq = [kb.sb(f"sq{i}", [128, 512], BF16) for i in range(2)]
        tmp = kb.sb("tmp", [128, 512], F32)
        rstd = kb.sb("rstd", [128, 512], F32)
        xo = [kb.split(kb.sb(f"xo{i}", [128, 8, 512], F32), 8) for i in range(2)]
        pst = kb.ps("pst", [128, 512])
        pq = [kb.ps(f"pq{i}", [128, 512]) for i in range(3)]
        xv = x_d[:].rearrange("(c p) t -> p c t", p=128)
        x1v = x1_d[:].rearrange("(c p) t -> p c t", p=128)
        ov = oT_d[:].rearrange("(c p) t -> p c t", p=128)

        def ld(tt):
            s = tt % 2
            kb.load(xs[s], xs[s][:], x_d, xv[:, :, tt * 512:(tt + 1) * 512])
            kb.load(os_[s], os_[s][:], oT_d, ov[:, :, tt * 512:(tt + 1) * 512])
        ld(0)
        cnt = 0
        for tt in range(NT):
            s = tt % 2
            if tt + 1 < NT:
                ld(tt + 1)
            x, o = xs[s], os_[s]
            for m in range(8):
                p = pq[cnt % 3]
                q = sq[cnt % 2]
                cnt += 1

                def mm(e):
                    for k in range(8):
                        i = e.matmul(p[:], lhsT=Wo[:, k, m * 128:(m + 1) * 128], rhs=o[:, k, :], start=(k == 0), stop=(k == 7))
                    return i
                kb.op(kb.pe, mm, [Wo, o], [p])
                kb.op(kb.act, lambda e: e.activation(out=q[:], in_=p[:], func=AF.Square), [p], [q])
                kb.op(kb.dve, lambda e: e.tensor_copy(out=y[:, m, :], in_=p[:]), [p], [y.c[m]])
                kb.op(kb.pe, lambda e: e.matmul(pst[:], lhsT=ones_bf[:], rhs=q[:], start=(m == 0), stop=(m == 7)), [ones_bf, q], [pst])
            rstd_from(pst, 512, D, 1e-6, rstd, tmp)
            for m in range(8):
                kb.op(kb.dve, lambda e: e.scalar_tensor_tensor(out=y[:, m, :], in0=y[:, m, :], scalar=gcol(layer, 1, m), in1=rstd[:],
                                                                 op0=ALU.mult, op1=ALU.mult), [y.c[m], rstd, normg], [y.c[m]])
                kb.op(kb.ew2, lambda e: e.tensor_tensor(out=xo[s][:, m, :], in0=y[:, m, :], in1=x[:, m, :], op=ALU.add), [y.c[m], x], [xo[s].c[m]])
            kb.store(x1_d, x1v[:, :, tt * 512:(tt + 1) * 512], xo[s], xo[s][:])
        kb.end_phase()

    def phase_ffn(layer, x1_d, x2_d):
        kb.begin_phase()
        NV = 256
        NW = NV + 2
        Win = kb.sb("Win", [128, 8, 2 * DFF], BF16)
        Wout = kb.sb("Wout", [128, 22, 1024], BF16)
        wsrc = f_win[layer].rearrange("(c p) n -> p c n", p=128)
        load_cast_w(Win, lambda a, b: Win[:, :, a:b], f_win, lambda a, b: wsrc[:, :, a:b], 2 * DFF)
        wsrc2 = f_wout[layer].rearrange("(c p) n -> p c n", p=128)
        for c0 in range(0, 22, 6):
            c1 = min(22, c0 + 6)
            kb.load(Wout, Wout[:, c0:c1, :], f_wout, wsrc2[:, c0:c1, :], st=kb.pool)
        cw = kb.sb("cw", [128, 44, 4], F32)
        kb.load(cw, cw[:], f_cw, f_cw[:, layer * 176:(layer + 1) * 176].rearrange("p (c f) -> p c f", f=4))
        xw = [kb.sb(f"xw{i}", [128, 8, NW], F32) for i in range(2)]
        h = kb.split(kb.sb("h", [128, 8, NW], BF16), 8)
        sq = [kb.sb(f"sq{i}", [128, NW], BF16) for i in range(2)]
        tmp = kb.sb("tmp", [128, NW], F32)
        rstd = kb.sb("rstd", [128, NW], F32)
        ta = [kb.sb(f"ta{i}", [128, NV], F32) for i in range(2)]
        tb_ = [kb.sb(f"tb{i}", [128, NV], F32) for i in range(2)]
        ga = [kb.sb(f"ga{i}", [128, NV], F32) for i in range(2)]
        gg = kb.split(kb.sb("gg", [128, 22, NV], BF16), 22)
        y = kb.split(kb.sb("y", [128, 8, NV], F32), 8)
        xo = [kb.split(kb.sb(f"xo{i}", [128, 8, NV], F32), 8) for i in range(2)]
        pst = kb.ps("pst", [128, 512])
        pa = [kb.ps(f"pa{i}", [128, 512]) for i in range(2)]
        pb = [kb.ps(f"pb{i}", [128, 512]) for i in range(2)]
        po = [kb.ps(f"po{i}", [128, 512]) for i in range(2)]
        xv = x1_d[:].rearrange("(c p) t -> p c t", p=128)
        x2v = x2_d[:].rearrange("(c p) t -> p c t", p=128)
        wins = list(range(0, T, NV))

        def ld(wi):
            s0 = wins[wi]
            b = xw[wi % 2]
            lo, hi = s0 - 1, s0 + NV + 1
            clo, chi = max(lo, 0), min(hi, T)
            if clo > lo:
                kb.op(kb.pool, lambda e: e.memset(b[:, :, 0:1], 0.0), [], [b])
            if chi < hi:
                kb.op(kb.pool, lambda e: e.memset(b[:, :, NW - 1:NW], 0.0), [], [b])
            kb.load(b, b[:, :, clo - lo:NW - (hi - chi)], x1_d, xv[:, :, clo:chi])
        ld(0)
        cnt = 0
        for wi, s0 in enumerate(wins):
            if wi + 1 < len(wins):
                ld(wi + 1)
            x = xw[wi % 2]
            for c in range(8):
                q = sq[c % 2]
                kb.op(kb.act, lambda e: e.activation(out=q[:], in_=x[:, c, :], func=AF.Square), [x], [q])
                kb.op(kb.pe, lambda e: e.matmul(pst[:, :NW], lhsT=ones_bf[:], rhs=q[:], start=(c == 0), stop=(c == 7)), [ones_bf, q], [pst])
            rstd_from(pst, NW, D, 1e-6, rstd, tmp)
            for c in range(8):
                kb.op(kb.dve, lambda e: e.scalar_tensor_tensor(out=h[:, c, :], in0=x[:, c, :], scalar=gcol(layer, 2, c), in1=rstd[:],
                                                                 op0=ALU.mult, op1=ALU.mult), [x, rstd, normg], [h.c[c]])
            if s0 == HALF:
                kb.op(kb.dve, lambda e: e.tensor_scalar_mul(out=h[:, :, 0:1], in0=h[:, :, 0:1], scalar1=flag[:, 0:1]), [h, flag], [h])
            if s0 + NV == HALF:
                kb.op(kb.dve, lambda e: e.tensor_scalar_mul(out=h[:, :, NW - 1:NW], in0=h[:, :, NW - 1:NW], scalar1=flag[:, 0:1]), [h, flag], [h])
            for jj in range(22):
                A, B = pa[jj % 2], pb[jj % 2]
                a_, b_, g_ = ta[jj % 2], tb_[jj % 2], ga[jj % 2]

                def mma(e):
                    for k in range(8):
                        i = e.matmul(A[:, :NW], lhsT=Win[:, k, jj * 128:(jj + 1) * 128], rhs=h[:, k, :], start=(k == 0), stop=(k == 7))
                    return i

                def mmb(e):
                    for k in range(8):
                        i = e.matmul(B[:, :NW], lhsT=Win[:, k, DFF + jj * 128:DFF + (jj + 1) * 128], rhs=h[:, k, :], start=(k == 0), stop=(k == 7))
                    return i
                kb.op(kb.pe, mma, [Win, h], [A])
                kb.op(kb.pe, mmb, [Win, h], [B])
                for (Pp, tt_, ci, eng2) in ((A, a_, jj, kb.dve), (B, b_, 22 + jj, kb.dve)):
                    kb.op(kb.act, lambda e: e.activation(out=tt_[:], in_=Pp[:, 1:NV + 1], func=AF.Identity,
                                                         scale=cw[:, ci, 1:2], bias=cw[:, ci, 3:4]), [Pp, cw], [tt_])
                    kb.op(eng2, lambda e: e.scalar_tensor_tensor(out=tt_[:], in0=Pp[:, 0:NV], scalar=cw[:, ci, 0:1], in1=tt_[:],
                                                                  op0=ALU.mult, op1=ALU.add), [Pp, cw, tt_], [tt_])
                    kb.op(eng2, lambda e: e.scalar_tensor_tensor(out=tt_[:], in0=Pp[:, 2:NV + 2], scalar=cw[:, ci, 2:3], in1=tt_[:],
                                                                  op0=ALU.mult, op1=ALU.add), [Pp, cw, tt_], [tt_])
                kb.op(kb.act, lambda e: e.activation(out=g_[:], in_=a_[:], func=AF.Gelu_apprx_tanh), [a_], [g_])
                kb.op(kb.pool, lambda e: e.tensor_tensor(out=gg[:, jj, :], in0=g_[:], in1=b_[:], op=ALU.mult), [g_, b_], [gg.c[jj]])
            for m in range(8):
                p = po[m % 2]
                q = sq[m % 2]

                def mm(e):
                    for k in range(22):
                        i = e.matmul(p[:, :NV], lhsT=Wout[:, k, m * 128:(m + 1) * 128], rhs=gg[:, k, :], start=(k == 0), stop=(k == 21))
                    return i
                kb.op(kb.pe, mm, [Wout, gg], [p])
                kb.op(kb.act, lambda e: e.activation(out=q[:, :NV], in_=p[:, :NV], func=AF.Square), [p], [q])
                kb.op(kb.dve, lambda e: e.tensor_copy(out=y[:, m, :], in_=p[:, :NV]), [p], [y.c[m]])
                kb.op(kb.pe, lambda e: e.matmul(pst[:, :NV], lhsT=ones_bf[:], rhs=q[:, :NV], start=(m == 0), stop=(m == 7)), [ones_bf, q], [pst])
            rstd_from(pst, NV, D, 1e-6, rstd, tmp)
            xo_ = xo[wi % 2]
            for m in range(8):
                kb.op(kb.dve, lambda e: e.scalar_tensor_tensor(out=y[:, m, :], in0=y[:, m, :], scalar=gcol(layer, 3, m), in1=rstd[:, :NV],
                                                                 op0=ALU.mult, op1=ALU.mult), [y.c[m], rstd, normg], [y.c[m]])
                kb.op(kb.pool, lambda e: e.tensor_tensor(out=xo_[:, m, :], in0=y[:, m, :], in1=x[:, m, 1:NV + 1], op=ALU.add), [y.c[m], x], [xo_.c[m]])
            kb.store(x2_d, x2v[:, :, s0:s0 + NV], xo_, xo_[:])
        kb.end_phase()

    QTb = [kb.dram(f"QTb{g}", [D, T], BF16) for g in range(3)]
    KTb = [kb.dram(f"KTb{g}", [D, T], BF16) for g in range(3)]
    Vb = [kb.dram(f"Vb{g}", [T, D], BF16) for g in range(3)]
    cq_d = [kb.dram(f"cq{d}", [D, T], BF16) for d in range(2)]
    ck_d = [kb.dram(f"ck{d}", [D, T], BF16) for d in range(2)]
    ckh_d = [kb.dram(f"ckh{d}", [T, D], BF16) for d in range(2)]
    cet_d = [kb.dram(f"cet{d}", [D, T // 64], F32) for d in range(2)]
    QT_d = kb.dram("QT_d", [D, T], BF16)
    KT_d = kb.dram("KT_d", [D, T], BF16)
    V_d = kb.dram("V_d", [T, D], BF16)
    oT_d = kb.dram("oT_d", [D, T], BF16)
    x1_d = kb.dram("x1_d", [D, T], F32)
    xa_d = kb.dram("xa_d", [D, T], F32)
    xb_d = kb.dram("xb_d", [D, T], F32)
    cur = xT_in
    for li, layer in enumerate(layers):
        last = li == len(layers) - 1
        nxt = yT_out if last else (xa_d if li % 2 == 0 else xb_d)
        kind = layer % 3
        j = layer // 3
        import os
        stop = int(os.environ.get("KSTOP", "99"))
        if kind == 0:
            if stop >= 1:
                phase_proj_A(layer, a_wqkv, a_wqkv[j], cur, QT_d, KT_d, V_d)
            if stop >= 2:
                phase_attn_A(layer, j, QT_d, KT_d, V_d, oT_d)
            if stop >= 3:
                phase_wo(layer, a_wo, a_wo[j], oT_d, cur, x1_d)
        elif kind == 1:
            for g in range(3):
                phase_proj_A(layer, b_wqkv, b_wqkv[0][:, g * 3072:(g + 1) * 3072], cur, QTb[g], KTb[g], Vb[g])
            phase_attn_B(layer, QTb, KTb, Vb, oT_d)
            phase_wo(layer, b_wo, b_wo[0], oT_d, cur, x1_d)
        else:
            phase_proj_C(layer, cur, cq_d, ck_d, ckh_d, cet_d, V_d, KT_d)
            phase_scan_C(layer, cq_d, ck_d, ckh_d, cet_d, V_d, KT_d, oT_d)
            phase_wo(layer, c_wo, c_wo[0], oT_d, cur, x1_d)
        if stop >= 4:
            phase_ffn(layer, x1_d, nxt)
        cur = nxt
    kb.begin_phase()
    kb.end_phase()
    return nc


def rope_tables(pos):
    half = 8
    inv = (500000.0 ** (-np.arange(half, dtype=np.float32) / half)).astype(np.float32)
    ang = pos.astype(np.float32)[:, None] * inv[None, :]
    cos, sin = np.cos(ang).astype(np.float32), np.sin(ang).astype(np.float32)
    T = pos.shape[0]
    tab = np.zeros((128, 3, T), np.float32)
    tab[:, 0, :] = 1.0
    for hd in range(2):
        b = hd * 64
        tab[b:b + 8, 0, :] = cos.T
        tab[b + 8:b + 16, 0, :] = cos.T
        tab[b:b + 8, 1, :] = -sin.T
        tab[b + 8:b + 16, 1, :] = sin.T
    tab[:, 2, :] = tab[:, 0, :] * np.float32(0.125)
    return tab


def const_mats():
    c = np.zeros((128, 7, 128), np.float32)
    for m in range(64):
        c[64 + m, 6, m] = 1.0
    c[:, 0, :] = 1.0
    c[:, 1, :] = np.eye(128, dtype=np.float32)
    for hd in range(2):
        for i in range(8):
            a, b = hd * 64 + i, hd * 64 + i + 8
            c[a, 2, b] = 1.0
            c[b, 2, a] = 1.0
    s = np.arange(128)
    c[:, 3, :] = (s[:, None] <= s[None, :]).astype(np.float32)
    same = (s[:, None] // 64) == (s[None, :] // 64)
    c[:, 4, :] = ((s[:, None] <= s[None, :]) & same).astype(np.float32)
    c[:, 5, :] = ((s[:, None] >= s[None, :]) & same).astype(np.float32)
    return c


def make_in_maps(inp, T, seqs_per_core):
    maps = []
    NT, NKC = T // 512, T // 128
    normg = np.ascontiguousarray(inp["norm_g"].reshape(4, 4, 8, 128).transpose(3, 0, 1, 2).reshape(128, 128))
    a_lam = np.ascontiguousarray(np.broadcast_to(inp["a_lambda"].reshape(1, -1), (128, 512)))
    a_sub = np.ascontiguousarray(inp["a_subln_g"].T)
    cwt = np.concatenate([inp["f_conv_w"], inp["f_conv_b"][:, None, :]], axis=1)
    f_cw = np.ascontiguousarray(cwt.reshape(4, 4, 44, 128).transpose(3, 0, 2, 1).reshape(128, 4 * 44 * 4))
    cm = const_mats()
    bm = bmask_table()
    c_lbl = np.ascontiguousarray(inp["c_lb_logits"].reshape(4, 8, 128).transpose(2, 0, 1))
    c_gn = np.ascontiguousarray(inp["c_gnorm_g"].reshape(1, 128).T)
    m01h = np.ones((128, 512), np.float32)
    m01h[:, ::64] = 0.0
    for seqs in seqs_per_core:
        xT = np.ascontiguousarray(np.concatenate(seqs, axis=0).T)
        pos = np.concatenate([np.arange(s.shape[0]) for s in seqs])
        sid = np.concatenate([np.full(s.shape[0], i) for i, s in enumerate(seqs)])
        ksid = sid[::128][:, None]
        qsid = sid[::512][None, :]
        ab = np.where(ksid == qsid, 0.0, NEG).astype(np.float32).reshape(1, NKC * NT)
        flag = np.zeros((128, 2), np.float32)
        flag[:, 0] = 1.0 if len(seqs) == 1 else 0.0
        m = {
            "xT": xT, "normg": normg, "rope": rope_tables(pos), "cmat": cm, "flag": flag,
            "abias": np.ascontiguousarray(np.broadcast_to(ab, (128, NKC * NT))),
            "a_w_qkv": inp["a_w_qkv"], "a_lam": a_lam, "a_sub": a_sub, "a_w_o": inp["a_w_o"],
            "c_w_in": inp["c_w_in"], "c_w_o": inp["c_w_o"], "c_lbl": c_lbl, "c_gn": c_gn, "m01": m01h,
            "b_w_qkv": inp["b_w_qkv"], "b_w_o": inp["b_w_o"], "bmask": bm,
            "f_w_in": inp["f_w_in"], "f_cw": f_cw, "f_w_out": inp["f_w_out"],
        }
        maps.append(m)
    return maps


_NC_CACHE = {}


def kernel(**inputs):
    inp = {k: np.asarray(v) for k, v in inputs.items()}
    T = 8192
    xp, xs = inp["x_prompt"], inp["x_sample"]
    seqs = [[xp[b]] for b in range(4)] + [[xs[2 * c], xs[2 * c + 1]] for c in range(4)]
    maps = make_in_maps(inp, T, seqs)
    if "nc" not in _NC_CACHE:
        _NC_CACHE["nc"] = build(T)
    res = run_bass_kernel_spmd(_NC_CACHE["nc"], maps, core_ids=list(range(8)))
    outs = [np.asarray(r["yT"]).T for r in res.results]
    y_prompt = np.stack(outs[:4], axis=0).astype(np.float32)
    y_sample = np.stack([o.reshape(2, 4096, D) for o in outs[4:]], axis=0).reshape(8, 4096, D).astype(np.float32)
    return (y_prompt, y_sample)
```
